# Optimizing a Trainium2 kernel written in Bass

```python
import math
import jax, jax.numpy as jnp
from jax import lax
import numpy as np

D_MODEL = 1024
BATCH = 4
SEQ = 4096
DEPTH = 1

NSA_HEADS = 8
NSA_KV_GROUPS = 2
NSA_HPG = NSA_HEADS // NSA_KV_GROUPS
NSA_HEAD_DIM = 64
NSA_WIDTH = NSA_HEADS * NSA_HEAD_DIM
KV_WIDTH = NSA_KV_GROUPS * NSA_HEAD_DIM
CMP_BLOCK = 32
CMP_STRIDE = 16
CMP_HIDDEN = 128
SEL_BLOCK = 64
SEL_TOPK = 16
WINDOW = 512
Q_BLOCK = 128

S5_GROUP = 16
S5_WIDTH = 512
S5_GROUPS = S5_WIDTH // S5_GROUP
S5_STATE = 64
DT_MIN = 1e-3
DT_MAX = 1e-1

D_FF = 2816
CONV_WIDTH = 3

COLS_Q = NSA_WIDTH
COLS_KV = 6 * KV_WIDTH
COLS_NSA_GATE = 3 * NSA_HEADS
COLS_S5 = S5_WIDTH
COLS_MERGE = 2 * D_MODEL
IN_COLS = COLS_Q + COLS_KV + COLS_NSA_GATE + COLS_S5 + COLS_MERGE

LN_EPS = 1e-5
NEG_INF = -1e30
SEL_FORCE = 1e9

kernel_name = 'hybrid_nsa_s5_convffn_deepnorm_adaln'


def deepnorm_alpha():
    return (2.0 * DEPTH) ** 0.25


def deepnorm_beta():
    return (8.0 * DEPTH) ** -0.25


def layer_norm(x):
    xf = x.astype(jnp.float32)
    mu = jnp.mean(xf, axis=-1, keepdims=True)
    var = jnp.mean(jnp.square(xf - mu), axis=-1, keepdims=True)
    return ((xf - mu) * lax.rsqrt(var + LN_EPS)).astype(x.dtype)


def masked_softmax(s, mask):
    s = jnp.where(mask, s.astype(jnp.float32), NEG_INF)
    p = jax.nn.softmax(s, axis=-1)
    return p * jnp.any(mask, axis=-1, keepdims=True)


def alibi_slopes(n):
    return jnp.asarray(2.0 ** (-8.0 * (np.arange(n) + 1) / n), dtype=jnp.float32)


def compress(raw, pe, w1, w2):
    b, s, g, dh = raw.shape
    ch = raw.reshape(b, s // CMP_STRIDE, CMP_STRIDE, g, dh)
    blocks = jnp.concatenate([ch[:, :-1], ch[:, 1:]], axis=2)
    blocks = blocks + pe[None, None, :, None, :]
    h = jax.nn.silu(jnp.einsum('bnlgd,lde->bnge', blocks, w1))
    return jnp.einsum('bnge,ed->bngd', h, w2)


def nsa_attention(q, kc, vc, ks, vs, kw, vw, gate_logits):
    b, s, g, hpg, dh = q.shape
    nqb = s // Q_BLOCK
    nc = kc.shape[1]
    ns = s // SEL_BLOCK
    topk = min(SEL_TOPK, ns)
    scale = dh ** -0.5
    slopes = alibi_slopes(NSA_HEADS).reshape(g, hpg)
    cmp_pos = jnp.arange(nc) * CMP_STRIDE + CMP_BLOCK - 1
    cstart = jnp.arange(nc) * CMP_STRIDE
    sstart = jnp.arange(ns) * SEL_BLOCK
    overlap = ((cstart[:, None] < sstart[None, :] + SEL_BLOCK)
               & (cstart[:, None] + CMP_BLOCK > sstart[None, :])).astype(jnp.float32)
    ks_blk = ks.reshape(b, ns, SEL_BLOCK, g, dh).transpose(0, 3, 1, 2, 4)
    vs_blk = vs.reshape(b, ns, SEL_BLOCK, g, dh).transpose(0, 3, 1, 2, 4)
    n_back = WINDOW // Q_BLOCK
    pad = ((0, 0), (WINDOW, 0), (0, 0), (0, 0))
    kw_pad = jnp.pad(kw, pad).reshape(b, nqb + n_back, Q_BLOCK, g, dh)
    vw_pad = jnp.pad(vw, pad).reshape(b, nqb + n_back, Q_BLOCK, g, dh)
    kw_band = jnp.concatenate([kw_pad[:, i:i + nqb] for i in range(n_back + 1)], axis=2)
    vw_band = jnp.concatenate([vw_pad[:, i:i + nqb] for i in range(n_back + 1)], axis=2)
    b_ix = jnp.arange(b)[:, None, None, None]
    g_ix = jnp.arange(g)[None, :, None, None]
    sel_iota = jnp.arange(ns)

    def block(args):
        qi, qb, gb, kwb, vwb = args
        t = qi * Q_BLOCK + jnp.arange(Q_BLOCK)
        qb = qb * scale
        dist_c = t[:, None] - cmp_pos[None, :]
        s_c = (jnp.einsum('btghd,bngd->bghtn', qb, kc).astype(jnp.float32)
               - slopes[:, :, None, None] * dist_c.astype(jnp.float32))
        p_c = masked_softmax(s_c, dist_c >= 0)
        o_c = jnp.einsum('bghtn,bngd->btghd', p_c.astype(vc.dtype), vc)
        imp = jnp.einsum('bghtn,ns->bgts', p_c, overlap)
        cur = t // SEL_BLOCK
        forced = ((sel_iota[None, :] == 0) | (sel_iota[None, :] == cur[:, None])
                  | (sel_iota[None, :] == cur[:, None] - 1))
        valid = sel_iota[None, :] <= cur[:, None]
        imp = jnp.where(forced, SEL_FORCE, jnp.where(valid, imp, -SEL_FORCE))
        _, idx = lax.top_k(imp, topk)
        sel_ok = idx <= cur[None, None, :, None]
        kg = ks_blk[b_ix, g_ix, idx]
        vg = vs_blk[b_ix, g_ix, idx]
        kpos = idx[..., None] * SEL_BLOCK + jnp.arange(SEL_BLOCK)
        dist_s = t[None, None, :, None, None] - kpos
        s_s = (jnp.einsum('btghd,bgtnkd->bghtnk', qb, kg).astype(jnp.float32)
               - slopes[None, :, :, None, None, None] * dist_s[:, :, None].astype(jnp.float32))
        mask_s = (sel_ok[..., None] & (dist_s >= 0))[:, :, None]
        p_s = masked_softmax(s_s.reshape(b, g, hpg, Q_BLOCK, topk * SEL_BLOCK),
                             mask_s.reshape(b, g, 1, Q_BLOCK, topk * SEL_BLOCK))
        o_s = jnp.einsum('bghtm,bgtmd->btghd', p_s.astype(vg.dtype),
                         vg.reshape(b, g, Q_BLOCK, topk * SEL_BLOCK, dh))
        kpos_w = qi * Q_BLOCK - WINDOW + jnp.arange(WINDOW + Q_BLOCK)
        dist_w = t[:, None] - kpos_w[None, :]
        mask_w = (dist_w >= 0) & (dist_w < WINDOW) & (kpos_w[None, :] >= 0)
        s_w = (jnp.einsum('btghd,bsgd->bghts', qb, kwb).astype(jnp.float32)
               - slopes[:, :, None, None] * dist_w.astype(jnp.float32))
        p_w = masked_softmax(s_w, mask_w)
        o_w = jnp.einsum('bghts,bsgd->btghd', p_w.astype(vwb.dtype), vwb)
        gt = jax.nn.sigmoid(gb)
        return gt[..., 0:1] * o_c + gt[..., 1:2] * o_s + gt[..., 2:3] * o_w

    xs = (jnp.arange(nqb),
          jnp.moveaxis(q.reshape(b, nqb, Q_BLOCK, g, hpg, dh), 1, 0),
          jnp.moveaxis(gate_logits.reshape(b, nqb, Q_BLOCK, g, hpg, 3), 1, 0),
          jnp.moveaxis(kw_band, 1, 0),
          jnp.moveaxis(vw_band, 1, 0))
    out = lax.map(block, xs)
    return jnp.moveaxis(out, 0, 1).reshape(b, s, g * hpg * dh)


def complex_affine_combine(e1, e2):
    a1r, a1i, b1r, b1i = e1
    a2r, a2i, b2r, b2i = e2
    return (a2r * a1r - a2i * a1i,
            a2r * a1i + a2i * a1r,
            a2r * b1r - a2i * b1i + b2r,
            a2r * b1i + a2i * b1r + b2i)


def s5_ssm(u, a_re, a_im, b_re, b_im, c_re, c_im, d_skip, log_dt):
    b, s, _ = u.shape
    u5 = u.reshape(b, s, S5_GROUPS, S5_GROUP)
    dt = jnp.exp(log_dt)[:, None]
    lam_re = jnp.minimum(a_re, -1e-4)
    lam_im = a_im
    mag = jnp.exp(lam_re * dt)
    ang = lam_im * dt
    lb_re = mag * jnp.cos(ang)
    lb_im = mag * jnp.sin(ang)
    den = lam_re * lam_re + lam_im * lam_im
    nr = lb_re - 1.0
    coef_re = (nr * lam_re + lb_im * lam_im) / den
    coef_im = (lb_im * lam_re - nr * lam_im) / den
    bb_re = coef_re[..., None] * b_re - coef_im[..., None] * b_im
    bb_im = coef_re[..., None] * b_im + coef_im[..., None] * b_re
    bu_re = jnp.einsum('bsgh,gph->bsgp', u5, bb_re)
    bu_im = jnp.einsum('bsgh,gph->bsgp', u5, bb_im)
    elems = (jnp.broadcast_to(lb_re, bu_re.shape), jnp.broadcast_to(lb_im, bu_im.shape), bu_re, bu_im)
    _, _, x_re, x_im = lax.associative_scan(complex_affine_combine, elems, axis=1)
    y = (jnp.einsum('bsgp,ghp->bsgh', x_re, c_re) - jnp.einsum('bsgp,ghp->bsgh', x_im, c_im)
         + d_skip * u5)
    return y.reshape(b, s, S5_WIDTH)


def causal_depthwise_conv(h, w, bias):
    ch = h.shape[-1]
    out = lax.conv_general_dilated(h, w[:, None, :], window_strides=(1,),
                                   padding=[(CONV_WIDTH - 1, 0)],
                                   dimension_numbers=('NWC', 'WIO', 'NWC'),
                                   feature_group_count=ch)
    return out + bias


def setup_inputs(seed: int = 0) -> dict:
    key = jax.random.key(seed)
    ks = jax.random.split(key, 32)
    f32 = jnp.float32
    L = DEPTH
    beta = deepnorm_beta()

    def nrm(k, shape, std):
        return jax.random.normal(k, shape, f32) * std

    a_re = -0.5 * (1.0 + 0.01 * jax.random.normal(ks[12], (L, S5_GROUPS, S5_STATE), f32))
    a_im = (jnp.pi * jnp.arange(S5_STATE, dtype=f32))[None, None, :] + 0.01 * jax.random.normal(ks[13], (L, S5_GROUPS, S5_STATE), f32)
    return {
        'x': nrm(ks[0], (BATCH, SEQ, D_MODEL), 1.0),
        'c': nrm(ks[1], (BATCH, D_MODEL), 1.0),
        'w_ada': nrm(ks[2], (L, D_MODEL, 6 * D_MODEL), D_MODEL ** -0.5),
        'b_ada': nrm(ks[3], (L, 6 * D_MODEL), 0.01),
        'w_in': nrm(ks[4], (L, D_MODEL, IN_COLS), D_MODEL ** -0.5),
        'pe_ck': nrm(ks[5], (L, CMP_BLOCK, NSA_HEAD_DIM), 0.1),
        'w_ck1': nrm(ks[6], (L, CMP_BLOCK, NSA_HEAD_DIM, CMP_HIDDEN), (CMP_BLOCK * NSA_HEAD_DIM) ** -0.5),
        'w_ck2': nrm(ks[7], (L, CMP_HIDDEN, NSA_HEAD_DIM), CMP_HIDDEN ** -0.5),
        'pe_cv': nrm(ks[8], (L, CMP_BLOCK, NSA_HEAD_DIM), 0.1),
        'w_cv1': nrm(ks[9], (L, CMP_BLOCK, NSA_HEAD_DIM, CMP_HIDDEN), (CMP_BLOCK * NSA_HEAD_DIM) ** -0.5),
        'w_cv2': nrm(ks[10], (L, CMP_HIDDEN, NSA_HEAD_DIM), CMP_HIDDEN ** -0.5),
        'w_nsa_out': nrm(ks[11], (L, NSA_WIDTH, D_MODEL), NSA_WIDTH ** -0.5),
        's5_a_re': a_re,
        's5_a_im': a_im,
        's5_b_re': nrm(ks[14], (L, S5_GROUPS, S5_STATE, S5_GROUP), (2 * S5_GROUP) ** -0.5),
        's5_b_im': nrm(ks[15], (L, S5_GROUPS, S5_STATE, S5_GROUP), (2 * S5_GROUP) ** -0.5),
        's5_c_re': nrm(ks[16], (L, S5_GROUPS, S5_GROUP, S5_STATE), S5_STATE ** -0.5),
        's5_c_im': nrm(ks[17], (L, S5_GROUPS, S5_GROUP, S5_STATE), S5_STATE ** -0.5),
        's5_d': nrm(ks[18], (L, S5_GROUPS, S5_GROUP), 1.0),
        's5_log_dt': jax.random.uniform(ks[19], (L, S5_GROUPS), f32, math.log(DT_MIN), math.log(DT_MAX)),
        'w_s5_glu': nrm(ks[20], (L, S5_WIDTH, 2 * D_MODEL), S5_WIDTH ** -0.5),
        'w_o': nrm(ks[21], (L, D_MODEL, D_MODEL), beta * D_MODEL ** -0.5),
        'ln1_g': 1.0 + nrm(ks[22], (L, D_MODEL), 0.01),
        'ln1_b': nrm(ks[23], (L, D_MODEL), 0.01),
        'w_up': nrm(ks[24], (L, D_MODEL, 2 * D_FF), D_MODEL ** -0.5),
        'conv_w': nrm(ks[25], (L, CONV_WIDTH, 2 * D_FF), CONV_WIDTH ** -0.5),
        'conv_b': nrm(ks[26], (L, 2 * D_FF), 0.01),
        'w_down': nrm(ks[27], (L, D_FF, D_MODEL), beta * D_FF ** -0.5),
        'ln2_g': 1.0 + nrm(ks[28], (L, D_MODEL), 0.01),
        'ln2_b': nrm(ks[29], (L, D_MODEL), 0.01),
    }


def reference(x, c, w_ada, b_ada, w_in, pe_ck, w_ck1, w_ck2, pe_cv, w_cv1, w_cv2, w_nsa_out,
              s5_a_re, s5_a_im, s5_b_re, s5_b_im, s5_c_re, s5_c_im, s5_d, s5_log_dt, w_s5_glu,
              w_o, ln1_g, ln1_b, w_up, conv_w, conv_b, w_down, ln2_g, ln2_b):
    b, s, d = x.shape
    alpha = deepnorm_alpha()
    splits = np.cumsum([COLS_Q, COLS_KV, COLS_NSA_GATE, COLS_S5]).tolist()
    for l in range(DEPTH):
        mod = jax.nn.silu(c) @ w_ada[l] + b_ada[l]
        shift1, scale1, gate1, shift2, scale2, gate2 = [m[:, None, :] for m in jnp.split(mod, 6, axis=-1)]

        h = layer_norm(x) * (1.0 + scale1) + shift1
        z = h @ w_in[l]
        zq, zkv, zg, zs, zm = jnp.split(z, splits, axis=-1)
        q = zq.reshape(b, s, NSA_KV_GROUPS, NSA_HPG, NSA_HEAD_DIM)
        kv = zkv.reshape(b, s, 6, NSA_KV_GROUPS, NSA_HEAD_DIM)
        kc = compress(kv[:, :, 0], pe_ck[l], w_ck1[l], w_ck2[l])
        vc = compress(kv[:, :, 1], pe_cv[l], w_cv1[l], w_cv2[l])
        o_a = nsa_attention(q, kc, vc, kv[:, :, 2], kv[:, :, 3], kv[:, :, 4], kv[:, :, 5],
                            zg.reshape(b, s, NSA_KV_GROUPS, NSA_HPG, 3))
        y_a = o_a @ w_nsa_out[l]
        y_s = s5_ssm(zs, s5_a_re[l], s5_a_im[l], s5_b_re[l], s5_b_im[l], s5_c_re[l], s5_c_im[l],
                     s5_d[l], s5_log_dt[l])
        zz = jax.nn.gelu(y_s) @ w_s5_glu[l]
        y_b = zz[..., :d] * jax.nn.sigmoid(zz[..., d:])
        g_a, g_b = jnp.split(zm, 2, axis=-1)
        mix = (jax.nn.sigmoid(g_a) * y_a + jax.nn.sigmoid(g_b) * y_b) @ w_o[l]
        x = layer_norm(alpha * x + gate1 * mix) * ln1_g[l] + ln1_b[l]

        h2 = layer_norm(x) * (1.0 + scale2) + shift2
        up = causal_depthwise_conv(h2 @ w_up[l], conv_w[l], conv_b[l])
        val, gte = jnp.split(up, 2, axis=-1)
        ff = (jax.nn.silu(gte) * val) @ w_down[l]
        x = layer_norm(alpha * x + gate2 * ff) * ln2_g[l] + ln2_b[l]
    return x
```

```python
from contextlib import ExitStack
import math
import numpy as np
import concourse.bass as bass
import concourse.mybir as mybir
from concourse.bass_utils import run_bass_kernel_spmd

F32 = mybir.dt.float32
BF16 = mybir.dt.bfloat16
AF = mybir.ActivationFunctionType
ALU = mybir.AluOpType

T = 4096
NT = 32
D = 1024
KC = 8
DFF = 2816
NEG = -30000.0
ALPHA = 2.0 ** 0.25
STOP = [99]
Q0 = 15
NQ = NT - Q0
DBG = {}


class Buf:
    __slots__ = ("name", "w", "r", "dsem", "dcnt")

    def __init__(self, name):
        self.name = name
        self.w = None
        self.r = {}
        self.dsem = None
        self.dcnt = 0


class KB:
    def __init__(self, nc, es):
        self.nc = nc
        self.ges = es
        self.es = es
        self.eng = {"pe": nc.tensor, "act": nc.scalar, "dve": nc.vector,
                    "pool": nc.gpsimd, "sp": nc.sync}
        self.sem = {}
        self.cnt = {}
        self.seen = {}
        for e in self.eng:
            self.sem[e] = es.enter_context(nc.semaphore("s_" + e))
            self.cnt[e] = 0
            self.seen[e] = {}
        self.pool_sems = []
        self.live = []
        self.n = 0
        self.uid = 0

    def sb(self, name, shape, dt):
        self.uid += 1
        return self.es.enter_context(self.nc.sbuf_tensor(f"{name}_{self.uid}", list(shape), dt))

    def ps(self, name, shape, dt):
        return self.ges.enter_context(self.nc.psum_tensor(name, list(shape), dt))

    def buf(self, name="b"):
        return Buf(name)

    def _wait(self, e, ev):
        if ev is None:
            return
        sem, val = ev
        key = id(sem)
        if self.seen[e].get(key, 0) >= val:
            return
        self.eng[e].wait_ge(sem, val)
        self.seen[e][key] = val

    def _deps(self, e, reads, writes):
        for b in reads:
            self._wait(e, b.w)
        for b in writes:
            self._wait(e, b.w)
            for ev in list(b.r.values()):
                self._wait(e, ev)

    def _record(self, ev, reads, writes):
        for b in reads:
            if b not in writes:
                b.r[id(ev[0])] = ev
        for b in writes:
            b.w = ev
            b.r = {}

    def op(self, e, fn, reads=(), writes=()):
        self._deps(e, reads, writes)
        ins = fn(self.eng[e])
        self.cnt[e] += 1
        ins.then_inc(self.sem[e], 1)
        ev = (self.sem[e], self.cnt[e])
        if e == "pe":
            self.seen[e][id(self.sem[e])] = self.cnt[e]
        self._record(ev, reads, writes)
        self.n += 1
        return ins

    def dma(self, q, out, in_, dbuf, reads=(), writes=(), **kw):
        self._deps(q, reads, writes)
        if dbuf.dsem is None:
            if self.pool_sems:
                dbuf.dsem, dbuf.dcnt = self.pool_sems.pop()
            else:
                dbuf.dsem = self.ges.enter_context(self.nc.semaphore(f"d{len(self.live)}_{self.n}"))
                dbuf.dcnt = 0
            self.live.append(dbuf)
        ins = self.eng[q].dma_start(out=out, in_=in_, **kw)
        dbuf.dcnt += 16
        ins.then_inc(dbuf.dsem, 16)
        ev = (dbuf.dsem, dbuf.dcnt)
        self._record(ev, reads, writes)
        self.n += 1
        return ins

    def barrier(self, release=True):
        for e in self.eng:
            for f in self.eng:
                if f != e and self.cnt[f] > 0:
                    self._wait(e, (self.sem[f], self.cnt[f]))
            for b in self.live:
                self._wait(e, (b.dsem, b.dcnt))
        if release:
            for b in self.live:
                self.pool_sems.append((b.dsem, b.dcnt))
                b.dsem = None
            self.live = []


def make_tables(hf=1):
    t = {}
    t["ident"] = np.eye(128, dtype=np.float32)
    slopes = (2.0 ** (-8.0 * (np.arange(8) + 1) / 8)).astype(np.float32)
    tl = np.arange(128, dtype=np.float32)
    qa = np.zeros((3, 8, 128), np.float32)
    qb = np.zeros((3, 8, 128), np.float32)
    for h in range(8):
        qa[0, h, :] = slopes[h]
        qa[1, h, :] = slopes[h] * 128.0
        qa[2, h, :] = -slopes[h] * tl
        qb[2, h, :] = -slopes[h] * 128.0
    t["qaugA"] = qa
    t["qaugB"] = qb
    key = np.arange(T)
    kt = np.zeros((64, T), np.float32)
    kt[0] = key % 128
    kt[1] = key // 128
    kt[2] = 1.0
    if hf == 0:
        kt[1, :T // 2] = -8192.0
    for j in range(1, 62):
        kt[2 + j] = (key // 64 == j)
    t["kaug_tok"] = kt
    m = np.arange(256)
    pos = 16 * m + 15
    kc = np.zeros((3, 256), np.float32)
    kc[0] = pos % 128
    kc[1] = pos // 128
    kc[2] = 1.0
    kc[1, 0] = -8192.0
    if hf == 0:
        kc[1, :129] = -8192.0
    t["kaug_cmp"] = kc
    tv = np.ones((128, NT), np.float32)
    f0 = np.full((128, 64), -3e9, np.float32)
    hv = np.ones((128, 1), np.float32)
    if hf == 0:
        tv[:, :NT // 2] = 0.0
        f0[:, 32] = 1e9
        hv[:] = 0.0
    else:
        f0[:, 0] = 1e9
    t["tilevalid"] = tv
    t["f0"] = f0
    t["hv"] = hv
    kl = np.arange(128)[:, None]
    tq = np.arange(128)[None, :]
    t["tric"] = np.where(kl > tq, NEG, 0.0).astype(np.float32)
    t["triw"] = np.where(kl <= tq, NEG, 0.0).astype(np.float32)
    mc = np.zeros((128, 16, 128), np.float32)
    for dl in range(16):
        mc[:, dl, :] = np.where(16 * kl + 15 - tq > 128 * dl, NEG, 0.0)
    t["maskc"] = mc
    ov = np.zeros((256, 64), np.float32)
    for mm in range(1, 256):
        for jj in range(64):
            if 4 * jj <= mm <= 4 * jj + 4:
                ov[mm, jj] = 1.0
    t["ov"] = ov.reshape(2, 128, 64).transpose(1, 0, 2).copy()
    mw = np.zeros((128, 128), np.float32)
    aw = np.zeros((128, 128), np.float32)
    up = (np.arange(128) >= 64).astype(np.float32)
    for jw in range(128):
        jr = jw - 64
        if jr < -1:
            mw[:, jw] = 1.0
        elif jr == -1:
            mw[:, jw] = up
            aw[:, jw] = (1.0 - up) * 1e9
        elif jr == 0:
            aw[:, jw] = 1e9
        elif jr == 1:
            aw[:, jw] = np.where(up > 0, 1e9, -1e9)
        else:
            aw[:, jw] = -1e9
    t["mskw"] = mw
    t["addw"] = aw
    par = np.zeros((128, 2), np.float32)
    kk = np.arange(128)
    par[:, 0] = ((kk // 16) % 2 == 0)
    par[:, 1] = ((kk // 16) % 2 == 1)
    t["par"] = par
    return t


TABLE_SHAPES = {k: v.shape for k, v in make_tables().items()}


def build():
    nc = bass.Bass("TRN2", target_bir_lowering=False)

    def din(name, shape):
        return nc.dram_tensor(name, list(shape), F32, kind="ExternalInput").ap()

    x_d = din("x", [T, D])
    c_d = din("cvec", [128, 8])
    w_ada = din("w_ada", [D, 6 * D])
    b_ada = din("b_ada", [1, 6 * D])
    w_in = din("w_in", [D, 3864])
    pe_ck = din("pe_ck", [32, 64])
    w_ck1 = din("w_ck1", [32, 64, 128])
    w_ck2 = din("w_ck2", [128, 64])
    pe_cv = din("pe_cv", [32, 64])
    w_cv1 = din("w_cv1", [32, 64, 128])
    w_cv2 = din("w_cv2", [128, 64])
    w_nsa_out = din("w_nsa_out", [512, D])
    a_re = din("s5_a_re", [32, 64])
    a_im = din("s5_a_im", [32, 64])
    b_re = din("s5_b_re", [32, 64, 16])
    b_im = din("s5_b_im", [32, 64, 16])
    c_re = din("s5_c_re", [32, 16, 64])
    c_im = din("s5_c_im", [32, 16, 64])
    s5_d = din("s5_d", [512, 1])
    log_dt = din("s5_log_dt", [1, 32])
    w_glu = din("w_s5_glu", [512, 2 * D])
    w_o = din("w_o", [D, D])
    ln1_g = din("ln1_g", [1, D])
    ln1_b = din("ln1_b", [1, D])
    w_up = din("w_up", [D, 2 * DFF])
    conv_w = din("conv_w", [3, 2 * DFF])
    conv_b = din("conv_b", [1, 2 * DFF])
    w_down = din("w_down", [DFF, D])
    ln2_g = din("ln2_g", [1, D])
    ln2_b = din("ln2_b", [1, D])
    tb = {k: din("tb_" + k, shp) for k, shp in TABLE_SHAPES.items()}
    out_d = nc.dram_tensor("out", [T // 2, D], F32, kind="ExternalOutput").ap()
    x1_d = nc.dram_tensor("x1_scratch", [NQ * 128, D], F32, kind="Internal").ap()
    dbg_d = {}
    for k, shp in DBG.items():
        dbg_d[k] = nc.dram_tensor("dbg_" + k, list(shp), F32, kind="ExternalOutput").ap()

    with ExitStack() as ges:
        kb = KB(nc, ges)
        op = kb.op
        PS = [kb.ps(f"ps{i}", [128, 512], F32) for i in range(7)]
        PSB = [kb.buf(f"ps{i}") for i in range(7)]
        PT = kb.ps("pt", [128, 1024], BF16)
        PTB = kb.buf("pt")
        PTB2 = PTB
        PTB3 = PTB
        ident_f = kb.sb("ident_f", [128, 128], F32)
        ident_b = kb.sb("ident_b", [128, 128], BF16)
        gates_bc = kb.sb("gates_bc", [128, 2, D], F32)
        modT = kb.sb("modT", [128, 32], F32)
        one11 = kb.sb("one11", [1, 1], F32)
        B_id = kb.buf("ident")
        B_gbc = kb.buf("gbc")
        B_modT = kb.buf("modT")
        B_one = kb.buf("one")
        tilevalid = kb.sb("tilevalid", [128, NT], F32)
        hv = kb.sb("hv", [128, 1], F32)
        B_tv = kb.buf("tv")
        kb.dma("sp", tilevalid[:], tb["tilevalid"], B_tv, writes=[B_tv])
        kb.dma("sp", hv[:], tb["hv"], B_tv, writes=[B_tv])
        kb.dma("sp", ident_f[:], tb["ident"], B_id, writes=[B_id])
        op("dve", lambda e: e.tensor_copy(out=ident_b[:], in_=ident_f[:]), reads=[B_id], writes=[B_id])
        op("pool", lambda e: e.memset(one11[:], 1.0), writes=[B_one])

        def dbg(name, src_ap, rbufs):
            if name in dbg_d:
                b = kb.buf("dbg")
                kb.dma("sp", dbg_d[name], src_ap, b, reads=rbufs)
                kb.live

        with ExitStack() as es0:
            kb.es = es0
            c_sb = kb.sb("c_sb", [128, 8], F32)
            sc = kb.sb("sc", [128, 8], F32)
            sc_bc = kb.sb("sc_bc", [128, 8, 128], F32)
            bada = kb.sb("bada", [128, 6 * D], F32)
            mod_bc = kb.sb("mod_bc", [128, 6 * D], F32)
            wst = [kb.sb(f"wst{i}", [128, 8, 512], F32) for i in range(2)]
            B_c, B_sc, B_bada, B_mod = kb.buf("c"), kb.buf("sc"), kb.buf("bada"), kb.buf("mod")
            B_wst = [kb.buf("wst0"), kb.buf("wst1")]
            kb.dma("sp", c_sb[:], c_d, B_c, writes=[B_c])
            kb.dma("sp", bada[:], b_ada.partition_broadcast(128).rearrange("p o n -> p (o n)"), B_bada, writes=[B_bada])
            op("act", lambda e: e.activation(out=sc[:], in_=c_sb[:], func=AF.Silu), reads=[B_c], writes=[B_sc])
            op("dve", lambda e: e.tensor_copy(out=sc_bc[:], in_=sc[:].unsqueeze(2).to_broadcast([128, 8, 128])),
               reads=[B_sc], writes=[B_sc])
            for j in range(12):
                st = wst[j % 2]
                kb.dma("sp", st[:], w_ada[:, j * 512:(j + 1) * 512].rearrange("(kc p) n -> p kc n", p=128),
                       B_wst[j % 2], writes=[B_wst[j % 2]])
                for kc in range(KC):
                    op("pe", lambda e: e.matmul(PS[j % 2][:, :], lhsT=sc_bc[:, kc, :], rhs=st[:, kc, :],
                                                start=(kc == 0), stop=(kc == KC - 1)),
                       reads=[B_sc, B_wst[j % 2]], writes=[PSB[j % 2]])
                op("dve", lambda e: e.tensor_tensor(out=mod_bc[:, j * 512:(j + 1) * 512], in0=PS[j % 2][:, :],
                                                     in1=bada[:, j * 512:(j + 1) * 512], op=ALU.add),
                   reads=[PSB[j % 2], B_bada], writes=[B_mod])
            op("dve", lambda e: e.tensor_copy(out=gates_bc[:, 0, :], in_=mod_bc[:, 2 * D:3 * D]), reads=[B_mod], writes=[B_gbc])
            op("dve", lambda e: e.tensor_copy(out=gates_bc[:, 1, :], in_=mod_bc[:, 5 * D:6 * D]), reads=[B_mod], writes=[B_gbc])
            for wi, w in enumerate((0, 1, 3, 4)):
                for fc in range(8):
                    col = w * D + fc * 128
                    idx = wi * 8 + fc
                    op("pe", lambda e: e.matmul(PS[2][:, idx:idx + 1], lhsT=mod_bc[0:1, col:col + 128],
                                                rhs=one11[0:1, 0:1], start=True, stop=True),
                       reads=[B_mod, B_one], writes=[PSB[2]])
            op("dve", lambda e: e.tensor_copy(out=modT[:], in_=PS[2][:, 0:32]), reads=[PSB[2]], writes=[B_modT])
            op("dve", lambda e: e.tensor_scalar(out=modT[:, 8:16], in0=modT[:, 8:16], scalar1=1.0, scalar2=None, op0=ALU.add),
               reads=[B_modT], writes=[B_modT])
            op("dve", lambda e: e.tensor_scalar(out=modT[:, 24:32], in0=modT[:, 24:32], scalar1=1.0, scalar2=None, op0=ALU.add),
               reads=[B_modT], writes=[B_modT])
            dbg("modT", modT[:], [B_modT])
            kb.barrier()
        kb.es = ges

        def ln_to_hT(xt, B_x, xn, B_xn, stats, mv, rstd, B_st, hT_out, B_hT, mcol):
            for hf in range(2):
                op("dve", lambda e: e.bn_stats(out=stats[:, hf, :], in_=xt[:, hf * 512:(hf + 1) * 512]),
                   reads=[B_x], writes=[B_st])
            op("dve", lambda e: e.bn_aggr(out=mv[:], in_=stats[:]), reads=[B_st], writes=[B_st])
            op("dve", lambda e: e.tensor_scalar(out=rstd[:], in0=mv[:, 1:2], scalar1=1e-5, scalar2=None, op0=ALU.add),
               reads=[B_st], writes=[B_st])
            op("act", lambda e: e.activation(out=rstd[:], in_=rstd[:], func=AF.Sqrt), reads=[B_st], writes=[B_st])
            op("dve", lambda e: e.reciprocal(out=rstd[:], in_=rstd[:]), reads=[B_st], writes=[B_st])
            op("dve", lambda e: e.tensor_scalar(out=xn[:], in0=xt[:], scalar1=mv[:, 0:1], scalar2=rstd[:, 0:1],
                                                 op0=ALU.subtract, op1=ALU.mult), reads=[B_x, B_st], writes=[B_xn])
            for kc in range(KC):
                op("pe", lambda e: e.transpose(out=PT[:, kc * 128:(kc + 1) * 128], in_=xn[:, kc * 128:(kc + 1) * 128],
                                               identity=ident_b[:]), reads=[B_xn, B_id], writes=[PTB, PTB2] if kc == 7 else [PTB])
            for kc in range(KC):
                op("act", lambda e: e.activation(out=hT_out[:, kc, :], in_=PT[:, kc * 128:(kc + 1) * 128], func=AF.Identity,
                                                 scale=modT[:, mcol + 8 + kc:mcol + 9 + kc], bias=modT[:, mcol + kc:mcol + kc + 1]),
                   reads=[PTB, B_modT], writes=[B_hT])

        def load_cast(dst_fn, src_ap, rows_kc, ncols, stg, B_stg, B_dst, eng="dve"):
            kb.dma("sp", stg[:, 0:rows_kc, 0:ncols], src_ap.rearrange("(kc p) n -> p kc n", p=128), B_stg, writes=[B_stg])
            for kc in range(rows_kc):
                if kc % 2 == 0:
                    op("dve", lambda e: e.tensor_copy(out=dst_fn(kc), in_=stg[:, kc, 0:ncols]), reads=[B_stg], writes=[B_dst])
                else:
                    op("act", lambda e: e.activation(out=dst_fn(kc), in_=stg[:, kc, 0:ncols], func=AF.Copy), reads=[B_stg], writes=[B_dst])

        with ExitStack() as esA:
            kb.es = esA
            oaT = kb.sb("oaT", [128, 4, NQ * 128], BF16)
            uT = kb.sb("uT", [128, 4, T], BF16)
            B_oaT = [kb.buf(f"oaT{i}") for i in range(NT)]
            B_uT = [kb.buf(f"uT{i}") for i in range(NT)]
            if STOP[0] >= 1:
                pass_a1(nc, kb, locals())
            kb.barrier()
            WMp = kb.sb("WMp", [128, 8, 2048], BF16)
            WGLp = kb.sb("WGLp", [128, 4, 2048], BF16)
            WOp = kb.sb("WOp", [128, 8, D], BF16)
            B_Wpre = kb.buf("Wpre")
            if STOP[0] >= 2:
                pass_s5(nc, kb, locals())
            kb.barrier()
            if STOP[0] >= 3:
                pass_a2(nc, kb, locals())
            kb.barrier()
        kb.es = ges
        if STOP[0] >= 4:
            with ExitStack() as esB:
                kb.es = esB
                pass_b(nc, kb, locals())
                kb.barrier()
            kb.es = ges
        kb.barrier(release=False)
    return nc


class NS:
    def __init__(self, d):
        self.__dict__.update(d)


def pass_a1(nc, kb, Ld):
    outer = Ld['esA']
    with ExitStack() as es1:
        d2 = dict(Ld)
        d2['esA'] = es1
        kb.es = es1
        _pass_a1_body(nc, kb, d2)
        kb.barrier()
    kb.es = outer


def _pass_a1_body(nc, kb, Ld):
    L = NS(Ld)
    op = kb.op
    PS, PSB, PT, PTB = L.PS, L.PSB, L.PT, L.PTB
    tb = L.tb
    sb, buf = kb.sb, kb.buf
    WQK = sb("WQK", [128, 8, 1088], BF16)
    WV = sb("WV", [128, 8, 280], BF16)
    WU = sb("WU", [128, 8, 512], BF16)
    KTs = sb("KTs", [128, 2, T], BF16)
    KTw = sb("KTw", [128, 2, 1024], BF16)
    V1 = sb("V1", [128, NT, 4, 65], BF16)
    kcTa = sb("kcTa", [128, 2, 256], BF16)
    vcx = sb("vcx", [128, 2, 2, 129], BF16)
    triC = sb("triC", [128, 4, 128], BF16)
    triW = sb("triW", [128, 4, 128], BF16)
    maskC = sb("maskC", [128, 16, 128], BF16)
    cw1 = sb("cw1", [128, 2, 32, 128], BF16)
    cw2 = sb("cw2", [128, 2, 64], BF16)
    peT = sb("peT", [128, 2, 32], BF16)
    cbias = sb("cbias", [128, 2], F32)
    qA = sb("qA", [67, 8, 128], BF16)
    qB = sb("qB", [67, 8, 128], BF16)
    f0t = sb("f0t", [128, 64], F32)
    mskw = sb("mskw", [128, 128], F32)
    addw = sb("addw", [128, 128], F32)
    B_W, B_KT, B_V1 = buf("W"), [buf(f"KT{i}") for i in range(NT)], [buf(f"V1{i}") for i in range(NT)]
    B_kc, B_vcx, B_cst = buf("kc"), buf("vcx"), buf("cst")
    with ExitStack() as ess:
        kb.es = ess
        stgs = [sb(f"stg{i}", [128, 8, 512], F32) for i in range(2)]
        B_stgs = [buf("stg0"), buf("stg1")]
        rotc = [0]

        def rot():
            rotc[0] += 1
            return stgs[rotc[0] % 2], B_stgs[rotc[0] % 2]

        stg, B_stg = rot()
        op("pool", lambda e: e.memset(KTs[:], 0.0), writes=B_KT)
        op("pool", lambda e: e.memset(KTw[:], 0.0), writes=B_KT)
        op("pool", lambda e: e.memset(cw1[:], 0.0), writes=[B_cst])
        op("pool", lambda e: e.memset(peT[:], 0.0), writes=[B_cst])
        op("pool", lambda e: e.memset(WQK[:, :, 1024:1088], 0.0), writes=[B_W])
        L.load_cast(lambda kc: WQK[:, kc, 0:512], L.w_in[:, 0:512], 8, 512, stg, B_stg, B_W)
        stg, B_stg = rot()
        kb.dma("sp", stg[:, :, :], L.w_in[:, 512:1024].rearrange("(kc p) n -> p kc n", p=128), B_stg, writes=[B_stg])
        for kc in range(8):
            op("dve", lambda e: e.tensor_copy(out=WQK[:, kc, 768:1024], in_=stg[:, kc, 0:256]), reads=[B_stg], writes=[B_W])
            op("dve", lambda e: e.tensor_copy(out=WQK[:, kc, 512:640], in_=stg[:, kc, 256:384]), reads=[B_stg], writes=[B_W])
            op("dve", lambda e: e.tensor_copy(out=WV[:, kc, 0:128], in_=stg[:, kc, 384:512]), reads=[B_stg], writes=[B_W])
        stg, B_stg = rot()
        kb.dma("sp", stg[:, :, 0:280], L.w_in[:, 1024:1304].rearrange("(kc p) n -> p kc n", p=128), B_stg, writes=[B_stg])
        for kc in range(8):
            op("dve", lambda e: e.tensor_copy(out=WQK[:, kc, 640:768], in_=stg[:, kc, 0:128]), reads=[B_stg], writes=[B_W])
            op("dve", lambda e: e.tensor_copy(out=WV[:, kc, 128:280], in_=stg[:, kc, 128:280]), reads=[B_stg], writes=[B_W])
        stg, B_stg = rot()
        L.load_cast(lambda kc: WU[:, kc, :], L.w_in[:, 1304:1816], 8, 512, stg, B_stg, B_W)
        for kv, (w1d, w2d, ped) in enumerate(((L.w_ck1, L.w_ck2, L.pe_ck), (L.w_cv1, L.w_cv2, L.pe_cv))):
            for lh in range(4):
                stg, B_stg = rot()
                kb.dma("sp", stg[0:64, :, :].rearrange("p a (b e) -> p (a b) e", e=128)[:, 0:8, :],
                       w1d[lh * 8:(lh + 1) * 8].rearrange("l d e -> d l e"), B_stg, writes=[B_stg])
                op("dve", lambda e: e.tensor_copy(out=cw1[0:64, kv, lh * 8:(lh + 1) * 8, :],
                                                   in_=stg[0:64, :, :].rearrange("p a (b e) -> p (a b) e", e=128)[:, 0:8, :]),
                   reads=[B_stg], writes=[B_cst])
            kb.dma("sp", stg[:, 0, 0:64], w2d, B_stg, writes=[B_stg])
            op("dve", lambda e: e.tensor_copy(out=cw2[:, kv, :], in_=stg[:, 0, 0:64]), reads=[B_stg], writes=[B_cst])
            kb.dma("sp", stg[0:64, 0, 0:32], ped.rearrange("l d -> d l"), B_stg, writes=[B_stg], allow_slow_non_contiguous=True)
            op("dve", lambda e: e.tensor_copy(out=peT[0:64, kv, :], in_=stg[0:64, 0, 0:32]), reads=[B_stg], writes=[B_cst])
        stg, B_stg = rot()
        for tname, dst in (("tric", triC), ("triw", triW)):
            kb.dma("sp", stg[:, 0, 0:128], tb[tname], B_stg, writes=[B_stg])
            op("dve", lambda e: e.tensor_copy(out=dst[:], in_=stg[:, 0, 0:128].unsqueeze(1).to_broadcast([128, 4, 128])),
               reads=[B_stg], writes=[B_cst])
        kb.dma("sp", stg[:, 0:4, :].rearrange("p a b -> p (a b)"), tb["maskc"].rearrange("p a b -> p (a b)"), B_stg, writes=[B_stg])
        op("dve", lambda e: e.tensor_copy(out=maskC[:].rearrange("p a b -> p (a b)"),
                                           in_=stg[:, 0:4, :].rearrange("p a b -> p (a b)")), reads=[B_stg], writes=[B_cst])
        stg, B_stg = rot()
        op("pool", lambda e: e.memset(vcx[:], 0.0), writes=[B_vcx])
        op("pool", lambda e: e.memset(vcx[:, :, :, 64:65], 1.0), writes=[B_vcx])
        kb.dma("sp", stg[:, 0, 0:128], tb["ov"].rearrange("p a b -> p (a b)"), B_stg, writes=[B_stg])
        for g in range(2):
            op("dve", lambda e: e.tensor_copy(out=vcx[:, :, g, 65:129],
                                               in_=stg[:, 0, 0:128].rearrange("p (a b) -> p a b", b=64)),
               reads=[B_stg], writes=[B_vcx])
        op("pool", lambda e: e.memset(V1[:], 1.0), writes=B_V1)
        op("pool", lambda e: e.memset(kcTa[:], 0.0), writes=[B_kc])
        stg, B_stg = rot()
        for q4 in range(2):
            stg, B_stg = rot()
            kb.dma("sp", stg[64:128, :, :].rearrange("p a b -> p (a b)")[:, 0:2048], tb["kaug_tok"][:, q4 * 2048:(q4 + 1) * 2048],
                   B_stg, writes=[B_stg])
            for g in range(2):
                op("dve", lambda e: e.tensor_copy(out=KTs[64:128, g, q4 * 2048:(q4 + 1) * 2048],
                                                   in_=stg[64:128, :, :].rearrange("p a b -> p (a b)")[:, 0:2048]), reads=[B_stg], writes=B_KT)
        stg, B_stg = rot()
        kb.dma("sp", stg[64:67, 0, 0:256], tb["kaug_cmp"], B_stg, writes=[B_stg])
        for g in range(2):
            op("dve", lambda e: e.tensor_copy(out=kcTa[64:67, g, :], in_=stg[64:67, 0, 0:256]), reads=[B_stg], writes=[B_kc])
        for qt, qn in ((qA, "qaugA"), (qB, "qaugB")):
            stg, B_stg = rot()
            kb.dma("sp", stg[64:67, 0:2, :].rearrange("p a b -> p (a b)"), tb[qn].rearrange("p a b -> p (a b)"), B_stg, writes=[B_stg])
            op("dve", lambda e: e.tensor_copy(out=qt[64:67, :, :].rearrange("p a b -> p (a b)"),
                                               in_=stg[64:67, 0:2, :].rearrange("p a b -> p (a b)")), reads=[B_stg], writes=[B_cst])
        kb.dma("sp", f0t[:], tb["f0"], B_cst, writes=[B_cst])
        kb.dma("sp", mskw[:], tb["mskw"], B_cst, writes=[B_cst])
        kb.dma("sp", addw[:], tb["addw"], B_cst, writes=[B_cst])
        for kv in range(2):
            for l in range(32):
                op("pe", lambda e: e.matmul(PS[2][:, kv:kv + 1], lhsT=cw1[:, kv, l, :], rhs=peT[:, kv, l:l + 1],
                                            start=(l == 0), stop=(l == 31)), reads=[B_cst], writes=[PSB[2]])
        op("dve", lambda e: e.tensor_copy(out=cbias[:], in_=PS[2][:, 0:2]), reads=[PSB[2]], writes=[B_cst])
        kb.barrier()
    kb.es = L.esA
    xt = [sb(f"xt{i}", [128, D], F32) for i in range(3)]
    B_xt = [buf("xt0"), buf("xt1"), buf("xt2")]
    xn = sb("xn", [128, D], BF16); B_xn = buf("xn")
    stats = sb("stats", [128, 2, 6], F32); mv = sb("mv", [128, 2], F32); rstd = sb("rstd", [128, 1], F32)
    B_st = buf("st")
    hTs = [sb(f"hT{i}", [128, 8, 128], BF16) for i in range(2)]; B_hTs = [buf("hT0"), buf("hT1")]
    QTa = sb("QTa", [128, 8, 128], BF16); B_Q = buf("Q")
    cmpT = sb("cmpT", [128, 2, 2, 144], BF16); B_cmp = buf("cmp")
    hact = sb("hact", [128, 2, 2, 8], BF16); B_hact = buf("hact")
    hpad = sb("hpad", [128, 2, 128], BF16); B_hpad = buf("hpad")
    gsig = sb("gsig", [128, 24], F32); B_gs = buf("gsig")
    Pt = [sb(f"Pt{i}", [128, 512], BF16) for i in range(4)]
    B_Pt = [buf(f"Pt{i}") for i in range(4)]
    osb = [sb(f"osb{g}", [128, 3, 4, 65], F32) for g in range(2)]; B_osb = [buf("osb0"), buf("osb1")]
    uimp = [sb(f"uimp{g}", [128, 4, 64], F32) for g in range(2)]
    PTB2 = L.PTB2
    imp = sb("imp", [128, 64], F32); imp2 = sb("imp2", [128, 64], F32); impw = sb("impw", [128, 64], F32)
    m8 = sb("m8", [128, 16], F32)
    B_imp = buf("imp")
    QS = [sb(f"QS{g}", [128, 4, 128], BF16) for g in range(2)]; B_QS = [buf("QS0"), buf("QS1")]
    den = [sb(f"den{g}", [128, 3, 4], F32) for g in range(2)]; fac = [sb(f"fac{g}", [128, 3, 4], F32) for g in range(2)]
    B_fac = [buf("fac0"), buf("fac1")]
    otok = sb("otok", [128, 8, 64], BF16); B_ot = buf("otok")
    otmp = sb("otmp", [128, 4, 64], F32); B_otmp = buf("otmp")
    op("pool", lambda e: e.memset(cmpT[:], 0.0), writes=[B_cmp])
    op("pool", lambda e: e.memset(QTa[:], 0.0), writes=[B_Q])
    for g in range(2):
        op("pool", lambda e: e.memset(QS[g][:], 0.0), writes=[B_QS[g]])
    sels = [sb(f"sel128_{g}", [128, 128], BF16) for g in range(2)]
    B_sel = [buf("sel0"), buf("sel1")]
    for g in range(2):
        op("pool", lambda e: e.memset(sels[g][:], 0.0), writes=[B_sel[g]])
    pcount = [0]

    def next_pt():
        pcount[0] += 1
        return pcount[0] % 4

    def scount_next(c=[0]):
        c[0] += 1
        return (0, 1, 6)[c[0] % 3]

    modT, ident_b = L.modT, L.ident_b

    def ln_a(t):
        x_, B_x = xt[t % 3], B_xt[t % 3]
        for hf in range(2):
            op("dve", lambda e: e.bn_stats(out=stats[:, hf, :], in_=x_[:, hf * 512:(hf + 1) * 512]), reads=[B_x], writes=[B_st])
        op("dve", lambda e: e.bn_aggr(out=mv[:], in_=stats[:]), reads=[B_st], writes=[B_st])
        op("dve", lambda e: e.tensor_scalar(out=rstd[:], in0=mv[:, 1:2], scalar1=1e-5, scalar2=None, op0=ALU.add), reads=[B_st], writes=[B_st])
        op("act", lambda e: e.activation(out=rstd[:], in_=rstd[:], func=AF.Sqrt), reads=[B_st], writes=[B_st])
        op("dve", lambda e: e.reciprocal(out=rstd[:], in_=rstd[:]), reads=[B_st], writes=[B_st])
        op("dve", lambda e: e.tensor_scalar(out=xn[:], in0=x_[:], scalar1=mv[:, 0:1], scalar2=rstd[:, 0:1], op0=ALU.subtract, op1=ALU.mult),
           reads=[B_x, B_st], writes=[B_xn])

    def ln_b(t):
        h_, B_h = hTs[t % 2], B_hTs[t % 2]
        for kc in range(KC):
            op("pe", lambda e: e.transpose(out=PT[:, kc * 128:(kc + 1) * 128], in_=xn[:, kc * 128:(kc + 1) * 128], identity=ident_b[:]),
               reads=[B_xn, L.B_id], writes=[PTB, PTB2] if kc == 7 else ([PTB, L.PTB3] if kc == 6 else [PTB]))
        for kc in range(KC):
            op("act", lambda e: e.activation(out=h_[:, kc, :], in_=PT[:, kc * 128:(kc + 1) * 128], func=AF.Identity,
                                             scale=modT[:, 8 + kc:9 + kc], bias=modT[:, kc:kc + 1]),
               reads=[PTB, PTB2, L.B_modT] if kc == 7 else ([PTB, L.PTB3, L.B_modT] if kc == 6 else [PTB, L.B_modT]), writes=[B_h])

    pend_ot = [None]

    def flush_ot():
        if pend_ot[0] is None:
            return
        ti = pend_ot[0]
        pend_ot[0] = None
        osl = slice((ti - Q0) * 128, (ti - Q0 + 1) * 128)
        for c in range(4):
            op("pe", lambda e: e.transpose(out=PT[:, c * 128:(c + 1) * 128], in_=otok[:, 2 * c:2 * c + 2, :].rearrange("p a b -> p (a b)"),
                                           identity=L.ident_b[:]), reads=[B_ot, L.B_id], writes=[PTB])
        op("act", lambda e: e.activation(out=L.oaT[:, :, osl], in_=PT[:, 0:512].rearrange("p (a b) -> p a b", b=128), func=AF.Copy),
           reads=[PTB], writes=[L.B_oaT[ti]])

    for t0 in range(2):
        kb.dma("sp", xt[t0][:], L.x_d[t0 * 128:(t0 + 1) * 128, :], B_xt[t0], writes=[B_xt[t0]])
    ln_a(0)
    ln_b(0)
    for i in range(NT):
        if i + 2 < NT:
            kb.dma("sp", xt[(i + 2) % 3][:], L.x_d[(i + 2) * 128:(i + 3) * 128, :], B_xt[(i + 2) % 3], writes=[B_xt[(i + 2) % 3]])
        if i + 1 < NT:
            ln_a(i + 1)
        hT, B_hT = hTs[i % 2], B_hTs[i % 2]
        lnb_done = [i + 1 >= NT]
        tsl = slice(i * 128, (i + 1) * 128)
        isq = i >= Q0
        for g in (range(2) if isq else ()):
            for hh in range(4):
                h = g * 4 + hh
                for kc in range(KC):
                    op("pe", lambda e: e.matmul(PS[g][:, hh * 128:(hh + 1) * 128], lhsT=WQK[:, kc, h * 64:h * 64 + 128],
                                                rhs=hT[:, kc, :], start=(kc == 0), stop=(kc == KC - 1)),
                       reads=[B_W, B_hT], writes=[PSB[g]])
            op("act", lambda e: e.activation(out=QTa[0:64, g * 4:(g + 1) * 4, :].rearrange("p a b -> p (a b)"),
                                             in_=PS[g][0:64, :], func=AF.Copy, scale=0.125), reads=[PSB[g]], writes=[B_Q])
        if isq:
            op("dve", lambda e: e.scalar_tensor_tensor(out=QTa[64:67, :, :], in0=qB[64:67, :, :], scalar=float(i), in1=qA[64:67, :, :],
                                                        op0=ALU.mult, op1=ALU.add), reads=[B_cst], writes=[B_Q])
        for grp in range(2):
            for s4 in range(4):
                c0 = 512 + grp * 256 + s4 * 64
                for kc in range(KC):
                    op("pe", lambda e: e.matmul(PS[2 + grp][:, s4 * 128:(s4 + 1) * 128], lhsT=WQK[:, kc, c0:c0 + 128],
                                                rhs=hT[:, kc, :], start=(kc == 0), stop=(kc == KC - 1)),
                       reads=[B_W, B_hT], writes=[PSB[2 + grp]])
        wsl = slice((i % 8) * 128, (i % 8 + 1) * 128)
        op("act", lambda e: e.activation(out=KTs[0:64, :, tsl], in_=PS[2][0:64, 0:256].rearrange("p (b c) -> p b c", b=2),
                                         func=AF.Copy), reads=[PSB[2]], writes=[B_KT[i]])
        op("act", lambda e: e.activation(out=KTw[0:64, :, wsl], in_=PS[2][0:64, 256:512].rearrange("p (b c) -> p b c", b=2),
                                         func=AF.Copy), reads=[PSB[2]], writes=[B_KT[i]])
        op("dve", lambda e: e.tensor_copy(out=KTw[64:67, :, wsl], in_=KTs[64:67, :, tsl]), reads=[B_cst], writes=[B_KT[i]])
        op("dve", lambda e: e.tensor_copy(out=cmpT[0:64, :, :, 16:144], in_=PS[3][0:64, :].rearrange("p (a b c) -> p a b c", a=2, b=2)),
           reads=[PSB[3]], writes=[B_cmp])
        flush_ot()
        for kc in range(KC):
            op("pe", lambda e: e.matmul(PS[4][:, 0:280], lhsT=hT[:, kc, :], rhs=WV[:, kc, :], start=(kc == 0), stop=(kc == KC - 1)),
               reads=[B_W, B_hT], writes=[PSB[4]])
        op("dve", lambda e: e.tensor_copy(out=V1[:, i, :, 0:64], in_=PS[4][:, 0:256].rearrange("p (a b) -> p a b", b=64)),
           reads=[PSB[4]], writes=[B_V1[i]])
        op("act", lambda e: e.activation(out=gsig[:], in_=PS[4][:, 256:280], func=AF.Sigmoid), reads=[PSB[4]], writes=[B_gs])
        if not isq and not lnb_done[0]:
            ln_b(i + 1)
            lnb_done[0] = True
        for ct in range(4):
            for kc in range(KC):
                op("pe", lambda e: e.matmul(PS[5][:, ct * 128:(ct + 1) * 128], lhsT=WU[:, kc, ct * 128:(ct + 1) * 128],
                                            rhs=hT[:, kc, :], start=(kc == 0), stop=(kc == KC - 1)),
                   reads=[B_W, B_hT], writes=[PSB[5]])
        op("act", lambda e: e.activation(out=L.uT[:, :, tsl], in_=PS[5][:, :].rearrange("p (a b) -> p a b", b=128), func=AF.Identity,
                                         scale=L.tilevalid[:, i:i + 1]), reads=[PSB[5], L.B_tv], writes=[L.B_uT[i]])
        for kv in range(2):
            for g in range(2):
                o0 = (kv * 2 + g) * 8
                for l in range(32):
                    op("pe", lambda e: e.matmul(PS[6][:, o0:o0 + 8], lhsT=cw1[:, kv, l, :], rhs=cmpT[:, kv, g, l:l + 113:16],
                                                start=(l == 0), stop=(l == 31)), reads=[B_cst, B_cmp], writes=[PSB[6]])
            op("act", lambda e: e.activation(out=hact[:, kv, :, :].rearrange("p a b -> p (a b)"), in_=PS[6][:, kv * 16:(kv + 1) * 16],
                                             func=AF.Silu, bias=cbias[:, kv:kv + 1]), reads=[PSB[6], B_cst], writes=[B_hact])
        op("dve", lambda e: e.tensor_copy(out=cmpT[0:64, :, :, 0:16], in_=cmpT[0:64, :, :, 128:144]), reads=[B_cmp], writes=[B_cmp])
        for g in range(2):
            op("pe", lambda e: e.matmul(PS[6][:, 64 + g * 8:64 + (g + 1) * 8], lhsT=cw2[:].rearrange("p a b -> p (a b)"), rhs=hact[:, 0, g, :],
                                        start=True, stop=True), reads=[B_cst, B_hact], writes=[PSB[6]])
        op("dve", lambda e: e.tensor_copy(out=kcTa[0:64, :, 8 * i:8 * i + 8], in_=PS[6][0:64, 64:80].rearrange("p (a b) -> p a b", b=8)),
           reads=[PSB[6]], writes=[B_kc])
        mt_i, mo = (8 * i) // 128, (8 * i) % 128
        op("pool", lambda e: e.memset(hpad[:], 0.0), writes=[B_hpad])
        op("pool", lambda e: e.tensor_copy(out=hpad[:, :, mo:mo + 8], in_=hact[:, 1, :, :]), reads=[B_hact], writes=[B_hpad])
        for g in range(2):
            op("pe", lambda e: e.matmul(PS[6][:, 128 + g * 64:128 + (g + 1) * 64], lhsT=hpad[:, g, :], rhs=cw2[:, 1, :],
                                        start=True, stop=True), reads=[B_cst, B_hpad], writes=[PSB[6]])
        op("dve", lambda e: e.tensor_tensor(out=vcx[:, mt_i, :, 0:64], in0=PS[6][:, 128:256].rearrange("p (a b) -> p a b", b=64),
                                             in1=vcx[:, mt_i, :, 0:64], op=ALU.add), reads=[PSB[6]], writes=[B_vcx])
        if isq:
            OB = [PS[2], PS[3], PS[4], PS[5]]
            OBB = [PSB[2], PSB[3], PSB[4], PSB[5]]
            Qgs = [QTa[:, g * 4:(g + 1) * 4, :].rearrange("p a b -> p (a b)") for g in range(2)]
            QSf = [QS[g][:].rearrange("p a b -> p (a b)") for g in range(2)]

            def emit_score(job):
                kind, g, kidx, first, last = job
                sbk = scount_next()
                extra = []
                if kind == "c":
                    mt = kidx
                    dl = i - 16 * mt
                    lhs, rl = kcTa[:, g, mt * 128:(mt + 1) * 128], [B_kc, B_Q]
                    if dl < 16:
                        for hh in range(4):
                            extra.append((PS[sbk][:, hh * 128:(hh + 1) * 128], L.ident_b[:], maskC[:, dl, :], [L.B_id, B_cst]))
                else:
                    kt = kidx
                    if kind == "s":
                        lhs = KTs[:, g, kt * 128:(kt + 1) * 128]
                    else:
                        lhs = KTw[:, g, (kt % 8) * 128:(kt % 8 + 1) * 128]
                        if kt == i - 4:
                            extra.append((PS[sbk][:, :], L.ident_b[:], triW[:].rearrange("p a b -> p (a b)"), [L.B_id, B_cst]))
                    if kt == i:
                        extra.append((PS[sbk][:, :], L.ident_b[:], triC[:].rearrange("p a b -> p (a b)"), [L.B_id, B_cst]))
                    rl = [B_KT[kt], B_QS[g] if kind == "s" else B_Q]
                op("pe", lambda e: e.matmul(PS[sbk][:, :], lhsT=lhs, rhs=(QSf[g] if kind == "s" else Qgs[g]), start=True, stop=(len(extra) == 0)),
                   reads=rl, writes=[PSB[sbk]])
                for xi, (oap, lt, rh, rb) in enumerate(extra):
                    op("pe", lambda e: e.matmul(oap, lhsT=lt, rhs=rh, start=False, stop=(xi == len(extra) - 1), skip_group_check=True),
                       reads=rb, writes=[PSB[sbk]])
                p = next_pt()
                op("act", lambda e: e.activation(out=Pt[p][:], in_=PS[sbk][:, :], func=AF.Exp), reads=[PSB[sbk]], writes=[B_Pt[p]])
                return p

            def emit_pv(job, p):
                kind, g, kidx, first, last = job
                if kind == "c":
                    rhs, ncol, rb = vcx[:, kidx, g, :], 129, B_vcx
                else:
                    vi = g if kind == "s" else 2 + g
                    rhs, ncol, rb = V1[:, kidx, vi, :], 65, B_V1[kidx]
                for hh in range(4):
                    op("pe", lambda e: e.matmul(OB[hh][:, 0:ncol], lhsT=Pt[p][:, hh * 128:(hh + 1) * 128], rhs=rhs,
                                                start=first, stop=last), reads=[B_Pt[p], rb], writes=[OBB[hh]])

            def epilogue(kind, g):
                bi = {"c": 0, "s": 1, "w": 2}[kind]
                o_, B_o = osb[g], B_osb[g]
                for hh in range(4):
                    if hh % 2:
                        op("dve", lambda e: e.tensor_copy(out=o_[:, bi, hh, :], in_=OB[hh][:, 0:65]), reads=[OBB[hh]], writes=[B_o])
                    else:
                        op("act", lambda e: e.activation(out=o_[:, bi, hh, :], in_=OB[hh][:, 0:65], func=AF.Copy), reads=[OBB[hh]], writes=[B_o])
                    if kind == "c":
                        op("dve", lambda e: e.tensor_copy(out=uimp[g][:, hh, :], in_=OB[hh][:, 65:129]), reads=[OBB[hh]], writes=[B_o])
                if kind == "c":
                    dn, B_d = den[g], B_fac[g]
                    op("dve", lambda e: e.tensor_scalar(out=dn[:, 0, :], in0=o_[:, 0, :, 64], scalar1=1e-30, scalar2=None, op0=ALU.max),
                       reads=[B_o], writes=[B_d])
                    op("dve", lambda e: e.reciprocal(out=dn[:, 0, :], in_=dn[:, 0, :]), reads=[B_d], writes=[B_d])
                    op("dve", lambda e: e.tensor_scalar(out=imp[:], in0=uimp[g][:, 0, :], scalar1=dn[:, 0, 0:1], scalar2=None, op0=ALU.mult),
                       reads=[B_o, B_d], writes=[B_imp])
                    for hh in range(1, 4):
                        op("dve", lambda e: e.scalar_tensor_tensor(out=imp[:], in0=uimp[g][:, hh, :], scalar=dn[:, 0, hh:hh + 1], in1=imp[:],
                                                                    op0=ALU.mult, op1=ALU.add), reads=[B_o, B_d], writes=[B_imp])
                    w0 = 64 - 2 * i
                    op("dve", lambda e: e.tensor_tensor(out=imp2[:], in0=imp[:], in1=mskw[:, w0:w0 + 64], op=ALU.mult),
                       reads=[B_imp, B_cst], writes=[B_imp])
                    op("dve", lambda e: e.tensor_tensor(out=imp2[:], in0=imp2[:], in1=addw[:, w0:w0 + 64], op=ALU.add),
                       reads=[B_imp, B_cst], writes=[B_imp])
                    op("dve", lambda e: e.tensor_tensor(out=imp2[:], in0=imp2[:], in1=f0t[:], op=ALU.max), reads=[B_cst], writes=[B_imp])
                    op("dve", lambda e: e.max(out=m8[:, 0:8], in_=imp2[:]), reads=[B_imp], writes=[B_imp])
                    op("dve", lambda e: e.match_replace(out=impw[:], in_to_replace=m8[:, 0:8], in_values=imp2[:], imm_value=-3e9),
                       reads=[B_imp], writes=[B_imp])
                    op("dve", lambda e: e.max(out=m8[:, 8:16], in_=impw[:]), reads=[B_imp], writes=[B_imp])
                    op("dve", lambda e: e.tensor_scalar(out=sels[g][:, 67:128], in0=imp2[:, 1:62], scalar1=m8[:, 15:16], scalar2=None, op0=ALU.is_ge),
                       reads=[B_imp], writes=[B_sel[g]])
                if kind == "s":
                    dn, fc_, B_d = den[g], fac[g], B_fac[g]
                    op("dve", lambda e: e.tensor_scalar(out=dn[:, 1:3, :], in0=o_[:, 1:3, :, 64], scalar1=1e-30, scalar2=None, op0=ALU.max),
                       reads=[B_o], writes=[B_d])
                    op("dve", lambda e: e.reciprocal(out=dn[:, 1:3, :], in_=dn[:, 1:3, :]), reads=[B_d], writes=[B_d])
                    op("dve", lambda e: e.tensor_tensor(out=fc_[:], in0=dn[:], in1=gsig[:, g * 12:(g + 1) * 12].rearrange("p (h b) -> p b h", b=3),
                                                         op=ALU.mult), reads=[B_d, B_gs], writes=[B_d])
                    for hh in range(4):
                        h = g * 4 + hh
                        op("dve", lambda e: e.tensor_scalar(out=otmp[:, hh, :], in0=o_[:, 0, hh, 0:64], scalar1=fc_[:, 0, hh:hh + 1], scalar2=None,
                                                             op0=ALU.mult), reads=[B_o, B_d], writes=[B_otmp])
                        op("dve", lambda e: e.scalar_tensor_tensor(out=otmp[:, hh, :], in0=o_[:, 1, hh, 0:64], scalar=fc_[:, 1, hh:hh + 1],
                                                                    in1=otmp[:, hh, :], op0=ALU.mult, op1=ALU.add),
                           reads=[B_o, B_d], writes=[B_otmp])
                        op("dve", lambda e: e.scalar_tensor_tensor(out=otok[:, h, :], in0=o_[:, 2, hh, 0:64], scalar=fc_[:, 2, hh:hh + 1],
                                                                    in1=otmp[:, hh, :], op0=ALU.mult, op1=ALU.add),
                           reads=[B_o, B_d, B_otmp], writes=[B_ot])

            def epi_c_tail(g):
                c0, pb_ = (896, PTB2) if g == 0 else (768, L.PTB3)
                op("pe", lambda e: e.transpose(out=PT[:, c0:c0 + 128], in_=sels[g][:], identity=L.ident_b[:]), reads=[B_sel[g], L.B_id], writes=[pb_])
                op("dve", lambda e: e.tensor_scalar(out=QS[g][64:128], in0=PT[64:128, c0:c0 + 128].unsqueeze(1).to_broadcast([64, 4, 128]),
                                                     scalar1=-1.0, scalar2=-NEG, op0=ALU.add, op1=ALU.mult), reads=[pb_], writes=[B_QS[g]])
                op("pool", lambda e: e.tensor_copy(out=QS[g][0:67], in_=QTa[0:67, g * 4:(g + 1) * 4, :]), reads=[B_Q], writes=[B_QS[g]])

            jobs = []
            for kind in ("c", "w", "s"):
                for g in range(2):
                    if kind == "c":
                        ks = [0] if 8 * i + 7 < 128 else [0, 1]
                    elif kind == "w":
                        ks = list(range(max(0, i - 4), i + 1))
                    else:
                        ks = list(range(0, i + 1))
                    for n_, k_ in enumerate(ks):
                        jobs.append((kind, g, k_, n_ == 0, n_ == len(ks) - 1))
            pend = []
            n_s = [0]
            for job in jobs:
                if job[0] == "s" and job[3] and job[1] == 0:
                    epi_c_tail(0)
                    epi_c_tail(1)
                    n_s[0] = 0
                if job[0] == "s":
                    n_s[0] += 1
                    if n_s[0] == 4 and not lnb_done[0]:
                        ln_b(i + 1)
                        lnb_done[0] = True
                p = emit_score(job)
                pend.append((job, p))
                if len(pend) > 2:
                    pj = pend.pop(0)
                    emit_pv(*pj)
                    if pj[0][4]:
                        epilogue(pj[0][0], pj[0][1])
            for pj in pend:
                emit_pv(*pj)
                if pj[0][4]:
                    epilogue(pj[0][0], pj[0][1])
        if not lnb_done[0]:
            ln_b(i + 1)
            lnb_done[0] = True
        if isq:
            pend_ot[0] = i
    flush_ot()
    if "oaT" in L.dbg_d:
        b = kb.buf("dbg")
        st2 = sb("dbgst", [128, 4, 512], F32)
        op("dve", lambda e: e.tensor_copy(out=st2[:], in_=L.oaT[:, :, 0:512]), reads=L.B_oaT, writes=[b])
        kb.dma("sp", L.dbg_d["oaT"], st2[:], b, reads=[b])


def pass_s5(nc, kb, Ld):
    L = NS(Ld)
    op = kb.op
    PS, PSB, PT, PTB = L.PS, L.PSB, L.PT, L.PTB
    sb, buf = kb.sb, kb.buf
    PI = math.pi
    with ExitStack() as es5:
        kb.es = es5
        BBT = sb("BBT", [128, 16, 2, 128], BF16)
        BBTn = sb("BBTn", [128, 16, 128], BF16)
        Cq = sb("Cq", [128, 16, 4, 128], BF16)
        EI = sb("EI", [128, 2, 16, 128], F32)
        EF = sb("EF", [128, 2, 16, 128], F32)
        L128 = sb("L128", [128, 2, 16], F32)
        dsk = sb("dsk", [128, 4], F32)
        ones = sb("ones", [128, 128], F32)
        carry = sb("carry", [128, 2, 16], F32)
        zl = sb("zl", [128, 2, 16], F32)
        B_tab, B_car, B_zl = buf("tab"), buf("carry"), buf("zl")
        with ExitStack() as est:
            kb.es = est
            ar = sb("ar", [128, 16], F32); ai = sb("ai", [128, 16], F32); ldt = sb("ldt", [128, 16], F32)
            dt = sb("dt", [128, 16], F32); lrd = sb("lrd", [128, 16], F32); ang = sb("ang", [128, 16], F32)
            mag = sb("mag", [128, 16], F32); mgi = sb("mgi", [128, 16], F32)
            sn = sb("sn", [128, 16], F32); cs = sb("cs", [128, 16], F32); tmp = sb("tmp", [128, 16], F32); tmp2 = sb("tmp2", [128, 16], F32)
            lb = sb("lb", [128, 2, 16], F32); lbi = sb("lbi", [128, 2, 16], F32); pw = sb("pw", [128, 2, 16], F32)
            coef = sb("coef", [128, 2, 16], F32); den = sb("dens", [128, 16], F32)
            bsb = sb("bsb", [128, 2, 16, 16], F32); bb = sb("bb", [128, 2, 16, 16], F32); bt = sb("bt", [128, 16, 16], F32)
            Apr = sb("Apr", [128, 128], BF16)
            Cn = sb("Cn", [128, 2, 4, 64], F32)
            par = sb("par", [128, 2], F32)
            et1 = sb("et1", [128, 16, 64], F32); et2 = sb("et2", [128, 16, 64], F32)
            B_s = buf("s5setup"); B_apr = buf("apr"); B_et = buf("et")
            kb.dma("sp", ar[:], L.a_re.rearrange("(pr g2) p -> (g2 p) pr", g2=2), B_s, writes=[B_s], allow_slow_non_contiguous=True)
            kb.dma("sp", ai[:], L.a_im.rearrange("(pr g2) p -> (g2 p) pr", g2=2), B_s, writes=[B_s], allow_slow_non_contiguous=True)
            for g2 in range(2):
                kb.dma("sp", ldt[g2 * 64:(g2 + 1) * 64, :],
                       L.log_dt.rearrange("o (pr g2) -> o g2 pr", g2=2)[:, g2, :].partition_broadcast(64).rearrange("p o n -> p (o n)"),
                       B_s, writes=[B_s], allow_slow_non_contiguous=True)
                for ri, bd in enumerate((L.b_re, L.b_im)):
                    kb.dma("sp", bsb[g2 * 64:(g2 + 1) * 64, ri, :, :], bd.rearrange("(pr g2) p h -> g2 p pr h", g2=2)[g2],
                           B_s, writes=[B_s])
            for ri, cd in enumerate((L.c_re, L.c_im)):
                kb.dma("sp", Cn[:, ri, :, :], cd.rearrange("(ct gl) h p -> (gl h) ct p", ct=4), B_s, writes=[B_s])
            kb.dma("sp", par[:], L.tb["par"], B_s, writes=[B_s])
            kb.dma("sp", dsk[:], L.s5_d.rearrange("(ct q) o -> q (ct o)", ct=4), B_tab, writes=[B_tab], allow_slow_non_contiguous=True)
            op("pool", lambda e: e.memset(ones[:], 1.0), writes=[B_tab])
            op("pool", lambda e: e.memset(carry[:], 0.0), writes=[B_car])
            op("pool", lambda e: e.memset(Cq[:], 0.0), writes=[B_tab])
            R, W = [B_s], [B_s]
            dv = lambda f: op("dve", f, reads=R, writes=W)
            ac = lambda f: op("act", f, reads=R, writes=W)
            ac(lambda e: e.activation(out=dt[:], in_=ldt[:], func=AF.Exp))
            dv(lambda e: e.tensor_scalar(out=ar[:], in0=ar[:], scalar1=-1e-4, scalar2=None, op0=ALU.min))
            dv(lambda e: e.tensor_tensor(out=lrd[:], in0=ar[:], in1=dt[:], op=ALU.mult))
            dv(lambda e: e.tensor_tensor(out=ang[:], in0=ai[:], in1=dt[:], op=ALU.mult))
            ac(lambda e: e.activation(out=mag[:], in_=lrd[:], func=AF.Exp))
            ac(lambda e: e.activation(out=mgi[:], in_=lrd[:], func=AF.Exp, scale=-1.0))
            ti = sb("ti", [128, 16], mybir.dt.int32)

            def rred(dst, shift):
                dv(lambda e: e.tensor_scalar(out=tmp2[:], in0=ang[:], scalar1=shift, scalar2=None, op0=ALU.add))
                dv(lambda e: e.tensor_scalar(out=tmp[:], in0=tmp2[:], scalar1=1.0 / (2 * PI), scalar2=None, op0=ALU.mult))
                dv(lambda e: e.tensor_copy(out=ti[:], in_=tmp[:]))
                dv(lambda e: e.tensor_copy(out=tmp[:], in_=ti[:]))
                dv(lambda e: e.scalar_tensor_tensor(out=tmp2[:], in0=tmp[:], scalar=-2 * PI, in1=tmp2[:], op0=ALU.mult, op1=ALU.add))
                dv(lambda e: e.tensor_scalar(out=tmp[:], in0=tmp2[:], scalar1=PI, scalar2=2 * PI, op0=ALU.is_gt, op1=ALU.mult))
                dv(lambda e: e.tensor_tensor(out=tmp2[:], in0=tmp2[:], in1=tmp[:], op=ALU.subtract))
                dv(lambda e: e.tensor_scalar(out=tmp[:], in0=tmp2[:], scalar1=-PI, scalar2=2 * PI, op0=ALU.is_lt, op1=ALU.mult))
                dv(lambda e: e.tensor_tensor(out=tmp2[:], in0=tmp2[:], in1=tmp[:], op=ALU.add))
                ac(lambda e: e.activation(out=dst[:], in_=tmp2[:], func=AF.Sin))

            rred(sn, 0.0)
            rred(cs, 0.5 * PI)
            dv(lambda e: e.tensor_tensor(out=lb[:, 0, :], in0=mag[:], in1=cs[:], op=ALU.mult))
            dv(lambda e: e.tensor_tensor(out=lb[:, 1, :], in0=mag[:], in1=sn[:], op=ALU.mult))
            dv(lambda e: e.tensor_tensor(out=lbi[:, 0, :], in0=mgi[:], in1=cs[:], op=ALU.mult))
            dv(lambda e: e.scalar_tensor_tensor(out=lbi[:, 1, :], in0=mgi[:], scalar=-1.0, in1=sn[:], op0=ALU.mult, op1=ALU.mult))
            dv(lambda e: e.tensor_tensor(out=den[:], in0=ar[:], in1=ar[:], op=ALU.mult))
            dv(lambda e: e.tensor_tensor(out=tmp[:], in0=ai[:], in1=ai[:], op=ALU.mult))
            dv(lambda e: e.tensor_tensor(out=den[:], in0=den[:], in1=tmp[:], op=ALU.add))
            dv(lambda e: e.reciprocal(out=den[:], in_=den[:]))
            dv(lambda e: e.tensor_scalar(out=tmp2[:], in0=lb[:, 0, :], scalar1=-1.0, scalar2=None, op0=ALU.add))
            dv(lambda e: e.tensor_tensor(out=tmp[:], in0=tmp2[:], in1=ar[:], op=ALU.mult))
            dv(lambda e: e.tensor_tensor(out=coef[:, 0, :], in0=lb[:, 1, :], in1=ai[:], op=ALU.mult))
            dv(lambda e: e.tensor_tensor(out=coef[:, 0, :], in0=coef[:, 0, :], in1=tmp[:], op=ALU.add))
            dv(lambda e: e.tensor_tensor(out=coef[:, 0, :], in0=coef[:, 0, :], in1=den[:], op=ALU.mult))
            dv(lambda e: e.tensor_tensor(out=tmp[:], in0=tmp2[:], in1=ai[:], op=ALU.mult))
            dv(lambda e: e.tensor_tensor(out=coef[:, 1, :], in0=lb[:, 1, :], in1=ar[:], op=ALU.mult))
            dv(lambda e: e.tensor_tensor(out=coef[:, 1, :], in0=coef[:, 1, :], in1=tmp[:], op=ALU.subtract))
            dv(lambda e: e.tensor_tensor(out=coef[:, 1, :], in0=coef[:, 1, :], in1=den[:], op=ALU.mult))
            cbr = lambda k: coef[:, k, :].unsqueeze(2).to_broadcast([128, 16, 16])
            dv(lambda e: e.tensor_tensor(out=bb[:, 0], in0=bsb[:, 0], in1=cbr(0), op=ALU.mult))
            dv(lambda e: e.tensor_tensor(out=bt[:], in0=bsb[:, 1], in1=cbr(1), op=ALU.mult))
            dv(lambda e: e.tensor_tensor(out=bb[:, 0], in0=bb[:, 0], in1=bt[:], op=ALU.subtract))
            dv(lambda e: e.tensor_tensor(out=bb[:, 1], in0=bsb[:, 1], in1=cbr(0), op=ALU.mult))
            dv(lambda e: e.tensor_tensor(out=bt[:], in0=bsb[:, 0], in1=cbr(1), op=ALU.mult))
            dv(lambda e: e.tensor_tensor(out=bb[:, 1], in0=bb[:, 1], in1=bt[:], op=ALU.add))
            for pr in range(16):
                prl = pr % 4
                for ri in range(2):
                    op("pool", lambda e: e.memset(Apr[:], 0.0), writes=[B_apr])
                    op("dve", lambda e: e.tensor_copy(out=Apr[0:64, 32 * prl:32 * prl + 16], in_=bb[0:64, ri, pr, :]), reads=[B_s], writes=[B_apr])
                    op("dve", lambda e: e.tensor_copy(out=Apr[64:128, 32 * prl + 16:32 * prl + 32], in_=bb[64:128, ri, pr, :]),
                       reads=[B_s], writes=[B_apr])
                    op("pe", lambda e: e.transpose(out=PT[:, 0:128], in_=Apr[:], identity=L.ident_b[:]), reads=[B_apr, L.B_id], writes=[PTB])
                    op("act", lambda e: e.activation(out=BBT[:, pr, ri, :], in_=PT[:, 0:128], func=AF.Copy), reads=[PTB], writes=[B_tab])
                    if ri == 1:
                        op("act", lambda e: e.activation(out=BBTn[:, pr, :], in_=PT[:, 0:128], func=AF.Copy, scale=-1.0),
                           reads=[PTB], writes=[B_tab])
            for ct in range(4):
                for ri in range(2):
                    for g2 in range(2):
                        op("dve", lambda e: e.tensor_scalar(out=Apr[:, g2 * 64:(g2 + 1) * 64], in0=Cn[:, ri, ct, :], scalar1=par[:, g2:g2 + 1],
                                                             scalar2=None, op0=ALU.mult), reads=[B_s], writes=[B_apr])
                    op("pe", lambda e: e.transpose(out=PT[:, 0:128], in_=Apr[:], identity=L.ident_b[:]), reads=[B_apr, L.B_id], writes=[PTB])
                    for prl in range(4):
                        pr = ct * 4 + prl
                        sl = slice(32 * prl, 32 * prl + 32)
                        if ri == 0:
                            op("dve", lambda e: e.tensor_copy(out=Cq[:, pr, 0, sl], in_=PT[:, sl]), reads=[PTB], writes=[B_tab])
                            op("dve", lambda e: e.tensor_scalar(out=Cq[:, pr, 3, sl], in0=PT[:, sl], scalar1=-1.0, scalar2=None, op0=ALU.mult),
                               reads=[PTB], writes=[B_tab])
                        else:
                            for k in (1, 2):
                                op("dve", lambda e: e.tensor_scalar(out=Cq[:, pr, k, sl], in0=PT[:, sl], scalar1=-1.0, scalar2=None,
                                                                     op0=ALU.mult), reads=[PTB], writes=[B_tab])
            for tabl, base in ((EF, lb), (EI, lbi)):
                op("pool", lambda e: e.memset(tabl[:, 0, :, 0:1], 1.0), writes=[B_tab])
                op("pool", lambda e: e.memset(tabl[:, 1, :, 0:1], 0.0), writes=[B_tab])
                op("dve", lambda e: e.tensor_copy(out=pw[:], in_=base[:]), reads=[B_s], writes=[B_s])
                for k in range(7):
                    n = 1 << k
                    pbr = lambda c: pw[:, c, :].unsqueeze(2).to_broadcast([128, 16, n])
                    RW = dict(reads=[B_s, B_tab, B_et], writes=[B_tab, B_et])
                    op("dve", lambda e: e.tensor_tensor(out=et1[:, :, 0:n], in0=tabl[:, 0, :, 0:n], in1=pbr(0), op=ALU.mult), **RW)
                    op("dve", lambda e: e.tensor_tensor(out=et2[:, :, 0:n], in0=tabl[:, 1, :, 0:n], in1=pbr(1), op=ALU.mult), **RW)
                    op("dve", lambda e: e.tensor_tensor(out=tabl[:, 0, :, n:2 * n], in0=et1[:, :, 0:n], in1=et2[:, :, 0:n], op=ALU.subtract), **RW)
                    op("dve", lambda e: e.tensor_tensor(out=et1[:, :, 0:n], in0=tabl[:, 0, :, 0:n], in1=pbr(1), op=ALU.mult), **RW)
                    op("dve", lambda e: e.tensor_tensor(out=et2[:, :, 0:n], in0=tabl[:, 1, :, 0:n], in1=pbr(0), op=ALU.mult), **RW)
                    op("dve", lambda e: e.tensor_tensor(out=tabl[:, 1, :, n:2 * n], in0=et1[:, :, 0:n], in1=et2[:, :, 0:n], op=ALU.add), **RW)
                    op("dve", lambda e: e.tensor_tensor(out=tmp[:], in0=pw[:, 0, :], in1=pw[:, 0, :], op=ALU.mult), **RW)
                    op("dve", lambda e: e.tensor_tensor(out=tmp2[:], in0=pw[:, 1, :], in1=pw[:, 1, :], op=ALU.mult), **RW)
                    op("dve", lambda e: e.tensor_tensor(out=pw[:, 1, :], in0=pw[:, 0, :], in1=pw[:, 1, :], op=ALU.mult), **RW)
                    op("dve", lambda e: e.tensor_scalar(out=pw[:, 1, :], in0=pw[:, 1, :], scalar1=2.0, scalar2=None, op0=ALU.mult), **RW)
                    op("dve", lambda e: e.tensor_tensor(out=pw[:, 0, :], in0=tmp[:], in1=tmp2[:], op=ALU.subtract), **RW)
                if tabl is EF:
                    op("dve", lambda e: e.tensor_copy(out=L128[:], in_=pw[:]), reads=[B_s], writes=[B_tab])
            kb.barrier()
        kb.es = es5
        tA = [sb(f"tA{i}", [128, 4, 128], F32) for i in range(2)]
        Wt = [sb(f"Wt{i}", [128, 2, 128], F32) for i in range(2)]
        Z = [sb(f"Z{i}", [128, 2, 128], F32) for i in range(2)]
        Qp = [sb(f"Qp{i}", [128, 4, 128], BF16) for i in range(2)]
        ys = [sb(f"ys{i}", [128, 128], F32) for i in range(2)]
        yt = [sb(f"yt{i}", [128, 128], F32) for i in range(2)]
        sg = [sb(f"sg{i}", [128, 128], F32) for i in range(2)]
        ctmp = sb("ctmp", [128, 2, 16], F32)
        wacc = [sb(f"wacc{i}", [128, 2], F32) for i in range(2)]
        B_wacc = [buf("wacc0"), buf("wacc1")]
        B_tA, B_W, B_Z, B_Qp = [[buf(f"{n}{i}") for i in range(2)] for n in ("tA", "Wt", "Z", "Qp")]
        B_ys = [buf("ys0"), buf("ys1")]
        def stage_a_pe(c, pr):
            ct, pb = pr // 4, pr % 2
            csl = slice(c * 128, (c + 1) * 128)
            lts = (BBT[:, pr, 0, :], BBT[:, pr, 1, :], BBTn[:, pr, :], BBT[:, pr, 0, :])
            for q4, lt in enumerate(lts):
                op("pe", lambda e: e.matmul(PS[pb][:, q4 * 128:(q4 + 1) * 128], lhsT=lt, rhs=L.uT[:, ct, csl],
                                            start=True, stop=True), reads=[B_tab, L.B_uT[c]], writes=[PSB[pb]])

        def stage_a(c, pr):
            ct, pb = pr // 4, pr % 2
            bu = PS[pb][:, :].rearrange("p (k r b) -> p k r b", k=2, r=2)
            if c < Q0:
                op("dve", lambda e: e.tensor_tensor(out=tA[pb][:].rearrange("p (r k) b -> p k r b", r=2), in0=bu,
                                                     in1=EI[:, :, pr, :].unsqueeze(2).to_broadcast([128, 2, 2, 128]), op=ALU.mult),
                   reads=[PSB[pb], B_tab], writes=[B_tA[pb]])
                return
            op("dve", lambda e: e.tensor_tensor(out=tA[pb][:].rearrange("p (k r) b -> p k r b", k=2), in0=bu,
                                                 in1=EI[:, :, pr, :].unsqueeze(2).to_broadcast([128, 2, 2, 128]), op=ALU.mult),
               reads=[PSB[pb], B_tab], writes=[B_tA[pb]])
            op("pool", lambda e: e.tensor_tensor(out=Wt[pb][:], in0=tA[pb][:, 0:2, :], in1=tA[pb][:, 2:4, :], op=ALU.add),
               reads=[B_tA[pb]], writes=[B_W[pb]])

        def stage_b(c, pr):
            ct, prl, pb = pr // 4, pr % 4, pr % 2
            csl = slice(c * 128, (c + 1) * 128)
            for ri in (range(2) if c < Q0 else ()):
                op("act", lambda e: e.activation(out=tA[pb][:, 2 * ri:2 * ri + 2, :], in_=tA[pb][:, 2 * ri:2 * ri + 2, :], func=AF.Copy,
                                                 accum_out=wacc[pb][:, ri:ri + 1]), reads=[B_tA[pb]], writes=[B_tA[pb], B_wacc[pb]])
            if c < Q0:
                op("pool", lambda e: e.tensor_tensor(out=zl[:, :, pr:pr + 1], in0=wacc[pb][:, :].unsqueeze(2), in1=carry[:, :, pr:pr + 1], op=ALU.add),
                   reads=[B_wacc[pb], B_car], writes=[B_zl])
            for ri in (range(2) if c >= Q0 else ()):
                op("dve", lambda e: e.tensor_tensor_scan(out=Z[pb][:, ri, :], data0=ones[:], data1=Wt[pb][:, ri, :],
                                                         initial=carry[:, ri, pr:pr + 1], op0=ALU.mult, op1=ALU.add),
                   reads=[B_W[pb], B_tab, B_car], writes=[B_Z[pb]])
            if c >= Q0:
                op("pool", lambda e: e.tensor_copy(out=zl[:, :, pr:pr + 1], in_=Z[pb][:, :, 127:128]), reads=[B_Z[pb]], writes=[B_zl])
                q = Qp[pb]
                op("dve", lambda e: e.tensor_tensor(out=q[:].rearrange("p (k r) b -> p k r b", k=2),
                                                     in0=Z[pb][:].unsqueeze(1).to_broadcast([128, 2, 2, 128]),
                                                     in1=EF[:, :, pr, :].unsqueeze(2).to_broadcast([128, 2, 2, 128]), op=ALU.mult),
                   reads=[B_Z[pb], B_tab], writes=[B_Qp[pb]])
                yb = 2 + ct % 2
                for k in range(4):
                    op("pe", lambda e: e.matmul(PS[yb][:, 0:128], lhsT=Cq[:, pr, k, :], rhs=q[:, k, :],
                                                start=(prl == 0 and k == 0), stop=(prl == 3 and k == 3)),
                       reads=[B_tab, B_Qp[pb]], writes=[PSB[yb]])
                if prl == 3:
                    cb2 = ct % 2
                    op("dve", lambda e: e.scalar_tensor_tensor(out=ys[cb2][:], in0=L.uT[:, ct, csl], scalar=dsk[:, ct:ct + 1], in1=PS[yb][:, 0:128],
                                                                op0=ALU.mult, op1=ALU.add), reads=[PSB[yb], L.B_uT[c], B_tab], writes=[B_ys[cb2]])
                    op("pool", lambda e: e.tensor_tensor(out=yt[cb2][:], in0=ys[cb2][:], in1=ys[cb2][:], op=ALU.mult),
                       reads=[B_ys[cb2]], writes=[B_ys[cb2]])
                    op("pool", lambda e: e.tensor_scalar(out=yt[cb2][:], in0=yt[cb2][:], scalar1=0.044715, scalar2=1.0, op0=ALU.mult, op1=ALU.add),
                       reads=[B_ys[cb2]], writes=[B_ys[cb2]])
                    op("pool", lambda e: e.tensor_tensor(out=yt[cb2][:], in0=yt[cb2][:], in1=ys[cb2][:], op=ALU.mult),
                       reads=[B_ys[cb2]], writes=[B_ys[cb2]])
                    op("act", lambda e: e.activation(out=sg[cb2][:], in_=yt[cb2][:], func=AF.Sigmoid, scale=1.5957691216057308),
                       reads=[B_ys[cb2]], writes=[B_ys[cb2]])
                    op("pool", lambda e: e.tensor_tensor(out=L.uT[:, ct, csl], in0=ys[cb2][:], in1=sg[cb2][:], op=ALU.mult),
                       reads=[B_ys[cb2]], writes=[L.B_uT[c]])
            if pr == 15:
                RWc = dict(reads=[B_zl, B_tab, B_car], writes=[B_car])
                op("dve", lambda e: e.tensor_tensor(out=ctmp[:, 0, :], in0=L128[:, 0, :], in1=zl[:, 0, :], op=ALU.mult), **RWc)
                op("dve", lambda e: e.tensor_tensor(out=ctmp[:, 1, :], in0=L128[:, 1, :], in1=zl[:, 1, :], op=ALU.mult), **RWc)
                op("dve", lambda e: e.tensor_tensor(out=carry[:, 0, :], in0=ctmp[:, 0, :], in1=ctmp[:, 1, :], op=ALU.subtract), **RWc)
                op("dve", lambda e: e.tensor_tensor(out=ctmp[:, 0, :], in0=L128[:, 0, :], in1=zl[:, 1, :], op=ALU.mult), **RWc)
                op("dve", lambda e: e.tensor_tensor(out=ctmp[:, 1, :], in0=L128[:, 1, :], in1=zl[:, 0, :], op=ALU.mult), **RWc)
                op("dve", lambda e: e.tensor_tensor(out=carry[:, 1, :], in0=ctmp[:, 0, :], in1=ctmp[:, 1, :], op=ALU.add), **RWc)

        stgp = sb("stgp", [128, 8, 256], F32)
        B_stgp = buf("stgp")

        def pre_step(dst_fn, src_ap, rows_kc):
            kb.dma("sp", stgp[:, 0:rows_kc, :], src_ap.rearrange("(kc p) n -> p kc n", p=128), B_stgp, writes=[B_stgp])
            for kc in range(rows_kc):
                op("act", lambda e: e.activation(out=dst_fn(kc), in_=stgp[:, kc, :], func=AF.Copy), reads=[B_stgp], writes=[L.B_Wpre])

        pre_steps = []
        for q in range(8):
            pre_steps.append((lambda kc, q=q: L.WMp[:, kc, q * 256:(q + 1) * 256], L.w_in[:, 1816 + q * 256:1816 + (q + 1) * 256], 8))
        for q in range(8):
            pre_steps.append((lambda kc, q=q: L.WGLp[:, kc, q * 256:(q + 1) * 256], L.w_glu[:, q * 256:(q + 1) * 256], 4))
        for q in range(4):
            pre_steps.append((lambda kc, q=q: L.WOp[:, kc, q * 256:(q + 1) * 256], L.w_o[:, q * 256:(q + 1) * 256], 8))
        seq = [(c, pr) for c in range(NT) for pr in range(16)]
        stage_a_pe(*seq[0])
        stage_a_pe(*seq[1])
        stage_a(*seq[0])
        for k in range(len(seq)):
            if seq[k][1] in (0, 8) and seq[k][0] >= Q0 + 1 and pre_steps:
                pre_step(*pre_steps.pop(0))
            if k + 2 < len(seq):
                stage_a_pe(*seq[k + 2])
            if k + 1 < len(seq):
                stage_a(*seq[k + 1])
            stage_b(*seq[k])
        while pre_steps:
            pre_step(*pre_steps.pop(0))
        if "gyT" in L.dbg_d:
            b = kb.buf("dbg")
            st2 = sb("dbgst5", [128, 4, 512], F32)
            op("dve", lambda e: e.tensor_copy(out=st2[:], in_=L.uT[:, :, 0:512]), reads=L.B_uT, writes=[b])
            kb.dma("sp", L.dbg_d["gyT"], st2[:], b, reads=[b])
        kb.barrier()
    kb.es = L.esA


def ln_affine_store(kb, L, pre, B_pre, stats, mv, rstd, B_st, gb, B_gb, dst_ap, q="sp"):
    op = kb.op
    for hf in range(2):
        op("dve", lambda e: e.bn_stats(out=stats[:, hf, :], in_=pre[:, hf * 512:(hf + 1) * 512]), reads=[B_pre], writes=[B_st])
    op("dve", lambda e: e.bn_aggr(out=mv[:], in_=stats[:]), reads=[B_st], writes=[B_st])
    op("dve", lambda e: e.tensor_scalar(out=rstd[:], in0=mv[:, 1:2], scalar1=1e-5, scalar2=None, op0=ALU.add),
       reads=[B_st], writes=[B_st])
    op("act", lambda e: e.activation(out=rstd[:], in_=rstd[:], func=AF.Sqrt), reads=[B_st], writes=[B_st])
    op("dve", lambda e: e.reciprocal(out=rstd[:], in_=rstd[:]), reads=[B_st], writes=[B_st])
    op("dve", lambda e: e.tensor_scalar(out=pre[:], in0=pre[:], scalar1=mv[:, 0:1], scalar2=rstd[:, 0:1], op0=ALU.subtract, op1=ALU.mult),
       reads=[B_st], writes=[B_pre])
    op("pool", lambda e: e.tensor_tensor(out=pre[:], in0=pre[:], in1=gb[:, 0, :], op=ALU.mult), reads=[B_gb], writes=[B_pre])
    op("pool", lambda e: e.tensor_tensor(out=pre[:], in0=pre[:], in1=gb[:, 1, :], op=ALU.add), reads=[B_gb], writes=[B_pre])
    kb.dma(q, dst_ap, pre[:], B_pre, reads=[B_pre])


def pass_a2(nc, kb, Ld):
    L = NS(Ld)
    op = kb.op
    PS, PSB, PT, PTB = L.PS, L.PSB, L.PT, L.PTB
    sb, buf = kb.sb, kb.buf
    with ExitStack() as es2:
        kb.es = es2
        WM, WGL, WO = L.WMp, L.WGLp, L.WOp
        WNO = sb("WNO", [128, 4, D], BF16)
        gb = sb("gb1", [128, 2, D], F32)
        B_W, B_gb = L.B_Wpre, buf("gb1")
        with ExitStack() as ess:
            kb.es = ess
            stgs = [sb(f"stg2{i}", [128, 8, 512], F32) for i in range(2)]
            B_stgs = [buf(f"stg2{i}") for i in range(2)]
            for q2 in range(2):
                L.load_cast(lambda kc: WNO[:, kc, q2 * 512:(q2 + 1) * 512], L.w_nsa_out[:, q2 * 512:(q2 + 1) * 512], 4, 512,
                            stgs[q2], B_stgs[q2], B_W)
            kb.dma("sp", gb[:, 0, :], L.ln1_g.partition_broadcast(128).rearrange("p o n -> p (o n)"), B_gb, writes=[B_gb])
            kb.dma("sp", gb[:, 1, :], L.ln1_b.partition_broadcast(128).rearrange("p o n -> p (o n)"), B_gb, writes=[B_gb])
            kb.barrier()
        kb.es = es2
        xts = [[sb(f"xt2{k}{i}", [128, D], F32) for i in range(4)] for k in range(2)]
        B_xts = [[buf(f"xt2{k}{i}") for i in range(4)] for k in range(2)]
        xn = sb("xn2", [128, D], BF16); B_xn = buf("xn2")
        stats = sb("stats2", [128, 2, 6], F32); mv = sb("mv2", [128, 2], F32); rstd = sb("rstd2", [128, 1], F32)
        B_st = buf("st2")
        hTs2 = [sb(f"hT2{k}", [128, 8, 512], BF16) for k in range(2)]; B_hTs2 = [buf("hT20"), buf("hT21")]
        sga = sb("sga", [128, 3, 512], F32); B_sg = [buf("sga0"), buf("sga1")]
        t1 = sb("t1", [128, 512], F32); t2 = sb("t2", [128, 512], F32); B_t = buf("t12")
        mixT = sb("mixT", [128, 8, 512], BF16); B_mix = buf("mixT")
        gtmp, B_gt = [t1, t2], [B_t, B_t]
        steps = [[0]] + [list(range(r, r + 4)) for r in range(1, NQ, 4)]
        modT, ident_b = L.modT, L.ident_b

        def ln_a2(x_, B_x):
            for hf in range(2):
                op("dve", lambda e: e.bn_stats(out=stats[:, hf, :], in_=x_[:, hf * 512:(hf + 1) * 512]), reads=[B_x], writes=[B_st])
            op("dve", lambda e: e.bn_aggr(out=mv[:], in_=stats[:]), reads=[B_st], writes=[B_st])
            op("dve", lambda e: e.tensor_scalar(out=rstd[:], in0=mv[:, 1:2], scalar1=1e-5, scalar2=None, op0=ALU.add), reads=[B_st], writes=[B_st])
            op("act", lambda e: e.activation(out=rstd[:], in_=rstd[:], func=AF.Sqrt), reads=[B_st], writes=[B_st])
            op("dve", lambda e: e.reciprocal(out=rstd[:], in_=rstd[:]), reads=[B_st], writes=[B_st])
            op("dve", lambda e: e.tensor_scalar(out=xn[:], in0=x_[:], scalar1=mv[:, 0:1], scalar2=rstd[:, 0:1], op0=ALU.subtract, op1=ALU.mult),
               reads=[B_x, B_st], writes=[B_xn])

        def ln_b2(dst, B_dst):
            for kc in range(KC):
                op("pe", lambda e: e.transpose(out=PT[:, kc * 128:(kc + 1) * 128], in_=xn[:, kc * 128:(kc + 1) * 128], identity=ident_b[:]),
                   reads=[B_xn, L.B_id], writes=[PTB])
            for kc in range(KC):
                op("act", lambda e: e.activation(out=dst[:, kc, :], in_=PT[:, kc * 128:(kc + 1) * 128], func=AF.Identity,
                                                 scale=modT[:, 8 + kc:9 + kc], bias=modT[:, kc:kc + 1]), reads=[PTB, L.B_modT], writes=[B_dst])

        def load_step(si):
            for tt, r in enumerate(steps[si]):
                i = Q0 + r
                kb.dma("sp", xts[si % 2][tt][:], L.x_d[i * 128:(i + 1) * 128, :], B_xts[si % 2][tt], writes=[B_xts[si % 2][tt]])

        load_step(0)
        for tt in range(len(steps[0])):
            ln_a2(xts[0][tt], B_xts[0][tt])
            ln_b2(hTs2[0][:, :, tt * 128:(tt + 1) * 128], B_hTs2[0])
        for si, rs in enumerate(steps):
            NTOK = 128 * len(rs)
            r0 = rs[0]
            xt, B_xt = xts[si % 2], B_xts[si % 2]
            hT, B_hT = hTs2[si % 2], B_hTs2[si % 2]
            nxt = steps[si + 1] if si + 1 < len(steps) else []
            if nxt:
                load_step(si + 1)
            osl = slice(r0 * 128, r0 * 128 + NTOK)
            tsl = slice((Q0 + r0) * 128, (Q0 + r0) * 128 + NTOK)
            B_us = [L.B_uT[Q0 + r] for r in rs]
            B_os = [L.B_oaT[Q0 + r] for r in rs]
            for fc in range(8):
                fsl = slice(fc * 128, (fc + 1) * 128)
                for half in range(2):
                    for kc in range(KC):
                        op("pe", lambda e: e.matmul(PS[half][:, 0:NTOK], lhsT=WM[:, kc, half * D + fc * 128:half * D + (fc + 1) * 128],
                                                    rhs=hT[:, kc, 0:NTOK], start=(kc == 0), stop=(kc == KC - 1)), reads=[B_W, B_hT], writes=[PSB[half]])
                    op("act", lambda e: e.activation(out=sga[:, half, 0:NTOK], in_=PS[half][:, 0:NTOK], func=AF.Sigmoid),
                       reads=[PSB[half]], writes=[B_sg[0]])
                for c in range(4):
                    op("pe", lambda e: e.matmul(PS[2][:, 0:NTOK], lhsT=WNO[:, c, fsl], rhs=L.oaT[:, c, osl], start=(c == 0), stop=(c == 3)),
                       reads=[B_W] + B_os, writes=[PSB[2]])
                op("dve", lambda e: e.tensor_tensor(out=t1[:, 0:NTOK], in0=sga[:, 0, 0:NTOK], in1=PS[2][:, 0:NTOK], op=ALU.mult),
                   reads=[B_sg[0], PSB[2]], writes=[B_t])
                for half in range(2):
                    for c in range(4):
                        op("pe", lambda e: e.matmul(PS[3 + half][:, 0:NTOK], lhsT=WGL[:, c, half * D + fc * 128:half * D + (fc + 1) * 128],
                                                    rhs=L.uT[:, c, tsl], start=(c == 0), stop=(c == 3)), reads=[B_W] + B_us, writes=[PSB[3 + half]])
                op("act", lambda e: e.activation(out=sga[:, 2, 0:NTOK], in_=PS[4][:, 0:NTOK], func=AF.Sigmoid), reads=[PSB[4]], writes=[B_sg[1]])
                op("dve", lambda e: e.tensor_tensor(out=t2[:, 0:NTOK], in0=sga[:, 2, 0:NTOK], in1=PS[3][:, 0:NTOK], op=ALU.mult),
                   reads=[B_sg[1], PSB[3]], writes=[B_t])
                op("dve", lambda e: e.tensor_tensor(out=t2[:, 0:NTOK], in0=t2[:, 0:NTOK], in1=sga[:, 1, 0:NTOK], op=ALU.mult),
                   reads=[B_sg[0]], writes=[B_t])
                op("dve", lambda e: e.tensor_tensor(out=mixT[:, fc, 0:NTOK], in0=t1[:, 0:NTOK], in1=t2[:, 0:NTOK], op=ALU.add),
                   reads=[B_t], writes=[B_mix])
                tn = fc // 2
                if tn < len(nxt):
                    if fc % 2 == 0:
                        ln_a2(xts[(si + 1) % 2][tn], B_xts[(si + 1) % 2][tn])
                    else:
                        ln_b2(hTs2[(si + 1) % 2][:, :, tn * 128:(tn + 1) * 128], B_hTs2[(si + 1) % 2])
            for tt, r in enumerate(rs):
                pr_, B_p = xt[tt], B_xt[tt]
                for half in range(2):
                    bk = ((5, 6), (0, 1))[tt % 2][half]
                    for fc in range(8):
                        op("pe", lambda e: e.matmul(PS[bk][:, :], lhsT=mixT[:, fc, tt * 128:(tt + 1) * 128], rhs=WO[:, fc, half * 512:(half + 1) * 512],
                                                    start=(fc == 0), stop=(fc == 7)), reads=[B_W, B_mix], writes=[PSB[bk]])
                    hs = slice(half * 512, (half + 1) * 512)
                    op("dve", lambda e: e.tensor_tensor(out=gtmp[half][:], in0=PS[bk][:, :], in1=L.gates_bc[:, 0, hs], op=ALU.mult),
                       reads=[PSB[bk], L.B_gbc], writes=[B_gt[half]])
                    op("dve", lambda e: e.scalar_tensor_tensor(out=pr_[:, hs], in0=pr_[:, hs], scalar=ALPHA, in1=gtmp[half][:], op0=ALU.mult, op1=ALU.add),
                       reads=[B_gt[half]], writes=[B_p])
                ln_affine_store(kb, L, pr_, B_p, stats, mv, rstd, B_st, gb, B_gb, L.x1_d[r * 128:(r + 1) * 128, :])
        kb.barrier()
    kb.es = L.esA


def pass_b(nc, kb, Ld):
    L = NS(Ld)
    op = kb.op
    PS, PSB, PT, PTB = L.PS, L.PSB, L.PT, L.PTB
    sb, buf = kb.sb, kb.buf
    NJ = 44
    WUP = sb("WUP", [128, 8, 2 * DFF], BF16)
    WDN = sb("WDN", [128, 22, D], BF16)
    cw = sb("cwt", [128, NJ, 3], F32)
    cb = sb("cbt", [128, NJ], F32)
    gb = sb("gb2", [128, 2, D], F32)
    halo = sb("halo", [128, NJ, 2], F32)
    B_W, B_gb, B_cw, B_halo = buf("WB"), buf("gb2"), buf("cw"), buf("halo")
    with ExitStack() as ess:
        kb.es = ess
        stgs = [sb(f"stgb{i}", [128, 8, 512], F32) for i in range(3)]
        B_stgs = [buf(f"stgb{i}") for i in range(3)]
        for q in range(11):
            L.load_cast(lambda kc: WUP[:, kc, q * 512:(q + 1) * 512], L.w_up[:, q * 512:(q + 1) * 512], 8, 512, stgs[q % 3], B_stgs[q % 3], B_W)
        for half in range(2):
            for q in range(3):
                stg, B_stg = stgs[(half * 3 + q + 2) % 3], B_stgs[(half * 3 + q + 2) % 3]
                r0, nr = q * 8, (8 if q < 2 else 6)
                kb.dma("sp", stg[:, 0:nr, :], L.w_down[r0 * 128:(r0 + nr) * 128, half * 512:(half + 1) * 512].rearrange("(kc p) n -> p kc n", p=128),
                       B_stg, writes=[B_stg])
                for kc in range(nr):
                    op("dve", lambda e: e.tensor_copy(out=WDN[:, r0 + kc, half * 512:(half + 1) * 512], in_=stg[:, kc, :]),
                       reads=[B_stg], writes=[B_W])
        for k3 in range(3):
            kb.dma("sp", cw[:, :, k3], L.conv_w[k3:k3 + 1, :].rearrange("o (j p) -> p (o j)", p=128), B_cw, writes=[B_cw],
                   allow_slow_non_contiguous=True)
        kb.dma("sp", cb[:], L.conv_b.rearrange("o (j p) -> p (o j)", p=128), B_cw, writes=[B_cw], allow_slow_non_contiguous=True)
        kb.dma("sp", gb[:, 0, :], L.ln2_g.partition_broadcast(128).rearrange("p o n -> p (o n)"), B_gb, writes=[B_gb])
        kb.dma("sp", gb[:, 1, :], L.ln2_b.partition_broadcast(128).rearrange("p o n -> p (o n)"), B_gb, writes=[B_gb])
        op("pool", lambda e: e.memset(halo[:], 0.0), writes=[B_halo])
        kb.barrier()
    kb.es = L.esB
    NTOK = 512
    x1t = [sb(f"x1t{i}", [128, D], F32) for i in range(4)]
    B_x1 = [buf(f"x1t{i}") for i in range(4)]
    xn = sb("xnb", [128, D], BF16); B_xn = buf("xnb")
    stats = sb("statsb", [128, 2, 6], F32); mv = sb("mvb", [128, 2], F32); rstd = sb("rstdb", [128, 1], F32)
    B_st = buf("stb")
    h2T = sb("h2T", [128, 8, NTOK], BF16); B_h2 = buf("h2T")
    upb = [sb(f"upb{i}", [128, NTOK + 2], F32) for i in range(2)]
    acc = [sb(f"acc{i}", [128, NTOK], F32) for i in range(2)]
    B_up = [buf("up0"), buf("up1")]
    B_acc = [buf("acc0"), buf("acc1")]
    ffT = sb("ffT", [128, 22, NTOK], BF16); B_ff = buf("ffT")
    gtmp, B_gt = acc, B_acc
    steps = [[0]] + [list(range(r, r + 4)) for r in range(1, NQ, 4)]
    def ln_tile(i, tt):
        kb.dma("sp", x1t[tt][:], L.x1_d[i * 128:(i + 1) * 128, :], B_x1[tt], writes=[B_x1[tt]])
        L.ln_to_hT(x1t[tt], B_x1[tt], xn, B_xn, stats, mv, rstd, B_st, h2T[:, :, tt * 128:(tt + 1) * 128], B_h2, 16)

    for tt, i in enumerate(steps[0]):
        ln_tile(i, tt)
    for s, rs in enumerate(steps):
        NTOK = 128 * len(rs)
        nxt = steps[s + 1] if s + 1 < len(steps) else []
        for j in range(22):
            for w, ch in enumerate((j, j + 22)):
                pbk = (2 * j + w) % 4
                for kc in range(KC):
                    op("pe", lambda e: e.matmul(PS[pbk][:, 0:NTOK], lhsT=WUP[:, kc, ch * 128:(ch + 1) * 128], rhs=h2T[:, kc, 0:NTOK],
                                                start=(kc == 0), stop=(kc == KC - 1)), reads=[B_W, B_h2], writes=[PSB[pbk]])
                u_, B_u, a_, B_a = upb[w], B_up[w], acc[w], B_acc[w]
                op("pool", lambda e: e.tensor_copy(out=u_[:, 0:2], in_=halo[:, ch, :]), reads=[B_halo], writes=[B_u])
                op("act", lambda e: e.activation(out=u_[:, 2:NTOK + 2], in_=PS[pbk][:, 0:NTOK], func=AF.Copy), reads=[PSB[pbk]], writes=[B_u])
                op("pool", lambda e: e.tensor_copy(out=halo[:, ch, :], in_=u_[:, NTOK:NTOK + 2]), reads=[B_u], writes=[B_halo])
                op("act", lambda e: e.activation(out=a_[:, 0:NTOK], in_=u_[:, 2:NTOK + 2], func=AF.Identity, scale=cw[:, ch, 2:3], bias=cb[:, ch:ch + 1]),
                   reads=[B_u, B_cw], writes=[B_a])
                op("dve", lambda e: e.scalar_tensor_tensor(out=a_[:, 0:NTOK], in0=u_[:, 1:NTOK + 1], scalar=cw[:, ch, 1:2], in1=a_[:, 0:NTOK],
                                                            op0=ALU.mult, op1=ALU.add), reads=[B_u, B_cw], writes=[B_a])
                op("dve", lambda e: e.scalar_tensor_tensor(out=a_[:, 0:NTOK], in0=u_[:, 0:NTOK], scalar=cw[:, ch, 0:1], in1=a_[:, 0:NTOK],
                                                            op0=ALU.mult, op1=ALU.add), reads=[B_u, B_cw], writes=[B_a])
            op("act", lambda e: e.activation(out=upb[1][:, 0:NTOK], in_=acc[1][:, 0:NTOK], func=AF.Silu), reads=[B_acc[1]], writes=[B_up[1]])
            op("dve", lambda e: e.tensor_tensor(out=ffT[:, j, 0:NTOK], in0=upb[1][:, 0:NTOK], in1=acc[0][:, 0:NTOK], op=ALU.mult),
               reads=[B_up[1], B_acc[0]], writes=[B_ff])
        if s == 0:
            op("pool", lambda e: e.tensor_scalar(out=halo[:].rearrange("p a b -> p (a b)"), in0=halo[:].rearrange("p a b -> p (a b)"),
                                                  scalar1=L.hv[:, 0:1], scalar2=None, op0=ALU.mult), reads=[L.B_tv], writes=[B_halo])
        for tt, i in enumerate(rs):
            pr_, B_p = x1t[tt], B_x1[tt]
            for half in range(2):
                bk = ((4, 5), (6, 0))[tt % 2][half]
                for j in range(22):
                    op("pe", lambda e: e.matmul(PS[bk][:, :], lhsT=ffT[:, j, tt * 128:(tt + 1) * 128], rhs=WDN[:, j, half * 512:(half + 1) * 512],
                                                start=(j == 0), stop=(j == 21)), reads=[B_W, B_ff], writes=[PSB[bk]])
                hs = slice(half * 512, (half + 1) * 512)
                if half == 1 and 1 <= tt and tt - 1 < len(nxt):
                    ln_tile(nxt[tt - 1], tt - 1)
                op("dve", lambda e: e.tensor_tensor(out=gtmp[half][:], in0=PS[bk][:, :], in1=L.gates_bc[:, 1, hs], op=ALU.mult),
                   reads=[PSB[bk], L.B_gbc], writes=[B_gt[half]])
                op("pool", lambda e: e.scalar_tensor_tensor(out=pr_[:, hs], in0=pr_[:, hs], scalar=ALPHA, in1=gtmp[half][:], op0=ALU.mult, op1=ALU.add),
                   reads=[B_gt[half]], writes=[B_p]) if False else \
                op("dve", lambda e: e.scalar_tensor_tensor(out=pr_[:, hs], in0=pr_[:, hs], scalar=ALPHA, in1=gtmp[half][:], op0=ALU.mult, op1=ALU.add),
                   reads=[B_gt[half]], writes=[B_p])
            if i >= 1:
                ln_affine_store(kb, L, pr_, B_p, stats, mv, rstd, B_st, gb, B_gb, L.out_d[(i - 1) * 128:i * 128, :])
        for tt in range(max(len(rs) - 1, 0), len(nxt)):
            ln_tile(nxt[tt], tt)


_NC = [None]


def kernel(**inputs):
    f = lambda a: np.ascontiguousarray(np.asarray(a, dtype=np.float32))
    if _NC[0] is None:
        _NC[0] = build()
    nc = _NC[0]
    tabs = [make_tables(0), make_tables(1)]
    shared = {
        "w_ada": f(inputs["w_ada"][0]), "b_ada": f(inputs["b_ada"][0]).reshape(1, -1), "w_in": f(inputs["w_in"][0]),
        "pe_ck": f(inputs["pe_ck"][0]), "w_ck1": f(inputs["w_ck1"][0]), "w_ck2": f(inputs["w_ck2"][0]),
        "pe_cv": f(inputs["pe_cv"][0]), "w_cv1": f(inputs["w_cv1"][0]), "w_cv2": f(inputs["w_cv2"][0]),
        "w_nsa_out": f(inputs["w_nsa_out"][0]),
        "s5_a_re": f(inputs["s5_a_re"][0]), "s5_a_im": f(inputs["s5_a_im"][0]),
        "s5_b_re": f(inputs["s5_b_re"][0]), "s5_b_im": f(inputs["s5_b_im"][0]),
        "s5_c_re": f(inputs["s5_c_re"][0]), "s5_c_im": f(inputs["s5_c_im"][0]),
        "s5_d": f(inputs["s5_d"][0]).reshape(512, 1), "s5_log_dt": f(inputs["s5_log_dt"][0]).reshape(1, 32),
        "w_s5_glu": f(inputs["w_s5_glu"][0]), "w_o": f(inputs["w_o"][0]),
        "ln1_g": f(inputs["ln1_g"][0]).reshape(1, -1), "ln1_b": f(inputs["ln1_b"][0]).reshape(1, -1),
        "w_up": f(inputs["w_up"][0]), "conv_w": f(inputs["conv_w"][0]), "conv_b": f(inputs["conv_b"][0]).reshape(1, -1),
        "w_down": f(inputs["w_down"][0]),
        "ln2_g": f(inputs["ln2_g"][0]).reshape(1, -1), "ln2_b": f(inputs["ln2_b"][0]).reshape(1, -1),
    }
    tabf = [{"tb_" + k: f(v) for k, v in tabs[hf].items()} for hf in range(2)]
    x = np.asarray(inputs["x"], dtype=np.float32)
    c = np.asarray(inputs["c"], dtype=np.float32)
    in_maps = []
    for core in range(8):
        b, hf = core // 2, core % 2
        m = dict(shared)
        m.update(tabf[hf])
        if hf == 1:
            m["x"] = f(x[b])
        else:
            xp = np.zeros((T, D), np.float32)
            xp[T // 2:] = x[b][:T // 2]
            m["x"] = xp
        m["cvec"] = f(c[b].reshape(8, 128).T)
        in_maps.append(m)
    res = run_bass_kernel_spmd(nc, in_maps, core_ids=list(range(8)))
    kernel.last = res
    out = np.empty((4, T, D), np.float32)
    for core in range(8):
        b, hf = core // 2, core % 2
        out[b, hf * (T // 2):(hf + 1) * (T // 2)] = np.asarray(res.results[core]["out"], dtype=np.float32)
    return out
```

```python
from contextlib import ExitStack
import math
import numpy as np
import concourse.bass as bass
import concourse.mybir as mybir
from concourse.bass_utils import run_bass_kernel_spmd

F32 = mybir.dt.float32
BF16 = mybir.dt.bfloat16
AF = mybir.ActivationFunctionType
ALU = mybir.AluOpType

T = 4096
NT = 32
D = 1024
KC = 8
DFF = 2816
NEG = -30000.0
ALPHA = 2.0 ** 0.25
STOP = [99]
Q0 = 15
NQ = NT - Q0
DBG = {}


class Buf:
    __slots__ = ("name", "w", "r", "dsem", "dcnt")

    def __init__(self, name):
        self.name = name
        self.w = None
        self.r = {}
        self.dsem = None
        self.dcnt = 0


class KB:
    def __init__(self, nc, es):
        self.nc = nc
        self.ges = es
        self.es = es
        self.eng = {"pe": nc.tensor, "act": nc.scalar, "dve": nc.vector,
                    "pool": nc.gpsimd, "sp": nc.sync}
        self.sem = {}
        self.cnt = {}
        self.seen = {}
        for e in self.eng:
            self.sem[e] = es.enter_context(nc.semaphore("s_" + e))
            self.cnt[e] = 0
            self.seen[e] = {}
        self.pool_sems = []
        self.live = []
        self.n = 0
        self.uid = 0

    def sb(self, name, shape, dt):
        self.uid += 1
        return self.es.enter_context(self.nc.sbuf_tensor(f"{name}_{self.uid}", list(shape), dt))

    def ps(self, name, shape, dt):
        return self.ges.enter_context(self.nc.psum_tensor(name, list(shape), dt))

    def buf(self, name="b"):
        return Buf(name)

    def _wait(self, e, ev):
        if ev is None:
            return
        sem, val = ev
        key = id(sem)
        if self.seen[e].get(key, 0) >= val:
            return
        self.eng[e].wait_ge(sem, val)
        self.seen[e][key] = val

    def _deps(self, e, reads, writes):
        for b in reads:
            self._wait(e, b.w)
        for b in writes:
            self._wait(e, b.w)
            for ev in list(b.r.values()):
                self._wait(e, ev)

    def _record(self, ev, reads, writes):
        for b in reads:
            if b not in writes:
                b.r[id(ev[0])] = ev
        for b in writes:
            b.w = ev
            b.r = {}

    def op(self, e, fn, reads=(), writes=()):
        self._deps(e, reads, writes)
        ins = fn(self.eng[e])
        self.cnt[e] += 1
        ins.then_inc(self.sem[e], 1)
        ev = (self.sem[e], self.cnt[e])
        if e == "pe":
            self.seen[e][id(self.sem[e])] = self.cnt[e]
        self._record(ev, reads, writes)
        self.n += 1
        return ins

    def dma(self, q, out, in_, dbuf, reads=(), writes=(), **kw):
        self._deps(q, reads, writes)
        if dbuf.dsem is None:
            if self.pool_sems:
                dbuf.dsem, dbuf.dcnt = self.pool_sems.pop()
            else:
                dbuf.dsem = self.ges.enter_context(self.nc.semaphore(f"d{len(self.live)}_{self.n}"))
                dbuf.dcnt = 0
            self.live.append(dbuf)
        ins = self.eng[q].dma_start(out=out, in_=in_, **kw)
        dbuf.dcnt += 16
        ins.then_inc(dbuf.dsem, 16)
        ev = (dbuf.dsem, dbuf.dcnt)
        self._record(ev, reads, writes)
        self.n += 1
        return ins

    def barrier(self, release=True):
        for e in self.eng:
            for f in self.eng:
                if f != e and self.cnt[f] > 0:
                    self._wait(e, (self.sem[f], self.cnt[f]))
            for b in self.live:
                self._wait(e, (b.dsem, b.dcnt))
        if release:
            for b in self.live:
                self.pool_sems.append((b.dsem, b.dcnt))
                b.dsem = None
            self.live = []


def make_tables(hf=1):
    t = {}
    t["ident"] = np.eye(128, dtype=np.float32)
    slopes = (2.0 ** (-8.0 * (np.arange(8) + 1) / 8)).astype(np.float32)
    tl = np.arange(128, dtype=np.float32)
    qa = np.zeros((3, 8, 128), np.float32)
    qb = np.zeros((3, 8, 128), np.float32)
    for h in range(8):
        qa[0, h, :] = slopes[h]
        qa[1, h, :] = slopes[h] * 128.0
        qa[2, h, :] = -slopes[h] * tl
        qb[2, h, :] = -slopes[h] * 128.0
    t["qaugA"] = qa
    t["qaugB"] = qb
    key = np.arange(T)
    kt = np.zeros((64, T), np.float32)
    kt[0] = key % 128
    kt[1] = key // 128
    kt[2] = 1.0
    if hf == 0:
        kt[1, :T // 2] = -8192.0
    for j in range(1, 62):
        kt[2 + j] = (key // 64 == j)
    t["kaug_tok"] = kt
    m = np.arange(256)
    pos = 16 * m + 15
    kc = np.zeros((3, 256), np.float32)
    kc[0] = pos % 128
    kc[1] = pos // 128
    kc[2] = 1.0
    kc[1, 0] = -8192.0
    if hf == 0:
        kc[1, :129] = -8192.0
    t["kaug_cmp"] = kc
    tv = np.ones((128, NT), np.float32)
    f0 = np.full((128, 64), -3e9, np.float32)
    hv = np.ones((128, 1), np.float32)
    if hf == 0:
        tv[:, :NT // 2] = 0.0
        f0[:, 32] = 1e9
        hv[:] = 0.0
    else:
        f0[:, 0] = 1e9
    t["tilevalid"] = tv
    t["f0"] = f0
    t["hv"] = hv
    kl = np.arange(128)[:, None]
    tq = np.arange(128)[None, :]
    t["tric"] = np.where(kl > tq, NEG, 0.0).astype(np.float32)
    t["triw"] = np.where(kl <= tq, NEG, 0.0).astype(np.float32)
    mc = np.zeros((128, 16, 128), np.float32)
    for dl in range(16):
        mc[:, dl, :] = np.where(16 * kl + 15 - tq > 128 * dl, NEG, 0.0)
    t["maskc"] = mc
    ov = np.zeros((256, 64), np.float32)
    for mm in range(1, 256):
        for jj in range(64):
            if 4 * jj <= mm <= 4 * jj + 4:
                ov[mm, jj] = 1.0
    t["ov"] = ov.reshape(2, 128, 64).transpose(1, 0, 2).copy()
    mw = np.zeros((128, 128), np.float32)
    aw = np.zeros((128, 128), np.float32)
    up = (np.arange(128) >= 64).astype(np.float32)
    for jw in range(128):
        jr = jw - 64
        if jr < -1:
            mw[:, jw] = 1.0
        elif jr == -1:
            mw[:, jw] = up
            aw[:, jw] = (1.0 - up) * 1e9
        elif jr == 0:
            aw[:, jw] = 1e9
        elif jr == 1:
            aw[:, jw] = np.where(up > 0, 1e9, -1e9)
        else:
            aw[:, jw] = -1e9
    t["mskw"] = mw
    t["addw"] = aw
    par = np.zeros((128, 2), np.float32)
    kk = np.arange(128)
    par[:, 0] = ((kk // 16) % 2 == 0)
    par[:, 1] = ((kk // 16) % 2 == 1)
    t["par"] = par
    return t


TABLE_SHAPES = {k: v.shape for k, v in make_tables().items()}


def build():
    nc = bass.Bass("TRN2", target_bir_lowering=False)

    def din(name, shape):
        return nc.dram_tensor(name, list(shape), F32, kind="ExternalInput").ap()

    x_d = din("x", [T, D])
    c_d = din("cvec", [128, 8])
    w_ada = din("w_ada", [D, 6 * D])
    b_ada = din("b_ada", [1, 6 * D])
    w_in = din("w_in", [D, 3864])
    pe_ck = din("pe_ck", [32, 64])
    w_ck1 = din("w_ck1", [32, 64, 128])
    w_ck2 = din("w_ck2", [128, 64])
    pe_cv = din("pe_cv", [32, 64])
    w_cv1 = din("w_cv1", [32, 64, 128])
    w_cv2 = din("w_cv2", [128, 64])
    w_nsa_out = din("w_nsa_out", [512, D])
    a_re = din("s5_a_re", [32, 64])
    a_im = din("s5_a_im", [32, 64])
    b_re = din("s5_b_re", [32, 64, 16])
    b_im = din("s5_b_im", [32, 64, 16])
    c_re = din("s5_c_re", [32, 16, 64])
    c_im = din("s5_c_im", [32, 16, 64])
    s5_d = din("s5_d", [512, 1])
    log_dt = din("s5_log_dt", [1, 32])
    w_glu = din("w_s5_glu", [512, 2 * D])
    w_o = din("w_o", [D, D])
    ln1_g = din("ln1_g", [1, D])
    ln1_b = din("ln1_b", [1, D])
    w_up = din("w_up", [D, 2 * DFF])
    conv_w = din("conv_w", [3, 2 * DFF])
    conv_b = din("conv_b", [1, 2 * DFF])
    w_down = din("w_down", [DFF, D])
    ln2_g = din("ln2_g", [1, D])
    ln2_b = din("ln2_b", [1, D])
    tb = {k: din("tb_" + k, shp) for k, shp in TABLE_SHAPES.items()}
    out_d = nc.dram_tensor("out", [T // 2, D], F32, kind="ExternalOutput").ap()
    x1_d = nc.dram_tensor("x1_scratch", [NQ * 128, D], F32, kind="Internal").ap()
    dbg_d = {}
    for k, shp in DBG.items():
        dbg_d[k] = nc.dram_tensor("dbg_" + k, list(shp), F32, kind="ExternalOutput").ap()

    with ExitStack() as ges:
        kb = KB(nc, ges)
        op = kb.op
        PS = [kb.ps(f"ps{i}", [128, 512], F32) for i in range(7)]
        PSB = [kb.buf(f"ps{i}") for i in range(7)]
        PT = kb.ps("pt", [128, 1024], BF16)
        PTB = kb.buf("pt")
        PTB2 = PTB
        PTB3 = PTB
        ident_f = kb.sb("ident_f", [128, 128], F32)
        ident_b = kb.sb("ident_b", [128, 128], BF16)
        gates_bc = kb.sb("gates_bc", [128, 2, D], F32)
        modT = kb.sb("modT", [128, 32], F32)
        one11 = kb.sb("one11", [1, 1], F32)
        B_id = kb.buf("ident")
        B_gbc = kb.buf("gbc")
        B_modT = kb.buf("modT")
        B_one = kb.buf("one")
        tilevalid = kb.sb("tilevalid", [128, NT], F32)
        hv = kb.sb("hv", [128, 1], F32)
        B_tv = kb.buf("tv")
        kb.dma("sp", tilevalid[:], tb["tilevalid"], B_tv, writes=[B_tv])
        kb.dma("sp", hv[:], tb["hv"], B_tv, writes=[B_tv])
        kb.dma("sp", ident_f[:], tb["ident"], B_id, writes=[B_id])
        op("dve", lambda e: e.tensor_copy(out=ident_b[:], in_=ident_f[:]), reads=[B_id], writes=[B_id])
        op("pool", lambda e: e.memset(one11[:], 1.0), writes=[B_one])

        def dbg(name, src_ap, rbufs):
            if name in dbg_d:
                b = kb.buf("dbg")
                kb.dma("sp", dbg_d[name], src_ap, b, reads=rbufs)
                kb.live

        with ExitStack() as es0:
            kb.es = es0
            c_sb = kb.sb("c_sb", [128, 8], F32)
            sc = kb.sb("sc", [128, 8], F32)
            sc_bc = kb.sb("sc_bc", [128, 8, 128], F32)
            bada = kb.sb("bada", [128, 6 * D], F32)
            mod_bc = kb.sb("mod_bc", [128, 6 * D], F32)
            wst = [kb.sb(f"wst{i}", [128, 8, 512], F32) for i in range(2)]
            B_c, B_sc, B_bada, B_mod = kb.buf("c"), kb.buf("sc"), kb.buf("bada"), kb.buf("mod")
            B_wst = [kb.buf("wst0"), kb.buf("wst1")]
            kb.dma("sp", c_sb[:], c_d, B_c, writes=[B_c])
            kb.dma("sp", bada[:], b_ada.partition_broadcast(128).rearrange("p o n -> p (o n)"), B_bada, writes=[B_bada])
            op("act", lambda e: e.activation(out=sc[:], in_=c_sb[:], func=AF.Silu), reads=[B_c], writes=[B_sc])
            op("dve", lambda e: e.tensor_copy(out=sc_bc[:], in_=sc[:].unsqueeze(2).to_broadcast([128, 8, 128])),
               reads=[B_sc], writes=[B_sc])
            for j in range(12):
                st = wst[j % 2]
                kb.dma("sp", st[:], w_ada[:, j * 512:(j + 1) * 512].rearrange("(kc p) n -> p kc n", p=128),
                       B_wst[j % 2], writes=[B_wst[j % 2]])
                for kc in range(KC):
                    op("pe", lambda e: e.matmul(PS[j % 2][:, :], lhsT=sc_bc[:, kc, :], rhs=st[:, kc, :],
                                                start=(kc == 0), stop=(kc == KC - 1)),
                       reads=[B_sc, B_wst[j % 2]], writes=[PSB[j % 2]])
                op("dve", lambda e: e.tensor_tensor(out=mod_bc[:, j * 512:(j + 1) * 512], in0=PS[j % 2][:, :],
                                                     in1=bada[:, j * 512:(j + 1) * 512], op=ALU.add),
                   reads=[PSB[j % 2], B_bada], writes=[B_mod])
            op("dve", lambda e: e.tensor_copy(out=gates_bc[:, 0, :], in_=mod_bc[:, 2 * D:3 * D]), reads=[B_mod], writes=[B_gbc])
            op("dve", lambda e: e.tensor_copy(out=gates_bc[:, 1, :], in_=mod_bc[:, 5 * D:6 * D]), reads=[B_mod], writes=[B_gbc])
            for wi, w in enumerate((0, 1, 3, 4)):
                for fc in range(8):
                    col = w * D + fc * 128
                    idx = wi * 8 + fc
                    op("pe", lambda e: e.matmul(PS[2][:, idx:idx + 1], lhsT=mod_bc[0:1, col:col + 128],
                                                rhs=one11[0:1, 0:1], start=True, stop=True),
                       reads=[B_mod, B_one], writes=[PSB[2]])
            op("dve", lambda e: e.tensor_copy(out=modT[:], in_=PS[2][:, 0:32]), reads=[PSB[2]], writes=[B_modT])
            op("dve", lambda e: e.tensor_scalar(out=modT[:, 8:16], in0=modT[:, 8:16], scalar1=1.0, scalar2=None, op0=ALU.add),
               reads=[B_modT], writes=[B_modT])
            op("dve", lambda e: e.tensor_scalar(out=modT[:, 24:32], in0=modT[:, 24:32], scalar1=1.0, scalar2=None, op0=ALU.add),
               reads=[B_modT], writes=[B_modT])
            dbg("modT", modT[:], [B_modT])
            kb.barrier()
        kb.es = ges

        def ln_to_hT(xt, B_x, xn, B_xn, stats, mv, rstd, B_st, hT_out, B_hT, mcol):
            for hf in range(2):
                op("dve", lambda e: e.bn_stats(out=stats[:, hf, :], in_=xt[:, hf * 512:(hf + 1) * 512]),
                   reads=[B_x], writes=[B_st])
            op("dve", lambda e: e.bn_aggr(out=mv[:], in_=stats[:]), reads=[B_st], writes=[B_st])
            op("dve", lambda e: e.tensor_scalar(out=rstd[:], in0=mv[:, 1:2], scalar1=1e-5, scalar2=None, op0=ALU.add),
               reads=[B_st], writes=[B_st])
            op("act", lambda e: e.activation(out=rstd[:], in_=rstd[:], func=AF.Sqrt), reads=[B_st], writes=[B_st])
            op("dve", lambda e: e.reciprocal(out=rstd[:], in_=rstd[:]), reads=[B_st], writes=[B_st])
            op("dve", lambda e: e.tensor_scalar(out=xn[:], in0=xt[:], scalar1=mv[:, 0:1], scalar2=rstd[:, 0:1],
                                                 op0=ALU.subtract, op1=ALU.mult), reads=[B_x, B_st], writes=[B_xn])
            for kc in range(KC):
                op("pe", lambda e: e.transpose(out=PT[:, kc * 128:(kc + 1) * 128], in_=xn[:, kc * 128:(kc + 1) * 128],
                                               identity=ident_b[:]), reads=[B_xn, B_id], writes=[PTB, PTB2] if kc == 7 else [PTB])
            for kc in range(KC):
                op("act", lambda e: e.activation(out=hT_out[:, kc, :], in_=PT[:, kc * 128:(kc + 1) * 128], func=AF.Identity,
                                                 scale=modT[:, mcol + 8 + kc:mcol + 9 + kc], bias=modT[:, mcol + kc:mcol + kc + 1]),
                   reads=[PTB, B_modT], writes=[B_hT])

        def load_cast(dst_fn, src_ap, rows_kc, ncols, stg, B_stg, B_dst, eng="dve"):
            kb.dma("sp", stg[:, 0:rows_kc, 0:ncols], src_ap.rearrange("(kc p) n -> p kc n", p=128), B_stg, writes=[B_stg])
            for kc in range(rows_kc):
                if kc % 2 == 0:
                    op("dve", lambda e: e.tensor_copy(out=dst_fn(kc), in_=stg[:, kc, 0:ncols]), reads=[B_stg], writes=[B_dst])
                else:
                    op("act", lambda e: e.activation(out=dst_fn(kc), in_=stg[:, kc, 0:ncols], func=AF.Copy), reads=[B_stg], writes=[B_dst])

        with ExitStack() as esA:
            kb.es = esA
            oaT = kb.sb("oaT", [128, 4, NQ * 128], BF16)
            uT = kb.sb("uT", [128, 4, T], BF16)
            B_oaT = [kb.buf(f"oaT{i}") for i in range(NT)]
            B_uT = [kb.buf(f"uT{i}") for i in range(NT)]
            if STOP[0] >= 1:
                pass_a1(nc, kb, locals())
            kb.barrier()
            WMp = kb.sb("WMp", [128, 8, 2048], BF16)
            WGLp = kb.sb("WGLp", [128, 4, 2048], BF16)
            WOp = kb.sb("WOp", [128, 8, D], BF16)
            B_Wpre = kb.buf("Wpre")
            if STOP[0] >= 2:
                pass_s5(nc, kb, locals())
            kb.barrier()
            if STOP[0] >= 3:
                pass_a2(nc, kb, locals())
            kb.barrier()
        kb.es = ges
        if STOP[0] >= 4:
            with ExitStack() as esB:
                kb.es = esB
                pass_b(nc, kb, locals())
                kb.barrier()
            kb.es = ges
        kb.barrier(release=False)
    return nc


class NS:
    def __init__(self, d):
        self.__dict__.update(d)


def pass_a1(nc, kb, Ld):
    outer = Ld['esA']
    with ExitStack() as es1:
        d2 = dict(Ld)
        d2['esA'] = es1
        kb.es = es1
        _pass_a1_body(nc, kb, d2)
        kb.barrier()
    kb.es = outer


def _pass_a1_body(nc, kb, Ld):
    L = NS(Ld)
    op = kb.op
    PS, PSB, PT, PTB = L.PS, L.PSB, L.PT, L.PTB
    tb = L.tb
    sb, buf = kb.sb, kb.buf
    WQK = sb("WQK", [128, 8, 1088], BF16)
    WV = sb("WV", [128, 8, 280], BF16)
    WU = sb("WU", [128, 8, 512], BF16)
    KTs = sb("KTs", [128, 2, T], BF16)
    KTw = sb("KTw", [128, 2, 1024], BF16)
    V1 = sb("V1", [128, NT, 4, 65], BF16)
    kcTa = sb("kcTa", [128, 2, 256], BF16)
    vcx = sb("vcx", [128, 2, 2, 129], BF16)
    triC = sb("triC", [128, 4, 128], BF16)
    triW = sb("triW", [128, 4, 128], BF16)
    maskC = sb("maskC", [128, 16, 128], BF16)
    cw1 = sb("cw1", [128, 2, 32, 128], BF16)
    cw2 = sb("cw2", [128, 2, 64], BF16)
    peT = sb("peT", [128, 2, 32], BF16)
    cbias = sb("cbias", [128, 2], F32)
    qA = sb("qA", [67, 8, 128], BF16)
    qB = sb("qB", [67, 8, 128], BF16)
    f0t = sb("f0t", [128, 64], F32)
    mskw = sb("mskw", [128, 128], F32)
    addw = sb("addw", [128, 128], F32)
    B_W, B_KT, B_V1 = buf("W"), [buf(f"KT{i}") for i in range(NT)], [buf(f"V1{i}") for i in range(NT)]
    B_kc, B_vcx, B_cst = buf("kc"), buf("vcx"), buf("cst")
    with ExitStack() as ess:
        kb.es = ess
        stgs = [sb(f"stg{i}", [128, 8, 512], F32) for i in range(3)]
        B_stgs = [buf(f"stg{i}") for i in range(3)]
        rotc = [0]

        def rot():
            rotc[0] += 1
            return stgs[rotc[0] % 3], B_stgs[rotc[0] % 3]

        stg, B_stg = rot()
        op("pool", lambda e: e.memset(KTw[:], 0.0), writes=B_KT)
        op("pool", lambda e: e.memset(cw1[:], 0.0), writes=[B_cst])
        op("pool", lambda e: e.memset(peT[:], 0.0), writes=[B_cst])
        op("pool", lambda e: e.memset(WQK[:, :, 1024:1088], 0.0), writes=[B_W])
        L.load_cast(lambda kc: WQK[:, kc, 0:512], L.w_in[:, 0:512], 8, 512, stg, B_stg, B_W)
        stg, B_stg = rot()
        kb.dma("sp", stg[:, :, :], L.w_in[:, 512:1024].rearrange("(kc p) n -> p kc n", p=128), B_stg, writes=[B_stg])
        for kc in range(8):
            op("dve", lambda e: e.tensor_copy(out=WQK[:, kc, 768:1024], in_=stg[:, kc, 0:256]), reads=[B_stg], writes=[B_W])
            op("dve", lambda e: e.tensor_copy(out=WQK[:, kc, 512:640], in_=stg[:, kc, 256:384]), reads=[B_stg], writes=[B_W])
            op("dve", lambda e: e.tensor_copy(out=WV[:, kc, 0:128], in_=stg[:, kc, 384:512]), reads=[B_stg], writes=[B_W])
        stg, B_stg = rot()
        kb.dma("sp", stg[:, :, 0:280], L.w_in[:, 1024:1304].rearrange("(kc p) n -> p kc n", p=128), B_stg, writes=[B_stg])
        for kc in range(8):
            op("dve", lambda e: e.tensor_copy(out=WQK[:, kc, 640:768], in_=stg[:, kc, 0:128]), reads=[B_stg], writes=[B_W])
            op("dve", lambda e: e.tensor_copy(out=WV[:, kc, 128:280], in_=stg[:, kc, 128:280]), reads=[B_stg], writes=[B_W])
        stg, B_stg = rot()
        L.load_cast(lambda kc: WU[:, kc, :], L.w_in[:, 1304:1816], 8, 512, stg, B_stg, B_W)
        for kv, (w1d, w2d, ped) in enumerate(((L.w_ck1, L.w_ck2, L.pe_ck), (L.w_cv1, L.w_cv2, L.pe_cv))):
            for lh in range(4):
                stg, B_stg = rot()
                kb.dma("sp", stg[0:64, :, :].rearrange("p a (b e) -> p (a b) e", e=128)[:, 0:8, :],
                       w1d[lh * 8:(lh + 1) * 8].rearrange("l d e -> d l e"), B_stg, writes=[B_stg])
                op("dve", lambda e: e.tensor_copy(out=cw1[0:64, kv, lh * 8:(lh + 1) * 8, :],
                                                   in_=stg[0:64, :, :].rearrange("p a (b e) -> p (a b) e", e=128)[:, 0:8, :]),
                   reads=[B_stg], writes=[B_cst])
            kb.dma("sp", stg[:, 0, 0:64], w2d, B_stg, writes=[B_stg])
            op("dve", lambda e: e.tensor_copy(out=cw2[:, kv, :], in_=stg[:, 0, 0:64]), reads=[B_stg], writes=[B_cst])
            kb.dma("sp", stg[0:64, 0, 0:32], ped.rearrange("l d -> d l"), B_stg, writes=[B_stg], allow_slow_non_contiguous=True)
            op("dve", lambda e: e.tensor_copy(out=peT[0:64, kv, :], in_=stg[0:64, 0, 0:32]), reads=[B_stg], writes=[B_cst])
        stg, B_stg = rot()
        for tname, dst in (("tric", triC), ("triw", triW)):
            kb.dma("sp", stg[:, 0, 0:128], tb[tname], B_stg, writes=[B_stg])
            op("dve", lambda e: e.tensor_copy(out=dst[:], in_=stg[:, 0, 0:128].unsqueeze(1).to_broadcast([128, 4, 128])),
               reads=[B_stg], writes=[B_cst])
        kb.dma("sp", stg[:, 0:4, :].rearrange("p a b -> p (a b)"), tb["maskc"].rearrange("p a b -> p (a b)"), B_stg, writes=[B_stg])
        op("dve", lambda e: e.tensor_copy(out=maskC[:].rearrange("p a b -> p (a b)"),
                                           in_=stg[:, 0:4, :].rearrange("p a b -> p (a b)")), reads=[B_stg], writes=[B_cst])
        stg, B_stg = rot()
        op("pool", lambda e: e.memset(vcx[:], 0.0), writes=[B_vcx])
        op("pool", lambda e: e.memset(vcx[:, :, :, 64:65], 1.0), writes=[B_vcx])
        kb.dma("sp", stg[:, 0, 0:128], tb["ov"].rearrange("p a b -> p (a b)"), B_stg, writes=[B_stg])
        for g in range(2):
            op("dve", lambda e: e.tensor_copy(out=vcx[:, :, g, 65:129],
                                               in_=stg[:, 0, 0:128].rearrange("p (a b) -> p a b", b=64)),
               reads=[B_stg], writes=[B_vcx])
        op("pool", lambda e: e.memset(V1[:, :, :, 64:65], 1.0), writes=B_V1)
        op("pool", lambda e: e.memset(kcTa[:], 0.0), writes=[B_kc])
        stg, B_stg = rot()
        for q4 in range(2):
            stg, B_stg = rot()
            kb.dma("sp", stg[64:128, :, :].rearrange("p a b -> p (a b)")[:, 0:2048], tb["kaug_tok"][:, q4 * 2048:(q4 + 1) * 2048],
                   B_stg, writes=[B_stg])
            for g in range(2):
                op("dve", lambda e: e.tensor_copy(out=KTs[64:128, g, q4 * 2048:(q4 + 1) * 2048],
                                                   in_=stg[64:128, :, :].rearrange("p a b -> p (a b)")[:, 0:2048]), reads=[B_stg], writes=B_KT)
        stg, B_stg = rot()
        kb.dma("sp", stg[64:67, 0, 0:256], tb["kaug_cmp"], B_stg, writes=[B_stg])
        for g in range(2):
            op("dve", lambda e: e.tensor_copy(out=kcTa[64:67, g, :], in_=stg[64:67, 0, 0:256]), reads=[B_stg], writes=[B_kc])
        for qt, qn in ((qA, "qaugA"), (qB, "qaugB")):
            stg, B_stg = rot()
            kb.dma("sp", stg[64:67, 0:2, :].rearrange("p a b -> p (a b)"), tb[qn].rearrange("p a b -> p (a b)"), B_stg, writes=[B_stg])
            op("dve", lambda e: e.tensor_copy(out=qt[64:67, :, :].rearrange("p a b -> p (a b)"),
                                               in_=stg[64:67, 0:2, :].rearrange("p a b -> p (a b)")), reads=[B_stg], writes=[B_cst])
        kb.dma("sp", f0t[:], tb["f0"], B_cst, writes=[B_cst])
        kb.dma("sp", mskw[:], tb["mskw"], B_cst, writes=[B_cst])
        kb.dma("sp", addw[:], tb["addw"], B_cst, writes=[B_cst])
        for kv in range(2):
            for l in range(32):
                op("pe", lambda e: e.matmul(PS[2][:, kv:kv + 1], lhsT=cw1[:, kv, l, :], rhs=peT[:, kv, l:l + 1],
                                            start=(l == 0), stop=(l == 31)), reads=[B_cst], writes=[PSB[2]])
        op("dve", lambda e: e.tensor_copy(out=cbias[:], in_=PS[2][:, 0:2]), reads=[PSB[2]], writes=[B_cst])
        kb.barrier()
    kb.es = L.esA
    xt = [sb(f"xt{i}", [128, D], F32) for i in range(3)]
    B_xt = [buf("xt0"), buf("xt1"), buf("xt2")]
    xn = sb("xn", [128, D], BF16); B_xn = buf("xn")
    stats = sb("stats", [128, 2, 6], F32); mv = sb("mv", [128, 2], F32); rstd = sb("rstd", [128, 1], F32)
    B_st = buf("st")
    hTs = [sb(f"hT{i}", [128, 8, 128], BF16) for i in range(2)]; B_hTs = [buf("hT0"), buf("hT1")]
    QTa = sb("QTa", [128, 8, 128], BF16); B_Q = buf("Q")
    cmpT = sb("cmpT", [128, 2, 2, 144], BF16); B_cmp = buf("cmp")
    hact = sb("hact", [128, 2, 2, 8], BF16); B_hact = buf("hact")
    hpad = sb("hpad", [128, 2, 128], BF16); B_hpad = buf("hpad")
    gsig = sb("gsig", [128, 24], F32); B_gs = buf("gsig")
    Pt = [sb(f"Pt{i}", [128, 512], BF16) for i in range(4)]
    B_Pt = [buf(f"Pt{i}") for i in range(4)]
    osb = [sb(f"osb{g}", [128, 3, 4, 65], F32) for g in range(2)]; B_osb = [buf("osb0"), buf("osb1")]
    uimp = [sb(f"uimp{g}", [128, 4, 64], F32) for g in range(2)]
    PTB2 = L.PTB2
    imp = sb("imp", [128, 64], F32); imp2 = sb("imp2", [128, 64], F32); impw = sb("impw", [128, 64], F32)
    m8 = sb("m8", [128, 16], F32)
    B_imp = buf("imp")
    QS = [sb(f"QS{g}", [128, 4, 128], BF16) for g in range(2)]; B_QS = [buf("QS0"), buf("QS1")]
    den = [sb(f"den{g}", [128, 3, 4], F32) for g in range(2)]; fac = [sb(f"fac{g}", [128, 3, 4], F32) for g in range(2)]
    B_fac = [buf("fac0"), buf("fac1")]
    otok = sb("otok", [128, 8, 64], BF16); B_ot = buf("otok")
    otmp = sb("otmp", [128, 4, 64], F32); B_otmp = buf("otmp")
    op("pool", lambda e: e.memset(cmpT[:], 0.0), writes=[B_cmp])
    op("pool", lambda e: e.memset(QTa[:], 0.0), writes=[B_Q])
    for g in range(2):
        op("pool", lambda e: e.memset(QS[g][:], 0.0), writes=[B_QS[g]])
    sels = [sb(f"sel128_{g}", [128, 128], BF16) for g in range(2)]
    B_sel = [buf("sel0"), buf("sel1")]
    for g in range(2):
        op("pool", lambda e: e.memset(sels[g][:], 0.0), writes=[B_sel[g]])
    pcount = [0]

    def next_pt():
        pcount[0] += 1
        return pcount[0] % 4

    def scount_next(c=[0]):
        c[0] += 1
        return (0, 1, 6)[c[0] % 3]

    modT, ident_b = L.modT, L.ident_b

    def ln_a(t):
        x_, B_x = xt[t % 3], B_xt[t % 3]
        for hf in range(2):
            op("dve", lambda e: e.bn_stats(out=stats[:, hf, :], in_=x_[:, hf * 512:(hf + 1) * 512]), reads=[B_x], writes=[B_st])
        op("dve", lambda e: e.bn_aggr(out=mv[:], in_=stats[:]), reads=[B_st], writes=[B_st])
        op("dve", lambda e: e.tensor_scalar(out=rstd[:], in0=mv[:, 1:2], scalar1=1e-5, scalar2=None, op0=ALU.add), reads=[B_st], writes=[B_st])
        op("act", lambda e: e.activation(out=rstd[:], in_=rstd[:], func=AF.Sqrt), reads=[B_st], writes=[B_st])
        op("dve", lambda e: e.reciprocal(out=rstd[:], in_=rstd[:]), reads=[B_st], writes=[B_st])
        op("dve", lambda e: e.tensor_scalar(out=xn[:], in0=x_[:], scalar1=mv[:, 0:1], scalar2=rstd[:, 0:1], op0=ALU.subtract, op1=ALU.mult),
           reads=[B_x, B_st], writes=[B_xn])

    def ln_b(t):
        h_, B_h = hTs[t % 2], B_hTs[t % 2]
        for kc in range(KC):
            op("pe", lambda e: e.transpose(out=PT[:, kc * 128:(kc + 1) * 128], in_=xn[:, kc * 128:(kc + 1) * 128], identity=ident_b[:]),
               reads=[B_xn, L.B_id], writes=[PTB, PTB2] if kc == 7 else ([PTB, L.PTB3] if kc == 6 else [PTB]))
        for kc in range(KC):
            op("act", lambda e: e.activation(out=h_[:, kc, :], in_=PT[:, kc * 128:(kc + 1) * 128], func=AF.Identity,
                                             scale=modT[:, 8 + kc:9 + kc], bias=modT[:, kc:kc + 1]),
               reads=[PTB, PTB2, L.B_modT] if kc == 7 else ([PTB, L.PTB3, L.B_modT] if kc == 6 else [PTB, L.B_modT]), writes=[B_h])

    pend_ot = [None]

    def flush_ot():
        if pend_ot[0] is None:
            return
        ti = pend_ot[0]
        pend_ot[0] = None
        osl = slice((ti - Q0) * 128, (ti - Q0 + 1) * 128)
        for c in range(4):
            op("pe", lambda e: e.transpose(out=PT[:, c * 128:(c + 1) * 128], in_=otok[:, 2 * c:2 * c + 2, :].rearrange("p a b -> p (a b)"),
                                           identity=L.ident_b[:]), reads=[B_ot, L.B_id], writes=[PTB])
        op("act", lambda e: e.activation(out=L.oaT[:, :, osl], in_=PT[:, 0:512].rearrange("p (a b) -> p a b", b=128), func=AF.Copy),
           reads=[PTB], writes=[L.B_oaT[ti]])

    for t0 in range(2):
        kb.dma("sp", xt[t0][:], L.x_d[t0 * 128:(t0 + 1) * 128, :], B_xt[t0], writes=[B_xt[t0]])
    ln_a(0)
    ln_b(0)
    for i in range(NT):
        if i + 2 < NT:
            kb.dma("sp", xt[(i + 2) % 3][:], L.x_d[(i + 2) * 128:(i + 3) * 128, :], B_xt[(i + 2) % 3], writes=[B_xt[(i + 2) % 3]])
        if i + 1 < NT:
            ln_a(i + 1)
        hT, B_hT = hTs[i % 2], B_hTs[i % 2]
        lnb_done = [i + 1 >= NT]
        tsl = slice(i * 128, (i + 1) * 128)
        isq = i >= Q0
        for g in (range(2) if isq else ()):
            for hh in range(4):
                h = g * 4 + hh
                for kc in range(KC):
                    op("pe", lambda e: e.matmul(PS[g][:, hh * 128:(hh + 1) * 128], lhsT=WQK[:, kc, h * 64:h * 64 + 128],
                                                rhs=hT[:, kc, :], start=(kc == 0), stop=(kc == KC - 1)),
                       reads=[B_W, B_hT], writes=[PSB[g]])
            op("act", lambda e: e.activation(out=QTa[0:64, g * 4:(g + 1) * 4, :].rearrange("p a b -> p (a b)"),
                                             in_=PS[g][0:64, :], func=AF.Copy, scale=0.125), reads=[PSB[g]], writes=[B_Q])
        if isq:
            op("dve", lambda e: e.scalar_tensor_tensor(out=QTa[64:67, :, :], in0=qB[64:67, :, :], scalar=float(i), in1=qA[64:67, :, :],
                                                        op0=ALU.mult, op1=ALU.add), reads=[B_cst], writes=[B_Q])
        for grp in range(2):
            for s4 in range(4):
                c0 = 512 + grp * 256 + s4 * 64
                for kc in range(KC):
                    op("pe", lambda e: e.matmul(PS[2 + grp][:, s4 * 128:(s4 + 1) * 128], lhsT=WQK[:, kc, c0:c0 + 128],
                                                rhs=hT[:, kc, :], start=(kc == 0), stop=(kc == KC - 1)),
                       reads=[B_W, B_hT], writes=[PSB[2 + grp]])
        wsl = slice((i % 8) * 128, (i % 8 + 1) * 128)
        op("act", lambda e: e.activation(out=KTs[0:64, :, tsl], in_=PS[2][0:64, 0:256].rearrange("p (b c) -> p b c", b=2),
                                         func=AF.Copy), reads=[PSB[2]], writes=[B_KT[i]])
        op("act", lambda e: e.activation(out=KTw[0:64, :, wsl], in_=PS[2][0:64, 256:512].rearrange("p (b c) -> p b c", b=2),
                                         func=AF.Copy), reads=[PSB[2]], writes=[B_KT[i]])
        op("dve", lambda e: e.tensor_copy(out=KTw[64:67, :, wsl], in_=KTs[64:67, :, tsl]), reads=[B_cst], writes=[B_KT[i]])
        op("dve", lambda e: e.tensor_copy(out=cmpT[0:64, :, :, 16:144], in_=PS[3][0:64, :].rearrange("p (a b c) -> p a b c", a=2, b=2)),
           reads=[PSB[3]], writes=[B_cmp])
        flush_ot()
        for kc in range(KC):
            op("pe", lambda e: e.matmul(PS[4][:, 0:280], lhsT=hT[:, kc, :], rhs=WV[:, kc, :], start=(kc == 0), stop=(kc == KC - 1)),
               reads=[B_W, B_hT], writes=[PSB[4]])
        op("dve", lambda e: e.tensor_copy(out=V1[:, i, :, 0:64], in_=PS[4][:, 0:256].rearrange("p (a b) -> p a b", b=64)),
           reads=[PSB[4]], writes=[B_V1[i]])
        op("act", lambda e: e.activation(out=gsig[:], in_=PS[4][:, 256:280], func=AF.Sigmoid), reads=[PSB[4]], writes=[B_gs])
        if not isq and not lnb_done[0]:
            ln_b(i + 1)
            lnb_done[0] = True
        for ct in range(4):
            for kc in range(KC):
                op("pe", lambda e: e.matmul(PS[5][:, ct * 128:(ct + 1) * 128], lhsT=WU[:, kc, ct * 128:(ct + 1) * 128],
                                            rhs=hT[:, kc, :], start=(kc == 0), stop=(kc == KC - 1)),
                   reads=[B_W, B_hT], writes=[PSB[5]])
        op("act", lambda e: e.activation(out=L.uT[:, :, tsl], in_=PS[5][:, :].rearrange("p (a b) -> p a b", b=128), func=AF.Identity,
                                         scale=L.tilevalid[:, i:i + 1]), reads=[PSB[5], L.B_tv], writes=[L.B_uT[i]])
        for kv in range(2):
            for g in range(2):
                o0 = (kv * 2 + g) * 8
                for l in range(32):
                    op("pe", lambda e: e.matmul(PS[6][:, o0:o0 + 8], lhsT=cw1[:, kv, l, :], rhs=cmpT[:, kv, g, l:l + 113:16],
                                                start=(l == 0), stop=(l == 31)), reads=[B_cst, B_cmp], writes=[PSB[6]])
            op("act", lambda e: e.activation(out=hact[:, kv, :, :].rearrange("p a b -> p (a b)"), in_=PS[6][:, kv * 16:(kv + 1) * 16],
                                             func=AF.Silu, bias=cbias[:, kv:kv + 1]), reads=[PSB[6], B_cst], writes=[B_hact])
        op("dve", lambda e: e.tensor_copy(out=cmpT[0:64, :, :, 0:16], in_=cmpT[0:64, :, :, 128:144]), reads=[B_cmp], writes=[B_cmp])
        for g in range(2):
            op("pe", lambda e: e.matmul(PS[6][:, 64 + g * 8:64 + (g + 1) * 8], lhsT=cw2[:].rearrange("p a b -> p (a b)"), rhs=hact[:, 0, g, :],
                                        start=True, stop=True), reads=[B_cst, B_hact], writes=[PSB[6]])
        op("dve", lambda e: e.tensor_copy(out=kcTa[0:64, :, 8 * i:8 * i + 8], in_=PS[6][0:64, 64:80].rearrange("p (a b) -> p a b", b=8)),
           reads=[PSB[6]], writes=[B_kc])
        mt_i, mo = (8 * i) // 128, (8 * i) % 128
        op("pool", lambda e: e.memset(hpad[:], 0.0), writes=[B_hpad])
        op("pool", lambda e: e.tensor_copy(out=hpad[:, :, mo:mo + 8], in_=hact[:, 1, :, :]), reads=[B_hact], writes=[B_hpad])
        for g in range(2):
            op("pe", lambda e: e.matmul(PS[6][:, 128 + g * 64:128 + (g + 1) * 64], lhsT=hpad[:, g, :], rhs=cw2[:, 1, :],
                                        start=True, stop=True), reads=[B_cst, B_hpad], writes=[PSB[6]])
        op("dve", lambda e: e.tensor_tensor(out=vcx[:, mt_i, :, 0:64], in0=PS[6][:, 128:256].rearrange("p (a b) -> p a b", b=64),
                                             in1=vcx[:, mt_i, :, 0:64], op=ALU.add), reads=[PSB[6]], writes=[B_vcx])
        if isq:
            OB = [PS[2], PS[3], PS[4], PS[5]]
            OBB = [PSB[2], PSB[3], PSB[4], PSB[5]]
            Qgs = [QTa[:, g * 4:(g + 1) * 4, :].rearrange("p a b -> p (a b)") for g in range(2)]
            QSf = [QS[g][:].rearrange("p a b -> p (a b)") for g in range(2)]

            def emit_score(job):
                kind, g, kidx, first, last = job
                sbk = scount_next()
                extra = []
                if kind == "c":
                    mt = kidx
                    dl = i - 16 * mt
                    lhs, rl = kcTa[:, g, mt * 128:(mt + 1) * 128], [B_kc, B_Q]
                    if dl < 16:
                        for hh in range(4):
                            extra.append((PS[sbk][:, hh * 128:(hh + 1) * 128], L.ident_b[:], maskC[:, dl, :], [L.B_id, B_cst]))
                else:
                    kt = kidx
                    if kind == "s":
                        lhs = KTs[:, g, kt * 128:(kt + 1) * 128]
                    else:
                        lhs = KTw[:, g, (kt % 8) * 128:(kt % 8 + 1) * 128]
                        if kt == i - 4:
                            extra.append((PS[sbk][:, :], L.ident_b[:], triW[:].rearrange("p a b -> p (a b)"), [L.B_id, B_cst]))
                    if kt == i:
                        extra.append((PS[sbk][:, :], L.ident_b[:], triC[:].rearrange("p a b -> p (a b)"), [L.B_id, B_cst]))
                    rl = [B_KT[kt], B_QS[g] if kind == "s" else B_Q]
                op("pe", lambda e: e.matmul(PS[sbk][:, :], lhsT=lhs, rhs=(QSf[g] if kind == "s" else Qgs[g]), start=True, stop=(len(extra) == 0)),
                   reads=rl, writes=[PSB[sbk]])
                for xi, (oap, lt, rh, rb) in enumerate(extra):
                    op("pe", lambda e: e.matmul(oap, lhsT=lt, rhs=rh, start=False, stop=(xi == len(extra) - 1), skip_group_check=True),
                       reads=rb, writes=[PSB[sbk]])
                p = next_pt()
                op("act", lambda e: e.activation(out=Pt[p][:], in_=PS[sbk][:, :], func=AF.Exp), reads=[PSB[sbk]], writes=[B_Pt[p]])
                return p

            def emit_pv(job, p):
                kind, g, kidx, first, last = job
                if kind == "c":
                    rhs, ncol, rb = vcx[:, kidx, g, :], 129, B_vcx
                else:
                    vi = g if kind == "s" else 2 + g
                    rhs, ncol, rb = V1[:, kidx, vi, :], 65, B_V1[kidx]
                for hh in range(4):
                    op("pe", lambda e: e.matmul(OB[hh][:, 0:ncol], lhsT=Pt[p][:, hh * 128:(hh + 1) * 128], rhs=rhs,
                                                start=first, stop=last), reads=[B_Pt[p], rb], writes=[OBB[hh]])

            def epilogue(kind, g):
                bi = {"c": 0, "s": 1, "w": 2}[kind]
                o_, B_o = osb[g], B_osb[g]
                for hh in range(4):
                    if hh % 2:
                        op("dve", lambda e: e.tensor_copy(out=o_[:, bi, hh, :], in_=OB[hh][:, 0:65]), reads=[OBB[hh]], writes=[B_o])
                    else:
                        op("act", lambda e: e.activation(out=o_[:, bi, hh, :], in_=OB[hh][:, 0:65], func=AF.Copy), reads=[OBB[hh]], writes=[B_o])
                    if kind == "c":
                        op("dve", lambda e: e.tensor_copy(out=uimp[g][:, hh, :], in_=OB[hh][:, 65:129]), reads=[OBB[hh]], writes=[B_o])
                if kind == "c":
                    dn, B_d = den[g], B_fac[g]
                    op("dve", lambda e: e.tensor_scalar(out=dn[:, 0, :], in0=o_[:, 0, :, 64], scalar1=1e-30, scalar2=None, op0=ALU.max),
                       reads=[B_o], writes=[B_d])
                    op("dve", lambda e: e.reciprocal(out=dn[:, 0, :], in_=dn[:, 0, :]), reads=[B_d], writes=[B_d])
                    op("dve", lambda e: e.tensor_scalar(out=imp[:], in0=uimp[g][:, 0, :], scalar1=dn[:, 0, 0:1], scalar2=None, op0=ALU.mult),
                       reads=[B_o, B_d], writes=[B_imp])
                    for hh in range(1, 4):
                        op("dve", lambda e: e.scalar_tensor_tensor(out=imp[:], in0=uimp[g][:, hh, :], scalar=dn[:, 0, hh:hh + 1], in1=imp[:],
                                                                    op0=ALU.mult, op1=ALU.add), reads=[B_o, B_d], writes=[B_imp])
                    w0 = 64 - 2 * i
                    op("dve", lambda e: e.tensor_tensor(out=imp2[:], in0=imp[:], in1=mskw[:, w0:w0 + 64], op=ALU.mult),
                       reads=[B_imp, B_cst], writes=[B_imp])
                    op("dve", lambda e: e.tensor_tensor(out=imp2[:], in0=imp2[:], in1=addw[:, w0:w0 + 64], op=ALU.add),
                       reads=[B_imp, B_cst], writes=[B_imp])
                    op("dve", lambda e: e.tensor_tensor(out=imp2[:], in0=imp2[:], in1=f0t[:], op=ALU.max), reads=[B_cst], writes=[B_imp])
                    op("dve", lambda e: e.max(out=m8[:, 0:8], in_=imp2[:]), reads=[B_imp], writes=[B_imp])
                    op("dve", lambda e: e.match_replace(out=impw[:], in_to_replace=m8[:, 0:8], in_values=imp2[:], imm_value=-3e9),
                       reads=[B_imp], writes=[B_imp])
                    op("dve", lambda e: e.max(out=m8[:, 8:16], in_=impw[:]), reads=[B_imp], writes=[B_imp])
                    op("dve", lambda e: e.tensor_scalar(out=sels[g][:, 67:128], in0=imp2[:, 1:62], scalar1=m8[:, 15:16], scalar2=None, op0=ALU.is_ge),
                       reads=[B_imp], writes=[B_sel[g]])
                if kind == "s":
                    dn, fc_, B_d = den[g], fac[g], B_fac[g]
                    op("dve", lambda e: e.tensor_scalar(out=dn[:, 1:3, :], in0=o_[:, 1:3, :, 64], scalar1=1e-30, scalar2=None, op0=ALU.max),
                       reads=[B_o], writes=[B_d])
                    op("dve", lambda e: e.reciprocal(out=dn[:, 1:3, :], in_=dn[:, 1:3, :]), reads=[B_d], writes=[B_d])
                    op("dve", lambda e: e.tensor_tensor(out=fc_[:], in0=dn[:], in1=gsig[:, g * 12:(g + 1) * 12].rearrange("p (h b) -> p b h", b=3),
                                                         op=ALU.mult), reads=[B_d, B_gs], writes=[B_d])
                    for hh in range(4):
                        h = g * 4 + hh
                        op("dve", lambda e: e.tensor_scalar(out=otmp[:, hh, :], in0=o_[:, 0, hh, 0:64], scalar1=fc_[:, 0, hh:hh + 1], scalar2=None,
                                                             op0=ALU.mult), reads=[B_o, B_d], writes=[B_otmp])
                        op("dve", lambda e: e.scalar_tensor_tensor(out=otmp[:, hh, :], in0=o_[:, 1, hh, 0:64], scalar=fc_[:, 1, hh:hh + 1],
                                                                    in1=otmp[:, hh, :], op0=ALU.mult, op1=ALU.add),
                           reads=[B_o, B_d], writes=[B_otmp])
                        op("dve", lambda e: e.scalar_tensor_tensor(out=otok[:, h, :], in0=o_[:, 2, hh, 0:64], scalar=fc_[:, 2, hh:hh + 1],
                                                                    in1=otmp[:, hh, :], op0=ALU.mult, op1=ALU.add),
                           reads=[B_o, B_d, B_otmp], writes=[B_ot])

            def epi_c_tail(g):
                c0, pb_ = (896, PTB2) if g == 0 else (768, L.PTB3)
                op("pe", lambda e: e.transpose(out=PT[:, c0:c0 + 128], in_=sels[g][:], identity=L.ident_b[:]), reads=[B_sel[g], L.B_id], writes=[pb_])
                op("dve", lambda e: e.tensor_scalar(out=QS[g][64:128], in0=PT[64:128, c0:c0 + 128].unsqueeze(1).to_broadcast([64, 4, 128]),
                                                     scalar1=-1.0, scalar2=-NEG, op0=ALU.add, op1=ALU.mult), reads=[pb_], writes=[B_QS[g]])
                op("pool", lambda e: e.tensor_copy(out=QS[g][0:67], in_=QTa[0:67, g * 4:(g + 1) * 4, :]), reads=[B_Q], writes=[B_QS[g]])

            jobs = []
            for kind in ("c", "w", "s"):
                for g in range(2):
                    if kind == "c":
                        ks = [0] if 8 * i + 7 < 128 else [0, 1]
                    elif kind == "w":
                        ks = list(range(max(0, i - 4), i + 1))
                    else:
                        ks = list(range(0, i + 1))
                    for n_, k_ in enumerate(ks):
                        jobs.append((kind, g, k_, n_ == 0, n_ == len(ks) - 1))
            pend = []
            n_s = [0]
            for job in jobs:
                if job[0] == "s" and job[3] and job[1] == 0:
                    epi_c_tail(0)
                    epi_c_tail(1)
                    n_s[0] = 0
                if job[0] == "s":
                    n_s[0] += 1
                    if n_s[0] == 4 and not lnb_done[0]:
                        ln_b(i + 1)
                        lnb_done[0] = True
                p = emit_score(job)
                pend.append((job, p))
                if len(pend) > 2:
                    pj = pend.pop(0)
                    emit_pv(*pj)
                    if pj[0][4]:
                        epilogue(pj[0][0], pj[0][1])
            for pj in pend:
                emit_pv(*pj)
                if pj[0][4]:
                    epilogue(pj[0][0], pj[0][1])
        if not lnb_done[0]:
            ln_b(i + 1)
            lnb_done[0] = True
        if isq:
            pend_ot[0] = i
    flush_ot()
    if "oaT" in L.dbg_d:
        b = kb.buf("dbg")
        st2 = sb("dbgst", [128, 4, 512], F32)
        op("dve", lambda e: e.tensor_copy(out=st2[:], in_=L.oaT[:, :, 0:512]), reads=L.B_oaT, writes=[b])
        kb.dma("sp", L.dbg_d["oaT"], st2[:], b, reads=[b])


def pass_s5(nc, kb, Ld):
    L = NS(Ld)
    op = kb.op
    PS, PSB, PT, PTB = L.PS, L.PSB, L.PT, L.PTB
    sb, buf = kb.sb, kb.buf
    PI = math.pi
    with ExitStack() as es5:
        kb.es = es5
        BBT = sb("BBT", [128, 16, 2, 128], BF16)
        BBTn = sb("BBTn", [128, 16, 128], BF16)
        Cq = sb("Cq", [128, 16, 4, 128], BF16)
        EI = sb("EI", [128, 2, 16, 128], F32)
        EF = sb("EF", [128, 2, 16, 128], F32)
        L128 = sb("L128", [128, 2, 16], F32)
        dsk = sb("dsk", [128, 4], F32)
        ones = sb("ones", [128, 128], F32)
        carry = sb("carry", [128, 2, 16], F32)
        zl = sb("zl", [128, 2, 16], F32)
        B_tab, B_car, B_zl = buf("tab"), buf("carry"), buf("zl")
        with ExitStack() as est:
            kb.es = est
            ar = sb("ar", [128, 16], F32); ai = sb("ai", [128, 16], F32); ldt = sb("ldt", [128, 16], F32)
            dt = sb("dt", [128, 16], F32); lrd = sb("lrd", [128, 16], F32); ang = sb("ang", [128, 16], F32)
            mag = sb("mag", [128, 16], F32); mgi = sb("mgi", [128, 16], F32)
            sn = sb("sn", [128, 16], F32); cs = sb("cs", [128, 16], F32); tmp = sb("tmp", [128, 16], F32); tmp2 = sb("tmp2", [128, 16], F32)
            lb = sb("lb", [128, 2, 16], F32); lbi = sb("lbi", [128, 2, 16], F32); pw = sb("pw", [128, 2, 16], F32)
            coef = sb("coef", [128, 2, 16], F32); den = sb("dens", [128, 16], F32)
            bsb = sb("bsb", [128, 2, 16, 16], F32); bb = sb("bb", [128, 2, 16, 16], F32); bt = sb("bt", [128, 16, 16], F32)
            Apr = sb("Apr", [128, 128], BF16)
            Cn = sb("Cn", [128, 2, 4, 64], F32)
            par = sb("par", [128, 2], F32)
            et1 = sb("et1", [128, 16, 64], F32); et2 = sb("et2", [128, 16, 64], F32)
            B_s = buf("s5setup"); B_apr = buf("apr"); B_et = buf("et")
            kb.dma("sp", ar[:], L.a_re.rearrange("(pr g2) p -> (g2 p) pr", g2=2), B_s, writes=[B_s], allow_slow_non_contiguous=True)
            kb.dma("sp", ai[:], L.a_im.rearrange("(pr g2) p -> (g2 p) pr", g2=2), B_s, writes=[B_s], allow_slow_non_contiguous=True)
            for g2 in range(2):
                kb.dma("sp", ldt[g2 * 64:(g2 + 1) * 64, :],
                       L.log_dt.rearrange("o (pr g2) -> o g2 pr", g2=2)[:, g2, :].partition_broadcast(64).rearrange("p o n -> p (o n)"),
                       B_s, writes=[B_s], allow_slow_non_contiguous=True)
                for ri, bd in enumerate((L.b_re, L.b_im)):
                    kb.dma("sp", bsb[g2 * 64:(g2 + 1) * 64, ri, :, :], bd.rearrange("(pr g2) p h -> g2 p pr h", g2=2)[g2],
                           B_s, writes=[B_s])
            for ri, cd in enumerate((L.c_re, L.c_im)):
                kb.dma("sp", Cn[:, ri, :, :], cd.rearrange("(ct gl) h p -> (gl h) ct p", ct=4), B_s, writes=[B_s])
            kb.dma("sp", par[:], L.tb["par"], B_s, writes=[B_s])
            kb.dma("sp", dsk[:], L.s5_d.rearrange("(ct q) o -> q (ct o)", ct=4), B_tab, writes=[B_tab], allow_slow_non_contiguous=True)
            op("pool", lambda e: e.memset(ones[:], 1.0), writes=[B_tab])
            op("pool", lambda e: e.memset(carry[:], 0.0), writes=[B_car])
            op("pool", lambda e: e.memset(Cq[:], 0.0), writes=[B_tab])
            R, W = [B_s], [B_s]
            dv = lambda f: op("dve", f, reads=R, writes=W)
            ac = lambda f: op("act", f, reads=R, writes=W)
            ac(lambda e: e.activation(out=dt[:], in_=ldt[:], func=AF.Exp))
            dv(lambda e: e.tensor_scalar(out=ar[:], in0=ar[:], scalar1=-1e-4, scalar2=None, op0=ALU.min))
            dv(lambda e: e.tensor_tensor(out=lrd[:], in0=ar[:], in1=dt[:], op=ALU.mult))
            dv(lambda e: e.tensor_tensor(out=ang[:], in0=ai[:], in1=dt[:], op=ALU.mult))
            ac(lambda e: e.activation(out=mag[:], in_=lrd[:], func=AF.Exp))
            ac(lambda e: e.activation(out=mgi[:], in_=lrd[:], func=AF.Exp, scale=-1.0))
            ti = sb("ti", [128, 16], mybir.dt.int32)

            def rred(dst, shift):
                dv(lambda e: e.tensor_scalar(out=tmp2[:], in0=ang[:], scalar1=shift, scalar2=None, op0=ALU.add))
                dv(lambda e: e.tensor_scalar(out=tmp[:], in0=tmp2[:], scalar1=1.0 / (2 * PI), scalar2=None, op0=ALU.mult))
                dv(lambda e: e.tensor_copy(out=ti[:], in_=tmp[:]))
                dv(lambda e: e.tensor_copy(out=tmp[:], in_=ti[:]))
                dv(lambda e: e.scalar_tensor_tensor(out=tmp2[:], in0=tmp[:], scalar=-2 * PI, in1=tmp2[:], op0=ALU.mult, op1=ALU.add))
                dv(lambda e: e.tensor_scalar(out=tmp[:], in0=tmp2[:], scalar1=PI, scalar2=2 * PI, op0=ALU.is_gt, op1=ALU.mult))
                dv(lambda e: e.tensor_tensor(out=tmp2[:], in0=tmp2[:], in1=tmp[:], op=ALU.subtract))
                dv(lambda e: e.tensor_scalar(out=tmp[:], in0=tmp2[:], scalar1=-PI, scalar2=2 * PI, op0=ALU.is_lt, op1=ALU.mult))
                dv(lambda e: e.tensor_tensor(out=tmp2[:], in0=tmp2[:], in1=tmp[:], op=ALU.add))
                ac(lambda e: e.activation(out=dst[:], in_=tmp2[:], func=AF.Sin))

            rred(sn, 0.0)
            rred(cs, 0.5 * PI)
            dv(lambda e: e.tensor_tensor(out=lb[:, 0, :], in0=mag[:], in1=cs[:], op=ALU.mult))
            dv(lambda e: e.tensor_tensor(out=lb[:, 1, :], in0=mag[:], in1=sn[:], op=ALU.mult))
            dv(lambda e: e.tensor_tensor(out=lbi[:, 0, :], in0=mgi[:], in1=cs[:], op=ALU.mult))
            dv(lambda e: e.scalar_tensor_tensor(out=lbi[:, 1, :], in0=mgi[:], scalar=-1.0, in1=sn[:], op0=ALU.mult, op1=ALU.mult))
            dv(lambda e: e.tensor_tensor(out=den[:], in0=ar[:], in1=ar[:], op=ALU.mult))
            dv(lambda e: e.tensor_tensor(out=tmp[:], in0=ai[:], in1=ai[:], op=ALU.mult))
            dv(lambda e: e.tensor_tensor(out=den[:], in0=den[:], in1=tmp[:], op=ALU.add))
            dv(lambda e: e.reciprocal(out=den[:], in_=den[:]))
            dv(lambda e: e.tensor_scalar(out=tmp2[:], in0=lb[:, 0, :], scalar1=-1.0, scalar2=None, op0=ALU.add))
            dv(lambda e: e.tensor_tensor(out=tmp[:], in0=tmp2[:], in1=ar[:], op=ALU.mult))
            dv(lambda e: e.tensor_tensor(out=coef[:, 0, :], in0=lb[:, 1, :], in1=ai[:], op=ALU.mult))
            dv(lambda e: e.tensor_tensor(out=coef[:, 0, :], in0=coef[:, 0, :], in1=tmp[:], op=ALU.add))
            dv(lambda e: e.tensor_tensor(out=coef[:, 0, :], in0=coef[:, 0, :], in1=den[:], op=ALU.mult))
            dv(lambda e: e.tensor_tensor(out=tmp[:], in0=tmp2[:], in1=ai[:], op=ALU.mult))
            dv(lambda e: e.tensor_tensor(out=coef[:, 1, :], in0=lb[:, 1, :], in1=ar[:], op=ALU.mult))
            dv(lambda e: e.tensor_tensor(out=coef[:, 1, :], in0=coef[:, 1, :], in1=tmp[:], op=ALU.subtract))
            dv(lambda e: e.tensor_tensor(out=coef[:, 1, :], in0=coef[:, 1, :], in1=den[:], op=ALU.mult))
            cbr = lambda k: coef[:, k, :].unsqueeze(2).to_broadcast([128, 16, 16])
            dv(lambda e: e.tensor_tensor(out=bb[:, 0], in0=bsb[:, 0], in1=cbr(0), op=ALU.mult))
            dv(lambda e: e.tensor_tensor(out=bt[:], in0=bsb[:, 1], in1=cbr(1), op=ALU.mult))
            dv(lambda e: e.tensor_tensor(out=bb[:, 0], in0=bb[:, 0], in1=bt[:], op=ALU.subtract))
            dv(lambda e: e.tensor_tensor(out=bb[:, 1], in0=bsb[:, 1], in1=cbr(0), op=ALU.mult))
            dv(lambda e: e.tensor_tensor(out=bt[:], in0=bsb[:, 0], in1=cbr(1), op=ALU.mult))
            dv(lambda e: e.tensor_tensor(out=bb[:, 1], in0=bb[:, 1], in1=bt[:], op=ALU.add))
            for pr in range(16):
                prl = pr % 4
                for ri in range(2):
                    op("pool", lambda e: e.memset(Apr[:], 0.0), writes=[B_apr])
                    op("dve", lambda e: e.tensor_copy(out=Apr[0:64, 32 * prl:32 * prl + 16], in_=bb[0:64, ri, pr, :]), reads=[B_s], writes=[B_apr])
                    op("dve", lambda e: e.tensor_copy(out=Apr[64:128, 32 * prl + 16:32 * prl + 32], in_=bb[64:128, ri, pr, :]),
                       reads=[B_s], writes=[B_apr])
                    op("pe", lambda e: e.transpose(out=PT[:, 0:128], in_=Apr[:], identity=L.ident_b[:]), reads=[B_apr, L.B_id], writes=[PTB])
                    op("act", lambda e: e.activation(out=BBT[:, pr, ri, :], in_=PT[:, 0:128], func=AF.Copy), reads=[PTB], writes=[B_tab])
                    if ri == 1:
                        op("act", lambda e: e.activation(out=BBTn[:, pr, :], in_=PT[:, 0:128], func=AF.Copy, scale=-1.0),
                           reads=[PTB], writes=[B_tab])
            for ct in range(4):
                for ri in range(2):
                    for g2 in range(2):
                        op("dve", lambda e: e.tensor_scalar(out=Apr[:, g2 * 64:(g2 + 1) * 64], in0=Cn[:, ri, ct, :], scalar1=par[:, g2:g2 + 1],
                                                             scalar2=None, op0=ALU.mult), reads=[B_s], writes=[B_apr])
                    op("pe", lambda e: e.transpose(out=PT[:, 0:128], in_=Apr[:], identity=L.ident_b[:]), reads=[B_apr, L.B_id], writes=[PTB])
                    for prl in range(4):
                        pr = ct * 4 + prl
                        sl = slice(32 * prl, 32 * prl + 32)
                        if ri == 0:
                            op("dve", lambda e: e.tensor_copy(out=Cq[:, pr, 0, sl], in_=PT[:, sl]), reads=[PTB], writes=[B_tab])
                            op("dve", lambda e: e.tensor_scalar(out=Cq[:, pr, 3, sl], in0=PT[:, sl], scalar1=-1.0, scalar2=None, op0=ALU.mult),
                               reads=[PTB], writes=[B_tab])
                        else:
                            for k in (1, 2):
                                op("dve", lambda e: e.tensor_scalar(out=Cq[:, pr, k, sl], in0=PT[:, sl], scalar1=-1.0, scalar2=None,
                                                                     op0=ALU.mult), reads=[PTB], writes=[B_tab])
            for tabl, base in ((EF, lb), (EI, lbi)):
                op("pool", lambda e: e.memset(tabl[:, 0, :, 0:1], 1.0), writes=[B_tab])
                op("pool", lambda e: e.memset(tabl[:, 1, :, 0:1], 0.0), writes=[B_tab])
                op("dve", lambda e: e.tensor_copy(out=pw[:], in_=base[:]), reads=[B_s], writes=[B_s])
                for k in range(7):
                    n = 1 << k
                    pbr = lambda c: pw[:, c, :].unsqueeze(2).to_broadcast([128, 16, n])
                    RW = dict(reads=[B_s, B_tab, B_et], writes=[B_tab, B_et])
                    op("dve", lambda e: e.tensor_tensor(out=et1[:, :, 0:n], in0=tabl[:, 0, :, 0:n], in1=pbr(0), op=ALU.mult), **RW)
                    op("dve", lambda e: e.tensor_tensor(out=et2[:, :, 0:n], in0=tabl[:, 1, :, 0:n], in1=pbr(1), op=ALU.mult), **RW)
                    op("dve", lambda e: e.tensor_tensor(out=tabl[:, 0, :, n:2 * n], in0=et1[:, :, 0:n], in1=et2[:, :, 0:n], op=ALU.subtract), **RW)
                    op("dve", lambda e: e.tensor_tensor(out=et1[:, :, 0:n], in0=tabl[:, 0, :, 0:n], in1=pbr(1), op=ALU.mult), **RW)
                    op("dve", lambda e: e.tensor_tensor(out=et2[:, :, 0:n], in0=tabl[:, 1, :, 0:n], in1=pbr(0), op=ALU.mult), **RW)
                    op("dve", lambda e: e.tensor_tensor(out=tabl[:, 1, :, n:2 * n], in0=et1[:, :, 0:n], in1=et2[:, :, 0:n], op=ALU.add), **RW)
                    op("dve", lambda e: e.tensor_tensor(out=tmp[:], in0=pw[:, 0, :], in1=pw[:, 0, :], op=ALU.mult), **RW)
                    op("dve", lambda e: e.tensor_tensor(out=tmp2[:], in0=pw[:, 1, :], in1=pw[:, 1, :], op=ALU.mult), **RW)
                    op("dve", lambda e: e.tensor_tensor(out=pw[:, 1, :], in0=pw[:, 0, :], in1=pw[:, 1, :], op=ALU.mult), **RW)
                    op("dve", lambda e: e.tensor_scalar(out=pw[:, 1, :], in0=pw[:, 1, :], scalar1=2.0, scalar2=None, op0=ALU.mult), **RW)
                    op("dve", lambda e: e.tensor_tensor(out=pw[:, 0, :], in0=tmp[:], in1=tmp2[:], op=ALU.subtract), **RW)
                if tabl is EF:
                    op("dve", lambda e: e.tensor_copy(out=L128[:], in_=pw[:]), reads=[B_s], writes=[B_tab])
            kb.barrier()
        kb.es = es5
        tA = [sb(f"tA{i}", [128, 4, 128], F32) for i in range(2)]
        Wt = [sb(f"Wt{i}", [128, 2, 128], F32) for i in range(2)]
        Z = [sb(f"Z{i}", [128, 2, 128], F32) for i in range(2)]
        Qp = [sb(f"Qp{i}", [128, 4, 128], BF16) for i in range(2)]
        ys = [sb(f"ys{i}", [128, 128], F32) for i in range(2)]
        yt = [sb(f"yt{i}", [128, 128], F32) for i in range(2)]
        sg = [sb(f"sg{i}", [128, 128], F32) for i in range(2)]
        ctmp = sb("ctmp", [128, 2, 16], F32)
        wacc = [sb(f"wacc{i}", [128, 2], F32) for i in range(2)]
        B_wacc = [buf("wacc0"), buf("wacc1")]
        B_tA, B_W, B_Z, B_Qp = [[buf(f"{n}{i}") for i in range(2)] for n in ("tA", "Wt", "Z", "Qp")]
        B_ys = [buf("ys0"), buf("ys1")]
        def stage_a_pe(c, pr):
            ct, pb = pr // 4, pr % 2
            csl = slice(c * 128, (c + 1) * 128)
            lts = (BBT[:, pr, 0, :], BBT[:, pr, 1, :], BBTn[:, pr, :], BBT[:, pr, 0, :])
            for q4, lt in enumerate(lts):
                op("pe", lambda e: e.matmul(PS[pb][:, q4 * 128:(q4 + 1) * 128], lhsT=lt, rhs=L.uT[:, ct, csl],
                                            start=True, stop=True), reads=[B_tab, L.B_uT[c]], writes=[PSB[pb]])

        def stage_a(c, pr):
            ct, pb = pr // 4, pr % 2
            bu = PS[pb][:, :].rearrange("p (k r b) -> p k r b", k=2, r=2)
            if c < Q0:
                op("dve", lambda e: e.tensor_tensor(out=tA[pb][:].rearrange("p (r k) b -> p k r b", r=2), in0=bu,
                                                     in1=EI[:, :, pr, :].unsqueeze(2).to_broadcast([128, 2, 2, 128]), op=ALU.mult),
                   reads=[PSB[pb], B_tab], writes=[B_tA[pb]])
                return
            op("dve", lambda e: e.tensor_tensor(out=tA[pb][:].rearrange("p (k r) b -> p k r b", k=2), in0=bu,
                                                 in1=EI[:, :, pr, :].unsqueeze(2).to_broadcast([128, 2, 2, 128]), op=ALU.mult),
               reads=[PSB[pb], B_tab], writes=[B_tA[pb]])
            op("pool", lambda e: e.tensor_tensor(out=Wt[pb][:], in0=tA[pb][:, 0:2, :], in1=tA[pb][:, 2:4, :], op=ALU.add),
               reads=[B_tA[pb]], writes=[B_W[pb]])

        def stage_b(c, pr):
            ct, prl, pb = pr // 4, pr % 4, pr % 2
            csl = slice(c * 128, (c + 1) * 128)
            for ri in (range(2) if c < Q0 else ()):
                op("act", lambda e: e.activation(out=tA[pb][:, 2 * ri:2 * ri + 2, :], in_=tA[pb][:, 2 * ri:2 * ri + 2, :], func=AF.Copy,
                                                 accum_out=wacc[pb][:, ri:ri + 1]), reads=[B_tA[pb]], writes=[B_tA[pb], B_wacc[pb]])
            if c < Q0:
                op("pool", lambda e: e.tensor_tensor(out=zl[:, :, pr:pr + 1], in0=wacc[pb][:, :].unsqueeze(2), in1=carry[:, :, pr:pr + 1], op=ALU.add),
                   reads=[B_wacc[pb], B_car], writes=[B_zl])
            for ri in (range(2) if c >= Q0 else ()):
                op("dve", lambda e: e.tensor_tensor_scan(out=Z[pb][:, ri, :], data0=ones[:], data1=Wt[pb][:, ri, :],
                                                         initial=carry[:, ri, pr:pr + 1], op0=ALU.mult, op1=ALU.add),
                   reads=[B_W[pb], B_tab, B_car], writes=[B_Z[pb]])
            if c >= Q0:
                op("pool", lambda e: e.tensor_copy(out=zl[:, :, pr:pr + 1], in_=Z[pb][:, :, 127:128]), reads=[B_Z[pb]], writes=[B_zl])
                q = Qp[pb]
                op("dve", lambda e: e.tensor_tensor(out=q[:].rearrange("p (k r) b -> p k r b", k=2),
                                                     in0=Z[pb][:].unsqueeze(1).to_broadcast([128, 2, 2, 128]),
                                                     in1=EF[:, :, pr, :].unsqueeze(2).to_broadcast([128, 2, 2, 128]), op=ALU.mult),
                   reads=[B_Z[pb], B_tab], writes=[B_Qp[pb]])
                yb = 2 + ct % 2
                for k in range(4):
                    op("pe", lambda e: e.matmul(PS[yb][:, 0:128], lhsT=Cq[:, pr, k, :], rhs=q[:, k, :],
                                                start=(prl == 0 and k == 0), stop=(prl == 3 and k == 3)),
                       reads=[B_tab, B_Qp[pb]], writes=[PSB[yb]])
                if prl == 3:
                    cb2 = ct % 2
                    op("dve", lambda e: e.scalar_tensor_tensor(out=ys[cb2][:], in0=L.uT[:, ct, csl], scalar=dsk[:, ct:ct + 1], in1=PS[yb][:, 0:128],
                                                                op0=ALU.mult, op1=ALU.add), reads=[PSB[yb], L.B_uT[c], B_tab], writes=[B_ys[cb2]])
                    op("pool", lambda e: e.tensor_tensor(out=yt[cb2][:], in0=ys[cb2][:], in1=ys[cb2][:], op=ALU.mult),
                       reads=[B_ys[cb2]], writes=[B_ys[cb2]])
                    op("pool", lambda e: e.tensor_scalar(out=yt[cb2][:], in0=yt[cb2][:], scalar1=0.044715, scalar2=1.0, op0=ALU.mult, op1=ALU.add),
                       reads=[B_ys[cb2]], writes=[B_ys[cb2]])
                    op("pool", lambda e: e.tensor_tensor(out=yt[cb2][:], in0=yt[cb2][:], in1=ys[cb2][:], op=ALU.mult),
                       reads=[B_ys[cb2]], writes=[B_ys[cb2]])
                    op("act", lambda e: e.activation(out=sg[cb2][:], in_=yt[cb2][:], func=AF.Sigmoid, scale=1.5957691216057308),
                       reads=[B_ys[cb2]], writes=[B_ys[cb2]])
                    op("pool", lambda e: e.tensor_tensor(out=L.uT[:, ct, csl], in0=ys[cb2][:], in1=sg[cb2][:], op=ALU.mult),
                       reads=[B_ys[cb2]], writes=[L.B_uT[c]])
            if pr == 15:
                RWc = dict(reads=[B_zl, B_tab, B_car], writes=[B_car])
                op("dve", lambda e: e.tensor_tensor(out=ctmp[:, 0, :], in0=L128[:, 0, :], in1=zl[:, 0, :], op=ALU.mult), **RWc)
                op("dve", lambda e: e.tensor_tensor(out=ctmp[:, 1, :], in0=L128[:, 1, :], in1=zl[:, 1, :], op=ALU.mult), **RWc)
                op("dve", lambda e: e.tensor_tensor(out=carry[:, 0, :], in0=ctmp[:, 0, :], in1=ctmp[:, 1, :], op=ALU.subtract), **RWc)
                op("dve", lambda e: e.tensor_tensor(out=ctmp[:, 0, :], in0=L128[:, 0, :], in1=zl[:, 1, :], op=ALU.mult), **RWc)
                op("dve", lambda e: e.tensor_tensor(out=ctmp[:, 1, :], in0=L128[:, 1, :], in1=zl[:, 0, :], op=ALU.mult), **RWc)
                op("dve", lambda e: e.tensor_tensor(out=carry[:, 1, :], in0=ctmp[:, 0, :], in1=ctmp[:, 1, :], op=ALU.add), **RWc)

        stgp = sb("stgp", [128, 8, 256], F32)
        B_stgp = buf("stgp")

        def pre_step(dst_fn, src_ap, rows_kc):
            kb.dma("sp", stgp[:, 0:rows_kc, :], src_ap.rearrange("(kc p) n -> p kc n", p=128), B_stgp, writes=[B_stgp])
            for kc in range(rows_kc):
                op("act", lambda e: e.activation(out=dst_fn(kc), in_=stgp[:, kc, :], func=AF.Copy), reads=[B_stgp], writes=[L.B_Wpre])

        pre_steps = []
        for q in range(8):
            pre_steps.append((lambda kc, q=q: L.WMp[:, kc, q * 256:(q + 1) * 256], L.w_in[:, 1816 + q * 256:1816 + (q + 1) * 256], 8))
        for q in range(8):
            pre_steps.append((lambda kc, q=q: L.WGLp[:, kc, q * 256:(q + 1) * 256], L.w_glu[:, q * 256:(q + 1) * 256], 4))
        for q in range(4):
            pre_steps.append((lambda kc, q=q: L.WOp[:, kc, q * 256:(q + 1) * 256], L.w_o[:, q * 256:(q + 1) * 256], 8))
        seq = [(c, pr) for c in range(NT) for pr in range(16)]
        stage_a_pe(*seq[0])
        stage_a_pe(*seq[1])
        stage_a(*seq[0])
        for k in range(len(seq)):
            if seq[k][1] in (0, 8) and seq[k][0] >= Q0 + 1 and pre_steps:
                pre_step(*pre_steps.pop(0))
            if k + 2 < len(seq):
                stage_a_pe(*seq[k + 2])
            if k + 1 < len(seq):
                stage_a(*seq[k + 1])
            stage_b(*seq[k])
        while pre_steps:
            pre_step(*pre_steps.pop(0))
        if "gyT" in L.dbg_d:
            b = kb.buf("dbg")
            st2 = sb("dbgst5", [128, 4, 512], F32)
            op("dve", lambda e: e.tensor_copy(out=st2[:], in_=L.uT[:, :, 0:512]), reads=L.B_uT, writes=[b])
            kb.dma("sp", L.dbg_d["gyT"], st2[:], b, reads=[b])
        kb.barrier()
    kb.es = L.esA


def ln_affine_store(kb, L, pre, B_pre, stats, mv, rstd, B_st, gb, B_gb, dst_ap, q="sp"):
    op = kb.op
    for hf in range(2):
        op("dve", lambda e: e.bn_stats(out=stats[:, hf, :], in_=pre[:, hf * 512:(hf + 1) * 512]), reads=[B_pre], writes=[B_st])
    op("dve", lambda e: e.bn_aggr(out=mv[:], in_=stats[:]), reads=[B_st], writes=[B_st])
    op("dve", lambda e: e.tensor_scalar(out=rstd[:], in0=mv[:, 1:2], scalar1=1e-5, scalar2=None, op0=ALU.add),
       reads=[B_st], writes=[B_st])
    op("act", lambda e: e.activation(out=rstd[:], in_=rstd[:], func=AF.Sqrt), reads=[B_st], writes=[B_st])
    op("dve", lambda e: e.reciprocal(out=rstd[:], in_=rstd[:]), reads=[B_st], writes=[B_st])
    op("dve", lambda e: e.tensor_scalar(out=pre[:], in0=pre[:], scalar1=mv[:, 0:1], scalar2=rstd[:, 0:1], op0=ALU.subtract, op1=ALU.mult),
       reads=[B_st], writes=[B_pre])
    op("pool", lambda e: e.tensor_tensor(out=pre[:], in0=pre[:], in1=gb[:, 0, :], op=ALU.mult), reads=[B_gb], writes=[B_pre])
    op("pool", lambda e: e.tensor_tensor(out=pre[:], in0=pre[:], in1=gb[:, 1, :], op=ALU.add), reads=[B_gb], writes=[B_pre])
    kb.dma(q, dst_ap, pre[:], B_pre, reads=[B_pre])


def pass_a2(nc, kb, Ld):
    L = NS(Ld)
    op = kb.op
    PS, PSB, PT, PTB = L.PS, L.PSB, L.PT, L.PTB
    sb, buf = kb.sb, kb.buf
    with ExitStack() as es2:
        kb.es = es2
        WM, WGL, WO = L.WMp, L.WGLp, L.WOp
        WNO = sb("WNO", [128, 4, D], BF16)
        gb = sb("gb1", [128, 2, D], F32)
        B_W, B_gb = L.B_Wpre, buf("gb1")
        with ExitStack() as ess:
            kb.es = ess
            stgs = [sb(f"stg2{i}", [128, 8, 512], F32) for i in range(2)]
            B_stgs = [buf(f"stg2{i}") for i in range(2)]
            for q2 in range(2):
                L.load_cast(lambda kc: WNO[:, kc, q2 * 512:(q2 + 1) * 512], L.w_nsa_out[:, q2 * 512:(q2 + 1) * 512], 4, 512,
                            stgs[q2], B_stgs[q2], B_W)
            kb.dma("sp", gb[:, 0, :], L.ln1_g.partition_broadcast(128).rearrange("p o n -> p (o n)"), B_gb, writes=[B_gb])
            kb.dma("sp", gb[:, 1, :], L.ln1_b.partition_broadcast(128).rearrange("p o n -> p (o n)"), B_gb, writes=[B_gb])
            kb.barrier()
        kb.es = es2
        xts = [[sb(f"xt2{k}{i}", [128, D], F32) for i in range(4)] for k in range(2)]
        B_xts = [[buf(f"xt2{k}{i}") for i in range(4)] for k in range(2)]
        xn = sb("xn2", [128, D], BF16); B_xn = buf("xn2")
        stats = sb("stats2", [128, 2, 6], F32); mv = sb("mv2", [128, 2], F32); rstd = sb("rstd2", [128, 1], F32)
        B_st = buf("st2")
        hTs2 = [sb(f"hT2{k}", [128, 8, 512], BF16) for k in range(2)]; B_hTs2 = [buf("hT20"), buf("hT21")]
        sga = sb("sga", [128, 3, 512], F32); B_sg = [buf("sga0"), buf("sga1")]
        t1 = sb("t1", [128, 512], F32); t2 = sb("t2", [128, 512], F32); B_t = buf("t12")
        mixT = sb("mixT", [128, 8, 512], BF16); B_mix = buf("mixT")
        gtmp, B_gt = [t1, t2], [B_t, B_t]
        steps = [[0]] + [list(range(r, r + 4)) for r in range(1, NQ, 4)]
        modT, ident_b = L.modT, L.ident_b

        def ln_a2(x_, B_x):
            for hf in range(2):
                op("dve", lambda e: e.bn_stats(out=stats[:, hf, :], in_=x_[:, hf * 512:(hf + 1) * 512]), reads=[B_x], writes=[B_st])
            op("dve", lambda e: e.bn_aggr(out=mv[:], in_=stats[:]), reads=[B_st], writes=[B_st])
            op("dve", lambda e: e.tensor_scalar(out=rstd[:], in0=mv[:, 1:2], scalar1=1e-5, scalar2=None, op0=ALU.add), reads=[B_st], writes=[B_st])
            op("act", lambda e: e.activation(out=rstd[:], in_=rstd[:], func=AF.Sqrt), reads=[B_st], writes=[B_st])
            op("dve", lambda e: e.reciprocal(out=rstd[:], in_=rstd[:]), reads=[B_st], writes=[B_st])
            op("dve", lambda e: e.tensor_scalar(out=xn[:], in0=x_[:], scalar1=mv[:, 0:1], scalar2=rstd[:, 0:1], op0=ALU.subtract, op1=ALU.mult),
               reads=[B_x, B_st], writes=[B_xn])

        def ln_b2(dst, B_dst):
            for kc in range(KC):
                op("pe", lambda e: e.transpose(out=PT[:, kc * 128:(kc + 1) * 128], in_=xn[:, kc * 128:(kc + 1) * 128], identity=ident_b[:]),
                   reads=[B_xn, L.B_id], writes=[PTB])
            for kc in range(KC):
                op("act", lambda e: e.activation(out=dst[:, kc, :], in_=PT[:, kc * 128:(kc + 1) * 128], func=AF.Identity,
                                                 scale=modT[:, 8 + kc:9 + kc], bias=modT[:, kc:kc + 1]), reads=[PTB, L.B_modT], writes=[B_dst])

        def load_step(si):
            for tt, r in enumerate(steps[si]):
                i = Q0 + r
                kb.dma("sp", xts[si % 2][tt][:], L.x_d[i * 128:(i + 1) * 128, :], B_xts[si % 2][tt], writes=[B_xts[si % 2][tt]])

        load_step(0)
        for tt in range(len(steps[0])):
            ln_a2(xts[0][tt], B_xts[0][tt])
            ln_b2(hTs2[0][:, :, tt * 128:(tt + 1) * 128], B_hTs2[0])
        for si, rs in enumerate(steps):
            NTOK = 128 * len(rs)
            r0 = rs[0]
            xt, B_xt = xts[si % 2], B_xts[si % 2]
            hT, B_hT = hTs2[si % 2], B_hTs2[si % 2]
            nxt = steps[si + 1] if si + 1 < len(steps) else []
            if nxt:
                load_step(si + 1)
            osl = slice(r0 * 128, r0 * 128 + NTOK)
            tsl = slice((Q0 + r0) * 128, (Q0 + r0) * 128 + NTOK)
            B_us = [L.B_uT[Q0 + r] for r in rs]
            B_os = [L.B_oaT[Q0 + r] for r in rs]
            for fc in range(8):
                fsl = slice(fc * 128, (fc + 1) * 128)
                for half in range(2):
                    for kc in range(KC):
                        op("pe", lambda e: e.matmul(PS[half][:, 0:NTOK], lhsT=WM[:, kc, half * D + fc * 128:half * D + (fc + 1) * 128],
                                                    rhs=hT[:, kc, 0:NTOK], start=(kc == 0), stop=(kc == KC - 1)), reads=[B_W, B_hT], writes=[PSB[half]])
                    op("act", lambda e: e.activation(out=sga[:, half, 0:NTOK], in_=PS[half][:, 0:NTOK], func=AF.Sigmoid),
                       reads=[PSB[half]], writes=[B_sg[0]])
                for c in range(4):
                    op("pe", lambda e: e.matmul(PS[2][:, 0:NTOK], lhsT=WNO[:, c, fsl], rhs=L.oaT[:, c, osl], start=(c == 0), stop=(c == 3)),
                       reads=[B_W] + B_os, writes=[PSB[2]])
                op("dve", lambda e: e.tensor_tensor(out=t1[:, 0:NTOK], in0=sga[:, 0, 0:NTOK], in1=PS[2][:, 0:NTOK], op=ALU.mult),
                   reads=[B_sg[0], PSB[2]], writes=[B_t])
                for half in range(2):
                    for c in range(4):
                        op("pe", lambda e: e.matmul(PS[3 + half][:, 0:NTOK], lhsT=WGL[:, c, half * D + fc * 128:half * D + (fc + 1) * 128],
                                                    rhs=L.uT[:, c, tsl], start=(c == 0), stop=(c == 3)), reads=[B_W] + B_us, writes=[PSB[3 + half]])
                op("act", lambda e: e.activation(out=sga[:, 2, 0:NTOK], in_=PS[4][:, 0:NTOK], func=AF.Sigmoid), reads=[PSB[4]], writes=[B_sg[1]])
                op("dve", lambda e: e.tensor_tensor(out=t2[:, 0:NTOK], in0=sga[:, 2, 0:NTOK], in1=PS[3][:, 0:NTOK], op=ALU.mult),
                   reads=[B_sg[1], PSB[3]], writes=[B_t])
                op("dve", lambda e: e.tensor_tensor(out=t2[:, 0:NTOK], in0=t2[:, 0:NTOK], in1=sga[:, 1, 0:NTOK], op=ALU.mult),
                   reads=[B_sg[0]], writes=[B_t])
                op("dve", lambda e: e.tensor_tensor(out=mixT[:, fc, 0:NTOK], in0=t1[:, 0:NTOK], in1=t2[:, 0:NTOK], op=ALU.add),
                   reads=[B_t], writes=[B_mix])
                tn = fc // 2
                if tn < len(nxt):
                    if fc % 2 == 0:
                        ln_a2(xts[(si + 1) % 2][tn], B_xts[(si + 1) % 2][tn])
                    else:
                        ln_b2(hTs2[(si + 1) % 2][:, :, tn * 128:(tn + 1) * 128], B_hTs2[(si + 1) % 2])
            for tt, r in enumerate(rs):
                pr_, B_p = xt[tt], B_xt[tt]
                for half in range(2):
                    for fc in range(8):
                        op("pe", lambda e: e.matmul(PS[5 + half][:, :], lhsT=mixT[:, fc, tt * 128:(tt + 1) * 128], rhs=WO[:, fc, half * 512:(half + 1) * 512],
                                                    start=(fc == 0), stop=(fc == 7)), reads=[B_W, B_mix], writes=[PSB[5 + half]])
                    hs = slice(half * 512, (half + 1) * 512)
                    op("dve", lambda e: e.tensor_tensor(out=gtmp[half][:], in0=PS[5 + half][:, :], in1=L.gates_bc[:, 0, hs], op=ALU.mult),
                       reads=[PSB[5 + half], L.B_gbc], writes=[B_gt[half]])
                    op("dve", lambda e: e.scalar_tensor_tensor(out=pr_[:, hs], in0=pr_[:, hs], scalar=ALPHA, in1=gtmp[half][:], op0=ALU.mult, op1=ALU.add),
                       reads=[B_gt[half]], writes=[B_p])
                ln_affine_store(kb, L, pr_, B_p, stats, mv, rstd, B_st, gb, B_gb, L.x1_d[r * 128:(r + 1) * 128, :])
        kb.barrier()
    kb.es = L.esA


def pass_b(nc, kb, Ld):
    L = NS(Ld)
    op = kb.op
    PS, PSB, PT, PTB = L.PS, L.PSB, L.PT, L.PTB
    sb, buf = kb.sb, kb.buf
    NJ = 44
    WUP = sb("WUP", [128, 8, 2 * DFF], BF16)
    WDN = sb("WDN", [128, 22, D], BF16)
    cw = sb("cwt", [128, NJ, 3], F32)
    cb = sb("cbt", [128, NJ], F32)
    gb = sb("gb2", [128, 2, D], F32)
    halo = sb("halo", [128, NJ, 2], F32)
    B_W, B_gb, B_cw, B_halo = buf("WB"), buf("gb2"), buf("cw"), buf("halo")
    with ExitStack() as ess:
        kb.es = ess
        stgs = [sb(f"stgb{i}", [128, 8, 512], F32) for i in range(3)]
        B_stgs = [buf(f"stgb{i}") for i in range(3)]
        for q in range(11):
            L.load_cast(lambda kc: WUP[:, kc, q * 512:(q + 1) * 512], L.w_up[:, q * 512:(q + 1) * 512], 8, 512, stgs[q % 3], B_stgs[q % 3], B_W)
        for half in range(2):
            for q in range(3):
                stg, B_stg = stgs[(half * 3 + q + 2) % 3], B_stgs[(half * 3 + q + 2) % 3]
                r0, nr = q * 8, (8 if q < 2 else 6)
                kb.dma("sp", stg[:, 0:nr, :], L.w_down[r0 * 128:(r0 + nr) * 128, half * 512:(half + 1) * 512].rearrange("(kc p) n -> p kc n", p=128),
                       B_stg, writes=[B_stg])
                for kc in range(nr):
                    op("dve", lambda e: e.tensor_copy(out=WDN[:, r0 + kc, half * 512:(half + 1) * 512], in_=stg[:, kc, :]),
                       reads=[B_stg], writes=[B_W])
        for k3 in range(3):
            kb.dma("sp", cw[:, :, k3], L.conv_w[k3:k3 + 1, :].rearrange("o (j p) -> p (o j)", p=128), B_cw, writes=[B_cw],
                   allow_slow_non_contiguous=True)
        kb.dma("sp", cb[:], L.conv_b.rearrange("o (j p) -> p (o j)", p=128), B_cw, writes=[B_cw], allow_slow_non_contiguous=True)
        kb.dma("sp", gb[:, 0, :], L.ln2_g.partition_broadcast(128).rearrange("p o n -> p (o n)"), B_gb, writes=[B_gb])
        kb.dma("sp", gb[:, 1, :], L.ln2_b.partition_broadcast(128).rearrange("p o n -> p (o n)"), B_gb, writes=[B_gb])
        op("pool", lambda e: e.memset(halo[:], 0.0), writes=[B_halo])
        kb.barrier()
    kb.es = L.esB
    NTOK = 512
    x1t = [sb(f"x1t{i}", [128, D], F32) for i in range(4)]
    B_x1 = [buf(f"x1t{i}") for i in range(4)]
    xn = sb("xnb", [128, D], BF16); B_xn = buf("xnb")
    stats = sb("statsb", [128, 2, 6], F32); mv = sb("mvb", [128, 2], F32); rstd = sb("rstdb", [128, 1], F32)
    B_st = buf("stb")
    h2T = sb("h2T", [128, 8, NTOK], BF16); B_h2 = buf("h2T")
    upb = [sb(f"upb{i}", [128, NTOK + 2], F32) for i in range(2)]
    acc = [sb(f"acc{i}", [128, NTOK], F32) for i in range(2)]
    B_up = [buf("up0"), buf("up1")]
    B_acc = [buf("acc0"), buf("acc1")]
    ffT = sb("ffT", [128, 22, NTOK], BF16); B_ff = buf("ffT")
    gtmp, B_gt = acc, B_acc
    steps = [[0]] + [list(range(r, r + 4)) for r in range(1, NQ, 4)]
    for s, rs in enumerate(steps):
        NTOK = 128 * len(rs)
        for tt, i in enumerate(rs):
            kb.dma("sp", x1t[tt][:], L.x1_d[i * 128:(i + 1) * 128, :], B_x1[tt], writes=[B_x1[tt]])
            L.ln_to_hT(x1t[tt], B_x1[tt], xn, B_xn, stats, mv, rstd, B_st, h2T[:, :, tt * 128:(tt + 1) * 128], B_h2, 16)
        for j in range(22):
            for w, ch in enumerate((j, j + 22)):
                pbk = (2 * j + w) % 4
                for kc in range(KC):
                    op("pe", lambda e: e.matmul(PS[pbk][:, 0:NTOK], lhsT=WUP[:, kc, ch * 128:(ch + 1) * 128], rhs=h2T[:, kc, 0:NTOK],
                                                start=(kc == 0), stop=(kc == KC - 1)), reads=[B_W, B_h2], writes=[PSB[pbk]])
                u_, B_u, a_, B_a = upb[w], B_up[w], acc[w], B_acc[w]
                op("pool", lambda e: e.tensor_copy(out=u_[:, 0:2], in_=halo[:, ch, :]), reads=[B_halo], writes=[B_u])
                op("act", lambda e: e.activation(out=u_[:, 2:NTOK + 2], in_=PS[pbk][:, 0:NTOK], func=AF.Copy), reads=[PSB[pbk]], writes=[B_u])
                op("pool", lambda e: e.tensor_copy(out=halo[:, ch, :], in_=u_[:, NTOK:NTOK + 2]), reads=[B_u], writes=[B_halo])
                op("act", lambda e: e.activation(out=a_[:, 0:NTOK], in_=u_[:, 2:NTOK + 2], func=AF.Identity, scale=cw[:, ch, 2:3], bias=cb[:, ch:ch + 1]),
                   reads=[B_u, B_cw], writes=[B_a])
                op("dve", lambda e: e.scalar_tensor_tensor(out=a_[:, 0:NTOK], in0=u_[:, 1:NTOK + 1], scalar=cw[:, ch, 1:2], in1=a_[:, 0:NTOK],
                                                            op0=ALU.mult, op1=ALU.add), reads=[B_u, B_cw], writes=[B_a])
                op("dve", lambda e: e.scalar_tensor_tensor(out=a_[:, 0:NTOK], in0=u_[:, 0:NTOK], scalar=cw[:, ch, 0:1], in1=a_[:, 0:NTOK],
                                                            op0=ALU.mult, op1=ALU.add), reads=[B_u, B_cw], writes=[B_a])
            op("act", lambda e: e.activation(out=upb[1][:, 0:NTOK], in_=acc[1][:, 0:NTOK], func=AF.Silu), reads=[B_acc[1]], writes=[B_up[1]])
            op("dve", lambda e: e.tensor_tensor(out=ffT[:, j, 0:NTOK], in0=upb[1][:, 0:NTOK], in1=acc[0][:, 0:NTOK], op=ALU.mult),
               reads=[B_up[1], B_acc[0]], writes=[B_ff])
        if s == 0:
            op("pool", lambda e: e.tensor_scalar(out=halo[:].rearrange("p a b -> p (a b)"), in0=halo[:].rearrange("p a b -> p (a b)"),
                                                  scalar1=L.hv[:, 0:1], scalar2=None, op0=ALU.mult), reads=[L.B_tv], writes=[B_halo])
        for tt, i in enumerate(rs):
            pr_, B_p = x1t[tt], B_x1[tt]
            for half in range(2):
                bk = ((4, 5), (6, 0))[tt % 2][half]
                for j in range(22):
                    op("pe", lambda e: e.matmul(PS[bk][:, :], lhsT=ffT[:, j, tt * 128:(tt + 1) * 128], rhs=WDN[:, j, half * 512:(half + 1) * 512],
                                                start=(j == 0), stop=(j == 21)), reads=[B_W, B_ff], writes=[PSB[bk]])
                hs = slice(half * 512, (half + 1) * 512)
                op("dve", lambda e: e.tensor_tensor(out=gtmp[half][:], in0=PS[bk][:, :], in1=L.gates_bc[:, 1, hs], op=ALU.mult),
                   reads=[PSB[bk], L.B_gbc], writes=[B_gt[half]])
                op("pool", lambda e: e.scalar_tensor_tensor(out=pr_[:, hs], in0=pr_[:, hs], scalar=ALPHA, in1=gtmp[half][:], op0=ALU.mult, op1=ALU.add),
                   reads=[B_gt[half]], writes=[B_p]) if False else \
                op("dve", lambda e: e.scalar_tensor_tensor(out=pr_[:, hs], in0=pr_[:, hs], scalar=ALPHA, in1=gtmp[half][:], op0=ALU.mult, op1=ALU.add),
                   reads=[B_gt[half]], writes=[B_p])
            if i >= 1:
                ln_affine_store(kb, L, pr_, B_p, stats, mv, rstd, B_st, gb, B_gb, L.out_d[(i - 1) * 128:i * 128, :])


_NC = [None]


def kernel(**inputs):
    f = lambda a: np.ascontiguousarray(np.asarray(a, dtype=np.float32))
    if _NC[0] is None:
        _NC[0] = build()
    nc = _NC[0]
    tabs = [make_tables(0), make_tables(1)]
    shared = {
        "w_ada": f(inputs["w_ada"][0]), "b_ada": f(inputs["b_ada"][0]).reshape(1, -1), "w_in": f(inputs["w_in"][0]),
        "pe_ck": f(inputs["pe_ck"][0]), "w_ck1": f(inputs["w_ck1"][0]), "w_ck2": f(inputs["w_ck2"][0]),
        "pe_cv": f(inputs["pe_cv"][0]), "w_cv1": f(inputs["w_cv1"][0]), "w_cv2": f(inputs["w_cv2"][0]),
        "w_nsa_out": f(inputs["w_nsa_out"][0]),
        "s5_a_re": f(inputs["s5_a_re"][0]), "s5_a_im": f(inputs["s5_a_im"][0]),
        "s5_b_re": f(inputs["s5_b_re"][0]), "s5_b_im": f(inputs["s5_b_im"][0]),
        "s5_c_re": f(inputs["s5_c_re"][0]), "s5_c_im": f(inputs["s5_c_im"][0]),
        "s5_d": f(inputs["s5_d"][0]).reshape(512, 1), "s5_log_dt": f(inputs["s5_log_dt"][0]).reshape(1, 32),
        "w_s5_glu": f(inputs["w_s5_glu"][0]), "w_o": f(inputs["w_o"][0]),
        "ln1_g": f(inputs["ln1_g"][0]).reshape(1, -1), "ln1_b": f(inputs["ln1_b"][0]).reshape(1, -1),
        "w_up": f(inputs["w_up"][0]), "conv_w": f(inputs["conv_w"][0]), "conv_b": f(inputs["conv_b"][0]).reshape(1, -1),
        "w_down": f(inputs["w_down"][0]),
        "ln2_g": f(inputs["ln2_g"][0]).reshape(1, -1), "ln2_b": f(inputs["ln2_b"][0]).reshape(1, -1),
    }
    tabf = [{"tb_" + k: f(v) for k, v in tabs[hf].items()} for hf in range(2)]
    x = np.asarray(inputs["x"], dtype=np.float32)
    c = np.asarray(inputs["c"], dtype=np.float32)
    in_maps = []
    for core in range(8):
        b, hf = core // 2, core % 2
        m = dict(shared)
        m.update(tabf[hf])
        if hf == 1:
            m["x"] = f(x[b])
        else:
            xp = np.zeros((T, D), np.float32)
            xp[T // 2:] = x[b][:T // 2]
            m["x"] = xp
        m["cvec"] = f(c[b].reshape(8, 128).T)
        in_maps.append(m)
    res = run_bass_kernel_spmd(nc, in_maps, core_ids=list(range(8)))
    kernel.last = res
    out = np.empty((4, T, D), np.float32)
    for core in range(8):
        b, hf = core // 2, core % 2
        out[b, hf * (T // 2):(hf + 1) * (T // 2)] = np.asarray(res.results[core]["out"], dtype=np.float32)
    return out
```

```python
from contextlib import ExitStack
import math
import numpy as np
import concourse.bass as bass
import concourse.mybir as mybir
from concourse.bass_utils import run_bass_kernel_spmd

F32 = mybir.dt.float32
BF16 = mybir.dt.bfloat16
AF = mybir.ActivationFunctionType
ALU = mybir.AluOpType

T = 4096
NT = 32
D = 1024
KC = 8
DFF = 2816
NEG = -30000.0
ALPHA = 2.0 ** 0.25
STOP = [99]
Q0 = 15
NQ = NT - Q0
DBG = {}


class Buf:
    __slots__ = ("name", "w", "r", "dsem", "dcnt")

    def __init__(self, name):
        self.name = name
        self.w = None
        self.r = {}
        self.dsem = None
        self.dcnt = 0


class KB:
    def __init__(self, nc, es):
        self.nc = nc
        self.ges = es
        self.es = es
        self.eng = {"pe": nc.tensor, "act": nc.scalar, "dve": nc.vector,
                    "pool": nc.gpsimd, "sp": nc.sync}
        self.sem = {}
        self.cnt = {}
        self.seen = {}
        for e in self.eng:
            self.sem[e] = es.enter_context(nc.semaphore("s_" + e))
            self.cnt[e] = 0
            self.seen[e] = {}
        self.pool_sems = []
        self.live = []
        self.n = 0
        self.uid = 0

    def sb(self, name, shape, dt):
        self.uid += 1
        return self.es.enter_context(self.nc.sbuf_tensor(f"{name}_{self.uid}", list(shape), dt))

    def ps(self, name, shape, dt):
        return self.ges.enter_context(self.nc.psum_tensor(name, list(shape), dt))

    def buf(self, name="b"):
        return Buf(name)

    def _wait(self, e, ev):
        if ev is None:
            return
        sem, val = ev
        key = id(sem)
        if self.seen[e].get(key, 0) >= val:
            return
        self.eng[e].wait_ge(sem, val)
        self.seen[e][key] = val

    def _deps(self, e, reads, writes):
        for b in reads:
            self._wait(e, b.w)
        for b in writes:
            self._wait(e, b.w)
            for ev in list(b.r.values()):
                self._wait(e, ev)

    def _record(self, ev, reads, writes):
        for b in reads:
            if b not in writes:
                b.r[id(ev[0])] = ev
        for b in writes:
            b.w = ev
            b.r = {}

    def op(self, e, fn, reads=(), writes=()):
        self._deps(e, reads, writes)
        ins = fn(self.eng[e])
        self.cnt[e] += 1
        ins.then_inc(self.sem[e], 1)
        ev = (self.sem[e], self.cnt[e])
        if e == "pe":
            self.seen[e][id(self.sem[e])] = self.cnt[e]
        self._record(ev, reads, writes)
        self.n += 1
        return ins

    def dma(self, q, out, in_, dbuf, reads=(), writes=(), **kw):
        self._deps(q, reads, writes)
        if dbuf.dsem is None:
            if self.pool_sems:
                dbuf.dsem, dbuf.dcnt = self.pool_sems.pop()
            else:
                dbuf.dsem = self.ges.enter_context(self.nc.semaphore(f"d{len(self.live)}_{self.n}"))
                dbuf.dcnt = 0
            self.live.append(dbuf)
        ins = self.eng[q].dma_start(out=out, in_=in_, **kw)
        dbuf.dcnt += 16
        ins.then_inc(dbuf.dsem, 16)
        ev = (dbuf.dsem, dbuf.dcnt)
        self._record(ev, reads, writes)
        self.n += 1
        return ins

    def barrier(self, release=True):
        for e in self.eng:
            for f in self.eng:
                if f != e and self.cnt[f] > 0:
                    self._wait(e, (self.sem[f], self.cnt[f]))
            for b in self.live:
                self._wait(e, (b.dsem, b.dcnt))
        if release:
            for b in self.live:
                self.pool_sems.append((b.dsem, b.dcnt))
                b.dsem = None
            self.live = []


def make_tables(hf=1):
    t = {}
    t["ident"] = np.eye(128, dtype=np.float32)
    slopes = (2.0 ** (-8.0 * (np.arange(8) + 1) / 8)).astype(np.float32)
    tl = np.arange(128, dtype=np.float32)
    qa = np.zeros((3, 8, 128), np.float32)
    qb = np.zeros((3, 8, 128), np.float32)
    for h in range(8):
        qa[0, h, :] = slopes[h]
        qa[1, h, :] = slopes[h] * 128.0
        qa[2, h, :] = -slopes[h] * tl
        qb[2, h, :] = -slopes[h] * 128.0
    t["qaugA"] = qa
    t["qaugB"] = qb
    key = np.arange(T)
    kt = np.zeros((64, T), np.float32)
    kt[0] = key % 128
    kt[1] = key // 128
    kt[2] = 1.0
    if hf == 0:
        kt[1, :T // 2] = -8192.0
    for j in range(1, 62):
        kt[2 + j] = (key // 64 == j)
    t["kaug_tok"] = kt
    m = np.arange(256)
    pos = 16 * m + 15
    kc = np.zeros((3, 256), np.float32)
    kc[0] = pos % 128
    kc[1] = pos // 128
    kc[2] = 1.0
    kc[1, 0] = -8192.0
    if hf == 0:
        kc[1, :129] = -8192.0
    t["kaug_cmp"] = kc
    tv = np.ones((128, NT), np.float32)
    f0 = np.full((128, 64), -3e9, np.float32)
    hv = np.ones((128, 1), np.float32)
    if hf == 0:
        tv[:, :NT // 2] = 0.0
        f0[:, 32] = 1e9
        hv[:] = 0.0
    else:
        f0[:, 0] = 1e9
    t["tilevalid"] = tv
    t["f0"] = f0
    t["hv"] = hv
    kl = np.arange(128)[:, None]
    tq = np.arange(128)[None, :]
    t["tric"] = np.where(kl > tq, NEG, 0.0).astype(np.float32)
    t["triw"] = np.where(kl <= tq, NEG, 0.0).astype(np.float32)
    mc = np.zeros((128, 16, 128), np.float32)
    for dl in range(16):
        mc[:, dl, :] = np.where(16 * kl + 15 - tq > 128 * dl, NEG, 0.0)
    t["maskc"] = mc
    ov = np.zeros((256, 64), np.float32)
    for mm in range(1, 256):
        for jj in range(64):
            if 4 * jj <= mm <= 4 * jj + 4:
                ov[mm, jj] = 1.0
    t["ov"] = ov.reshape(2, 128, 64).transpose(1, 0, 2).copy()
    mw = np.zeros((128, 128), np.float32)
    aw = np.zeros((128, 128), np.float32)
    up = (np.arange(128) >= 64).astype(np.float32)
    for jw in range(128):
        jr = jw - 64
        if jr < -1:
            mw[:, jw] = 1.0
        elif jr == -1:
            mw[:, jw] = up
            aw[:, jw] = (1.0 - up) * 1e9
        elif jr == 0:
            aw[:, jw] = 1e9
        elif jr == 1:
            aw[:, jw] = np.where(up > 0, 1e9, -1e9)
        else:
            aw[:, jw] = -1e9
    t["mskw"] = mw
    t["addw"] = aw
    par = np.zeros((128, 2), np.float32)
    kk = np.arange(128)
    par[:, 0] = ((kk // 16) % 2 == 0)
    par[:, 1] = ((kk // 16) % 2 == 1)
    t["par"] = par
    return t


TABLE_SHAPES = {k: v.shape for k, v in make_tables().items()}


def build():
    nc = bass.Bass("TRN2", target_bir_lowering=False)

    def din(name, shape):
        return nc.dram_tensor(name, list(shape), F32, kind="ExternalInput").ap()

    x_d = din("x", [T, D])
    c_d = din("cvec", [128, 8])
    w_ada = din("w_ada", [D, 6 * D])
    b_ada = din("b_ada", [1, 6 * D])
    w_in = din("w_in", [D, 3864])
    pe_ck = din("pe_ck", [32, 64])
    w_ck1 = din("w_ck1", [32, 64, 128])
    w_ck2 = din("w_ck2", [128, 64])
    pe_cv = din("pe_cv", [32, 64])
    w_cv1 = din("w_cv1", [32, 64, 128])
    w_cv2 = din("w_cv2", [128, 64])
    w_nsa_out = din("w_nsa_out", [512, D])
    a_re = din("s5_a_re", [32, 64])
    a_im = din("s5_a_im", [32, 64])
    b_re = din("s5_b_re", [32, 64, 16])
    b_im = din("s5_b_im", [32, 64, 16])
    c_re = din("s5_c_re", [32, 16, 64])
    c_im = din("s5_c_im", [32, 16, 64])
    s5_d = din("s5_d", [512, 1])
    log_dt = din("s5_log_dt", [1, 32])
    w_glu = din("w_s5_glu", [512, 2 * D])
    w_o = din("w_o", [D, D])
    ln1_g = din("ln1_g", [1, D])
    ln1_b = din("ln1_b", [1, D])
    w_up = din("w_up", [D, 2 * DFF])
    conv_w = din("conv_w", [3, 2 * DFF])
    conv_b = din("conv_b", [1, 2 * DFF])
    w_down = din("w_down", [DFF, D])
    ln2_g = din("ln2_g", [1, D])
    ln2_b = din("ln2_b", [1, D])
    tb = {k: din("tb_" + k, shp) for k, shp in TABLE_SHAPES.items()}
    out_d = nc.dram_tensor("out", [T // 2, D], F32, kind="ExternalOutput").ap()
    x1_d = nc.dram_tensor("x1_scratch", [NQ * 128, D], F32, kind="Internal").ap()
    dbg_d = {}
    for k, shp in DBG.items():
        dbg_d[k] = nc.dram_tensor("dbg_" + k, list(shp), F32, kind="ExternalOutput").ap()

    with ExitStack() as ges:
        kb = KB(nc, ges)
        op = kb.op
        PS = [kb.ps(f"ps{i}", [128, 512], F32) for i in range(7)]
        PSB = [kb.buf(f"ps{i}") for i in range(7)]
        PT = kb.ps("pt", [128, 1024], BF16)
        PTB = kb.buf("pt")
        PTB2 = PTB
        PTB3 = PTB
        ident_f = kb.sb("ident_f", [128, 128], F32)
        ident_b = kb.sb("ident_b", [128, 128], BF16)
        gates_bc = kb.sb("gates_bc", [128, 2, D], F32)
        modT = kb.sb("modT", [128, 32], F32)
        one11 = kb.sb("one11", [1, 1], F32)
        B_id = kb.buf("ident")
        B_gbc = kb.buf("gbc")
        B_modT = kb.buf("modT")
        B_one = kb.buf("one")
        tilevalid = kb.sb("tilevalid", [128, NT], F32)
        hv = kb.sb("hv", [128, 1], F32)
        B_tv = kb.buf("tv")
        kb.dma("sp", tilevalid[:], tb["tilevalid"], B_tv, writes=[B_tv])
        kb.dma("sp", hv[:], tb["hv"], B_tv, writes=[B_tv])
        kb.dma("sp", ident_f[:], tb["ident"], B_id, writes=[B_id])
        op("dve", lambda e: e.tensor_copy(out=ident_b[:], in_=ident_f[:]), reads=[B_id], writes=[B_id])
        op("pool", lambda e: e.memset(one11[:], 1.0), writes=[B_one])

        def dbg(name, src_ap, rbufs):
            if name in dbg_d:
                b = kb.buf("dbg")
                kb.dma("sp", dbg_d[name], src_ap, b, reads=rbufs)
                kb.live

        with ExitStack() as es0:
            kb.es = es0
            c_sb = kb.sb("c_sb", [128, 8], F32)
            sc = kb.sb("sc", [128, 8], F32)
            sc_bc = kb.sb("sc_bc", [128, 8, 128], F32)
            bada = kb.sb("bada", [128, 6 * D], F32)
            mod_bc = kb.sb("mod_bc", [128, 6 * D], F32)
            wst = [kb.sb(f"wst{i}", [128, 8, 512], F32) for i in range(2)]
            B_c, B_sc, B_bada, B_mod = kb.buf("c"), kb.buf("sc"), kb.buf("bada"), kb.buf("mod")
            B_wst = [kb.buf("wst0"), kb.buf("wst1")]
            kb.dma("sp", c_sb[:], c_d, B_c, writes=[B_c])
            kb.dma("sp", bada[:], b_ada.partition_broadcast(128).rearrange("p o n -> p (o n)"), B_bada, writes=[B_bada])
            op("act", lambda e: e.activation(out=sc[:], in_=c_sb[:], func=AF.Silu), reads=[B_c], writes=[B_sc])
            op("dve", lambda e: e.tensor_copy(out=sc_bc[:], in_=sc[:].unsqueeze(2).to_broadcast([128, 8, 128])),
               reads=[B_sc], writes=[B_sc])
            for j in range(12):
                st = wst[j % 2]
                kb.dma("sp", st[:], w_ada[:, j * 512:(j + 1) * 512].rearrange("(kc p) n -> p kc n", p=128),
                       B_wst[j % 2], writes=[B_wst[j % 2]])
                for kc in range(KC):
                    op("pe", lambda e: e.matmul(PS[j % 2][:, :], lhsT=sc_bc[:, kc, :], rhs=st[:, kc, :],
                                                start=(kc == 0), stop=(kc == KC - 1)),
                       reads=[B_sc, B_wst[j % 2]], writes=[PSB[j % 2]])
                op("dve", lambda e: e.tensor_tensor(out=mod_bc[:, j * 512:(j + 1) * 512], in0=PS[j % 2][:, :],
                                                     in1=bada[:, j * 512:(j + 1) * 512], op=ALU.add),
                   reads=[PSB[j % 2], B_bada], writes=[B_mod])
            op("dve", lambda e: e.tensor_copy(out=gates_bc[:, 0, :], in_=mod_bc[:, 2 * D:3 * D]), reads=[B_mod], writes=[B_gbc])
            op("dve", lambda e: e.tensor_copy(out=gates_bc[:, 1, :], in_=mod_bc[:, 5 * D:6 * D]), reads=[B_mod], writes=[B_gbc])
            for wi, w in enumerate((0, 1, 3, 4)):
                for fc in range(8):
                    col = w * D + fc * 128
                    idx = wi * 8 + fc
                    op("pe", lambda e: e.matmul(PS[2][:, idx:idx + 1], lhsT=mod_bc[0:1, col:col + 128],
                                                rhs=one11[0:1, 0:1], start=True, stop=True),
                       reads=[B_mod, B_one], writes=[PSB[2]])
            op("dve", lambda e: e.tensor_copy(out=modT[:], in_=PS[2][:, 0:32]), reads=[PSB[2]], writes=[B_modT])
            op("dve", lambda e: e.tensor_scalar(out=modT[:, 8:16], in0=modT[:, 8:16], scalar1=1.0, scalar2=None, op0=ALU.add),
               reads=[B_modT], writes=[B_modT])
            op("dve", lambda e: e.tensor_scalar(out=modT[:, 24:32], in0=modT[:, 24:32], scalar1=1.0, scalar2=None, op0=ALU.add),
               reads=[B_modT], writes=[B_modT])
            dbg("modT", modT[:], [B_modT])
            kb.barrier()
        kb.es = ges

        def ln_to_hT(xt, B_x, xn, B_xn, stats, mv, rstd, B_st, hT_out, B_hT, mcol):
            for hf in range(2):
                op("dve", lambda e: e.bn_stats(out=stats[:, hf, :], in_=xt[:, hf * 512:(hf + 1) * 512]),
                   reads=[B_x], writes=[B_st])
            op("dve", lambda e: e.bn_aggr(out=mv[:], in_=stats[:]), reads=[B_st], writes=[B_st])
            op("dve", lambda e: e.tensor_scalar(out=rstd[:], in0=mv[:, 1:2], scalar1=1e-5, scalar2=None, op0=ALU.add),
               reads=[B_st], writes=[B_st])
            op("act", lambda e: e.activation(out=rstd[:], in_=rstd[:], func=AF.Sqrt), reads=[B_st], writes=[B_st])
            op("dve", lambda e: e.reciprocal(out=rstd[:], in_=rstd[:]), reads=[B_st], writes=[B_st])
            op("dve", lambda e: e.tensor_scalar(out=xn[:], in0=xt[:], scalar1=mv[:, 0:1], scalar2=rstd[:, 0:1],
                                                 op0=ALU.subtract, op1=ALU.mult), reads=[B_x, B_st], writes=[B_xn])
            for kc in range(KC):
                op("pe", lambda e: e.transpose(out=PT[:, kc * 128:(kc + 1) * 128], in_=xn[:, kc * 128:(kc + 1) * 128],
                                               identity=ident_b[:]), reads=[B_xn, B_id], writes=[PTB, PTB2] if kc == 7 else [PTB])
            for kc in range(KC):
                op("act", lambda e: e.activation(out=hT_out[:, kc, :], in_=PT[:, kc * 128:(kc + 1) * 128], func=AF.Identity,
                                                 scale=modT[:, mcol + 8 + kc:mcol + 9 + kc], bias=modT[:, mcol + kc:mcol + kc + 1]),
                   reads=[PTB, B_modT], writes=[B_hT])

        def load_cast(dst_fn, src_ap, rows_kc, ncols, stg, B_stg, B_dst, eng="dve"):
            kb.dma("sp", stg[:, 0:rows_kc, 0:ncols], src_ap.rearrange("(kc p) n -> p kc n", p=128), B_stg, writes=[B_stg])
            for kc in range(rows_kc):
                if kc % 2 == 0:
                    op("dve", lambda e: e.tensor_copy(out=dst_fn(kc), in_=stg[:, kc, 0:ncols]), reads=[B_stg], writes=[B_dst])
                else:
                    op("act", lambda e: e.activation(out=dst_fn(kc), in_=stg[:, kc, 0:ncols], func=AF.Copy), reads=[B_stg], writes=[B_dst])

        with ExitStack() as esA:
            kb.es = esA
            oaT = kb.sb("oaT", [128, 4, NQ * 128], BF16)
            uT = kb.sb("uT", [128, 4, T], BF16)
            B_oaT = [kb.buf(f"oaT{i}") for i in range(NT)]
            B_uT = [kb.buf(f"uT{i}") for i in range(NT)]
            if STOP[0] >= 1:
                pass_a1(nc, kb, locals())
            kb.barrier()
            WMp = kb.sb("WMp", [128, 8, 2048], BF16)
            WGLp = kb.sb("WGLp", [128, 4, 2048], BF16)
            WOp = kb.sb("WOp", [128, 8, D], BF16)
            B_Wpre = kb.buf("Wpre")
            if STOP[0] >= 2:
                pass_s5(nc, kb, locals())
            kb.barrier()
            if STOP[0] >= 3:
                pass_a2(nc, kb, locals())
            kb.barrier()
        kb.es = ges
        if STOP[0] >= 4:
            with ExitStack() as esB:
                kb.es = esB
                pass_b(nc, kb, locals())
                kb.barrier()
            kb.es = ges
        kb.barrier(release=False)
    return nc


class NS:
    def __init__(self, d):
        self.__dict__.update(d)


def pass_a1(nc, kb, Ld):
    outer = Ld['esA']
    with ExitStack() as es1:
        d2 = dict(Ld)
        d2['esA'] = es1
        kb.es = es1
        _pass_a1_body(nc, kb, d2)
        kb.barrier()
    kb.es = outer


def _pass_a1_body(nc, kb, Ld):
    L = NS(Ld)
    op = kb.op
    PS, PSB, PT, PTB = L.PS, L.PSB, L.PT, L.PTB
    tb = L.tb
    sb, buf = kb.sb, kb.buf
    WQK = sb("WQK", [128, 8, 1088], BF16)
    WV = sb("WV", [128, 8, 280], BF16)
    WU = sb("WU", [128, 8, 512], BF16)
    KTs = sb("KTs", [128, 2, T], BF16)
    KTw = sb("KTw", [128, 2, 1024], BF16)
    V1 = sb("V1", [128, NT, 4, 65], BF16)
    kcTa = sb("kcTa", [128, 2, 256], BF16)
    vcx = sb("vcx", [128, 2, 2, 129], BF16)
    triC = sb("triC", [128, 4, 128], BF16)
    triW = sb("triW", [128, 4, 128], BF16)
    maskC = sb("maskC", [128, 16, 128], BF16)
    cw1 = sb("cw1", [128, 2, 32, 128], BF16)
    cw2 = sb("cw2", [128, 2, 64], BF16)
    peT = sb("peT", [128, 2, 32], BF16)
    cbias = sb("cbias", [128, 2], F32)
    qA = sb("qA", [67, 8, 128], BF16)
    qB = sb("qB", [67, 8, 128], BF16)
    f0t = sb("f0t", [128, 64], F32)
    mskw = sb("mskw", [128, 128], F32)
    addw = sb("addw", [128, 128], F32)
    B_W, B_KT, B_V1 = buf("W"), [buf(f"KT{i}") for i in range(NT)], [buf(f"V1{i}") for i in range(NT)]
    B_kc, B_vcx, B_cst = buf("kc"), buf("vcx"), buf("cst")
    with ExitStack() as ess:
        kb.es = ess
        stgs = [sb(f"stg{i}", [128, 8, 512], F32) for i in range(3)]
        B_stgs = [buf(f"stg{i}") for i in range(3)]
        rotc = [0]

        def rot():
            rotc[0] += 1
            return stgs[rotc[0] % 3], B_stgs[rotc[0] % 3]

        stg, B_stg = rot()
        op("pool", lambda e: e.memset(KTw[:], 0.0), writes=B_KT)
        op("pool", lambda e: e.memset(cw1[:], 0.0), writes=[B_cst])
        op("pool", lambda e: e.memset(peT[:], 0.0), writes=[B_cst])
        op("pool", lambda e: e.memset(WQK[:, :, 1024:1088], 0.0), writes=[B_W])
        L.load_cast(lambda kc: WQK[:, kc, 0:512], L.w_in[:, 0:512], 8, 512, stg, B_stg, B_W)
        stg, B_stg = rot()
        kb.dma("sp", stg[:, :, :], L.w_in[:, 512:1024].rearrange("(kc p) n -> p kc n", p=128), B_stg, writes=[B_stg])
        for kc in range(8):
            op("dve", lambda e: e.tensor_copy(out=WQK[:, kc, 768:1024], in_=stg[:, kc, 0:256]), reads=[B_stg], writes=[B_W])
            op("dve", lambda e: e.tensor_copy(out=WQK[:, kc, 512:640], in_=stg[:, kc, 256:384]), reads=[B_stg], writes=[B_W])
            op("dve", lambda e: e.tensor_copy(out=WV[:, kc, 0:128], in_=stg[:, kc, 384:512]), reads=[B_stg], writes=[B_W])
        stg, B_stg = rot()
        kb.dma("sp", stg[:, :, 0:280], L.w_in[:, 1024:1304].rearrange("(kc p) n -> p kc n", p=128), B_stg, writes=[B_stg])
        for kc in range(8):
            op("dve", lambda e: e.tensor_copy(out=WQK[:, kc, 640:768], in_=stg[:, kc, 0:128]), reads=[B_stg], writes=[B_W])
            op("dve", lambda e: e.tensor_copy(out=WV[:, kc, 128:280], in_=stg[:, kc, 128:280]), reads=[B_stg], writes=[B_W])
        stg, B_stg = rot()
        L.load_cast(lambda kc: WU[:, kc, :], L.w_in[:, 1304:1816], 8, 512, stg, B_stg, B_W)
        for kv, (w1d, w2d, ped) in enumerate(((L.w_ck1, L.w_ck2, L.pe_ck), (L.w_cv1, L.w_cv2, L.pe_cv))):
            for lh in range(4):
                stg, B_stg = rot()
                kb.dma("sp", stg[0:64, :, :].rearrange("p a (b e) -> p (a b) e", e=128)[:, 0:8, :],
                       w1d[lh * 8:(lh + 1) * 8].rearrange("l d e -> d l e"), B_stg, writes=[B_stg])
                op("dve", lambda e: e.tensor_copy(out=cw1[0:64, kv, lh * 8:(lh + 1) * 8, :],
                                                   in_=stg[0:64, :, :].rearrange("p a (b e) -> p (a b) e", e=128)[:, 0:8, :]),
                   reads=[B_stg], writes=[B_cst])
            kb.dma("sp", stg[:, 0, 0:64], w2d, B_stg, writes=[B_stg])
            op("dve", lambda e: e.tensor_copy(out=cw2[:, kv, :], in_=stg[:, 0, 0:64]), reads=[B_stg], writes=[B_cst])
            kb.dma("sp", stg[0:64, 0, 0:32], ped.rearrange("l d -> d l"), B_stg, writes=[B_stg], allow_slow_non_contiguous=True)
            op("dve", lambda e: e.tensor_copy(out=peT[0:64, kv, :], in_=stg[0:64, 0, 0:32]), reads=[B_stg], writes=[B_cst])
        stg, B_stg = rot()
        for tname, dst in (("tric", triC), ("triw", triW)):
            kb.dma("sp", stg[:, 0, 0:128], tb[tname], B_stg, writes=[B_stg])
            op("dve", lambda e: e.tensor_copy(out=dst[:], in_=stg[:, 0, 0:128].unsqueeze(1).to_broadcast([128, 4, 128])),
               reads=[B_stg], writes=[B_cst])
        kb.dma("sp", stg[:, 0:4, :].rearrange("p a b -> p (a b)"), tb["maskc"].rearrange("p a b -> p (a b)"), B_stg, writes=[B_stg])
        op("dve", lambda e: e.tensor_copy(out=maskC[:].rearrange("p a b -> p (a b)"),
                                           in_=stg[:, 0:4, :].rearrange("p a b -> p (a b)")), reads=[B_stg], writes=[B_cst])
        stg, B_stg = rot()
        op("pool", lambda e: e.memset(vcx[:], 0.0), writes=[B_vcx])
        op("pool", lambda e: e.memset(vcx[:, :, :, 64:65], 1.0), writes=[B_vcx])
        kb.dma("sp", stg[:, 0, 0:128], tb["ov"].rearrange("p a b -> p (a b)"), B_stg, writes=[B_stg])
        for g in range(2):
            op("dve", lambda e: e.tensor_copy(out=vcx[:, :, g, 65:129],
                                               in_=stg[:, 0, 0:128].rearrange("p (a b) -> p a b", b=64)),
               reads=[B_stg], writes=[B_vcx])
        op("pool", lambda e: e.memset(V1[:, :, :, 64:65], 1.0), writes=B_V1)
        op("pool", lambda e: e.memset(kcTa[:], 0.0), writes=[B_kc])
        stg, B_stg = rot()
        for q4 in range(2):
            stg, B_stg = rot()
            kb.dma("sp", stg[64:128, :, :].rearrange("p a b -> p (a b)")[:, 0:2048], tb["kaug_tok"][:, q4 * 2048:(q4 + 1) * 2048],
                   B_stg, writes=[B_stg])
            for g in range(2):
                op("dve", lambda e: e.tensor_copy(out=KTs[64:128, g, q4 * 2048:(q4 + 1) * 2048],
                                                   in_=stg[64:128, :, :].rearrange("p a b -> p (a b)")[:, 0:2048]), reads=[B_stg], writes=B_KT)
        stg, B_stg = rot()
        kb.dma("sp", stg[64:67, 0, 0:256], tb["kaug_cmp"], B_stg, writes=[B_stg])
        for g in range(2):
            op("dve", lambda e: e.tensor_copy(out=kcTa[64:67, g, :], in_=stg[64:67, 0, 0:256]), reads=[B_stg], writes=[B_kc])
        for qt, qn in ((qA, "qaugA"), (qB, "qaugB")):
            stg, B_stg = rot()
            kb.dma("sp", stg[64:67, 0:2, :].rearrange("p a b -> p (a b)"), tb[qn].rearrange("p a b -> p (a b)"), B_stg, writes=[B_stg])
            op("dve", lambda e: e.tensor_copy(out=qt[64:67, :, :].rearrange("p a b -> p (a b)"),
                                               in_=stg[64:67, 0:2, :].rearrange("p a b -> p (a b)")), reads=[B_stg], writes=[B_cst])
        kb.dma("sp", f0t[:], tb["f0"], B_cst, writes=[B_cst])
        kb.dma("sp", mskw[:], tb["mskw"], B_cst, writes=[B_cst])
        kb.dma("sp", addw[:], tb["addw"], B_cst, writes=[B_cst])
        for kv in range(2):
            for l in range(32):
                op("pe", lambda e: e.matmul(PS[2][:, kv:kv + 1], lhsT=cw1[:, kv, l, :], rhs=peT[:, kv, l:l + 1],
                                            start=(l == 0), stop=(l == 31)), reads=[B_cst], writes=[PSB[2]])
        op("dve", lambda e: e.tensor_copy(out=cbias[:], in_=PS[2][:, 0:2]), reads=[PSB[2]], writes=[B_cst])
        kb.barrier()
    kb.es = L.esA
    xt = [sb(f"xt{i}", [128, D], F32) for i in range(3)]
    B_xt = [buf("xt0"), buf("xt1"), buf("xt2")]
    xn = sb("xn", [128, D], BF16); B_xn = buf("xn")
    stats = sb("stats", [128, 2, 6], F32); mv = sb("mv", [128, 2], F32); rstd = sb("rstd", [128, 1], F32)
    B_st = buf("st")
    hTs = [sb(f"hT{i}", [128, 8, 128], BF16) for i in range(2)]; B_hTs = [buf("hT0"), buf("hT1")]
    QTa = sb("QTa", [128, 8, 128], BF16); B_Q = buf("Q")
    cmpT = sb("cmpT", [128, 2, 2, 144], BF16); B_cmp = buf("cmp")
    hact = sb("hact", [128, 2, 2, 8], BF16); B_hact = buf("hact")
    hpad = sb("hpad", [128, 2, 128], BF16); B_hpad = buf("hpad")
    gsig = sb("gsig", [128, 24], F32); B_gs = buf("gsig")
    Pt = [sb(f"Pt{i}", [128, 512], BF16) for i in range(4)]
    B_Pt = [buf(f"Pt{i}") for i in range(4)]
    osb = [sb(f"osb{g}", [128, 3, 4, 65], F32) for g in range(2)]; B_osb = [buf("osb0"), buf("osb1")]
    uimp = [sb(f"uimp{g}", [128, 4, 64], F32) for g in range(2)]
    PTB2 = L.PTB2
    imp = sb("imp", [128, 64], F32); imp2 = sb("imp2", [128, 64], F32); impw = sb("impw", [128, 64], F32)
    m8 = sb("m8", [128, 16], F32)
    B_imp = buf("imp")
    QS = [sb(f"QS{g}", [128, 4, 128], BF16) for g in range(2)]; B_QS = [buf("QS0"), buf("QS1")]
    den = [sb(f"den{g}", [128, 3, 4], F32) for g in range(2)]; fac = [sb(f"fac{g}", [128, 3, 4], F32) for g in range(2)]
    B_fac = [buf("fac0"), buf("fac1")]
    otok = sb("otok", [128, 8, 64], BF16); B_ot = buf("otok")
    otmp = sb("otmp", [128, 4, 64], F32); B_otmp = buf("otmp")
    op("pool", lambda e: e.memset(cmpT[:], 0.0), writes=[B_cmp])
    op("pool", lambda e: e.memset(QTa[:], 0.0), writes=[B_Q])
    for g in range(2):
        op("pool", lambda e: e.memset(QS[g][:], 0.0), writes=[B_QS[g]])
    sels = [sb(f"sel128_{g}", [128, 128], BF16) for g in range(2)]
    B_sel = [buf("sel0"), buf("sel1")]
    for g in range(2):
        op("pool", lambda e: e.memset(sels[g][:], 0.0), writes=[B_sel[g]])
    pcount = [0]

    def next_pt():
        pcount[0] += 1
        return pcount[0] % 4

    def scount_next(c=[0]):
        c[0] += 1
        return (0, 1, 6)[c[0] % 3]

    modT, ident_b = L.modT, L.ident_b

    def ln_a(t):
        x_, B_x = xt[t % 3], B_xt[t % 3]
        for hf in range(2):
            op("dve", lambda e: e.bn_stats(out=stats[:, hf, :], in_=x_[:, hf * 512:(hf + 1) * 512]), reads=[B_x], writes=[B_st])
        op("dve", lambda e: e.bn_aggr(out=mv[:], in_=stats[:]), reads=[B_st], writes=[B_st])
        op("dve", lambda e: e.tensor_scalar(out=rstd[:], in0=mv[:, 1:2], scalar1=1e-5, scalar2=None, op0=ALU.add), reads=[B_st], writes=[B_st])
        op("act", lambda e: e.activation(out=rstd[:], in_=rstd[:], func=AF.Sqrt), reads=[B_st], writes=[B_st])
        op("dve", lambda e: e.reciprocal(out=rstd[:], in_=rstd[:]), reads=[B_st], writes=[B_st])
        op("dve", lambda e: e.tensor_scalar(out=xn[:], in0=x_[:], scalar1=mv[:, 0:1], scalar2=rstd[:, 0:1], op0=ALU.subtract, op1=ALU.mult),
           reads=[B_x, B_st], writes=[B_xn])

    def ln_b(t):
        h_, B_h = hTs[t % 2], B_hTs[t % 2]
        for kc in range(KC):
            op("pe", lambda e: e.transpose(out=PT[:, kc * 128:(kc + 1) * 128], in_=xn[:, kc * 128:(kc + 1) * 128], identity=ident_b[:]),
               reads=[B_xn, L.B_id], writes=[PTB, PTB2] if kc == 7 else ([PTB, L.PTB3] if kc == 6 else [PTB]))
        for kc in range(KC):
            op("act", lambda e: e.activation(out=h_[:, kc, :], in_=PT[:, kc * 128:(kc + 1) * 128], func=AF.Identity,
                                             scale=modT[:, 8 + kc:9 + kc], bias=modT[:, kc:kc + 1]),
               reads=[PTB, PTB2, L.B_modT] if kc == 7 else ([PTB, L.PTB3, L.B_modT] if kc == 6 else [PTB, L.B_modT]), writes=[B_h])

    pend_ot = [None]

    def flush_ot():
        if pend_ot[0] is None:
            return
        ti = pend_ot[0]
        pend_ot[0] = None
        osl = slice((ti - Q0) * 128, (ti - Q0 + 1) * 128)
        for c in range(4):
            op("pe", lambda e: e.transpose(out=PT[:, c * 128:(c + 1) * 128], in_=otok[:, 2 * c:2 * c + 2, :].rearrange("p a b -> p (a b)"),
                                           identity=L.ident_b[:]), reads=[B_ot, L.B_id], writes=[PTB])
        op("act", lambda e: e.activation(out=L.oaT[:, :, osl], in_=PT[:, 0:512].rearrange("p (a b) -> p a b", b=128), func=AF.Copy),
           reads=[PTB], writes=[L.B_oaT[ti]])

    for t0 in range(2):
        kb.dma("sp", xt[t0][:], L.x_d[t0 * 128:(t0 + 1) * 128, :], B_xt[t0], writes=[B_xt[t0]])
    ln_a(0)
    ln_b(0)
    for i in range(NT):
        if i + 2 < NT:
            kb.dma("sp", xt[(i + 2) % 3][:], L.x_d[(i + 2) * 128:(i + 3) * 128, :], B_xt[(i + 2) % 3], writes=[B_xt[(i + 2) % 3]])
        if i + 1 < NT:
            ln_a(i + 1)
        hT, B_hT = hTs[i % 2], B_hTs[i % 2]
        lnb_done = [i + 1 >= NT]
        tsl = slice(i * 128, (i + 1) * 128)
        isq = i >= Q0
        for g in (range(2) if isq else ()):
            for hh in range(4):
                h = g * 4 + hh
                for kc in range(KC):
                    op("pe", lambda e: e.matmul(PS[g][:, hh * 128:(hh + 1) * 128], lhsT=WQK[:, kc, h * 64:h * 64 + 128],
                                                rhs=hT[:, kc, :], start=(kc == 0), stop=(kc == KC - 1)),
                       reads=[B_W, B_hT], writes=[PSB[g]])
            op("act", lambda e: e.activation(out=QTa[0:64, g * 4:(g + 1) * 4, :].rearrange("p a b -> p (a b)"),
                                             in_=PS[g][0:64, :], func=AF.Copy, scale=0.125), reads=[PSB[g]], writes=[B_Q])
        if isq:
            op("dve", lambda e: e.scalar_tensor_tensor(out=QTa[64:67, :, :], in0=qB[64:67, :, :], scalar=float(i), in1=qA[64:67, :, :],
                                                        op0=ALU.mult, op1=ALU.add), reads=[B_cst], writes=[B_Q])
        for grp in range(2):
            for s4 in range(4):
                c0 = 512 + grp * 256 + s4 * 64
                for kc in range(KC):
                    op("pe", lambda e: e.matmul(PS[2 + grp][:, s4 * 128:(s4 + 1) * 128], lhsT=WQK[:, kc, c0:c0 + 128],
                                                rhs=hT[:, kc, :], start=(kc == 0), stop=(kc == KC - 1)),
                       reads=[B_W, B_hT], writes=[PSB[2 + grp]])
        wsl = slice((i % 8) * 128, (i % 8 + 1) * 128)
        op("act", lambda e: e.activation(out=KTs[0:64, :, tsl], in_=PS[2][0:64, 0:256].rearrange("p (b c) -> p b c", b=2),
                                         func=AF.Copy), reads=[PSB[2]], writes=[B_KT[i]])
        op("act", lambda e: e.activation(out=KTw[0:64, :, wsl], in_=PS[2][0:64, 256:512].rearrange("p (b c) -> p b c", b=2),
                                         func=AF.Copy), reads=[PSB[2]], writes=[B_KT[i]])
        op("dve", lambda e: e.tensor_copy(out=KTw[64:67, :, wsl], in_=KTs[64:67, :, tsl]), reads=[B_cst], writes=[B_KT[i]])
        op("dve", lambda e: e.tensor_copy(out=cmpT[0:64, :, :, 16:144], in_=PS[3][0:64, :].rearrange("p (a b c) -> p a b c", a=2, b=2)),
           reads=[PSB[3]], writes=[B_cmp])
        flush_ot()
        for kc in range(KC):
            op("pe", lambda e: e.matmul(PS[4][:, 0:280], lhsT=hT[:, kc, :], rhs=WV[:, kc, :], start=(kc == 0), stop=(kc == KC - 1)),
               reads=[B_W, B_hT], writes=[PSB[4]])
        op("dve", lambda e: e.tensor_copy(out=V1[:, i, :, 0:64], in_=PS[4][:, 0:256].rearrange("p (a b) -> p a b", b=64)),
           reads=[PSB[4]], writes=[B_V1[i]])
        op("act", lambda e: e.activation(out=gsig[:], in_=PS[4][:, 256:280], func=AF.Sigmoid), reads=[PSB[4]], writes=[B_gs])
        if not isq and not lnb_done[0]:
            ln_b(i + 1)
            lnb_done[0] = True
        for ct in range(4):
            for kc in range(KC):
                op("pe", lambda e: e.matmul(PS[5][:, ct * 128:(ct + 1) * 128], lhsT=WU[:, kc, ct * 128:(ct + 1) * 128],
                                            rhs=hT[:, kc, :], start=(kc == 0), stop=(kc == KC - 1)),
                   reads=[B_W, B_hT], writes=[PSB[5]])
        op("act", lambda e: e.activation(out=L.uT[:, :, tsl], in_=PS[5][:, :].rearrange("p (a b) -> p a b", b=128), func=AF.Identity,
                                         scale=L.tilevalid[:, i:i + 1]), reads=[PSB[5], L.B_tv], writes=[L.B_uT[i]])
        for kv in range(2):
            for g in range(2):
                o0 = (kv * 2 + g) * 8
                for l in range(32):
                    op("pe", lambda e: e.matmul(PS[6][:, o0:o0 + 8], lhsT=cw1[:, kv, l, :], rhs=cmpT[:, kv, g, l:l + 113:16],
                                                start=(l == 0), stop=(l == 31)), reads=[B_cst, B_cmp], writes=[PSB[6]])
            op("act", lambda e: e.activation(out=hact[:, kv, :, :].rearrange("p a b -> p (a b)"), in_=PS[6][:, kv * 16:(kv + 1) * 16],
                                             func=AF.Silu, bias=cbias[:, kv:kv + 1]), reads=[PSB[6], B_cst], writes=[B_hact])
        op("dve", lambda e: e.tensor_copy(out=cmpT[0:64, :, :, 0:16], in_=cmpT[0:64, :, :, 128:144]), reads=[B_cmp], writes=[B_cmp])
        for g in range(2):
            op("pe", lambda e: e.matmul(PS[6][:, 64 + g * 8:64 + (g + 1) * 8], lhsT=cw2[:].rearrange("p a b -> p (a b)"), rhs=hact[:, 0, g, :],
                                        start=True, stop=True), reads=[B_cst, B_hact], writes=[PSB[6]])
        op("dve", lambda e: e.tensor_copy(out=kcTa[0:64, :, 8 * i:8 * i + 8], in_=PS[6][0:64, 64:80].rearrange("p (a b) -> p a b", b=8)),
           reads=[PSB[6]], writes=[B_kc])
        mt_i, mo = (8 * i) // 128, (8 * i) % 128
        op("pool", lambda e: e.memset(hpad[:], 0.0), writes=[B_hpad])
        op("pool", lambda e: e.tensor_copy(out=hpad[:, :, mo:mo + 8], in_=hact[:, 1, :, :]), reads=[B_hact], writes=[B_hpad])
        for g in range(2):
            op("pe", lambda e: e.matmul(PS[6][:, 128 + g * 64:128 + (g + 1) * 64], lhsT=hpad[:, g, :], rhs=cw2[:, 1, :],
                                        start=True, stop=True), reads=[B_cst, B_hpad], writes=[PSB[6]])
        op("dve", lambda e: e.tensor_tensor(out=vcx[:, mt_i, :, 0:64], in0=PS[6][:, 128:256].rearrange("p (a b) -> p a b", b=64),
                                             in1=vcx[:, mt_i, :, 0:64], op=ALU.add), reads=[PSB[6]], writes=[B_vcx])
        if isq:
            OB = [PS[2], PS[3], PS[4], PS[5]]
            OBB = [PSB[2], PSB[3], PSB[4], PSB[5]]
            Qgs = [QTa[:, g * 4:(g + 1) * 4, :].rearrange("p a b -> p (a b)") for g in range(2)]
            QSf = [QS[g][:].rearrange("p a b -> p (a b)") for g in range(2)]

            def emit_score(job):
                kind, g, kidx, first, last = job
                sbk = scount_next()
                extra = []
                if kind == "c":
                    mt = kidx
                    dl = i - 16 * mt
                    lhs, rl = kcTa[:, g, mt * 128:(mt + 1) * 128], [B_kc, B_Q]
                    if dl < 16:
                        for hh in range(4):
                            extra.append((PS[sbk][:, hh * 128:(hh + 1) * 128], L.ident_b[:], maskC[:, dl, :], [L.B_id, B_cst]))
                else:
                    kt = kidx
                    if kind == "s":
                        lhs = KTs[:, g, kt * 128:(kt + 1) * 128]
                    else:
                        lhs = KTw[:, g, (kt % 8) * 128:(kt % 8 + 1) * 128]
                        if kt == i - 4:
                            extra.append((PS[sbk][:, :], L.ident_b[:], triW[:].rearrange("p a b -> p (a b)"), [L.B_id, B_cst]))
                    if kt == i:
                        extra.append((PS[sbk][:, :], L.ident_b[:], triC[:].rearrange("p a b -> p (a b)"), [L.B_id, B_cst]))
                    rl = [B_KT[kt], B_QS[g] if kind == "s" else B_Q]
                op("pe", lambda e: e.matmul(PS[sbk][:, :], lhsT=lhs, rhs=(QSf[g] if kind == "s" else Qgs[g]), start=True, stop=(len(extra) == 0)),
                   reads=rl, writes=[PSB[sbk]])
                for xi, (oap, lt, rh, rb) in enumerate(extra):
                    op("pe", lambda e: e.matmul(oap, lhsT=lt, rhs=rh, start=False, stop=(xi == len(extra) - 1), skip_group_check=True),
                       reads=rb, writes=[PSB[sbk]])
                p = next_pt()
                op("act", lambda e: e.activation(out=Pt[p][:], in_=PS[sbk][:, :], func=AF.Exp), reads=[PSB[sbk]], writes=[B_Pt[p]])
                return p

            def emit_pv(job, p):
                kind, g, kidx, first, last = job
                if kind == "c":
                    rhs, ncol, rb = vcx[:, kidx, g, :], 129, B_vcx
                else:
                    vi = g if kind == "s" else 2 + g
                    rhs, ncol, rb = V1[:, kidx, vi, :], 65, B_V1[kidx]
                for hh in range(4):
                    op("pe", lambda e: e.matmul(OB[hh][:, 0:ncol], lhsT=Pt[p][:, hh * 128:(hh + 1) * 128], rhs=rhs,
                                                start=first, stop=last), reads=[B_Pt[p], rb], writes=[OBB[hh]])

            def epilogue(kind, g):
                bi = {"c": 0, "s": 1, "w": 2}[kind]
                o_, B_o = osb[g], B_osb[g]
                for hh in range(4):
                    if hh % 2:
                        op("dve", lambda e: e.tensor_copy(out=o_[:, bi, hh, :], in_=OB[hh][:, 0:65]), reads=[OBB[hh]], writes=[B_o])
                    else:
                        op("act", lambda e: e.activation(out=o_[:, bi, hh, :], in_=OB[hh][:, 0:65], func=AF.Copy), reads=[OBB[hh]], writes=[B_o])
                    if kind == "c":
                        op("dve", lambda e: e.tensor_copy(out=uimp[g][:, hh, :], in_=OB[hh][:, 65:129]), reads=[OBB[hh]], writes=[B_o])
                if kind == "c":
                    dn, B_d = den[g], B_fac[g]
                    op("dve", lambda e: e.tensor_scalar(out=dn[:, 0, :], in0=o_[:, 0, :, 64], scalar1=1e-30, scalar2=None, op0=ALU.max),
                       reads=[B_o], writes=[B_d])
                    op("dve", lambda e: e.reciprocal(out=dn[:, 0, :], in_=dn[:, 0, :]), reads=[B_d], writes=[B_d])
                    op("dve", lambda e: e.tensor_scalar(out=imp[:], in0=uimp[g][:, 0, :], scalar1=dn[:, 0, 0:1], scalar2=None, op0=ALU.mult),
                       reads=[B_o, B_d], writes=[B_imp])
                    for hh in range(1, 4):
                        op("dve", lambda e: e.scalar_tensor_tensor(out=imp[:], in0=uimp[g][:, hh, :], scalar=dn[:, 0, hh:hh + 1], in1=imp[:],
                                                                    op0=ALU.mult, op1=ALU.add), reads=[B_o, B_d], writes=[B_imp])
                    w0 = 64 - 2 * i
                    op("dve", lambda e: e.tensor_tensor(out=imp2[:], in0=imp[:], in1=mskw[:, w0:w0 + 64], op=ALU.mult),
                       reads=[B_imp, B_cst], writes=[B_imp])
                    op("dve", lambda e: e.tensor_tensor(out=imp2[:], in0=imp2[:], in1=addw[:, w0:w0 + 64], op=ALU.add),
                       reads=[B_imp, B_cst], writes=[B_imp])
                    op("dve", lambda e: e.tensor_tensor(out=imp2[:], in0=imp2[:], in1=f0t[:], op=ALU.max), reads=[B_cst], writes=[B_imp])
                    op("dve", lambda e: e.max(out=m8[:, 0:8], in_=imp2[:]), reads=[B_imp], writes=[B_imp])
                    op("dve", lambda e: e.match_replace(out=impw[:], in_to_replace=m8[:, 0:8], in_values=imp2[:], imm_value=-3e9),
                       reads=[B_imp], writes=[B_imp])
                    op("dve", lambda e: e.max(out=m8[:, 8:16], in_=impw[:]), reads=[B_imp], writes=[B_imp])
                    op("dve", lambda e: e.tensor_scalar(out=sels[g][:, 67:128], in0=imp2[:, 1:62], scalar1=m8[:, 15:16], scalar2=None, op0=ALU.is_ge),
                       reads=[B_imp], writes=[B_sel[g]])
                if kind == "s":
                    dn, fc_, B_d = den[g], fac[g], B_fac[g]
                    op("dve", lambda e: e.tensor_scalar(out=dn[:, 1:3, :], in0=o_[:, 1:3, :, 64], scalar1=1e-30, scalar2=None, op0=ALU.max),
                       reads=[B_o], writes=[B_d])
                    op("dve", lambda e: e.reciprocal(out=dn[:, 1:3, :], in_=dn[:, 1:3, :]), reads=[B_d], writes=[B_d])
                    op("dve", lambda e: e.tensor_tensor(out=fc_[:], in0=dn[:], in1=gsig[:, g * 12:(g + 1) * 12].rearrange("p (h b) -> p b h", b=3),
                                                         op=ALU.mult), reads=[B_d, B_gs], writes=[B_d])
                    for hh in range(4):
                        h = g * 4 + hh
                        op("dve", lambda e: e.tensor_scalar(out=otmp[:, hh, :], in0=o_[:, 0, hh, 0:64], scalar1=fc_[:, 0, hh:hh + 1], scalar2=None,
                                                             op0=ALU.mult), reads=[B_o, B_d], writes=[B_otmp])
                        op("dve", lambda e: e.scalar_tensor_tensor(out=otmp[:, hh, :], in0=o_[:, 1, hh, 0:64], scalar=fc_[:, 1, hh:hh + 1],
                                                                    in1=otmp[:, hh, :], op0=ALU.mult, op1=ALU.add),
                           reads=[B_o, B_d], writes=[B_otmp])
                        op("dve", lambda e: e.scalar_tensor_tensor(out=otok[:, h, :], in0=o_[:, 2, hh, 0:64], scalar=fc_[:, 2, hh:hh + 1],
                                                                    in1=otmp[:, hh, :], op0=ALU.mult, op1=ALU.add),
                           reads=[B_o, B_d, B_otmp], writes=[B_ot])

            def epi_c_tail(g):
                c0, pb_ = (896, PTB2) if g == 0 else (768, L.PTB3)
                op("pe", lambda e: e.transpose(out=PT[:, c0:c0 + 128], in_=sels[g][:], identity=L.ident_b[:]), reads=[B_sel[g], L.B_id], writes=[pb_])
                op("dve", lambda e: e.tensor_scalar(out=QS[g][64:128], in0=PT[64:128, c0:c0 + 128].unsqueeze(1).to_broadcast([64, 4, 128]),
                                                     scalar1=-1.0, scalar2=-NEG, op0=ALU.add, op1=ALU.mult), reads=[pb_], writes=[B_QS[g]])
                op("pool", lambda e: e.tensor_copy(out=QS[g][0:67], in_=QTa[0:67, g * 4:(g + 1) * 4, :]), reads=[B_Q], writes=[B_QS[g]])

            jobs = []
            for kind in ("c", "w", "s"):
                for g in range(2):
                    if kind == "c":
                        ks = [0] if 8 * i + 7 < 128 else [0, 1]
                    elif kind == "w":
                        ks = list(range(max(0, i - 4), i + 1))
                    else:
                        ks = list(range(0, i + 1))
                    for n_, k_ in enumerate(ks):
                        jobs.append((kind, g, k_, n_ == 0, n_ == len(ks) - 1))
            pend = []
            n_s = [0]
            for job in jobs:
                if job[0] == "s" and job[3] and job[1] == 0:
                    epi_c_tail(0)
                    epi_c_tail(1)
                    n_s[0] = 0
                if job[0] == "s":
                    n_s[0] += 1
                    if n_s[0] == 4 and not lnb_done[0]:
                        ln_b(i + 1)
                        lnb_done[0] = True
                p = emit_score(job)
                pend.append((job, p))
                if len(pend) > 2:
                    pj = pend.pop(0)
                    emit_pv(*pj)
                    if pj[0][4]:
                        epilogue(pj[0][0], pj[0][1])
            for pj in pend:
                emit_pv(*pj)
                if pj[0][4]:
                    epilogue(pj[0][0], pj[0][1])
        if not lnb_done[0]:
            ln_b(i + 1)
            lnb_done[0] = True
        if isq:
            pend_ot[0] = i
    flush_ot()
    if "oaT" in L.dbg_d:
        b = kb.buf("dbg")
        st2 = sb("dbgst", [128, 4, 512], F32)
        op("dve", lambda e: e.tensor_copy(out=st2[:], in_=L.oaT[:, :, 0:512]), reads=L.B_oaT, writes=[b])
        kb.dma("sp", L.dbg_d["oaT"], st2[:], b, reads=[b])


def pass_s5(nc, kb, Ld):
    L = NS(Ld)
    op = kb.op
    PS, PSB, PT, PTB = L.PS, L.PSB, L.PT, L.PTB
    sb, buf = kb.sb, kb.buf
    PI = math.pi
    with ExitStack() as es5:
        kb.es = es5
        BBT = sb("BBT", [128, 16, 2, 128], BF16)
        BBTn = sb("BBTn", [128, 16, 128], BF16)
        Cq = sb("Cq", [128, 16, 4, 128], BF16)
        EI = sb("EI", [128, 2, 16, 128], F32)
        EF = sb("EF", [128, 2, 16, 128], F32)
        L128 = sb("L128", [128, 2, 16], F32)
        dsk = sb("dsk", [128, 4], F32)
        ones = sb("ones", [128, 128], F32)
        carry = sb("carry", [128, 2, 16], F32)
        zl = sb("zl", [128, 2, 16], F32)
        B_tab, B_car, B_zl = buf("tab"), buf("carry"), buf("zl")
        with ExitStack() as est:
            kb.es = est
            ar = sb("ar", [128, 16], F32); ai = sb("ai", [128, 16], F32); ldt = sb("ldt", [128, 16], F32)
            dt = sb("dt", [128, 16], F32); lrd = sb("lrd", [128, 16], F32); ang = sb("ang", [128, 16], F32)
            mag = sb("mag", [128, 16], F32); mgi = sb("mgi", [128, 16], F32)
            sn = sb("sn", [128, 16], F32); cs = sb("cs", [128, 16], F32); tmp = sb("tmp", [128, 16], F32); tmp2 = sb("tmp2", [128, 16], F32)
            lb = sb("lb", [128, 2, 16], F32); lbi = sb("lbi", [128, 2, 16], F32); pw = sb("pw", [128, 2, 16], F32)
            coef = sb("coef", [128, 2, 16], F32); den = sb("dens", [128, 16], F32)
            bsb = sb("bsb", [128, 2, 16, 16], F32); bb = sb("bb", [128, 2, 16, 16], F32); bt = sb("bt", [128, 16, 16], F32)
            Apr = sb("Apr", [128, 128], BF16)
            Cn = sb("Cn", [128, 2, 4, 64], F32)
            par = sb("par", [128, 2], F32)
            et1 = sb("et1", [128, 16, 64], F32); et2 = sb("et2", [128, 16, 64], F32)
            B_s = buf("s5setup"); B_apr = buf("apr"); B_et = buf("et")
            kb.dma("sp", ar[:], L.a_re.rearrange("(pr g2) p -> (g2 p) pr", g2=2), B_s, writes=[B_s], allow_slow_non_contiguous=True)
            kb.dma("sp", ai[:], L.a_im.rearrange("(pr g2) p -> (g2 p) pr", g2=2), B_s, writes=[B_s], allow_slow_non_contiguous=True)
            for g2 in range(2):
                kb.dma("sp", ldt[g2 * 64:(g2 + 1) * 64, :],
                       L.log_dt.rearrange("o (pr g2) -> o g2 pr", g2=2)[:, g2, :].partition_broadcast(64).rearrange("p o n -> p (o n)"),
                       B_s, writes=[B_s], allow_slow_non_contiguous=True)
                for ri, bd in enumerate((L.b_re, L.b_im)):
                    kb.dma("sp", bsb[g2 * 64:(g2 + 1) * 64, ri, :, :], bd.rearrange("(pr g2) p h -> g2 p pr h", g2=2)[g2],
                           B_s, writes=[B_s])
            for ri, cd in enumerate((L.c_re, L.c_im)):
                kb.dma("sp", Cn[:, ri, :, :], cd.rearrange("(ct gl) h p -> (gl h) ct p", ct=4), B_s, writes=[B_s])
            kb.dma("sp", par[:], L.tb["par"], B_s, writes=[B_s])
            kb.dma("sp", dsk[:], L.s5_d.rearrange("(ct q) o -> q (ct o)", ct=4), B_tab, writes=[B_tab], allow_slow_non_contiguous=True)
            op("pool", lambda e: e.memset(ones[:], 1.0), writes=[B_tab])
            op("pool", lambda e: e.memset(carry[:], 0.0), writes=[B_car])
            op("pool", lambda e: e.memset(Cq[:], 0.0), writes=[B_tab])
            R, W = [B_s], [B_s]
            dv = lambda f: op("dve", f, reads=R, writes=W)
            ac = lambda f: op("act", f, reads=R, writes=W)
            ac(lambda e: e.activation(out=dt[:], in_=ldt[:], func=AF.Exp))
            dv(lambda e: e.tensor_scalar(out=ar[:], in0=ar[:], scalar1=-1e-4, scalar2=None, op0=ALU.min))
            dv(lambda e: e.tensor_tensor(out=lrd[:], in0=ar[:], in1=dt[:], op=ALU.mult))
            dv(lambda e: e.tensor_tensor(out=ang[:], in0=ai[:], in1=dt[:], op=ALU.mult))
            ac(lambda e: e.activation(out=mag[:], in_=lrd[:], func=AF.Exp))
            ac(lambda e: e.activation(out=mgi[:], in_=lrd[:], func=AF.Exp, scale=-1.0))
            ti = sb("ti", [128, 16], mybir.dt.int32)

            def rred(dst, shift):
                dv(lambda e: e.tensor_scalar(out=tmp2[:], in0=ang[:], scalar1=shift, scalar2=None, op0=ALU.add))
                dv(lambda e: e.tensor_scalar(out=tmp[:], in0=tmp2[:], scalar1=1.0 / (2 * PI), scalar2=None, op0=ALU.mult))
                dv(lambda e: e.tensor_copy(out=ti[:], in_=tmp[:]))
                dv(lambda e: e.tensor_copy(out=tmp[:], in_=ti[:]))
                dv(lambda e: e.scalar_tensor_tensor(out=tmp2[:], in0=tmp[:], scalar=-2 * PI, in1=tmp2[:], op0=ALU.mult, op1=ALU.add))
                dv(lambda e: e.tensor_scalar(out=tmp[:], in0=tmp2[:], scalar1=PI, scalar2=2 * PI, op0=ALU.is_gt, op1=ALU.mult))
                dv(lambda e: e.tensor_tensor(out=tmp2[:], in0=tmp2[:], in1=tmp[:], op=ALU.subtract))
                dv(lambda e: e.tensor_scalar(out=tmp[:], in0=tmp2[:], scalar1=-PI, scalar2=2 * PI, op0=ALU.is_lt, op1=ALU.mult))
                dv(lambda e: e.tensor_tensor(out=tmp2[:], in0=tmp2[:], in1=tmp[:], op=ALU.add))
                ac(lambda e: e.activation(out=dst[:], in_=tmp2[:], func=AF.Sin))

            rred(sn, 0.0)
            rred(cs, 0.5 * PI)
            dv(lambda e: e.tensor_tensor(out=lb[:, 0, :], in0=mag[:], in1=cs[:], op=ALU.mult))
            dv(lambda e: e.tensor_tensor(out=lb[:, 1, :], in0=mag[:], in1=sn[:], op=ALU.mult))
            dv(lambda e: e.tensor_tensor(out=lbi[:, 0, :], in0=mgi[:], in1=cs[:], op=ALU.mult))
            dv(lambda e: e.scalar_tensor_tensor(out=lbi[:, 1, :], in0=mgi[:], scalar=-1.0, in1=sn[:], op0=ALU.mult, op1=ALU.mult))
            dv(lambda e: e.tensor_tensor(out=den[:], in0=ar[:], in1=ar[:], op=ALU.mult))
            dv(lambda e: e.tensor_tensor(out=tmp[:], in0=ai[:], in1=ai[:], op=ALU.mult))
            dv(lambda e: e.tensor_tensor(out=den[:], in0=den[:], in1=tmp[:], op=ALU.add))
            dv(lambda e: e.reciprocal(out=den[:], in_=den[:]))
            dv(lambda e: e.tensor_scalar(out=tmp2[:], in0=lb[:, 0, :], scalar1=-1.0, scalar2=None, op0=ALU.add))
            dv(lambda e: e.tensor_tensor(out=tmp[:], in0=tmp2[:], in1=ar[:], op=ALU.mult))
            dv(lambda e: e.tensor_tensor(out=coef[:, 0, :], in0=lb[:, 1, :], in1=ai[:], op=ALU.mult))
            dv(lambda e: e.tensor_tensor(out=coef[:, 0, :], in0=coef[:, 0, :], in1=tmp[:], op=ALU.add))
            dv(lambda e: e.tensor_tensor(out=coef[:, 0, :], in0=coef[:, 0, :], in1=den[:], op=ALU.mult))
            dv(lambda e: e.tensor_tensor(out=tmp[:], in0=tmp2[:], in1=ai[:], op=ALU.mult))
            dv(lambda e: e.tensor_tensor(out=coef[:, 1, :], in0=lb[:, 1, :], in1=ar[:], op=ALU.mult))
            dv(lambda e: e.tensor_tensor(out=coef[:, 1, :], in0=coef[:, 1, :], in1=tmp[:], op=ALU.subtract))
            dv(lambda e: e.tensor_tensor(out=coef[:, 1, :], in0=coef[:, 1, :], in1=den[:], op=ALU.mult))
            cbr = lambda k: coef[:, k, :].unsqueeze(2).to_broadcast([128, 16, 16])
            dv(lambda e: e.tensor_tensor(out=bb[:, 0], in0=bsb[:, 0], in1=cbr(0), op=ALU.mult))
            dv(lambda e: e.tensor_tensor(out=bt[:], in0=bsb[:, 1], in1=cbr(1), op=ALU.mult))
            dv(lambda e: e.tensor_tensor(out=bb[:, 0], in0=bb[:, 0], in1=bt[:], op=ALU.subtract))
            dv(lambda e: e.tensor_tensor(out=bb[:, 1], in0=bsb[:, 1], in1=cbr(0), op=ALU.mult))
            dv(lambda e: e.tensor_tensor(out=bt[:], in0=bsb[:, 0], in1=cbr(1), op=ALU.mult))
            dv(lambda e: e.tensor_tensor(out=bb[:, 1], in0=bb[:, 1], in1=bt[:], op=ALU.add))
            Apr2 = sb("Apr2", [128, 128], BF16)
            Aprs, B_aprs = [Apr, Apr2], [B_apr, buf("apr2")]
            ptv = [PS[0][:, 0:64].bitcast(BF16), PS[1][:, 0:64].bitcast(BF16)]
            it = 0
            for pr in range(16):
                prl = pr % 4
                for ri in range(2):
                    A_, B_A, pv, B_pv = Aprs[it % 2], B_aprs[it % 2], ptv[it % 2], PSB[it % 2]
                    it += 1
                    op("pool", lambda e: e.memset(A_[:], 0.0), writes=[B_A])
                    op("dve", lambda e: e.tensor_copy(out=A_[0:64, 32 * prl:32 * prl + 16], in_=bb[0:64, ri, pr, :]), reads=[B_s], writes=[B_A])
                    op("dve", lambda e: e.tensor_copy(out=A_[64:128, 32 * prl + 16:32 * prl + 32], in_=bb[64:128, ri, pr, :]),
                       reads=[B_s], writes=[B_A])
                    op("pe", lambda e: e.transpose(out=pv, in_=A_[:], identity=L.ident_b[:]), reads=[B_A, L.B_id], writes=[B_pv])
                    op("act", lambda e: e.activation(out=BBT[:, pr, ri, :], in_=pv, func=AF.Copy), reads=[B_pv], writes=[B_tab])
                    if ri == 1:
                        op("act", lambda e: e.activation(out=BBTn[:, pr, :], in_=pv, func=AF.Copy, scale=-1.0),
                           reads=[B_pv], writes=[B_tab])
            for ct in range(4):
                for ri in range(2):
                    for g2 in range(2):
                        op("dve", lambda e: e.tensor_scalar(out=Apr[:, g2 * 64:(g2 + 1) * 64], in0=Cn[:, ri, ct, :], scalar1=par[:, g2:g2 + 1],
                                                             scalar2=None, op0=ALU.mult), reads=[B_s], writes=[B_apr])
                    op("pe", lambda e: e.transpose(out=PT[:, 0:128], in_=Apr[:], identity=L.ident_b[:]), reads=[B_apr, L.B_id], writes=[PTB])
                    for prl in range(4):
                        pr = ct * 4 + prl
                        sl = slice(32 * prl, 32 * prl + 32)
                        if ri == 0:
                            op("dve", lambda e: e.tensor_copy(out=Cq[:, pr, 0, sl], in_=PT[:, sl]), reads=[PTB], writes=[B_tab])
                            op("dve", lambda e: e.tensor_scalar(out=Cq[:, pr, 3, sl], in0=PT[:, sl], scalar1=-1.0, scalar2=None, op0=ALU.mult),
                               reads=[PTB], writes=[B_tab])
                        else:
                            for k in (1, 2):
                                op("dve", lambda e: e.tensor_scalar(out=Cq[:, pr, k, sl], in0=PT[:, sl], scalar1=-1.0, scalar2=None,
                                                                     op0=ALU.mult), reads=[PTB], writes=[B_tab])
            for tabl, base in ((EF, lb), (EI, lbi)):
                op("pool", lambda e: e.memset(tabl[:, 0, :, 0:1], 1.0), writes=[B_tab])
                op("pool", lambda e: e.memset(tabl[:, 1, :, 0:1], 0.0), writes=[B_tab])
                op("dve", lambda e: e.tensor_copy(out=pw[:], in_=base[:]), reads=[B_s], writes=[B_s])
                for k in range(7):
                    n = 1 << k
                    pbr = lambda c: pw[:, c, :].unsqueeze(2).to_broadcast([128, 16, n])
                    RW = dict(reads=[B_s, B_tab, B_et], writes=[B_tab, B_et])
                    op("dve", lambda e: e.tensor_tensor(out=et1[:, :, 0:n], in0=tabl[:, 0, :, 0:n], in1=pbr(0), op=ALU.mult), **RW)
                    op("dve", lambda e: e.tensor_tensor(out=et2[:, :, 0:n], in0=tabl[:, 1, :, 0:n], in1=pbr(1), op=ALU.mult), **RW)
                    op("dve", lambda e: e.tensor_tensor(out=tabl[:, 0, :, n:2 * n], in0=et1[:, :, 0:n], in1=et2[:, :, 0:n], op=ALU.subtract), **RW)
                    op("dve", lambda e: e.tensor_tensor(out=et1[:, :, 0:n], in0=tabl[:, 0, :, 0:n], in1=pbr(1), op=ALU.mult), **RW)
                    op("dve", lambda e: e.tensor_tensor(out=et2[:, :, 0:n], in0=tabl[:, 1, :, 0:n], in1=pbr(0), op=ALU.mult), **RW)
                    op("dve", lambda e: e.tensor_tensor(out=tabl[:, 1, :, n:2 * n], in0=et1[:, :, 0:n], in1=et2[:, :, 0:n], op=ALU.add), **RW)
                    op("dve", lambda e: e.tensor_tensor(out=tmp[:], in0=pw[:, 0, :], in1=pw[:, 0, :], op=ALU.mult), **RW)
                    op("dve", lambda e: e.tensor_tensor(out=tmp2[:], in0=pw[:, 1, :], in1=pw[:, 1, :], op=ALU.mult), **RW)
                    op("dve", lambda e: e.tensor_tensor(out=pw[:, 1, :], in0=pw[:, 0, :], in1=pw[:, 1, :], op=ALU.mult), **RW)
                    op("dve", lambda e: e.tensor_scalar(out=pw[:, 1, :], in0=pw[:, 1, :], scalar1=2.0, scalar2=None, op0=ALU.mult), **RW)
                    op("dve", lambda e: e.tensor_tensor(out=pw[:, 0, :], in0=tmp[:], in1=tmp2[:], op=ALU.subtract), **RW)
                if tabl is EF:
                    op("dve", lambda e: e.tensor_copy(out=L128[:], in_=pw[:]), reads=[B_s], writes=[B_tab])
            kb.barrier()
        kb.es = es5
        tA = [sb(f"tA{i}", [128, 4, 128], F32) for i in range(2)]
        Wt = [sb(f"Wt{i}", [128, 2, 128], F32) for i in range(2)]
        Z = [sb(f"Z{i}", [128, 2, 128], F32) for i in range(2)]
        Qp = [sb(f"Qp{i}", [128, 4, 128], BF16) for i in range(2)]
        ys = [sb(f"ys{i}", [128, 128], F32) for i in range(2)]
        yt = [sb(f"yt{i}", [128, 128], F32) for i in range(2)]
        sg = [sb(f"sg{i}", [128, 128], F32) for i in range(2)]
        ctmp = sb("ctmp", [128, 2, 16], F32)
        wacc = [sb(f"wacc{i}", [128, 2], F32) for i in range(2)]
        B_wacc = [buf("wacc0"), buf("wacc1")]
        B_tA, B_W, B_Z, B_Qp = [[buf(f"{n}{i}") for i in range(2)] for n in ("tA", "Wt", "Z", "Qp")]
        B_ys = [buf("ys0"), buf("ys1")]
        def stage_a_pe(c, pr):
            ct, pb = pr // 4, pr % 2
            csl = slice(c * 128, (c + 1) * 128)
            lts = (BBT[:, pr, 0, :], BBT[:, pr, 1, :], BBTn[:, pr, :], BBT[:, pr, 0, :])
            for q4, lt in enumerate(lts):
                op("pe", lambda e: e.matmul(PS[pb][:, q4 * 128:(q4 + 1) * 128], lhsT=lt, rhs=L.uT[:, ct, csl],
                                            start=True, stop=True), reads=[B_tab, L.B_uT[c]], writes=[PSB[pb]])

        def stage_a(c, pr):
            ct, pb = pr // 4, pr % 2
            bu = PS[pb][:, :].rearrange("p (k r b) -> p k r b", k=2, r=2)
            if c < Q0:
                op("dve", lambda e: e.tensor_tensor(out=tA[pb][:].rearrange("p (r k) b -> p k r b", r=2), in0=bu,
                                                     in1=EI[:, :, pr, :].unsqueeze(2).to_broadcast([128, 2, 2, 128]), op=ALU.mult),
                   reads=[PSB[pb], B_tab], writes=[B_tA[pb]])
                return
            op("dve", lambda e: e.tensor_tensor(out=tA[pb][:].rearrange("p (k r) b -> p k r b", k=2), in0=bu,
                                                 in1=EI[:, :, pr, :].unsqueeze(2).to_broadcast([128, 2, 2, 128]), op=ALU.mult),
               reads=[PSB[pb], B_tab], writes=[B_tA[pb]])
            op("pool", lambda e: e.tensor_tensor(out=Wt[pb][:], in0=tA[pb][:, 0:2, :], in1=tA[pb][:, 2:4, :], op=ALU.add),
               reads=[B_tA[pb]], writes=[B_W[pb]])

        def stage_b(c, pr):
            ct, prl, pb = pr // 4, pr % 4, pr % 2
            csl = slice(c * 128, (c + 1) * 128)
            for ri in (range(2) if c < Q0 else ()):
                op("act", lambda e: e.activation(out=tA[pb][:, 2 * ri:2 * ri + 2, :], in_=tA[pb][:, 2 * ri:2 * ri + 2, :], func=AF.Copy,
                                                 accum_out=wacc[pb][:, ri:ri + 1]), reads=[B_tA[pb]], writes=[B_tA[pb], B_wacc[pb]])
            if c < Q0:
                op("pool", lambda e: e.tensor_tensor(out=zl[:, :, pr:pr + 1], in0=wacc[pb][:, :].unsqueeze(2), in1=carry[:, :, pr:pr + 1], op=ALU.add),
                   reads=[B_wacc[pb], B_car], writes=[B_zl])
            for ri in (range(2) if c >= Q0 else ()):
                op("dve", lambda e: e.tensor_tensor_scan(out=Z[pb][:, ri, :], data0=ones[:], data1=Wt[pb][:, ri, :],
                                                         initial=carry[:, ri, pr:pr + 1], op0=ALU.mult, op1=ALU.add),
                   reads=[B_W[pb], B_tab, B_car], writes=[B_Z[pb]])
            if c >= Q0:
                op("pool", lambda e: e.tensor_copy(out=zl[:, :, pr:pr + 1], in_=Z[pb][:, :, 127:128]), reads=[B_Z[pb]], writes=[B_zl])
                q = Qp[pb]
                op("dve", lambda e: e.tensor_tensor(out=q[:].rearrange("p (k r) b -> p k r b", k=2),
                                                     in0=Z[pb][:].unsqueeze(1).to_broadcast([128, 2, 2, 128]),
                                                     in1=EF[:, :, pr, :].unsqueeze(2).to_broadcast([128, 2, 2, 128]), op=ALU.mult),
                   reads=[B_Z[pb], B_tab], writes=[B_Qp[pb]])
                yb = 2 + ct % 2
                for k in range(4):
                    op("pe", lambda e: e.matmul(PS[yb][:, 0:128], lhsT=Cq[:, pr, k, :], rhs=q[:, k, :],
                                                start=(prl == 0 and k == 0), stop=(prl == 3 and k == 3)),
                       reads=[B_tab, B_Qp[pb]], writes=[PSB[yb]])
                if prl == 3:
                    cb2 = ct % 2
                    op("dve", lambda e: e.scalar_tensor_tensor(out=ys[cb2][:], in0=L.uT[:, ct, csl], scalar=dsk[:, ct:ct + 1], in1=PS[yb][:, 0:128],
                                                                op0=ALU.mult, op1=ALU.add), reads=[PSB[yb], L.B_uT[c], B_tab], writes=[B_ys[cb2]])
                    op("pool", lambda e: e.tensor_tensor(out=yt[cb2][:], in0=ys[cb2][:], in1=ys[cb2][:], op=ALU.mult),
                       reads=[B_ys[cb2]], writes=[B_ys[cb2]])
                    op("pool", lambda e: e.tensor_scalar(out=yt[cb2][:], in0=yt[cb2][:], scalar1=0.044715, scalar2=1.0, op0=ALU.mult, op1=ALU.add),
                       reads=[B_ys[cb2]], writes=[B_ys[cb2]])
                    op("pool", lambda e: e.tensor_tensor(out=yt[cb2][:], in0=yt[cb2][:], in1=ys[cb2][:], op=ALU.mult),
                       reads=[B_ys[cb2]], writes=[B_ys[cb2]])
                    op("act", lambda e: e.activation(out=sg[cb2][:], in_=yt[cb2][:], func=AF.Sigmoid, scale=1.5957691216057308),
                       reads=[B_ys[cb2]], writes=[B_ys[cb2]])
                    op("pool", lambda e: e.tensor_tensor(out=L.uT[:, ct, csl], in0=ys[cb2][:], in1=sg[cb2][:], op=ALU.mult),
                       reads=[B_ys[cb2]], writes=[L.B_uT[c]])
            if pr == 15:
                RWc = dict(reads=[B_zl, B_tab, B_car], writes=[B_car])
                op("dve", lambda e: e.tensor_tensor(out=ctmp[:, 0, :], in0=L128[:, 0, :], in1=zl[:, 0, :], op=ALU.mult), **RWc)
                op("dve", lambda e: e.tensor_tensor(out=ctmp[:, 1, :], in0=L128[:, 1, :], in1=zl[:, 1, :], op=ALU.mult), **RWc)
                op("dve", lambda e: e.tensor_tensor(out=carry[:, 0, :], in0=ctmp[:, 0, :], in1=ctmp[:, 1, :], op=ALU.subtract), **RWc)
                op("dve", lambda e: e.tensor_tensor(out=ctmp[:, 0, :], in0=L128[:, 0, :], in1=zl[:, 1, :], op=ALU.mult), **RWc)
                op("dve", lambda e: e.tensor_tensor(out=ctmp[:, 1, :], in0=L128[:, 1, :], in1=zl[:, 0, :], op=ALU.mult), **RWc)
                op("dve", lambda e: e.tensor_tensor(out=carry[:, 1, :], in0=ctmp[:, 0, :], in1=ctmp[:, 1, :], op=ALU.add), **RWc)

        stgp = sb("stgp", [128, 8, 256], F32)
        B_stgp = buf("stgp")

        def pre_step(dst_fn, src_ap, rows_kc):
            kb.dma("sp", stgp[:, 0:rows_kc, :], src_ap.rearrange("(kc p) n -> p kc n", p=128), B_stgp, writes=[B_stgp])
            for kc in range(rows_kc):
                op("act", lambda e: e.activation(out=dst_fn(kc), in_=stgp[:, kc, :], func=AF.Copy), reads=[B_stgp], writes=[L.B_Wpre])

        pre_steps = []
        for q in range(8):
            pre_steps.append((lambda kc, q=q: L.WMp[:, kc, q * 256:(q + 1) * 256], L.w_in[:, 1816 + q * 256:1816 + (q + 1) * 256], 8))
        for q in range(8):
            pre_steps.append((lambda kc, q=q: L.WGLp[:, kc, q * 256:(q + 1) * 256], L.w_glu[:, q * 256:(q + 1) * 256], 4))
        for q in range(4):
            pre_steps.append((lambda kc, q=q: L.WOp[:, kc, q * 256:(q + 1) * 256], L.w_o[:, q * 256:(q + 1) * 256], 8))
        seq = [(c, pr) for c in range(NT) for pr in range(16)]
        stage_a_pe(*seq[0])
        stage_a_pe(*seq[1])
        stage_a(*seq[0])
        for k in range(len(seq)):
            if seq[k][1] in (0, 8) and seq[k][0] >= Q0 + 1 and pre_steps:
                pre_step(*pre_steps.pop(0))
            if k + 2 < len(seq):
                stage_a_pe(*seq[k + 2])
            if k + 1 < len(seq):
                stage_a(*seq[k + 1])
            stage_b(*seq[k])
        while pre_steps:
            pre_step(*pre_steps.pop(0))
        if "gyT" in L.dbg_d:
            b = kb.buf("dbg")
            st2 = sb("dbgst5", [128, 4, 512], F32)
            op("dve", lambda e: e.tensor_copy(out=st2[:], in_=L.uT[:, :, 0:512]), reads=L.B_uT, writes=[b])
            kb.dma("sp", L.dbg_d["gyT"], st2[:], b, reads=[b])
        kb.barrier()
    kb.es = L.esA


def ln_affine_store(kb, L, pre, B_pre, stats, mv, rstd, B_st, gb, B_gb, dst_ap, q="sp"):
    op = kb.op
    for hf in range(2):
        op("dve", lambda e: e.bn_stats(out=stats[:, hf, :], in_=pre[:, hf * 512:(hf + 1) * 512]), reads=[B_pre], writes=[B_st])
    op("dve", lambda e: e.bn_aggr(out=mv[:], in_=stats[:]), reads=[B_st], writes=[B_st])
    op("dve", lambda e: e.tensor_scalar(out=rstd[:], in0=mv[:, 1:2], scalar1=1e-5, scalar2=None, op0=ALU.add),
       reads=[B_st], writes=[B_st])
    op("act", lambda e: e.activation(out=rstd[:], in_=rstd[:], func=AF.Sqrt), reads=[B_st], writes=[B_st])
    op("dve", lambda e: e.reciprocal(out=rstd[:], in_=rstd[:]), reads=[B_st], writes=[B_st])
    op("dve", lambda e: e.tensor_scalar(out=pre[:], in0=pre[:], scalar1=mv[:, 0:1], scalar2=rstd[:, 0:1], op0=ALU.subtract, op1=ALU.mult),
       reads=[B_st], writes=[B_pre])
    op("pool", lambda e: e.tensor_tensor(out=pre[:], in0=pre[:], in1=gb[:, 0, :], op=ALU.mult), reads=[B_gb], writes=[B_pre])
    op("pool", lambda e: e.tensor_tensor(out=pre[:], in0=pre[:], in1=gb[:, 1, :], op=ALU.add), reads=[B_gb], writes=[B_pre])
    kb.dma(q, dst_ap, pre[:], B_pre, reads=[B_pre])


def pass_a2(nc, kb, Ld):
    L = NS(Ld)
    op = kb.op
    PS, PSB, PT, PTB = L.PS, L.PSB, L.PT, L.PTB
    sb, buf = kb.sb, kb.buf
    with ExitStack() as es2:
        kb.es = es2
        WM, WGL, WO = L.WMp, L.WGLp, L.WOp
        WNO = sb("WNO", [128, 4, D], BF16)
        gb = sb("gb1", [128, 2, D], F32)
        B_W, B_gb = L.B_Wpre, buf("gb1")
        with ExitStack() as ess:
            kb.es = ess
            stgs = [sb(f"stg2{i}", [128, 8, 512], F32) for i in range(2)]
            B_stgs = [buf(f"stg2{i}") for i in range(2)]
            for q2 in range(2):
                L.load_cast(lambda kc: WNO[:, kc, q2 * 512:(q2 + 1) * 512], L.w_nsa_out[:, q2 * 512:(q2 + 1) * 512], 4, 512,
                            stgs[q2], B_stgs[q2], B_W)
            kb.dma("sp", gb[:, 0, :], L.ln1_g.partition_broadcast(128).rearrange("p o n -> p (o n)"), B_gb, writes=[B_gb])
            kb.dma("sp", gb[:, 1, :], L.ln1_b.partition_broadcast(128).rearrange("p o n -> p (o n)"), B_gb, writes=[B_gb])
            kb.barrier()
        kb.es = es2
        xts = [[sb(f"xt2{k}{i}", [128, D], F32) for i in range(4)] for k in range(2)]
        B_xts = [[buf(f"xt2{k}{i}") for i in range(4)] for k in range(2)]
        xn = sb("xn2", [128, D], BF16); B_xn = buf("xn2")
        stats = sb("stats2", [128, 2, 6], F32); mv = sb("mv2", [128, 2], F32); rstd = sb("rstd2", [128, 1], F32)
        B_st = buf("st2")
        hTs2 = [sb(f"hT2{k}", [128, 8, 512], BF16) for k in range(2)]; B_hTs2 = [buf("hT20"), buf("hT21")]
        sga = sb("sga", [128, 3, 512], F32); B_sg = [buf("sga0"), buf("sga1")]
        t1 = sb("t1", [128, 512], F32); t2 = sb("t2", [128, 512], F32); B_t = buf("t12")
        mixT = sb("mixT", [128, 8, 512], BF16); B_mix = buf("mixT")
        gtmp, B_gt = [t1, t2], [B_t, B_t]
        steps = [[0]] + [list(range(r, r + 4)) for r in range(1, NQ, 4)]
        modT, ident_b = L.modT, L.ident_b

        def ln_a2(x_, B_x):
            for hf in range(2):
                op("dve", lambda e: e.bn_stats(out=stats[:, hf, :], in_=x_[:, hf * 512:(hf + 1) * 512]), reads=[B_x], writes=[B_st])
            op("dve", lambda e: e.bn_aggr(out=mv[:], in_=stats[:]), reads=[B_st], writes=[B_st])
            op("dve", lambda e: e.tensor_scalar(out=rstd[:], in0=mv[:, 1:2], scalar1=1e-5, scalar2=None, op0=ALU.add), reads=[B_st], writes=[B_st])
            op("act", lambda e: e.activation(out=rstd[:], in_=rstd[:], func=AF.Sqrt), reads=[B_st], writes=[B_st])
            op("dve", lambda e: e.reciprocal(out=rstd[:], in_=rstd[:]), reads=[B_st], writes=[B_st])
            op("dve", lambda e: e.tensor_scalar(out=xn[:], in0=x_[:], scalar1=mv[:, 0:1], scalar2=rstd[:, 0:1], op0=ALU.subtract, op1=ALU.mult),
               reads=[B_x, B_st], writes=[B_xn])

        def ln_b2(dst, B_dst):
            for kc in range(KC):
                op("pe", lambda e: e.transpose(out=PT[:, kc * 128:(kc + 1) * 128], in_=xn[:, kc * 128:(kc + 1) * 128], identity=ident_b[:]),
                   reads=[B_xn, L.B_id], writes=[PTB])
            for kc in range(KC):
                op("act", lambda e: e.activation(out=dst[:, kc, :], in_=PT[:, kc * 128:(kc + 1) * 128], func=AF.Identity,
                                                 scale=modT[:, 8 + kc:9 + kc], bias=modT[:, kc:kc + 1]), reads=[PTB, L.B_modT], writes=[B_dst])

        def load_step(si):
            for tt, r in enumerate(steps[si]):
                i = Q0 + r
                kb.dma("sp", xts[si % 2][tt][:], L.x_d[i * 128:(i + 1) * 128, :], B_xts[si % 2][tt], writes=[B_xts[si % 2][tt]])

        load_step(0)
        for tt in range(len(steps[0])):
            ln_a2(xts[0][tt], B_xts[0][tt])
            ln_b2(hTs2[0][:, :, tt * 128:(tt + 1) * 128], B_hTs2[0])
        for si, rs in enumerate(steps):
            NTOK = 128 * len(rs)
            r0 = rs[0]
            xt, B_xt = xts[si % 2], B_xts[si % 2]
            hT, B_hT = hTs2[si % 2], B_hTs2[si % 2]
            nxt = steps[si + 1] if si + 1 < len(steps) else []
            if nxt:
                load_step(si + 1)
            osl = slice(r0 * 128, r0 * 128 + NTOK)
            tsl = slice((Q0 + r0) * 128, (Q0 + r0) * 128 + NTOK)
            B_us = [L.B_uT[Q0 + r] for r in rs]
            B_os = [L.B_oaT[Q0 + r] for r in rs]
            for fc in range(8):
                fsl = slice(fc * 128, (fc + 1) * 128)
                for half in range(2):
                    for kc in range(KC):
                        op("pe", lambda e: e.matmul(PS[half][:, 0:NTOK], lhsT=WM[:, kc, half * D + fc * 128:half * D + (fc + 1) * 128],
                                                    rhs=hT[:, kc, 0:NTOK], start=(kc == 0), stop=(kc == KC - 1)), reads=[B_W, B_hT], writes=[PSB[half]])
                    op("act", lambda e: e.activation(out=sga[:, half, 0:NTOK], in_=PS[half][:, 0:NTOK], func=AF.Sigmoid),
                       reads=[PSB[half]], writes=[B_sg[0]])
                for c in range(4):
                    op("pe", lambda e: e.matmul(PS[2][:, 0:NTOK], lhsT=WNO[:, c, fsl], rhs=L.oaT[:, c, osl], start=(c == 0), stop=(c == 3)),
                       reads=[B_W] + B_os, writes=[PSB[2]])
                op("dve", lambda e: e.tensor_tensor(out=t1[:, 0:NTOK], in0=sga[:, 0, 0:NTOK], in1=PS[2][:, 0:NTOK], op=ALU.mult),
                   reads=[B_sg[0], PSB[2]], writes=[B_t])
                for half in range(2):
                    for c in range(4):
                        op("pe", lambda e: e.matmul(PS[3 + half][:, 0:NTOK], lhsT=WGL[:, c, half * D + fc * 128:half * D + (fc + 1) * 128],
                                                    rhs=L.uT[:, c, tsl], start=(c == 0), stop=(c == 3)), reads=[B_W] + B_us, writes=[PSB[3 + half]])
                op("act", lambda e: e.activation(out=sga[:, 2, 0:NTOK], in_=PS[4][:, 0:NTOK], func=AF.Sigmoid), reads=[PSB[4]], writes=[B_sg[1]])
                op("dve", lambda e: e.tensor_tensor(out=t2[:, 0:NTOK], in0=sga[:, 2, 0:NTOK], in1=PS[3][:, 0:NTOK], op=ALU.mult),
                   reads=[B_sg[1], PSB[3]], writes=[B_t])
                op("dve", lambda e: e.tensor_tensor(out=t2[:, 0:NTOK], in0=t2[:, 0:NTOK], in1=sga[:, 1, 0:NTOK], op=ALU.mult),
                   reads=[B_sg[0]], writes=[B_t])
                op("dve", lambda e: e.tensor_tensor(out=mixT[:, fc, 0:NTOK], in0=t1[:, 0:NTOK], in1=t2[:, 0:NTOK], op=ALU.add),
                   reads=[B_t], writes=[B_mix])
                tn = fc // 2
                if tn < len(nxt):
                    if fc % 2 == 0:
                        ln_a2(xts[(si + 1) % 2][tn], B_xts[(si + 1) % 2][tn])
                    else:
                        ln_b2(hTs2[(si + 1) % 2][:, :, tn * 128:(tn + 1) * 128], B_hTs2[(si + 1) % 2])
            for tt, r in enumerate(rs):
                pr_, B_p = xt[tt], B_xt[tt]
                for half in range(2):
                    for fc in range(8):
                        op("pe", lambda e: e.matmul(PS[5 + half][:, :], lhsT=mixT[:, fc, tt * 128:(tt + 1) * 128], rhs=WO[:, fc, half * 512:(half + 1) * 512],
                                                    start=(fc == 0), stop=(fc == 7)), reads=[B_W, B_mix], writes=[PSB[5 + half]])
                    hs = slice(half * 512, (half + 1) * 512)
                    op("dve", lambda e: e.tensor_tensor(out=gtmp[half][:], in0=PS[5 + half][:, :], in1=L.gates_bc[:, 0, hs], op=ALU.mult),
                       reads=[PSB[5 + half], L.B_gbc], writes=[B_gt[half]])
                    op("dve", lambda e: e.scalar_tensor_tensor(out=pr_[:, hs], in0=pr_[:, hs], scalar=ALPHA, in1=gtmp[half][:], op0=ALU.mult, op1=ALU.add),
                       reads=[B_gt[half]], writes=[B_p])
                ln_affine_store(kb, L, pr_, B_p, stats, mv, rstd, B_st, gb, B_gb, L.x1_d[r * 128:(r + 1) * 128, :])
        kb.barrier()
    kb.es = L.esA


def pass_b(nc, kb, Ld):
    L = NS(Ld)
    op = kb.op
    PS, PSB, PT, PTB = L.PS, L.PSB, L.PT, L.PTB
    sb, buf = kb.sb, kb.buf
    NJ = 44
    WUP = sb("WUP", [128, 8, 2 * DFF], BF16)
    WDN = sb("WDN", [128, 22, D], BF16)
    cw = sb("cwt", [128, NJ, 3], F32)
    cb = sb("cbt", [128, NJ], F32)
    gb = sb("gb2", [128, 2, D], F32)
    halo = sb("halo", [128, NJ, 2], F32)
    B_W, B_gb, B_cw, B_halo = buf("WB"), buf("gb2"), buf("cw"), buf("halo")
    with ExitStack() as ess:
        kb.es = ess
        stgs = [sb(f"stgb{i}", [128, 8, 512], F32) for i in range(3)]
        B_stgs = [buf(f"stgb{i}") for i in range(3)]
        for q in range(11):
            L.load_cast(lambda kc: WUP[:, kc, q * 512:(q + 1) * 512], L.w_up[:, q * 512:(q + 1) * 512], 8, 512, stgs[q % 3], B_stgs[q % 3], B_W)
        for half in range(2):
            for q in range(3):
                stg, B_stg = stgs[(half * 3 + q + 2) % 3], B_stgs[(half * 3 + q + 2) % 3]
                r0, nr = q * 8, (8 if q < 2 else 6)
                kb.dma("sp", stg[:, 0:nr, :], L.w_down[r0 * 128:(r0 + nr) * 128, half * 512:(half + 1) * 512].rearrange("(kc p) n -> p kc n", p=128),
                       B_stg, writes=[B_stg])
                for kc in range(nr):
                    op("dve", lambda e: e.tensor_copy(out=WDN[:, r0 + kc, half * 512:(half + 1) * 512], in_=stg[:, kc, :]),
                       reads=[B_stg], writes=[B_W])
        for k3 in range(3):
            kb.dma("sp", cw[:, :, k3], L.conv_w[k3:k3 + 1, :].rearrange("o (j p) -> p (o j)", p=128), B_cw, writes=[B_cw],
                   allow_slow_non_contiguous=True)
        kb.dma("sp", cb[:], L.conv_b.rearrange("o (j p) -> p (o j)", p=128), B_cw, writes=[B_cw], allow_slow_non_contiguous=True)
        kb.dma("sp", gb[:, 0, :], L.ln2_g.partition_broadcast(128).rearrange("p o n -> p (o n)"), B_gb, writes=[B_gb])
        kb.dma("sp", gb[:, 1, :], L.ln2_b.partition_broadcast(128).rearrange("p o n -> p (o n)"), B_gb, writes=[B_gb])
        op("pool", lambda e: e.memset(halo[:], 0.0), writes=[B_halo])
        kb.barrier()
    kb.es = L.esB
    NTOK = 512
    x1t = [sb(f"x1t{i}", [128, D], F32) for i in range(4)]
    B_x1 = [buf(f"x1t{i}") for i in range(4)]
    xn = sb("xnb", [128, D], BF16); B_xn = buf("xnb")
    stats = sb("statsb", [128, 2, 6], F32); mv = sb("mvb", [128, 2], F32); rstd = sb("rstdb", [128, 1], F32)
    B_st = buf("stb")
    h2T = sb("h2T", [128, 8, NTOK], BF16); B_h2 = buf("h2T")
    upb = [sb(f"upb{i}", [128, NTOK + 2], F32) for i in range(2)]
    acc = [sb(f"acc{i}", [128, NTOK], F32) for i in range(2)]
    B_up = [buf("up0"), buf("up1")]
    B_acc = [buf("acc0"), buf("acc1")]
    ffT = sb("ffT", [128, 22, NTOK], BF16); B_ff = buf("ffT")
    gtmp, B_gt = acc, B_acc
    steps = [[0]] + [list(range(r, r + 4)) for r in range(1, NQ, 4)]
    for s, rs in enumerate(steps):
        NTOK = 128 * len(rs)
        for tt, i in enumerate(rs):
            kb.dma("sp", x1t[tt][:], L.x1_d[i * 128:(i + 1) * 128, :], B_x1[tt], writes=[B_x1[tt]])
            L.ln_to_hT(x1t[tt], B_x1[tt], xn, B_xn, stats, mv, rstd, B_st, h2T[:, :, tt * 128:(tt + 1) * 128], B_h2, 16)
        for j in range(22):
            for w, ch in enumerate((j, j + 22)):
                pbk = (2 * j + w) % 4
                for kc in range(KC):
                    op("pe", lambda e: e.matmul(PS[pbk][:, 0:NTOK], lhsT=WUP[:, kc, ch * 128:(ch + 1) * 128], rhs=h2T[:, kc, 0:NTOK],
                                                start=(kc == 0), stop=(kc == KC - 1)), reads=[B_W, B_h2], writes=[PSB[pbk]])
                u_, B_u, a_, B_a = upb[w], B_up[w], acc[w], B_acc[w]
                op("pool", lambda e: e.tensor_copy(out=u_[:, 0:2], in_=halo[:, ch, :]), reads=[B_halo], writes=[B_u])
                op("act", lambda e: e.activation(out=u_[:, 2:NTOK + 2], in_=PS[pbk][:, 0:NTOK], func=AF.Copy), reads=[PSB[pbk]], writes=[B_u])
                op("pool", lambda e: e.tensor_copy(out=halo[:, ch, :], in_=u_[:, NTOK:NTOK + 2]), reads=[B_u], writes=[B_halo])
                op("act", lambda e: e.activation(out=a_[:, 0:NTOK], in_=u_[:, 2:NTOK + 2], func=AF.Identity, scale=cw[:, ch, 2:3], bias=cb[:, ch:ch + 1]),
                   reads=[B_u, B_cw], writes=[B_a])
                op("dve", lambda e: e.scalar_tensor_tensor(out=a_[:, 0:NTOK], in0=u_[:, 1:NTOK + 1], scalar=cw[:, ch, 1:2], in1=a_[:, 0:NTOK],
                                                            op0=ALU.mult, op1=ALU.add), reads=[B_u, B_cw], writes=[B_a])
                op("dve", lambda e: e.scalar_tensor_tensor(out=a_[:, 0:NTOK], in0=u_[:, 0:NTOK], scalar=cw[:, ch, 0:1], in1=a_[:, 0:NTOK],
                                                            op0=ALU.mult, op1=ALU.add), reads=[B_u, B_cw], writes=[B_a])
            op("act", lambda e: e.activation(out=upb[1][:, 0:NTOK], in_=acc[1][:, 0:NTOK], func=AF.Silu), reads=[B_acc[1]], writes=[B_up[1]])
            op("dve", lambda e: e.tensor_tensor(out=ffT[:, j, 0:NTOK], in0=upb[1][:, 0:NTOK], in1=acc[0][:, 0:NTOK], op=ALU.mult),
               reads=[B_up[1], B_acc[0]], writes=[B_ff])
        if s == 0:
            op("pool", lambda e: e.tensor_scalar(out=halo[:].rearrange("p a b -> p (a b)"), in0=halo[:].rearrange("p a b -> p (a b)"),
                                                  scalar1=L.hv[:, 0:1], scalar2=None, op0=ALU.mult), reads=[L.B_tv], writes=[B_halo])
        for tt, i in enumerate(rs):
            pr_, B_p = x1t[tt], B_x1[tt]
            for half in range(2):
                bk = ((4, 5), (6, 0))[tt % 2][half]
                for j in range(22):
                    op("pe", lambda e: e.matmul(PS[bk][:, :], lhsT=ffT[:, j, tt * 128:(tt + 1) * 128], rhs=WDN[:, j, half * 512:(half + 1) * 512],
                                                start=(j == 0), stop=(j == 21)), reads=[B_W, B_ff], writes=[PSB[bk]])
                hs = slice(half * 512, (half + 1) * 512)
                op("dve", lambda e: e.tensor_tensor(out=gtmp[half][:], in0=PS[bk][:, :], in1=L.gates_bc[:, 1, hs], op=ALU.mult),
                   reads=[PSB[bk], L.B_gbc], writes=[B_gt[half]])
                op("pool", lambda e: e.scalar_tensor_tensor(out=pr_[:, hs], in0=pr_[:, hs], scalar=ALPHA, in1=gtmp[half][:], op0=ALU.mult, op1=ALU.add),
                   reads=[B_gt[half]], writes=[B_p]) if False else \
                op("dve", lambda e: e.scalar_tensor_tensor(out=pr_[:, hs], in0=pr_[:, hs], scalar=ALPHA, in1=gtmp[half][:], op0=ALU.mult, op1=ALU.add),
                   reads=[B_gt[half]], writes=[B_p])
            if i >= 1:
                ln_affine_store(kb, L, pr_, B_p, stats, mv, rstd, B_st, gb, B_gb, L.out_d[(i - 1) * 128:i * 128, :])


_NC = [None]


def kernel(**inputs):
    f = lambda a: np.ascontiguousarray(np.asarray(a, dtype=np.float32))
    if _NC[0] is None:
        _NC[0] = build()
    nc = _NC[0]
    tabs = [make_tables(0), make_tables(1)]
    shared = {
        "w_ada": f(inputs["w_ada"][0]), "b_ada": f(inputs["b_ada"][0]).reshape(1, -1), "w_in": f(inputs["w_in"][0]),
        "pe_ck": f(inputs["pe_ck"][0]), "w_ck1": f(inputs["w_ck1"][0]), "w_ck2": f(inputs["w_ck2"][0]),
        "pe_cv": f(inputs["pe_cv"][0]), "w_cv1": f(inputs["w_cv1"][0]), "w_cv2": f(inputs["w_cv2"][0]),
        "w_nsa_out": f(inputs["w_nsa_out"][0]),
        "s5_a_re": f(inputs["s5_a_re"][0]), "s5_a_im": f(inputs["s5_a_im"][0]),
        "s5_b_re": f(inputs["s5_b_re"][0]), "s5_b_im": f(inputs["s5_b_im"][0]),
        "s5_c_re": f(inputs["s5_c_re"][0]), "s5_c_im": f(inputs["s5_c_im"][0]),
        "s5_d": f(inputs["s5_d"][0]).reshape(512, 1), "s5_log_dt": f(inputs["s5_log_dt"][0]).reshape(1, 32),
        "w_s5_glu": f(inputs["w_s5_glu"][0]), "w_o": f(inputs["w_o"][0]),
        "ln1_g": f(inputs["ln1_g"][0]).reshape(1, -1), "ln1_b": f(inputs["ln1_b"][0]).reshape(1, -1),
        "w_up": f(inputs["w_up"][0]), "conv_w": f(inputs["conv_w"][0]), "conv_b": f(inputs["conv_b"][0]).reshape(1, -1),
        "w_down": f(inputs["w_down"][0]),
        "ln2_g": f(inputs["ln2_g"][0]).reshape(1, -1), "ln2_b": f(inputs["ln2_b"][0]).reshape(1, -1),
    }
    tabf = [{"tb_" + k: f(v) for k, v in tabs[hf].items()} for hf in range(2)]
    x = np.asarray(inputs["x"], dtype=np.float32)
    c = np.asarray(inputs["c"], dtype=np.float32)
    in_maps = []
    for core in range(8):
        b, hf = core // 2, core % 2
        m = dict(shared)
        m.update(tabf[hf])
        if hf == 1:
            m["x"] = f(x[b])
        else:
            xp = np.zeros((T, D), np.float32)
            xp[T // 2:] = x[b][:T // 2]
            m["x"] = xp
        m["cvec"] = f(c[b].reshape(8, 128).T)
        in_maps.append(m)
    res = run_bass_kernel_spmd(nc, in_maps, core_ids=list(range(8)))
    kernel.last = res
    out = np.empty((4, T, D), np.float32)
    for core in range(8):
        b, hf = core // 2, core % 2
        out[b, hf * (T // 2):(hf + 1) * (T // 2)] = np.asarray(res.results[core]["out"], dtype=np.float32)
    return out
```

```python
from contextlib import ExitStack
import math
import numpy as np
import concourse.bass as bass
import concourse.mybir as mybir
from concourse.bass_utils import run_bass_kernel_spmd

F32 = mybir.dt.float32
BF16 = mybir.dt.bfloat16
AF = mybir.ActivationFunctionType
ALU = mybir.AluOpType

T = 4096
NT = 32
D = 1024
KC = 8
DFF = 2816
NEG = -30000.0
ALPHA = 2.0 ** 0.25
STOP = [99]
Q0 = 15
NQ = NT - Q0
DBG = {}


class Buf:
    __slots__ = ("name", "w", "r", "dsem", "dcnt")

    def __init__(self, name):
        self.name = name
        self.w = None
        self.r = {}
        self.dsem = None
        self.dcnt = 0


class KB:
    def __init__(self, nc, es):
        self.nc = nc
        self.ges = es
        self.es = es
        self.eng = {"pe": nc.tensor, "act": nc.scalar, "dve": nc.vector,
                    "pool": nc.gpsimd, "sp": nc.sync}
        self.sem = {}
        self.cnt = {}
        self.seen = {}
        for e in self.eng:
            self.sem[e] = es.enter_context(nc.semaphore("s_" + e))
            self.cnt[e] = 0
            self.seen[e] = {}
        self.pool_sems = []
        self.live = []
        self.n = 0
        self.uid = 0

    def sb(self, name, shape, dt):
        self.uid += 1
        return self.es.enter_context(self.nc.sbuf_tensor(f"{name}_{self.uid}", list(shape), dt))

    def ps(self, name, shape, dt):
        return self.ges.enter_context(self.nc.psum_tensor(name, list(shape), dt))

    def buf(self, name="b"):
        return Buf(name)

    def _wait(self, e, ev):
        if ev is None:
            return
        sem, val = ev
        key = id(sem)
        if self.seen[e].get(key, 0) >= val:
            return
        self.eng[e].wait_ge(sem, val)
        self.seen[e][key] = val

    def _deps(self, e, reads, writes):
        for b in reads:
            self._wait(e, b.w)
        for b in writes:
            self._wait(e, b.w)
            for ev in list(b.r.values()):
                self._wait(e, ev)

    def _record(self, ev, reads, writes):
        for b in reads:
            if b not in writes:
                b.r[id(ev[0])] = ev
        for b in writes:
            b.w = ev
            b.r = {}

    def op(self, e, fn, reads=(), writes=()):
        self._deps(e, reads, writes)
        ins = fn(self.eng[e])
        self.cnt[e] += 1
        ins.then_inc(self.sem[e], 1)
        ev = (self.sem[e], self.cnt[e])
        if e == "pe":
            self.seen[e][id(self.sem[e])] = self.cnt[e]
        self._record(ev, reads, writes)
        self.n += 1
        return ins

    def dma(self, q, out, in_, dbuf, reads=(), writes=(), **kw):
        self._deps(q, reads, writes)
        if dbuf.dsem is None:
            if self.pool_sems:
                dbuf.dsem, dbuf.dcnt = self.pool_sems.pop()
            else:
                dbuf.dsem = self.ges.enter_context(self.nc.semaphore(f"d{len(self.live)}_{self.n}"))
                dbuf.dcnt = 0
            self.live.append(dbuf)
        ins = self.eng[q].dma_start(out=out, in_=in_, **kw)
        dbuf.dcnt += 16
        ins.then_inc(dbuf.dsem, 16)
        ev = (dbuf.dsem, dbuf.dcnt)
        self._record(ev, reads, writes)
        self.n += 1
        return ins

    def barrier(self, release=True):
        for e in self.eng:
            for f in self.eng:
                if f != e and self.cnt[f] > 0:
                    self._wait(e, (self.sem[f], self.cnt[f]))
            for b in self.live:
                self._wait(e, (b.dsem, b.dcnt))
        if release:
            for b in self.live:
                self.pool_sems.append((b.dsem, b.dcnt))
                b.dsem = None
            self.live = []


def make_tables(hf=1):
    t = {}
    t["ident"] = np.eye(128, dtype=np.float32)
    slopes = (2.0 ** (-8.0 * (np.arange(8) + 1) / 8)).astype(np.float32)
    tl = np.arange(128, dtype=np.float32)
    qa = np.zeros((3, 8, 128), np.float32)
    qb = np.zeros((3, 8, 128), np.float32)
    for h in range(8):
        qa[0, h, :] = slopes[h]
        qa[1, h, :] = slopes[h] * 128.0
        qa[2, h, :] = -slopes[h] * tl
        qb[2, h, :] = -slopes[h] * 128.0
    t["qaugA"] = qa
    t["qaugB"] = qb
    key = np.arange(T)
    kt = np.zeros((64, T), np.float32)
    kt[0] = key % 128
    kt[1] = key // 128
    kt[2] = 1.0
    if hf == 0:
        kt[1, :T // 2] = -8192.0
    for j in range(1, 62):
        kt[2 + j] = (key // 64 == j)
    t["kaug_tok"] = kt
    m = np.arange(256)
    pos = 16 * m + 15
    kc = np.zeros((3, 256), np.float32)
    kc[0] = pos % 128
    kc[1] = pos // 128
    kc[2] = 1.0
    kc[1, 0] = -8192.0
    if hf == 0:
        kc[1, :129] = -8192.0
    t["kaug_cmp"] = kc
    tv = np.ones((128, NT), np.float32)
    f0 = np.full((128, 64), -3e9, np.float32)
    hv = np.ones((128, 1), np.float32)
    if hf == 0:
        tv[:, :NT // 2] = 0.0
        f0[:, 32] = 1e9
        hv[:] = 0.0
    else:
        f0[:, 0] = 1e9
    t["tilevalid"] = tv
    t["f0"] = f0
    t["hv"] = hv
    kl = np.arange(128)[:, None]
    tq = np.arange(128)[None, :]
    t["tric"] = np.where(kl > tq, NEG, 0.0).astype(np.float32)
    t["triw"] = np.where(kl <= tq, NEG, 0.0).astype(np.float32)
    mc = np.zeros((128, 16, 128), np.float32)
    for dl in range(16):
        mc[:, dl, :] = np.where(16 * kl + 15 - tq > 128 * dl, NEG, 0.0)
    t["maskc"] = mc
    ov = np.zeros((256, 64), np.float32)
    for mm in range(1, 256):
        for jj in range(64):
            if 4 * jj <= mm <= 4 * jj + 4:
                ov[mm, jj] = 1.0
    t["ov"] = ov.reshape(2, 128, 64).transpose(1, 0, 2).copy()
    mw = np.zeros((128, 128), np.float32)
    aw = np.zeros((128, 128), np.float32)
    up = (np.arange(128) >= 64).astype(np.float32)
    for jw in range(128):
        jr = jw - 64
        if jr < -1:
            mw[:, jw] = 1.0
        elif jr == -1:
            mw[:, jw] = up
            aw[:, jw] = (1.0 - up) * 1e9
        elif jr == 0:
            aw[:, jw] = 1e9
        elif jr == 1:
            aw[:, jw] = np.where(up > 0, 1e9, -1e9)
        else:
            aw[:, jw] = -1e9
    t["mskw"] = mw
    t["addw"] = aw
    par = np.zeros((128, 2), np.float32)
    kk = np.arange(128)
    par[:, 0] = ((kk // 16) % 2 == 0)
    par[:, 1] = ((kk // 16) % 2 == 1)
    t["par"] = par
    return t


TABLE_SHAPES = {k: v.shape for k, v in make_tables().items()}


def build():
    nc = bass.Bass("TRN2", target_bir_lowering=False)

    def din(name, shape):
        return nc.dram_tensor(name, list(shape), F32, kind="ExternalInput").ap()

    x_d = din("x", [T, D])
    c_d = din("cvec", [128, 8])
    w_ada = din("w_ada", [D, 6 * D])
    b_ada = din("b_ada", [1, 6 * D])
    w_in = din("w_in", [D, 3864])
    pe_ck = din("pe_ck", [32, 64])
    w_ck1 = din("w_ck1", [32, 64, 128])
    w_ck2 = din("w_ck2", [128, 64])
    pe_cv = din("pe_cv", [32, 64])
    w_cv1 = din("w_cv1", [32, 64, 128])
    w_cv2 = din("w_cv2", [128, 64])
    w_nsa_out = din("w_nsa_out", [512, D])
    a_re = din("s5_a_re", [32, 64])
    a_im = din("s5_a_im", [32, 64])
    b_re = din("s5_b_re", [32, 64, 16])
    b_im = din("s5_b_im", [32, 64, 16])
    c_re = din("s5_c_re", [32, 16, 64])
    c_im = din("s5_c_im", [32, 16, 64])
    s5_d = din("s5_d", [512, 1])
    log_dt = din("s5_log_dt", [1, 32])
    w_glu = din("w_s5_glu", [512, 2 * D])
    w_o = din("w_o", [D, D])
    ln1_g = din("ln1_g", [1, D])
    ln1_b = din("ln1_b", [1, D])
    w_up = din("w_up", [D, 2 * DFF])
    conv_w = din("conv_w", [3, 2 * DFF])
    conv_b = din("conv_b", [1, 2 * DFF])
    w_down = din("w_down", [DFF, D])
    ln2_g = din("ln2_g", [1, D])
    ln2_b = din("ln2_b", [1, D])
    tb = {k: din("tb_" + k, shp) for k, shp in TABLE_SHAPES.items()}
    out_d = nc.dram_tensor("out", [T // 2, D], F32, kind="ExternalOutput").ap()
    x1_d = nc.dram_tensor("x1_scratch", [NQ * 128, D], F32, kind="Internal").ap()
    dbg_d = {}
    for k, shp in DBG.items():
        dbg_d[k] = nc.dram_tensor("dbg_" + k, list(shp), F32, kind="ExternalOutput").ap()

    with ExitStack() as ges:
        kb = KB(nc, ges)
        op = kb.op
        PS = [kb.ps(f"ps{i}", [128, 512], F32) for i in range(7)]
        PSB = [kb.buf(f"ps{i}") for i in range(7)]
        PT = kb.ps("pt", [128, 1024], BF16)
        PTB = kb.buf("pt")
        PTB2 = PTB
        PTB3 = PTB
        ident_f = kb.sb("ident_f", [128, 128], F32)
        ident_b = kb.sb("ident_b", [128, 128], BF16)
        gates_bc = kb.sb("gates_bc", [128, 2, D], F32)
        modT = kb.sb("modT", [128, 32], F32)
        one11 = kb.sb("one11", [1, 1], F32)
        B_id = kb.buf("ident")
        B_gbc = kb.buf("gbc")
        B_modT = kb.buf("modT")
        B_one = kb.buf("one")
        tilevalid = kb.sb("tilevalid", [128, NT], F32)
        hv = kb.sb("hv", [128, 1], F32)
        B_tv = kb.buf("tv")
        kb.dma("sp", tilevalid[:], tb["tilevalid"], B_tv, writes=[B_tv])
        kb.dma("sp", hv[:], tb["hv"], B_tv, writes=[B_tv])
        kb.dma("sp", ident_f[:], tb["ident"], B_id, writes=[B_id])
        op("dve", lambda e: e.tensor_copy(out=ident_b[:], in_=ident_f[:]), reads=[B_id], writes=[B_id])
        op("pool", lambda e: e.memset(one11[:], 1.0), writes=[B_one])

        def dbg(name, src_ap, rbufs):
            if name in dbg_d:
                b = kb.buf("dbg")
                kb.dma("sp", dbg_d[name], src_ap, b, reads=rbufs)
                kb.live

        with ExitStack() as es0:
            kb.es = es0
            c_sb = kb.sb("c_sb", [128, 8], F32)
            sc = kb.sb("sc", [128, 8], F32)
            sc_bc = kb.sb("sc_bc", [128, 8, 128], F32)
            bada = kb.sb("bada", [128, 6 * D], F32)
            mod_bc = kb.sb("mod_bc", [128, 6 * D], F32)
            wst = [kb.sb(f"wst{i}", [128, 8, 512], F32) for i in range(2)]
            B_c, B_sc, B_bada, B_mod = kb.buf("c"), kb.buf("sc"), kb.buf("bada"), kb.buf("mod")
            B_wst = [kb.buf("wst0"), kb.buf("wst1")]
            kb.dma("sp", c_sb[:], c_d, B_c, writes=[B_c])
            kb.dma("sp", bada[:], b_ada.partition_broadcast(128).rearrange("p o n -> p (o n)"), B_bada, writes=[B_bada])
            op("act", lambda e: e.activation(out=sc[:], in_=c_sb[:], func=AF.Silu), reads=[B_c], writes=[B_sc])
            op("dve", lambda e: e.tensor_copy(out=sc_bc[:], in_=sc[:].unsqueeze(2).to_broadcast([128, 8, 128])),
               reads=[B_sc], writes=[B_sc])
            for j in range(12):
                st = wst[j % 2]
                kb.dma("sp", st[:], w_ada[:, j * 512:(j + 1) * 512].rearrange("(kc p) n -> p kc n", p=128),
                       B_wst[j % 2], writes=[B_wst[j % 2]])
                for kc in range(KC):
                    op("pe", lambda e: e.matmul(PS[j % 2][:, :], lhsT=sc_bc[:, kc, :], rhs=st[:, kc, :],
                                                start=(kc == 0), stop=(kc == KC - 1)),
                       reads=[B_sc, B_wst[j % 2]], writes=[PSB[j % 2]])
                op("dve", lambda e: e.tensor_tensor(out=mod_bc[:, j * 512:(j + 1) * 512], in0=PS[j % 2][:, :],
                                                     in1=bada[:, j * 512:(j + 1) * 512], op=ALU.add),
                   reads=[PSB[j % 2], B_bada], writes=[B_mod])
            op("dve", lambda e: e.tensor_copy(out=gates_bc[:, 0, :], in_=mod_bc[:, 2 * D:3 * D]), reads=[B_mod], writes=[B_gbc])
            op("dve", lambda e: e.tensor_copy(out=gates_bc[:, 1, :], in_=mod_bc[:, 5 * D:6 * D]), reads=[B_mod], writes=[B_gbc])
            for wi, w in enumerate((0, 1, 3, 4)):
                for fc in range(8):
                    col = w * D + fc * 128
                    idx = wi * 8 + fc
                    op("pe", lambda e: e.matmul(PS[2][:, idx:idx + 1], lhsT=mod_bc[0:1, col:col + 128],
                                                rhs=one11[0:1, 0:1], start=True, stop=True),
                       reads=[B_mod, B_one], writes=[PSB[2]])
            op("dve", lambda e: e.tensor_copy(out=modT[:], in_=PS[2][:, 0:32]), reads=[PSB[2]], writes=[B_modT])
            op("dve", lambda e: e.tensor_scalar(out=modT[:, 8:16], in0=modT[:, 8:16], scalar1=1.0, scalar2=None, op0=ALU.add),
               reads=[B_modT], writes=[B_modT])
            op("dve", lambda e: e.tensor_scalar(out=modT[:, 24:32], in0=modT[:, 24:32], scalar1=1.0, scalar2=None, op0=ALU.add),
               reads=[B_modT], writes=[B_modT])
            dbg("modT", modT[:], [B_modT])
            kb.barrier()
        kb.es = ges

        def ln_to_hT(xt, B_x, xn, B_xn, stats, mv, rstd, B_st, hT_out, B_hT, mcol):
            for hf in range(2):
                op("dve", lambda e: e.bn_stats(out=stats[:, hf, :], in_=xt[:, hf * 512:(hf + 1) * 512]),
                   reads=[B_x], writes=[B_st])
            op("dve", lambda e: e.bn_aggr(out=mv[:], in_=stats[:]), reads=[B_st], writes=[B_st])
            op("dve", lambda e: e.tensor_scalar(out=rstd[:], in0=mv[:, 1:2], scalar1=1e-5, scalar2=None, op0=ALU.add),
               reads=[B_st], writes=[B_st])
            op("act", lambda e: e.activation(out=rstd[:], in_=rstd[:], func=AF.Sqrt), reads=[B_st], writes=[B_st])
            op("dve", lambda e: e.reciprocal(out=rstd[:], in_=rstd[:]), reads=[B_st], writes=[B_st])
            op("dve", lambda e: e.tensor_scalar(out=xn[:], in0=xt[:], scalar1=mv[:, 0:1], scalar2=rstd[:, 0:1],
                                                 op0=ALU.subtract, op1=ALU.mult), reads=[B_x, B_st], writes=[B_xn])
            for kc in range(KC):
                op("pe", lambda e: e.transpose(out=PT[:, kc * 128:(kc + 1) * 128], in_=xn[:, kc * 128:(kc + 1) * 128],
                                               identity=ident_b[:]), reads=[B_xn, B_id], writes=[PTB, PTB2] if kc == 7 else [PTB])
            for kc in range(KC):
                op("act", lambda e: e.activation(out=hT_out[:, kc, :], in_=PT[:, kc * 128:(kc + 1) * 128], func=AF.Identity,
                                                 scale=modT[:, mcol + 8 + kc:mcol + 9 + kc], bias=modT[:, mcol + kc:mcol + kc + 1]),
                   reads=[PTB, B_modT], writes=[B_hT])

        def load_cast(dst_fn, src_ap, rows_kc, ncols, stg, B_stg, B_dst, eng="dve"):
            kb.dma("sp", stg[:, 0:rows_kc, 0:ncols], src_ap.rearrange("(kc p) n -> p kc n", p=128), B_stg, writes=[B_stg])
            for kc in range(rows_kc):
                if kc % 2 == 0:
                    op("dve", lambda e: e.tensor_copy(out=dst_fn(kc), in_=stg[:, kc, 0:ncols]), reads=[B_stg], writes=[B_dst])
                else:
                    op("act", lambda e: e.activation(out=dst_fn(kc), in_=stg[:, kc, 0:ncols], func=AF.Copy), reads=[B_stg], writes=[B_dst])

        with ExitStack() as esA:
            kb.es = esA
            oaT = kb.sb("oaT", [128, 4, NQ * 128], BF16)
            uT = kb.sb("uT", [128, 4, T], BF16)
            B_oaT = [kb.buf(f"oaT{i}") for i in range(NT)]
            B_uT = [kb.buf(f"uT{i}") for i in range(NT)]
            s5_ar = kb.sb("s5ar", [128, 16], F32); s5_ai = kb.sb("s5ai", [128, 16], F32); s5_ldt = kb.sb("s5ldt", [128, 16], F32)
            s5_par = kb.sb("s5par", [128, 2], F32); s5_dsk = kb.sb("s5dsk", [128, 4], F32)
            B_s5in = kb.buf("s5in")
            kb.dma("sp", s5_ar[:], a_re.rearrange("(pr g2) p -> (g2 p) pr", g2=2), B_s5in, writes=[B_s5in], allow_slow_non_contiguous=True)
            kb.dma("sp", s5_ai[:], a_im.rearrange("(pr g2) p -> (g2 p) pr", g2=2), B_s5in, writes=[B_s5in], allow_slow_non_contiguous=True)
            for g2 in range(2):
                kb.dma("sp", s5_ldt[g2 * 64:(g2 + 1) * 64, :],
                       log_dt.rearrange("o (pr g2) -> o g2 pr", g2=2)[:, g2, :].partition_broadcast(64).rearrange("p o n -> p (o n)"),
                       B_s5in, writes=[B_s5in], allow_slow_non_contiguous=True)
            kb.dma("sp", s5_par[:], tb["par"], B_s5in, writes=[B_s5in])
            kb.dma("sp", s5_dsk[:], s5_d.rearrange("(ct q) o -> q (ct o)", ct=4), B_s5in, writes=[B_s5in], allow_slow_non_contiguous=True)
            if STOP[0] >= 1:
                pass_a1(nc, kb, locals())
            kb.barrier()
            WMp = kb.sb("WMp", [128, 8, 2048], BF16)
            WGLp = kb.sb("WGLp", [128, 4, 2048], BF16)
            WOp = kb.sb("WOp", [128, 8, D], BF16)
            B_Wpre = kb.buf("Wpre")
            if STOP[0] >= 2:
                pass_s5(nc, kb, locals())
            kb.barrier()
            if STOP[0] >= 3:
                pass_a2(nc, kb, locals())
            kb.barrier()
        kb.es = ges
        if STOP[0] >= 4:
            with ExitStack() as esB:
                kb.es = esB
                pass_b(nc, kb, locals())
                kb.barrier()
            kb.es = ges
        kb.barrier(release=False)
    return nc


class NS:
    def __init__(self, d):
        self.__dict__.update(d)


def pass_a1(nc, kb, Ld):
    outer = Ld['esA']
    with ExitStack() as es1:
        d2 = dict(Ld)
        d2['esA'] = es1
        kb.es = es1
        _pass_a1_body(nc, kb, d2)
        kb.barrier()
    kb.es = outer


def _pass_a1_body(nc, kb, Ld):
    L = NS(Ld)
    op = kb.op
    PS, PSB, PT, PTB = L.PS, L.PSB, L.PT, L.PTB
    tb = L.tb
    sb, buf = kb.sb, kb.buf
    WQK = sb("WQK", [128, 8, 1088], BF16)
    WV = sb("WV", [128, 8, 280], BF16)
    WU = sb("WU", [128, 8, 512], BF16)
    KTs = sb("KTs", [128, 2, T], BF16)
    KTw = sb("KTw", [128, 2, 1024], BF16)
    V1 = sb("V1", [128, NT, 4, 65], BF16)
    kcTa = sb("kcTa", [128, 2, 256], BF16)
    vcx = sb("vcx", [128, 2, 2, 129], BF16)
    triC = sb("triC", [128, 4, 128], BF16)
    triW = sb("triW", [128, 4, 128], BF16)
    maskC = sb("maskC", [128, 16, 128], BF16)
    cw1 = sb("cw1", [128, 2, 32, 128], BF16)
    cw2 = sb("cw2", [128, 2, 64], BF16)
    peT = sb("peT", [128, 2, 32], BF16)
    cbias = sb("cbias", [128, 2], F32)
    qA = sb("qA", [67, 8, 128], BF16)
    qB = sb("qB", [67, 8, 128], BF16)
    f0t = sb("f0t", [128, 64], F32)
    mskw = sb("mskw", [128, 128], F32)
    addw = sb("addw", [128, 128], F32)
    B_W, B_KT, B_V1 = buf("W"), [buf(f"KT{i}") for i in range(NT)], [buf(f"V1{i}") for i in range(NT)]
    B_kc, B_vcx, B_cst = buf("kc"), buf("vcx"), buf("cst")
    with ExitStack() as ess:
        kb.es = ess
        stgs = [sb(f"stg{i}", [128, 8, 512], F32) for i in range(3)]
        B_stgs = [buf(f"stg{i}") for i in range(3)]
        rotc = [0]

        def rot():
            rotc[0] += 1
            return stgs[rotc[0] % 3], B_stgs[rotc[0] % 3]

        stg, B_stg = rot()
        op("pool", lambda e: e.memset(KTw[:], 0.0), writes=B_KT)
        op("pool", lambda e: e.memset(cw1[:], 0.0), writes=[B_cst])
        op("pool", lambda e: e.memset(peT[:], 0.0), writes=[B_cst])
        op("pool", lambda e: e.memset(WQK[:, :, 1024:1088], 0.0), writes=[B_W])
        L.load_cast(lambda kc: WQK[:, kc, 0:512], L.w_in[:, 0:512], 8, 512, stg, B_stg, B_W)
        stg, B_stg = rot()
        kb.dma("sp", stg[:, :, :], L.w_in[:, 512:1024].rearrange("(kc p) n -> p kc n", p=128), B_stg, writes=[B_stg])
        for kc in range(8):
            op("dve", lambda e: e.tensor_copy(out=WQK[:, kc, 768:1024], in_=stg[:, kc, 0:256]), reads=[B_stg], writes=[B_W])
            op("dve", lambda e: e.tensor_copy(out=WQK[:, kc, 512:640], in_=stg[:, kc, 256:384]), reads=[B_stg], writes=[B_W])
            op("dve", lambda e: e.tensor_copy(out=WV[:, kc, 0:128], in_=stg[:, kc, 384:512]), reads=[B_stg], writes=[B_W])
        stg, B_stg = rot()
        kb.dma("sp", stg[:, :, 0:280], L.w_in[:, 1024:1304].rearrange("(kc p) n -> p kc n", p=128), B_stg, writes=[B_stg])
        for kc in range(8):
            op("dve", lambda e: e.tensor_copy(out=WQK[:, kc, 640:768], in_=stg[:, kc, 0:128]), reads=[B_stg], writes=[B_W])
            op("dve", lambda e: e.tensor_copy(out=WV[:, kc, 128:280], in_=stg[:, kc, 128:280]), reads=[B_stg], writes=[B_W])
        stg, B_stg = rot()
        L.load_cast(lambda kc: WU[:, kc, :], L.w_in[:, 1304:1816], 8, 512, stg, B_stg, B_W)
        for kv, (w1d, w2d, ped) in enumerate(((L.w_ck1, L.w_ck2, L.pe_ck), (L.w_cv1, L.w_cv2, L.pe_cv))):
            for lh in range(4):
                stg, B_stg = rot()
                kb.dma("sp", stg[0:64, :, :].rearrange("p a (b e) -> p (a b) e", e=128)[:, 0:8, :],
                       w1d[lh * 8:(lh + 1) * 8].rearrange("l d e -> d l e"), B_stg, writes=[B_stg])
                op("dve", lambda e: e.tensor_copy(out=cw1[0:64, kv, lh * 8:(lh + 1) * 8, :],
                                                   in_=stg[0:64, :, :].rearrange("p a (b e) -> p (a b) e", e=128)[:, 0:8, :]),
                   reads=[B_stg], writes=[B_cst])
            kb.dma("sp", stg[:, 0, 0:64], w2d, B_stg, writes=[B_stg])
            op("dve", lambda e: e.tensor_copy(out=cw2[:, kv, :], in_=stg[:, 0, 0:64]), reads=[B_stg], writes=[B_cst])
            kb.dma("sp", stg[0:64, 0, 0:32], ped.rearrange("l d -> d l"), B_stg, writes=[B_stg], allow_slow_non_contiguous=True)
            op("dve", lambda e: e.tensor_copy(out=peT[0:64, kv, :], in_=stg[0:64, 0, 0:32]), reads=[B_stg], writes=[B_cst])
        stg, B_stg = rot()
        for tname, dst in (("tric", triC), ("triw", triW)):
            kb.dma("sp", stg[:, 0, 0:128], tb[tname], B_stg, writes=[B_stg])
            op("dve", lambda e: e.tensor_copy(out=dst[:], in_=stg[:, 0, 0:128].unsqueeze(1).to_broadcast([128, 4, 128])),
               reads=[B_stg], writes=[B_cst])
        kb.dma("sp", stg[:, 0:4, :].rearrange("p a b -> p (a b)"), tb["maskc"].rearrange("p a b -> p (a b)"), B_stg, writes=[B_stg])
        op("dve", lambda e: e.tensor_copy(out=maskC[:].rearrange("p a b -> p (a b)"),
                                           in_=stg[:, 0:4, :].rearrange("p a b -> p (a b)")), reads=[B_stg], writes=[B_cst])
        stg, B_stg = rot()
        op("pool", lambda e: e.memset(vcx[:], 0.0), writes=[B_vcx])
        op("pool", lambda e: e.memset(vcx[:, :, :, 64:65], 1.0), writes=[B_vcx])
        kb.dma("sp", stg[:, 0, 0:128], tb["ov"].rearrange("p a b -> p (a b)"), B_stg, writes=[B_stg])
        for g in range(2):
            op("dve", lambda e: e.tensor_copy(out=vcx[:, :, g, 65:129],
                                               in_=stg[:, 0, 0:128].rearrange("p (a b) -> p a b", b=64)),
               reads=[B_stg], writes=[B_vcx])
        op("pool", lambda e: e.memset(V1[:, :, :, 64:65], 1.0), writes=B_V1)
        op("pool", lambda e: e.memset(kcTa[:], 0.0), writes=[B_kc])
        stg, B_stg = rot()
        for q4 in range(2):
            stg, B_stg = rot()
            kb.dma("sp", stg[64:128, :, :].rearrange("p a b -> p (a b)")[:, 0:2048], tb["kaug_tok"][:, q4 * 2048:(q4 + 1) * 2048],
                   B_stg, writes=[B_stg])
            for g in range(2):
                op("dve", lambda e: e.tensor_copy(out=KTs[64:128, g, q4 * 2048:(q4 + 1) * 2048],
                                                   in_=stg[64:128, :, :].rearrange("p a b -> p (a b)")[:, 0:2048]), reads=[B_stg], writes=B_KT)
        stg, B_stg = rot()
        kb.dma("sp", stg[64:67, 0, 0:256], tb["kaug_cmp"], B_stg, writes=[B_stg])
        for g in range(2):
            op("dve", lambda e: e.tensor_copy(out=kcTa[64:67, g, :], in_=stg[64:67, 0, 0:256]), reads=[B_stg], writes=[B_kc])
        for qt, qn in ((qA, "qaugA"), (qB, "qaugB")):
            stg, B_stg = rot()
            kb.dma("sp", stg[64:67, 0:2, :].rearrange("p a b -> p (a b)"), tb[qn].rearrange("p a b -> p (a b)"), B_stg, writes=[B_stg])
            op("dve", lambda e: e.tensor_copy(out=qt[64:67, :, :].rearrange("p a b -> p (a b)"),
                                               in_=stg[64:67, 0:2, :].rearrange("p a b -> p (a b)")), reads=[B_stg], writes=[B_cst])
        kb.dma("sp", f0t[:], tb["f0"], B_cst, writes=[B_cst])
        kb.dma("sp", mskw[:], tb["mskw"], B_cst, writes=[B_cst])
        kb.dma("sp", addw[:], tb["addw"], B_cst, writes=[B_cst])
        for kv in range(2):
            for l in range(32):
                op("pe", lambda e: e.matmul(PS[2][:, kv:kv + 1], lhsT=cw1[:, kv, l, :], rhs=peT[:, kv, l:l + 1],
                                            start=(l == 0), stop=(l == 31)), reads=[B_cst], writes=[PSB[2]])
        op("dve", lambda e: e.tensor_copy(out=cbias[:], in_=PS[2][:, 0:2]), reads=[PSB[2]], writes=[B_cst])
        kb.barrier()
    kb.es = L.esA
    xt = [sb(f"xt{i}", [128, D], F32) for i in range(3)]
    B_xt = [buf("xt0"), buf("xt1"), buf("xt2")]
    xn = sb("xn", [128, D], BF16); B_xn = buf("xn")
    stats = sb("stats", [128, 2, 6], F32); mv = sb("mv", [128, 2], F32); rstd = sb("rstd", [128, 1], F32)
    B_st = buf("st")
    hTs = [sb(f"hT{i}", [128, 8, 128], BF16) for i in range(2)]; B_hTs = [buf("hT0"), buf("hT1")]
    QTa = sb("QTa", [128, 8, 128], BF16); B_Q = buf("Q")
    cmpT = sb("cmpT", [128, 2, 2, 144], BF16); B_cmp = buf("cmp")
    hact = sb("hact", [128, 2, 2, 8], BF16); B_hact = buf("hact")
    hpad = sb("hpad", [128, 2, 128], BF16); B_hpad = buf("hpad")
    gsig = sb("gsig", [128, 24], F32); B_gs = buf("gsig")
    Pt = [sb(f"Pt{i}", [128, 512], BF16) for i in range(4)]
    B_Pt = [buf(f"Pt{i}") for i in range(4)]
    osb = [sb(f"osb{g}", [128, 3, 4, 65], F32) for g in range(2)]; B_osb = [buf("osb0"), buf("osb1")]
    uimp = [sb(f"uimp{g}", [128, 4, 64], F32) for g in range(2)]
    PTB2 = L.PTB2
    imp = sb("imp", [128, 64], F32); imp2 = sb("imp2", [128, 64], F32); impw = sb("impw", [128, 64], F32)
    m8 = sb("m8", [128, 16], F32)
    B_imp = buf("imp")
    QS = [sb(f"QS{g}", [128, 4, 128], BF16) for g in range(2)]; B_QS = [buf("QS0"), buf("QS1")]
    den = [sb(f"den{g}", [128, 3, 4], F32) for g in range(2)]; fac = [sb(f"fac{g}", [128, 3, 4], F32) for g in range(2)]
    B_fac = [buf("fac0"), buf("fac1")]
    otok = sb("otok", [128, 8, 64], BF16); B_ot = buf("otok")
    otmp = sb("otmp", [128, 4, 64], F32); B_otmp = buf("otmp")
    op("pool", lambda e: e.memset(cmpT[:], 0.0), writes=[B_cmp])
    op("pool", lambda e: e.memset(QTa[:], 0.0), writes=[B_Q])
    for g in range(2):
        op("pool", lambda e: e.memset(QS[g][:], 0.0), writes=[B_QS[g]])
    sels = [sb(f"sel128_{g}", [128, 128], BF16) for g in range(2)]
    B_sel = [buf("sel0"), buf("sel1")]
    for g in range(2):
        op("pool", lambda e: e.memset(sels[g][:], 0.0), writes=[B_sel[g]])
    pcount = [0]

    def next_pt():
        pcount[0] += 1
        return pcount[0] % 4

    def scount_next(c=[0]):
        c[0] += 1
        return (0, 1, 6)[c[0] % 3]

    modT, ident_b = L.modT, L.ident_b

    def ln_a(t):
        x_, B_x = xt[t % 3], B_xt[t % 3]
        for hf in range(2):
            op("dve", lambda e: e.bn_stats(out=stats[:, hf, :], in_=x_[:, hf * 512:(hf + 1) * 512]), reads=[B_x], writes=[B_st])
        op("dve", lambda e: e.bn_aggr(out=mv[:], in_=stats[:]), reads=[B_st], writes=[B_st])
        op("dve", lambda e: e.tensor_scalar(out=rstd[:], in0=mv[:, 1:2], scalar1=1e-5, scalar2=None, op0=ALU.add), reads=[B_st], writes=[B_st])
        op("act", lambda e: e.activation(out=rstd[:], in_=rstd[:], func=AF.Sqrt), reads=[B_st], writes=[B_st])
        op("dve", lambda e: e.reciprocal(out=rstd[:], in_=rstd[:]), reads=[B_st], writes=[B_st])
        op("dve", lambda e: e.tensor_scalar(out=xn[:], in0=x_[:], scalar1=mv[:, 0:1], scalar2=rstd[:, 0:1], op0=ALU.subtract, op1=ALU.mult),
           reads=[B_x, B_st], writes=[B_xn])

    def ln_b(t):
        h_, B_h = hTs[t % 2], B_hTs[t % 2]
        for kc in range(KC):
            op("pe", lambda e: e.transpose(out=PT[:, kc * 128:(kc + 1) * 128], in_=xn[:, kc * 128:(kc + 1) * 128], identity=ident_b[:]),
               reads=[B_xn, L.B_id], writes=[PTB, PTB2] if kc == 7 else ([PTB, L.PTB3] if kc == 6 else [PTB]))
        for kc in range(KC):
            op("act", lambda e: e.activation(out=h_[:, kc, :], in_=PT[:, kc * 128:(kc + 1) * 128], func=AF.Identity,
                                             scale=modT[:, 8 + kc:9 + kc], bias=modT[:, kc:kc + 1]),
               reads=[PTB, PTB2, L.B_modT] if kc == 7 else ([PTB, L.PTB3, L.B_modT] if kc == 6 else [PTB, L.B_modT]), writes=[B_h])

    pend_ot = [None]

    def flush_ot():
        if pend_ot[0] is None:
            return
        ti = pend_ot[0]
        pend_ot[0] = None
        osl = slice((ti - Q0) * 128, (ti - Q0 + 1) * 128)
        for c in range(4):
            op("pe", lambda e: e.transpose(out=PT[:, c * 128:(c + 1) * 128], in_=otok[:, 2 * c:2 * c + 2, :].rearrange("p a b -> p (a b)"),
                                           identity=L.ident_b[:]), reads=[B_ot, L.B_id], writes=[PTB])
        op("act", lambda e: e.activation(out=L.oaT[:, :, osl], in_=PT[:, 0:512].rearrange("p (a b) -> p a b", b=128), func=AF.Copy),
           reads=[PTB], writes=[L.B_oaT[ti]])

    for t0 in range(2):
        kb.dma("sp", xt[t0][:], L.x_d[t0 * 128:(t0 + 1) * 128, :], B_xt[t0], writes=[B_xt[t0]])
    ln_a(0)
    ln_b(0)
    for i in range(NT):
        if i + 2 < NT:
            kb.dma("sp", xt[(i + 2) % 3][:], L.x_d[(i + 2) * 128:(i + 3) * 128, :], B_xt[(i + 2) % 3], writes=[B_xt[(i + 2) % 3]])
        if i + 1 < NT:
            ln_a(i + 1)
        hT, B_hT = hTs[i % 2], B_hTs[i % 2]
        lnb_done = [i + 1 >= NT]
        tsl = slice(i * 128, (i + 1) * 128)
        isq = i >= Q0
        for g in (range(2) if isq else ()):
            for hh in range(4):
                h = g * 4 + hh
                for kc in range(KC):
                    op("pe", lambda e: e.matmul(PS[g][:, hh * 128:(hh + 1) * 128], lhsT=WQK[:, kc, h * 64:h * 64 + 128],
                                                rhs=hT[:, kc, :], start=(kc == 0), stop=(kc == KC - 1)),
                       reads=[B_W, B_hT], writes=[PSB[g]])
            op("act", lambda e: e.activation(out=QTa[0:64, g * 4:(g + 1) * 4, :].rearrange("p a b -> p (a b)"),
                                             in_=PS[g][0:64, :], func=AF.Copy, scale=0.125), reads=[PSB[g]], writes=[B_Q])
        if isq:
            op("dve", lambda e: e.scalar_tensor_tensor(out=QTa[64:67, :, :], in0=qB[64:67, :, :], scalar=float(i), in1=qA[64:67, :, :],
                                                        op0=ALU.mult, op1=ALU.add), reads=[B_cst], writes=[B_Q])
        for grp in range(2):
            for s4 in range(4):
                c0 = 512 + grp * 256 + s4 * 64
                for kc in range(KC):
                    op("pe", lambda e: e.matmul(PS[2 + grp][:, s4 * 128:(s4 + 1) * 128], lhsT=WQK[:, kc, c0:c0 + 128],
                                                rhs=hT[:, kc, :], start=(kc == 0), stop=(kc == KC - 1)),
                       reads=[B_W, B_hT], writes=[PSB[2 + grp]])
        wsl = slice((i % 8) * 128, (i % 8 + 1) * 128)
        op("act", lambda e: e.activation(out=KTs[0:64, :, tsl], in_=PS[2][0:64, 0:256].rearrange("p (b c) -> p b c", b=2),
                                         func=AF.Copy), reads=[PSB[2]], writes=[B_KT[i]])
        op("act", lambda e: e.activation(out=KTw[0:64, :, wsl], in_=PS[2][0:64, 256:512].rearrange("p (b c) -> p b c", b=2),
                                         func=AF.Copy), reads=[PSB[2]], writes=[B_KT[i]])
        op("dve", lambda e: e.tensor_copy(out=KTw[64:67, :, wsl], in_=KTs[64:67, :, tsl]), reads=[B_cst], writes=[B_KT[i]])
        op("dve", lambda e: e.tensor_copy(out=cmpT[0:64, :, :, 16:144], in_=PS[3][0:64, :].rearrange("p (a b c) -> p a b c", a=2, b=2)),
           reads=[PSB[3]], writes=[B_cmp])
        flush_ot()
        for kc in range(KC):
            op("pe", lambda e: e.matmul(PS[4][:, 0:280], lhsT=hT[:, kc, :], rhs=WV[:, kc, :], start=(kc == 0), stop=(kc == KC - 1)),
               reads=[B_W, B_hT], writes=[PSB[4]])
        op("dve", lambda e: e.tensor_copy(out=V1[:, i, :, 0:64], in_=PS[4][:, 0:256].rearrange("p (a b) -> p a b", b=64)),
           reads=[PSB[4]], writes=[B_V1[i]])
        op("act", lambda e: e.activation(out=gsig[:], in_=PS[4][:, 256:280], func=AF.Sigmoid), reads=[PSB[4]], writes=[B_gs])
        if not isq and not lnb_done[0]:
            ln_b(i + 1)
            lnb_done[0] = True
        for ct in range(4):
            for kc in range(KC):
                op("pe", lambda e: e.matmul(PS[5][:, ct * 128:(ct + 1) * 128], lhsT=WU[:, kc, ct * 128:(ct + 1) * 128],
                                            rhs=hT[:, kc, :], start=(kc == 0), stop=(kc == KC - 1)),
                   reads=[B_W, B_hT], writes=[PSB[5]])
        op("act", lambda e: e.activation(out=L.uT[:, :, tsl], in_=PS[5][:, :].rearrange("p (a b) -> p a b", b=128), func=AF.Identity,
                                         scale=L.tilevalid[:, i:i + 1]), reads=[PSB[5], L.B_tv], writes=[L.B_uT[i]])
        for kv in range(2):
            for g in range(2):
                o0 = (kv * 2 + g) * 8
                for l in range(32):
                    op("pe", lambda e: e.matmul(PS[6][:, o0:o0 + 8], lhsT=cw1[:, kv, l, :], rhs=cmpT[:, kv, g, l:l + 113:16],
                                                start=(l == 0), stop=(l == 31)), reads=[B_cst, B_cmp], writes=[PSB[6]])
            op("act", lambda e: e.activation(out=hact[:, kv, :, :].rearrange("p a b -> p (a b)"), in_=PS[6][:, kv * 16:(kv + 1) * 16],
                                             func=AF.Silu, bias=cbias[:, kv:kv + 1]), reads=[PSB[6], B_cst], writes=[B_hact])
        op("dve", lambda e: e.tensor_copy(out=cmpT[0:64, :, :, 0:16], in_=cmpT[0:64, :, :, 128:144]), reads=[B_cmp], writes=[B_cmp])
        for g in range(2):
            op("pe", lambda e: e.matmul(PS[6][:, 64 + g * 8:64 + (g + 1) * 8], lhsT=cw2[:].rearrange("p a b -> p (a b)"), rhs=hact[:, 0, g, :],
                                        start=True, stop=True), reads=[B_cst, B_hact], writes=[PSB[6]])
        op("dve", lambda e: e.tensor_copy(out=kcTa[0:64, :, 8 * i:8 * i + 8], in_=PS[6][0:64, 64:80].rearrange("p (a b) -> p a b", b=8)),
           reads=[PSB[6]], writes=[B_kc])
        mt_i, mo = (8 * i) // 128, (8 * i) % 128
        op("pool", lambda e: e.memset(hpad[:], 0.0), writes=[B_hpad])
        op("pool", lambda e: e.tensor_copy(out=hpad[:, :, mo:mo + 8], in_=hact[:, 1, :, :]), reads=[B_hact], writes=[B_hpad])
        for g in range(2):
            op("pe", lambda e: e.matmul(PS[6][:, 128 + g * 64:128 + (g + 1) * 64], lhsT=hpad[:, g, :], rhs=cw2[:, 1, :],
                                        start=True, stop=True), reads=[B_cst, B_hpad], writes=[PSB[6]])
        op("dve", lambda e: e.tensor_tensor(out=vcx[:, mt_i, :, 0:64], in0=PS[6][:, 128:256].rearrange("p (a b) -> p a b", b=64),
                                             in1=vcx[:, mt_i, :, 0:64], op=ALU.add), reads=[PSB[6]], writes=[B_vcx])
        if isq:
            OB = [PS[2], PS[3], PS[4], PS[5]]
            OBB = [PSB[2], PSB[3], PSB[4], PSB[5]]
            Qgs = [QTa[:, g * 4:(g + 1) * 4, :].rearrange("p a b -> p (a b)") for g in range(2)]
            QSf = [QS[g][:].rearrange("p a b -> p (a b)") for g in range(2)]

            def emit_score(job):
                kind, g, kidx, first, last = job
                sbk = scount_next()
                extra = []
                if kind == "c":
                    mt = kidx
                    dl = i - 16 * mt
                    lhs, rl = kcTa[:, g, mt * 128:(mt + 1) * 128], [B_kc, B_Q]
                    if dl < 16:
                        for hh in range(4):
                            extra.append((PS[sbk][:, hh * 128:(hh + 1) * 128], L.ident_b[:], maskC[:, dl, :], [L.B_id, B_cst]))
                else:
                    kt = kidx
                    if kind == "s":
                        lhs = KTs[:, g, kt * 128:(kt + 1) * 128]
                    else:
                        lhs = KTw[:, g, (kt % 8) * 128:(kt % 8 + 1) * 128]
                        if kt == i - 4:
                            extra.append((PS[sbk][:, :], L.ident_b[:], triW[:].rearrange("p a b -> p (a b)"), [L.B_id, B_cst]))
                    if kt == i:
                        extra.append((PS[sbk][:, :], L.ident_b[:], triC[:].rearrange("p a b -> p (a b)"), [L.B_id, B_cst]))
                    rl = [B_KT[kt], B_QS[g] if kind == "s" else B_Q]
                op("pe", lambda e: e.matmul(PS[sbk][:, :], lhsT=lhs, rhs=(QSf[g] if kind == "s" else Qgs[g]), start=True, stop=(len(extra) == 0)),
                   reads=rl, writes=[PSB[sbk]])
                for xi, (oap, lt, rh, rb) in enumerate(extra):
                    op("pe", lambda e: e.matmul(oap, lhsT=lt, rhs=rh, start=False, stop=(xi == len(extra) - 1), skip_group_check=True),
                       reads=rb, writes=[PSB[sbk]])
                p = next_pt()
                op("act", lambda e: e.activation(out=Pt[p][:], in_=PS[sbk][:, :], func=AF.Exp), reads=[PSB[sbk]], writes=[B_Pt[p]])
                return p

            def emit_pv(job, p):
                kind, g, kidx, first, last = job
                if kind == "c":
                    rhs, ncol, rb = vcx[:, kidx, g, :], 129, B_vcx
                else:
                    vi = g if kind == "s" else 2 + g
                    rhs, ncol, rb = V1[:, kidx, vi, :], 65, B_V1[kidx]
                for hh in range(4):
                    op("pe", lambda e: e.matmul(OB[hh][:, 0:ncol], lhsT=Pt[p][:, hh * 128:(hh + 1) * 128], rhs=rhs,
                                                start=first, stop=last), reads=[B_Pt[p], rb], writes=[OBB[hh]])

            def epilogue(kind, g):
                bi = {"c": 0, "s": 1, "w": 2}[kind]
                o_, B_o = osb[g], B_osb[g]
                for hh in range(4):
                    if hh % 2:
                        op("dve", lambda e: e.tensor_copy(out=o_[:, bi, hh, :], in_=OB[hh][:, 0:65]), reads=[OBB[hh]], writes=[B_o])
                    else:
                        op("act", lambda e: e.activation(out=o_[:, bi, hh, :], in_=OB[hh][:, 0:65], func=AF.Copy), reads=[OBB[hh]], writes=[B_o])
                    if kind == "c":
                        op("dve", lambda e: e.tensor_copy(out=uimp[g][:, hh, :], in_=OB[hh][:, 65:129]), reads=[OBB[hh]], writes=[B_o])
                if kind == "c":
                    dn, B_d = den[g], B_fac[g]
                    op("dve", lambda e: e.tensor_scalar(out=dn[:, 0, :], in0=o_[:, 0, :, 64], scalar1=1e-30, scalar2=None, op0=ALU.max),
                       reads=[B_o], writes=[B_d])
                    op("dve", lambda e: e.reciprocal(out=dn[:, 0, :], in_=dn[:, 0, :]), reads=[B_d], writes=[B_d])
                    op("dve", lambda e: e.tensor_scalar(out=imp[:], in0=uimp[g][:, 0, :], scalar1=dn[:, 0, 0:1], scalar2=None, op0=ALU.mult),
                       reads=[B_o, B_d], writes=[B_imp])
                    for hh in range(1, 4):
                        op("dve", lambda e: e.scalar_tensor_tensor(out=imp[:], in0=uimp[g][:, hh, :], scalar=dn[:, 0, hh:hh + 1], in1=imp[:],
                                                                    op0=ALU.mult, op1=ALU.add), reads=[B_o, B_d], writes=[B_imp])
                    w0 = 64 - 2 * i
                    op("dve", lambda e: e.tensor_tensor(out=imp2[:], in0=imp[:], in1=mskw[:, w0:w0 + 64], op=ALU.mult),
                       reads=[B_imp, B_cst], writes=[B_imp])
                    op("dve", lambda e: e.tensor_tensor(out=imp2[:], in0=imp2[:], in1=addw[:, w0:w0 + 64], op=ALU.add),
                       reads=[B_imp, B_cst], writes=[B_imp])
                    op("dve", lambda e: e.tensor_tensor(out=imp2[:], in0=imp2[:], in1=f0t[:], op=ALU.max), reads=[B_cst], writes=[B_imp])
                    op("dve", lambda e: e.max(out=m8[:, 0:8], in_=imp2[:]), reads=[B_imp], writes=[B_imp])
                    op("dve", lambda e: e.match_replace(out=impw[:], in_to_replace=m8[:, 0:8], in_values=imp2[:], imm_value=-3e9),
                       reads=[B_imp], writes=[B_imp])
                    op("dve", lambda e: e.max(out=m8[:, 8:16], in_=impw[:]), reads=[B_imp], writes=[B_imp])
                    op("dve", lambda e: e.tensor_scalar(out=sels[g][:, 67:128], in0=imp2[:, 1:62], scalar1=m8[:, 15:16], scalar2=None, op0=ALU.is_ge),
                       reads=[B_imp], writes=[B_sel[g]])
                if kind == "s":
                    dn, fc_, B_d = den[g], fac[g], B_fac[g]
                    op("dve", lambda e: e.tensor_scalar(out=dn[:, 1:3, :], in0=o_[:, 1:3, :, 64], scalar1=1e-30, scalar2=None, op0=ALU.max),
                       reads=[B_o], writes=[B_d])
                    op("dve", lambda e: e.reciprocal(out=dn[:, 1:3, :], in_=dn[:, 1:3, :]), reads=[B_d], writes=[B_d])
                    op("dve", lambda e: e.tensor_tensor(out=fc_[:], in0=dn[:], in1=gsig[:, g * 12:(g + 1) * 12].rearrange("p (h b) -> p b h", b=3),
                                                         op=ALU.mult), reads=[B_d, B_gs], writes=[B_d])
                    for hh in range(4):
                        h = g * 4 + hh
                        op("dve", lambda e: e.tensor_scalar(out=otmp[:, hh, :], in0=o_[:, 0, hh, 0:64], scalar1=fc_[:, 0, hh:hh + 1], scalar2=None,
                                                             op0=ALU.mult), reads=[B_o, B_d], writes=[B_otmp])
                        op("dve", lambda e: e.scalar_tensor_tensor(out=otmp[:, hh, :], in0=o_[:, 1, hh, 0:64], scalar=fc_[:, 1, hh:hh + 1],
                                                                    in1=otmp[:, hh, :], op0=ALU.mult, op1=ALU.add),
                           reads=[B_o, B_d], writes=[B_otmp])
                        op("dve", lambda e: e.scalar_tensor_tensor(out=otok[:, h, :], in0=o_[:, 2, hh, 0:64], scalar=fc_[:, 2, hh:hh + 1],
                                                                    in1=otmp[:, hh, :], op0=ALU.mult, op1=ALU.add),
                           reads=[B_o, B_d, B_otmp], writes=[B_ot])

            def epi_c_tail(g):
                c0, pb_ = (896, PTB2) if g == 0 else (768, L.PTB3)
                op("pe", lambda e: e.transpose(out=PT[:, c0:c0 + 128], in_=sels[g][:], identity=L.ident_b[:]), reads=[B_sel[g], L.B_id], writes=[pb_])
                op("dve", lambda e: e.tensor_scalar(out=QS[g][64:128], in0=PT[64:128, c0:c0 + 128].unsqueeze(1).to_broadcast([64, 4, 128]),
                                                     scalar1=-1.0, scalar2=-NEG, op0=ALU.add, op1=ALU.mult), reads=[pb_], writes=[B_QS[g]])
                op("pool", lambda e: e.tensor_copy(out=QS[g][0:67], in_=QTa[0:67, g * 4:(g + 1) * 4, :]), reads=[B_Q], writes=[B_QS[g]])

            jobs = []
            for kind in ("c", "w", "s"):
                for g in range(2):
                    if kind == "c":
                        ks = [0] if 8 * i + 7 < 128 else [0, 1]
                    elif kind == "w":
                        ks = list(range(max(0, i - 4), i + 1))
                    else:
                        ks = list(range(0, i + 1))
                    for n_, k_ in enumerate(ks):
                        jobs.append((kind, g, k_, n_ == 0, n_ == len(ks) - 1))
            pend = []
            n_s = [0]
            for job in jobs:
                if job[0] == "s" and job[3] and job[1] == 0:
                    epi_c_tail(0)
                    epi_c_tail(1)
                    n_s[0] = 0
                if job[0] == "s":
                    n_s[0] += 1
                    if n_s[0] == 4 and not lnb_done[0]:
                        ln_b(i + 1)
                        lnb_done[0] = True
                p = emit_score(job)
                pend.append((job, p))
                if len(pend) > 2:
                    pj = pend.pop(0)
                    emit_pv(*pj)
                    if pj[0][4]:
                        epilogue(pj[0][0], pj[0][1])
            for pj in pend:
                emit_pv(*pj)
                if pj[0][4]:
                    epilogue(pj[0][0], pj[0][1])
        if not lnb_done[0]:
            ln_b(i + 1)
            lnb_done[0] = True
        if isq:
            pend_ot[0] = i
    flush_ot()
    if "oaT" in L.dbg_d:
        b = kb.buf("dbg")
        st2 = sb("dbgst", [128, 4, 512], F32)
        op("dve", lambda e: e.tensor_copy(out=st2[:], in_=L.oaT[:, :, 0:512]), reads=L.B_oaT, writes=[b])
        kb.dma("sp", L.dbg_d["oaT"], st2[:], b, reads=[b])


def pass_s5(nc, kb, Ld):
    L = NS(Ld)
    op = kb.op
    PS, PSB, PT, PTB = L.PS, L.PSB, L.PT, L.PTB
    sb, buf = kb.sb, kb.buf
    PI = math.pi
    with ExitStack() as es5:
        kb.es = es5
        BBT = sb("BBT", [128, 16, 2, 128], BF16)
        BBTn = sb("BBTn", [128, 16, 128], BF16)
        Cq = sb("Cq", [128, 16, 4, 128], BF16)
        EI = sb("EI", [128, 2, 16, 128], F32)
        EF = sb("EF", [128, 2, 16, 128], F32)
        L128 = sb("L128", [128, 2, 16], F32)
        dsk = sb("dsk", [128, 4], F32)
        ones = sb("ones", [128, 128], F32)
        carry = sb("carry", [128, 2, 16], F32)
        zl = sb("zl", [128, 2, 16], F32)
        B_tab, B_car, B_zl = buf("tab"), buf("carry"), buf("zl")
        with ExitStack() as est:
            kb.es = est
            ar, ai, ldt = L.s5_ar, L.s5_ai, L.s5_ldt
            dt = sb("dt", [128, 16], F32); lrd = sb("lrd", [128, 16], F32); ang = sb("ang", [128, 16], F32)
            mag = sb("mag", [128, 16], F32); mgi = sb("mgi", [128, 16], F32)
            sn = sb("sn", [128, 16], F32); cs = sb("cs", [128, 16], F32); tmp = sb("tmp", [128, 16], F32); tmp2 = sb("tmp2", [128, 16], F32)
            lb = sb("lb", [128, 2, 16], F32); lbi = sb("lbi", [128, 2, 16], F32); pw = sb("pw", [128, 2, 16], F32)
            coef = sb("coef", [128, 2, 16], F32); den = sb("dens", [128, 16], F32)
            bsb = sb("bsb", [128, 2, 16, 16], F32); bb = sb("bb", [128, 2, 16, 16], F32); bt = sb("bt", [128, 16, 16], F32)
            Apr = sb("Apr", [128, 128], BF16)
            Cn = sb("Cn", [128, 2, 4, 64], F32)
            par = L.s5_par
            et1 = sb("et1", [128, 16, 64], F32); et2 = sb("et2", [128, 16, 64], F32)
            B_s = L.B_s5in; B_apr = buf("apr"); B_et = buf("et")
            for g2 in range(2):
                for ri, bd in enumerate((L.b_re, L.b_im)):
                    kb.dma("sp", bsb[g2 * 64:(g2 + 1) * 64, ri, :, :], bd.rearrange("(pr g2) p h -> g2 p pr h", g2=2)[g2],
                           B_s, writes=[B_s])
            for ri, cd in enumerate((L.c_re, L.c_im)):
                kb.dma("sp", Cn[:, ri, :, :], cd.rearrange("(ct gl) h p -> (gl h) ct p", ct=4), B_s, writes=[B_s])
            op("dve", lambda e: e.tensor_copy(out=dsk[:], in_=L.s5_dsk[:]), reads=[B_s], writes=[B_tab])
            op("pool", lambda e: e.memset(ones[:], 1.0), writes=[B_tab])
            op("pool", lambda e: e.memset(carry[:], 0.0), writes=[B_car])
            op("pool", lambda e: e.memset(Cq[:], 0.0), writes=[B_tab])
            R, W = [B_s], [B_s]
            dv = lambda f: op("dve", f, reads=R, writes=W)
            ac = lambda f: op("act", f, reads=R, writes=W)
            ac(lambda e: e.activation(out=dt[:], in_=ldt[:], func=AF.Exp))
            dv(lambda e: e.tensor_scalar(out=ar[:], in0=ar[:], scalar1=-1e-4, scalar2=None, op0=ALU.min))
            dv(lambda e: e.tensor_tensor(out=lrd[:], in0=ar[:], in1=dt[:], op=ALU.mult))
            dv(lambda e: e.tensor_tensor(out=ang[:], in0=ai[:], in1=dt[:], op=ALU.mult))
            ac(lambda e: e.activation(out=mag[:], in_=lrd[:], func=AF.Exp))
            ac(lambda e: e.activation(out=mgi[:], in_=lrd[:], func=AF.Exp, scale=-1.0))
            ti = sb("ti", [128, 16], mybir.dt.int32)

            def rred(dst, shift):
                dv(lambda e: e.tensor_scalar(out=tmp2[:], in0=ang[:], scalar1=shift, scalar2=None, op0=ALU.add))
                dv(lambda e: e.tensor_scalar(out=tmp[:], in0=tmp2[:], scalar1=1.0 / (2 * PI), scalar2=None, op0=ALU.mult))
                dv(lambda e: e.tensor_copy(out=ti[:], in_=tmp[:]))
                dv(lambda e: e.tensor_copy(out=tmp[:], in_=ti[:]))
                dv(lambda e: e.scalar_tensor_tensor(out=tmp2[:], in0=tmp[:], scalar=-2 * PI, in1=tmp2[:], op0=ALU.mult, op1=ALU.add))
                dv(lambda e: e.tensor_scalar(out=tmp[:], in0=tmp2[:], scalar1=PI, scalar2=2 * PI, op0=ALU.is_gt, op1=ALU.mult))
                dv(lambda e: e.tensor_tensor(out=tmp2[:], in0=tmp2[:], in1=tmp[:], op=ALU.subtract))
                dv(lambda e: e.tensor_scalar(out=tmp[:], in0=tmp2[:], scalar1=-PI, scalar2=2 * PI, op0=ALU.is_lt, op1=ALU.mult))
                dv(lambda e: e.tensor_tensor(out=tmp2[:], in0=tmp2[:], in1=tmp[:], op=ALU.add))
                ac(lambda e: e.activation(out=dst[:], in_=tmp2[:], func=AF.Sin))

            rred(sn, 0.0)
            rred(cs, 0.5 * PI)
            dv(lambda e: e.tensor_tensor(out=lb[:, 0, :], in0=mag[:], in1=cs[:], op=ALU.mult))
            dv(lambda e: e.tensor_tensor(out=lb[:, 1, :], in0=mag[:], in1=sn[:], op=ALU.mult))
            dv(lambda e: e.tensor_tensor(out=lbi[:, 0, :], in0=mgi[:], in1=cs[:], op=ALU.mult))
            dv(lambda e: e.scalar_tensor_tensor(out=lbi[:, 1, :], in0=mgi[:], scalar=-1.0, in1=sn[:], op0=ALU.mult, op1=ALU.mult))
            dv(lambda e: e.tensor_tensor(out=den[:], in0=ar[:], in1=ar[:], op=ALU.mult))
            dv(lambda e: e.tensor_tensor(out=tmp[:], in0=ai[:], in1=ai[:], op=ALU.mult))
            dv(lambda e: e.tensor_tensor(out=den[:], in0=den[:], in1=tmp[:], op=ALU.add))
            dv(lambda e: e.reciprocal(out=den[:], in_=den[:]))
            dv(lambda e: e.tensor_scalar(out=tmp2[:], in0=lb[:, 0, :], scalar1=-1.0, scalar2=None, op0=ALU.add))
            dv(lambda e: e.tensor_tensor(out=tmp[:], in0=tmp2[:], in1=ar[:], op=ALU.mult))
            dv(lambda e: e.tensor_tensor(out=coef[:, 0, :], in0=lb[:, 1, :], in1=ai[:], op=ALU.mult))
            dv(lambda e: e.tensor_tensor(out=coef[:, 0, :], in0=coef[:, 0, :], in1=tmp[:], op=ALU.add))
            dv(lambda e: e.tensor_tensor(out=coef[:, 0, :], in0=coef[:, 0, :], in1=den[:], op=ALU.mult))
            dv(lambda e: e.tensor_tensor(out=tmp[:], in0=tmp2[:], in1=ai[:], op=ALU.mult))
            dv(lambda e: e.tensor_tensor(out=coef[:, 1, :], in0=lb[:, 1, :], in1=ar[:], op=ALU.mult))
            dv(lambda e: e.tensor_tensor(out=coef[:, 1, :], in0=coef[:, 1, :], in1=tmp[:], op=ALU.subtract))
            dv(lambda e: e.tensor_tensor(out=coef[:, 1, :], in0=coef[:, 1, :], in1=den[:], op=ALU.mult))
            cbr = lambda k: coef[:, k, :].unsqueeze(2).to_broadcast([128, 16, 16])
            dv(lambda e: e.tensor_tensor(out=bb[:, 0], in0=bsb[:, 0], in1=cbr(0), op=ALU.mult))
            dv(lambda e: e.tensor_tensor(out=bt[:], in0=bsb[:, 1], in1=cbr(1), op=ALU.mult))
            dv(lambda e: e.tensor_tensor(out=bb[:, 0], in0=bb[:, 0], in1=bt[:], op=ALU.subtract))
            dv(lambda e: e.tensor_tensor(out=bb[:, 1], in0=bsb[:, 1], in1=cbr(0), op=ALU.mult))
            dv(lambda e: e.tensor_tensor(out=bt[:], in0=bsb[:, 0], in1=cbr(1), op=ALU.mult))
            dv(lambda e: e.tensor_tensor(out=bb[:, 1], in0=bb[:, 1], in1=bt[:], op=ALU.add))
            Apr2 = sb("Apr2", [128, 128], BF16)
            Aprs, B_aprs = [Apr, Apr2], [B_apr, buf("apr2")]
            ptv = [PS[0][:, 0:64].bitcast(BF16), PS[1][:, 0:64].bitcast(BF16)]
            it = 0
            for pr in range(16):
                prl = pr % 4
                for ri in range(2):
                    A_, B_A, pv, B_pv = Aprs[it % 2], B_aprs[it % 2], ptv[it % 2], PSB[it % 2]
                    it += 1
                    op("pool", lambda e: e.memset(A_[:], 0.0), writes=[B_A])
                    op("dve", lambda e: e.tensor_copy(out=A_[0:64, 32 * prl:32 * prl + 16], in_=bb[0:64, ri, pr, :]), reads=[B_s], writes=[B_A])
                    op("dve", lambda e: e.tensor_copy(out=A_[64:128, 32 * prl + 16:32 * prl + 32], in_=bb[64:128, ri, pr, :]),
                       reads=[B_s], writes=[B_A])
                    op("pe", lambda e: e.transpose(out=pv, in_=A_[:], identity=L.ident_b[:]), reads=[B_A, L.B_id], writes=[B_pv])
                    op("act", lambda e: e.activation(out=BBT[:, pr, ri, :], in_=pv, func=AF.Copy), reads=[B_pv], writes=[B_tab])
                    if ri == 1:
                        op("act", lambda e: e.activation(out=BBTn[:, pr, :], in_=pv, func=AF.Copy, scale=-1.0),
                           reads=[B_pv], writes=[B_tab])
            for ct in range(4):
                for ri in range(2):
                    for g2 in range(2):
                        op("dve", lambda e: e.tensor_scalar(out=Apr[:, g2 * 64:(g2 + 1) * 64], in0=Cn[:, ri, ct, :], scalar1=par[:, g2:g2 + 1],
                                                             scalar2=None, op0=ALU.mult), reads=[B_s], writes=[B_apr])
                    op("pe", lambda e: e.transpose(out=PT[:, 0:128], in_=Apr[:], identity=L.ident_b[:]), reads=[B_apr, L.B_id], writes=[PTB])
                    for prl in range(4):
                        pr = ct * 4 + prl
                        sl = slice(32 * prl, 32 * prl + 32)
                        if ri == 0:
                            op("dve", lambda e: e.tensor_copy(out=Cq[:, pr, 0, sl], in_=PT[:, sl]), reads=[PTB], writes=[B_tab])
                            op("dve", lambda e: e.tensor_scalar(out=Cq[:, pr, 3, sl], in0=PT[:, sl], scalar1=-1.0, scalar2=None, op0=ALU.mult),
                               reads=[PTB], writes=[B_tab])
                        else:
                            for k in (1, 2):
                                op("dve", lambda e: e.tensor_scalar(out=Cq[:, pr, k, sl], in0=PT[:, sl], scalar1=-1.0, scalar2=None,
                                                                     op0=ALU.mult), reads=[PTB], writes=[B_tab])
            for tabl, base in ((EF, lb), (EI, lbi)):
                op("pool", lambda e: e.memset(tabl[:, 0, :, 0:1], 1.0), writes=[B_tab])
                op("pool", lambda e: e.memset(tabl[:, 1, :, 0:1], 0.0), writes=[B_tab])
                op("dve", lambda e: e.tensor_copy(out=pw[:], in_=base[:]), reads=[B_s], writes=[B_s])
                for k in range(7):
                    n = 1 << k
                    pbr = lambda c: pw[:, c, :].unsqueeze(2).to_broadcast([128, 16, n])
                    RW = dict(reads=[B_s, B_tab, B_et], writes=[B_tab, B_et])
                    op("dve", lambda e: e.tensor_tensor(out=et1[:, :, 0:n], in0=tabl[:, 0, :, 0:n], in1=pbr(0), op=ALU.mult), **RW)
                    op("dve", lambda e: e.tensor_tensor(out=et2[:, :, 0:n], in0=tabl[:, 1, :, 0:n], in1=pbr(1), op=ALU.mult), **RW)
                    op("dve", lambda e: e.tensor_tensor(out=tabl[:, 0, :, n:2 * n], in0=et1[:, :, 0:n], in1=et2[:, :, 0:n], op=ALU.subtract), **RW)
                    op("dve", lambda e: e.tensor_tensor(out=et1[:, :, 0:n], in0=tabl[:, 0, :, 0:n], in1=pbr(1), op=ALU.mult), **RW)
                    op("dve", lambda e: e.tensor_tensor(out=et2[:, :, 0:n], in0=tabl[:, 1, :, 0:n], in1=pbr(0), op=ALU.mult), **RW)
                    op("dve", lambda e: e.tensor_tensor(out=tabl[:, 1, :, n:2 * n], in0=et1[:, :, 0:n], in1=et2[:, :, 0:n], op=ALU.add), **RW)
                    op("dve", lambda e: e.tensor_tensor(out=tmp[:], in0=pw[:, 0, :], in1=pw[:, 0, :], op=ALU.mult), **RW)
                    op("dve", lambda e: e.tensor_tensor(out=tmp2[:], in0=pw[:, 1, :], in1=pw[:, 1, :], op=ALU.mult), **RW)
                    op("dve", lambda e: e.tensor_tensor(out=pw[:, 1, :], in0=pw[:, 0, :], in1=pw[:, 1, :], op=ALU.mult), **RW)
                    op("dve", lambda e: e.tensor_scalar(out=pw[:, 1, :], in0=pw[:, 1, :], scalar1=2.0, scalar2=None, op0=ALU.mult), **RW)
                    op("dve", lambda e: e.tensor_tensor(out=pw[:, 0, :], in0=tmp[:], in1=tmp2[:], op=ALU.subtract), **RW)
                if tabl is EF:
                    op("dve", lambda e: e.tensor_copy(out=L128[:], in_=pw[:]), reads=[B_s], writes=[B_tab])
            kb.barrier()
        kb.es = es5
        tA = [sb(f"tA{i}", [128, 4, 128], F32) for i in range(2)]
        Wt = [sb(f"Wt{i}", [128, 2, 128], F32) for i in range(2)]
        Z = [sb(f"Z{i}", [128, 2, 128], F32) for i in range(2)]
        Qp = [sb(f"Qp{i}", [128, 4, 128], BF16) for i in range(2)]
        ys = [sb(f"ys{i}", [128, 128], F32) for i in range(2)]
        yt = [sb(f"yt{i}", [128, 128], F32) for i in range(2)]
        sg = [sb(f"sg{i}", [128, 128], F32) for i in range(2)]
        ctmp = sb("ctmp", [128, 2, 16], F32)
        wacc = [sb(f"wacc{i}", [128, 2], F32) for i in range(2)]
        B_wacc = [buf("wacc0"), buf("wacc1")]
        B_tA, B_W, B_Z, B_Qp = [[buf(f"{n}{i}") for i in range(2)] for n in ("tA", "Wt", "Z", "Qp")]
        B_ys = [buf("ys0"), buf("ys1")]
        def stage_a_pe(c, pr):
            ct, pb = pr // 4, pr % 2
            csl = slice(c * 128, (c + 1) * 128)
            lts = (BBT[:, pr, 0, :], BBT[:, pr, 1, :], BBTn[:, pr, :], BBT[:, pr, 0, :])
            for q4, lt in enumerate(lts):
                op("pe", lambda e: e.matmul(PS[pb][:, q4 * 128:(q4 + 1) * 128], lhsT=lt, rhs=L.uT[:, ct, csl],
                                            start=True, stop=True), reads=[B_tab, L.B_uT[c]], writes=[PSB[pb]])

        def stage_a(c, pr):
            ct, pb = pr // 4, pr % 2
            bu = PS[pb][:, :].rearrange("p (k r b) -> p k r b", k=2, r=2)
            if c < Q0:
                op("dve", lambda e: e.tensor_tensor(out=tA[pb][:].rearrange("p (r k) b -> p k r b", r=2), in0=bu,
                                                     in1=EI[:, :, pr, :].unsqueeze(2).to_broadcast([128, 2, 2, 128]), op=ALU.mult),
                   reads=[PSB[pb], B_tab], writes=[B_tA[pb]])
                return
            op("dve", lambda e: e.tensor_tensor(out=tA[pb][:].rearrange("p (k r) b -> p k r b", k=2), in0=bu,
                                                 in1=EI[:, :, pr, :].unsqueeze(2).to_broadcast([128, 2, 2, 128]), op=ALU.mult),
               reads=[PSB[pb], B_tab], writes=[B_tA[pb]])
            op("pool", lambda e: e.tensor_tensor(out=Wt[pb][:], in0=tA[pb][:, 0:2, :], in1=tA[pb][:, 2:4, :], op=ALU.add),
               reads=[B_tA[pb]], writes=[B_W[pb]])

        def stage_b(c, pr):
            ct, prl, pb = pr // 4, pr % 4, pr % 2
            csl = slice(c * 128, (c + 1) * 128)
            for ri in (range(2) if c < Q0 else ()):
                op("act", lambda e: e.activation(out=tA[pb][:, 2 * ri:2 * ri + 2, :], in_=tA[pb][:, 2 * ri:2 * ri + 2, :], func=AF.Copy,
                                                 accum_out=wacc[pb][:, ri:ri + 1]), reads=[B_tA[pb]], writes=[B_tA[pb], B_wacc[pb]])
            if c < Q0:
                op("pool", lambda e: e.tensor_tensor(out=zl[:, :, pr:pr + 1], in0=wacc[pb][:, :].unsqueeze(2), in1=carry[:, :, pr:pr + 1], op=ALU.add),
                   reads=[B_wacc[pb], B_car], writes=[B_zl])
            for ri in (range(2) if c >= Q0 else ()):
                op("dve", lambda e: e.tensor_tensor_scan(out=Z[pb][:, ri, :], data0=ones[:], data1=Wt[pb][:, ri, :],
                                                         initial=carry[:, ri, pr:pr + 1], op0=ALU.mult, op1=ALU.add),
                   reads=[B_W[pb], B_tab, B_car], writes=[B_Z[pb]])
            if c >= Q0:
                op("pool", lambda e: e.tensor_copy(out=zl[:, :, pr:pr + 1], in_=Z[pb][:, :, 127:128]), reads=[B_Z[pb]], writes=[B_zl])
                q = Qp[pb]
                op("dve", lambda e: e.tensor_tensor(out=q[:].rearrange("p (k r) b -> p k r b", k=2),
                                                     in0=Z[pb][:].unsqueeze(1).to_broadcast([128, 2, 2, 128]),
                                                     in1=EF[:, :, pr, :].unsqueeze(2).to_broadcast([128, 2, 2, 128]), op=ALU.mult),
                   reads=[B_Z[pb], B_tab], writes=[B_Qp[pb]])
                yb = 2 + ct % 2
                for k in range(4):
                    op("pe", lambda e: e.matmul(PS[yb][:, 0:128], lhsT=Cq[:, pr, k, :], rhs=q[:, k, :],
                                                start=(prl == 0 and k == 0), stop=(prl == 3 and k == 3)),
                       reads=[B_tab, B_Qp[pb]], writes=[PSB[yb]])
                if prl == 3:
                    cb2 = ct % 2
                    op("dve", lambda e: e.scalar_tensor_tensor(out=ys[cb2][:], in0=L.uT[:, ct, csl], scalar=dsk[:, ct:ct + 1], in1=PS[yb][:, 0:128],
                                                                op0=ALU.mult, op1=ALU.add), reads=[PSB[yb], L.B_uT[c], B_tab], writes=[B_ys[cb2]])
                    op("pool", lambda e: e.tensor_tensor(out=yt[cb2][:], in0=ys[cb2][:], in1=ys[cb2][:], op=ALU.mult),
                       reads=[B_ys[cb2]], writes=[B_ys[cb2]])
                    op("pool", lambda e: e.tensor_scalar(out=yt[cb2][:], in0=yt[cb2][:], scalar1=0.044715, scalar2=1.0, op0=ALU.mult, op1=ALU.add),
                       reads=[B_ys[cb2]], writes=[B_ys[cb2]])
                    op("pool", lambda e: e.tensor_tensor(out=yt[cb2][:], in0=yt[cb2][:], in1=ys[cb2][:], op=ALU.mult),
                       reads=[B_ys[cb2]], writes=[B_ys[cb2]])
                    op("act", lambda e: e.activation(out=sg[cb2][:], in_=yt[cb2][:], func=AF.Sigmoid, scale=1.5957691216057308),
                       reads=[B_ys[cb2]], writes=[B_ys[cb2]])
                    op("pool", lambda e: e.tensor_tensor(out=L.uT[:, ct, csl], in0=ys[cb2][:], in1=sg[cb2][:], op=ALU.mult),
                       reads=[B_ys[cb2]], writes=[L.B_uT[c]])
            if pr == 15:
                RWc = dict(reads=[B_zl, B_tab, B_car], writes=[B_car])
                op("dve", lambda e: e.tensor_tensor(out=ctmp[:, 0, :], in0=L128[:, 0, :], in1=zl[:, 0, :], op=ALU.mult), **RWc)
                op("dve", lambda e: e.tensor_tensor(out=ctmp[:, 1, :], in0=L128[:, 1, :], in1=zl[:, 1, :], op=ALU.mult), **RWc)
                op("dve", lambda e: e.tensor_tensor(out=carry[:, 0, :], in0=ctmp[:, 0, :], in1=ctmp[:, 1, :], op=ALU.subtract), **RWc)
                op("dve", lambda e: e.tensor_tensor(out=ctmp[:, 0, :], in0=L128[:, 0, :], in1=zl[:, 1, :], op=ALU.mult), **RWc)
                op("dve", lambda e: e.tensor_tensor(out=ctmp[:, 1, :], in0=L128[:, 1, :], in1=zl[:, 0, :], op=ALU.mult), **RWc)
                op("dve", lambda e: e.tensor_tensor(out=carry[:, 1, :], in0=ctmp[:, 0, :], in1=ctmp[:, 1, :], op=ALU.add), **RWc)

        stgp = sb("stgp", [128, 8, 128], F32)
        B_stgp = buf("stgp")

        def pre_step(dst_fn, src_ap, rows_kc):
            kb.dma("sp", stgp[:, 0:rows_kc, :], src_ap.rearrange("(kc p) n -> p kc n", p=128), B_stgp, writes=[B_stgp])
            for kc in range(rows_kc):
                op("act", lambda e: e.activation(out=dst_fn(kc), in_=stgp[:, kc, :], func=AF.Copy), reads=[B_stgp], writes=[L.B_Wpre])

        pre_steps = []
        for q in range(16):
            pre_steps.append((lambda kc, q=q: L.WMp[:, kc, q * 128:(q + 1) * 128], L.w_in[:, 1816 + q * 128:1816 + (q + 1) * 128], 8))
        for q in range(16):
            pre_steps.append((lambda kc, q=q: L.WGLp[:, kc, q * 128:(q + 1) * 128], L.w_glu[:, q * 128:(q + 1) * 128], 4))
        for q in range(8):
            pre_steps.append((lambda kc, q=q: L.WOp[:, kc, q * 128:(q + 1) * 128], L.w_o[:, q * 128:(q + 1) * 128], 8))
        seq = [(c, pr) for c in range(NT) for pr in range(16)]
        stage_a_pe(*seq[0])
        stage_a_pe(*seq[1])
        stage_a(*seq[0])
        for k in range(len(seq)):
            if seq[k][1] in (0, 4, 8, 12) and seq[k][0] >= Q0 + 1 and pre_steps:
                pre_step(*pre_steps.pop(0))
            if k + 2 < len(seq):
                stage_a_pe(*seq[k + 2])
            if k + 1 < len(seq):
                stage_a(*seq[k + 1])
            stage_b(*seq[k])
        while pre_steps:
            pre_step(*pre_steps.pop(0))
        if "gyT" in L.dbg_d:
            b = kb.buf("dbg")
            st2 = sb("dbgst5", [128, 4, 512], F32)
            op("dve", lambda e: e.tensor_copy(out=st2[:], in_=L.uT[:, :, 0:512]), reads=L.B_uT, writes=[b])
            kb.dma("sp", L.dbg_d["gyT"], st2[:], b, reads=[b])
        kb.barrier()
    kb.es = L.esA


def ln_affine_store(kb, L, pre, B_pre, stats, mv, rstd, B_st, gb, B_gb, dst_ap, q="sp"):
    op = kb.op
    for hf in range(2):
        op("dve", lambda e: e.bn_stats(out=stats[:, hf, :], in_=pre[:, hf * 512:(hf + 1) * 512]), reads=[B_pre], writes=[B_st])
    op("dve", lambda e: e.bn_aggr(out=mv[:], in_=stats[:]), reads=[B_st], writes=[B_st])
    op("dve", lambda e: e.tensor_scalar(out=rstd[:], in0=mv[:, 1:2], scalar1=1e-5, scalar2=None, op0=ALU.add),
       reads=[B_st], writes=[B_st])
    op("act", lambda e: e.activation(out=rstd[:], in_=rstd[:], func=AF.Sqrt), reads=[B_st], writes=[B_st])
    op("dve", lambda e: e.reciprocal(out=rstd[:], in_=rstd[:]), reads=[B_st], writes=[B_st])
    op("dve", lambda e: e.tensor_scalar(out=pre[:], in0=pre[:], scalar1=mv[:, 0:1], scalar2=rstd[:, 0:1], op0=ALU.subtract, op1=ALU.mult),
       reads=[B_st], writes=[B_pre])
    op("pool", lambda e: e.tensor_tensor(out=pre[:], in0=pre[:], in1=gb[:, 0, :], op=ALU.mult), reads=[B_gb], writes=[B_pre])
    op("pool", lambda e: e.tensor_tensor(out=pre[:], in0=pre[:], in1=gb[:, 1, :], op=ALU.add), reads=[B_gb], writes=[B_pre])
    kb.dma(q, dst_ap, pre[:], B_pre, reads=[B_pre])


def pass_a2(nc, kb, Ld):
    L = NS(Ld)
    op = kb.op
    PS, PSB, PT, PTB = L.PS, L.PSB, L.PT, L.PTB
    sb, buf = kb.sb, kb.buf
    with ExitStack() as es2:
        kb.es = es2
        WM, WGL, WO = L.WMp, L.WGLp, L.WOp
        WNO = sb("WNO", [128, 4, D], BF16)
        gb = sb("gb1", [128, 2, D], F32)
        B_W, B_gb = L.B_Wpre, buf("gb1")
        with ExitStack() as ess:
            kb.es = ess
            stgs = [sb(f"stg2{i}", [128, 8, 512], F32) for i in range(2)]
            B_stgs = [buf(f"stg2{i}") for i in range(2)]
            for q2 in range(2):
                L.load_cast(lambda kc: WNO[:, kc, q2 * 512:(q2 + 1) * 512], L.w_nsa_out[:, q2 * 512:(q2 + 1) * 512], 4, 512,
                            stgs[q2], B_stgs[q2], B_W)
            kb.dma("sp", gb[:, 0, :], L.ln1_g.partition_broadcast(128).rearrange("p o n -> p (o n)"), B_gb, writes=[B_gb])
            kb.dma("sp", gb[:, 1, :], L.ln1_b.partition_broadcast(128).rearrange("p o n -> p (o n)"), B_gb, writes=[B_gb])
            kb.barrier()
        kb.es = es2
        xts = [[sb(f"xt2{k}{i}", [128, D], F32) for i in range(4)] for k in range(2)]
        B_xts = [[buf(f"xt2{k}{i}") for i in range(4)] for k in range(2)]
        xn = sb("xn2", [128, D], BF16); B_xn = buf("xn2")
        stats = sb("stats2", [128, 2, 6], F32); mv = sb("mv2", [128, 2], F32); rstd = sb("rstd2", [128, 1], F32)
        B_st = buf("st2")
        hTs2 = [sb(f"hT2{k}", [128, 8, 512], BF16) for k in range(2)]; B_hTs2 = [buf("hT20"), buf("hT21")]
        sga = sb("sga", [128, 3, 512], F32); B_sg = [buf("sga0"), buf("sga1")]
        t1 = sb("t1", [128, 512], F32); t2 = sb("t2", [128, 512], F32); B_t = buf("t12")
        mixT = sb("mixT", [128, 8, 512], BF16); B_mix = buf("mixT")
        gtmp, B_gt = [t1, t2], [B_t, B_t]
        steps = [[0]] + [list(range(r, r + 4)) for r in range(1, NQ, 4)]
        modT, ident_b = L.modT, L.ident_b

        def ln_a2(x_, B_x):
            for hf in range(2):
                op("dve", lambda e: e.bn_stats(out=stats[:, hf, :], in_=x_[:, hf * 512:(hf + 1) * 512]), reads=[B_x], writes=[B_st])
            op("dve", lambda e: e.bn_aggr(out=mv[:], in_=stats[:]), reads=[B_st], writes=[B_st])
            op("dve", lambda e: e.tensor_scalar(out=rstd[:], in0=mv[:, 1:2], scalar1=1e-5, scalar2=None, op0=ALU.add), reads=[B_st], writes=[B_st])
            op("act", lambda e: e.activation(out=rstd[:], in_=rstd[:], func=AF.Sqrt), reads=[B_st], writes=[B_st])
            op("dve", lambda e: e.reciprocal(out=rstd[:], in_=rstd[:]), reads=[B_st], writes=[B_st])
            op("dve", lambda e: e.tensor_scalar(out=xn[:], in0=x_[:], scalar1=mv[:, 0:1], scalar2=rstd[:, 0:1], op0=ALU.subtract, op1=ALU.mult),
               reads=[B_x, B_st], writes=[B_xn])

        def ln_b2(dst, B_dst):
            for kc in range(KC):
                op("pe", lambda e: e.transpose(out=PT[:, kc * 128:(kc + 1) * 128], in_=xn[:, kc * 128:(kc + 1) * 128], identity=ident_b[:]),
                   reads=[B_xn, L.B_id], writes=[PTB])
            for kc in range(KC):
                op("act", lambda e: e.activation(out=dst[:, kc, :], in_=PT[:, kc * 128:(kc + 1) * 128], func=AF.Identity,
                                                 scale=modT[:, 8 + kc:9 + kc], bias=modT[:, kc:kc + 1]), reads=[PTB, L.B_modT], writes=[B_dst])

        def load_step(si):
            for tt, r in enumerate(steps[si]):
                i = Q0 + r
                kb.dma("sp", xts[si % 2][tt][:], L.x_d[i * 128:(i + 1) * 128, :], B_xts[si % 2][tt], writes=[B_xts[si % 2][tt]])

        load_step(0)
        for tt in range(len(steps[0])):
            ln_a2(xts[0][tt], B_xts[0][tt])
            ln_b2(hTs2[0][:, :, tt * 128:(tt + 1) * 128], B_hTs2[0])
        for si, rs in enumerate(steps):
            NTOK = 128 * len(rs)
            r0 = rs[0]
            xt, B_xt = xts[si % 2], B_xts[si % 2]
            hT, B_hT = hTs2[si % 2], B_hTs2[si % 2]
            nxt = steps[si + 1] if si + 1 < len(steps) else []
            if nxt:
                load_step(si + 1)
            osl = slice(r0 * 128, r0 * 128 + NTOK)
            tsl = slice((Q0 + r0) * 128, (Q0 + r0) * 128 + NTOK)
            B_us = [L.B_uT[Q0 + r] for r in rs]
            B_os = [L.B_oaT[Q0 + r] for r in rs]
            for fc in range(8):
                fsl = slice(fc * 128, (fc + 1) * 128)
                for half in range(2):
                    for kc in range(KC):
                        op("pe", lambda e: e.matmul(PS[half][:, 0:NTOK], lhsT=WM[:, kc, half * D + fc * 128:half * D + (fc + 1) * 128],
                                                    rhs=hT[:, kc, 0:NTOK], start=(kc == 0), stop=(kc == KC - 1)), reads=[B_W, B_hT], writes=[PSB[half]])
                    op("act", lambda e: e.activation(out=sga[:, half, 0:NTOK], in_=PS[half][:, 0:NTOK], func=AF.Sigmoid),
                       reads=[PSB[half]], writes=[B_sg[0]])
                for c in range(4):
                    op("pe", lambda e: e.matmul(PS[2][:, 0:NTOK], lhsT=WNO[:, c, fsl], rhs=L.oaT[:, c, osl], start=(c == 0), stop=(c == 3)),
                       reads=[B_W] + B_os, writes=[PSB[2]])
                op("dve", lambda e: e.tensor_tensor(out=t1[:, 0:NTOK], in0=sga[:, 0, 0:NTOK], in1=PS[2][:, 0:NTOK], op=ALU.mult),
                   reads=[B_sg[0], PSB[2]], writes=[B_t])
                for half in range(2):
                    for c in range(4):
                        op("pe", lambda e: e.matmul(PS[3 + half][:, 0:NTOK], lhsT=WGL[:, c, half * D + fc * 128:half * D + (fc + 1) * 128],
                                                    rhs=L.uT[:, c, tsl], start=(c == 0), stop=(c == 3)), reads=[B_W] + B_us, writes=[PSB[3 + half]])
                op("act", lambda e: e.activation(out=sga[:, 2, 0:NTOK], in_=PS[4][:, 0:NTOK], func=AF.Sigmoid), reads=[PSB[4]], writes=[B_sg[1]])
                op("dve", lambda e: e.tensor_tensor(out=t2[:, 0:NTOK], in0=sga[:, 2, 0:NTOK], in1=PS[3][:, 0:NTOK], op=ALU.mult),
                   reads=[B_sg[1], PSB[3]], writes=[B_t])
                op("dve", lambda e: e.tensor_tensor(out=t2[:, 0:NTOK], in0=t2[:, 0:NTOK], in1=sga[:, 1, 0:NTOK], op=ALU.mult),
                   reads=[B_sg[0]], writes=[B_t])
                op("dve", lambda e: e.tensor_tensor(out=mixT[:, fc, 0:NTOK], in0=t1[:, 0:NTOK], in1=t2[:, 0:NTOK], op=ALU.add),
                   reads=[B_t], writes=[B_mix])
                tn = fc // 2
                if tn < len(nxt):
                    if fc % 2 == 0:
                        ln_a2(xts[(si + 1) % 2][tn], B_xts[(si + 1) % 2][tn])
                    else:
                        ln_b2(hTs2[(si + 1) % 2][:, :, tn * 128:(tn + 1) * 128], B_hTs2[(si + 1) % 2])
            for tt, r in enumerate(rs):
                pr_, B_p = xt[tt], B_xt[tt]
                for half in range(2):
                    for fc in range(8):
                        op("pe", lambda e: e.matmul(PS[5 + half][:, :], lhsT=mixT[:, fc, tt * 128:(tt + 1) * 128], rhs=WO[:, fc, half * 512:(half + 1) * 512],
                                                    start=(fc == 0), stop=(fc == 7)), reads=[B_W, B_mix], writes=[PSB[5 + half]])
                    hs = slice(half * 512, (half + 1) * 512)
                    op("dve", lambda e: e.tensor_tensor(out=gtmp[half][:], in0=PS[5 + half][:, :], in1=L.gates_bc[:, 0, hs], op=ALU.mult),
                       reads=[PSB[5 + half], L.B_gbc], writes=[B_gt[half]])
                    op("dve", lambda e: e.scalar_tensor_tensor(out=pr_[:, hs], in0=pr_[:, hs], scalar=ALPHA, in1=gtmp[half][:], op0=ALU.mult, op1=ALU.add),
                       reads=[B_gt[half]], writes=[B_p])
                ln_affine_store(kb, L, pr_, B_p, stats, mv, rstd, B_st, gb, B_gb, L.x1_d[r * 128:(r + 1) * 128, :])
        kb.barrier()
    kb.es = L.esA


def pass_b(nc, kb, Ld):
    L = NS(Ld)
    op = kb.op
    PS, PSB, PT, PTB = L.PS, L.PSB, L.PT, L.PTB
    sb, buf = kb.sb, kb.buf
    NJ = 44
    WUP = sb("WUP", [128, 8, 2 * DFF], BF16)
    WDN = sb("WDN", [128, 22, D], BF16)
    cw = sb("cwt", [128, NJ, 3], F32)
    cb = sb("cbt", [128, NJ], F32)
    gb = sb("gb2", [128, 2, D], F32)
    halo = sb("halo", [128, NJ, 2], F32)
    B_W, B_gb, B_cw, B_halo = buf("WB"), buf("gb2"), buf("cw"), buf("halo")
    with ExitStack() as ess:
        kb.es = ess
        stgs = [sb(f"stgb{i}", [128, 8, 512], F32) for i in range(3)]
        B_stgs = [buf(f"stgb{i}") for i in range(3)]
        for q in range(11):
            L.load_cast(lambda kc: WUP[:, kc, q * 512:(q + 1) * 512], L.w_up[:, q * 512:(q + 1) * 512], 8, 512, stgs[q % 3], B_stgs[q % 3], B_W)
        for half in range(2):
            for q in range(3):
                stg, B_stg = stgs[(half * 3 + q + 2) % 3], B_stgs[(half * 3 + q + 2) % 3]
                r0, nr = q * 8, (8 if q < 2 else 6)
                kb.dma("sp", stg[:, 0:nr, :], L.w_down[r0 * 128:(r0 + nr) * 128, half * 512:(half + 1) * 512].rearrange("(kc p) n -> p kc n", p=128),
                       B_stg, writes=[B_stg])
                for kc in range(nr):
                    op("dve", lambda e: e.tensor_copy(out=WDN[:, r0 + kc, half * 512:(half + 1) * 512], in_=stg[:, kc, :]),
                       reads=[B_stg], writes=[B_W])
        for k3 in range(3):
            kb.dma("sp", cw[:, :, k3], L.conv_w[k3:k3 + 1, :].rearrange("o (j p) -> p (o j)", p=128), B_cw, writes=[B_cw],
                   allow_slow_non_contiguous=True)
        kb.dma("sp", cb[:], L.conv_b.rearrange("o (j p) -> p (o j)", p=128), B_cw, writes=[B_cw], allow_slow_non_contiguous=True)
        kb.dma("sp", gb[:, 0, :], L.ln2_g.partition_broadcast(128).rearrange("p o n -> p (o n)"), B_gb, writes=[B_gb])
        kb.dma("sp", gb[:, 1, :], L.ln2_b.partition_broadcast(128).rearrange("p o n -> p (o n)"), B_gb, writes=[B_gb])
        op("pool", lambda e: e.memset(halo[:], 0.0), writes=[B_halo])
        kb.barrier()
    kb.es = L.esB
    NTOK = 512
    x1t = [sb(f"x1t{i}", [128, D], F32) for i in range(4)]
    B_x1 = [buf(f"x1t{i}") for i in range(4)]
    xn = sb("xnb", [128, D], BF16); B_xn = buf("xnb")
    stats = sb("statsb", [128, 2, 6], F32); mv = sb("mvb", [128, 2], F32); rstd = sb("rstdb", [128, 1], F32)
    B_st = buf("stb")
    h2T = sb("h2T", [128, 8, NTOK], BF16); B_h2 = buf("h2T")
    upb = [sb(f"upb{i}", [128, NTOK + 2], F32) for i in range(2)]
    acc = [sb(f"acc{i}", [128, NTOK], F32) for i in range(2)]
    B_up = [buf("up0"), buf("up1")]
    B_acc = [buf("acc0"), buf("acc1")]
    ffT = sb("ffT", [128, 22, NTOK], BF16); B_ff = buf("ffT")
    gtmp, B_gt = acc, B_acc
    steps = [[0]] + [list(range(r, r + 4)) for r in range(1, NQ, 4)]
    for s, rs in enumerate(steps):
        NTOK = 128 * len(rs)
        for tt, i in enumerate(rs):
            kb.dma("sp", x1t[tt][:], L.x1_d[i * 128:(i + 1) * 128, :], B_x1[tt], writes=[B_x1[tt]])
            L.ln_to_hT(x1t[tt], B_x1[tt], xn, B_xn, stats, mv, rstd, B_st, h2T[:, :, tt * 128:(tt + 1) * 128], B_h2, 16)
        for j in range(22):
            for w, ch in enumerate((j, j + 22)):
                pbk = (2 * j + w) % 4
                for kc in range(KC):
                    op("pe", lambda e: e.matmul(PS[pbk][:, 0:NTOK], lhsT=WUP[:, kc, ch * 128:(ch + 1) * 128], rhs=h2T[:, kc, 0:NTOK],
                                                start=(kc == 0), stop=(kc == KC - 1)), reads=[B_W, B_h2], writes=[PSB[pbk]])
                u_, B_u, a_, B_a = upb[w], B_up[w], acc[w], B_acc[w]
                op("pool", lambda e: e.tensor_copy(out=u_[:, 0:2], in_=halo[:, ch, :]), reads=[B_halo], writes=[B_u])
                op("act", lambda e: e.activation(out=u_[:, 2:NTOK + 2], in_=PS[pbk][:, 0:NTOK], func=AF.Copy), reads=[PSB[pbk]], writes=[B_u])
                op("pool", lambda e: e.tensor_copy(out=halo[:, ch, :], in_=u_[:, NTOK:NTOK + 2]), reads=[B_u], writes=[B_halo])
                op("act", lambda e: e.activation(out=a_[:, 0:NTOK], in_=u_[:, 2:NTOK + 2], func=AF.Identity, scale=cw[:, ch, 2:3], bias=cb[:, ch:ch + 1]),
                   reads=[B_u, B_cw], writes=[B_a])
                op("dve", lambda e: e.scalar_tensor_tensor(out=a_[:, 0:NTOK], in0=u_[:, 1:NTOK + 1], scalar=cw[:, ch, 1:2], in1=a_[:, 0:NTOK],
                                                            op0=ALU.mult, op1=ALU.add), reads=[B_u, B_cw], writes=[B_a])
                op("dve", lambda e: e.scalar_tensor_tensor(out=a_[:, 0:NTOK], in0=u_[:, 0:NTOK], scalar=cw[:, ch, 0:1], in1=a_[:, 0:NTOK],
                                                            op0=ALU.mult, op1=ALU.add), reads=[B_u, B_cw], writes=[B_a])
            op("act", lambda e: e.activation(out=upb[1][:, 0:NTOK], in_=acc[1][:, 0:NTOK], func=AF.Silu), reads=[B_acc[1]], writes=[B_up[1]])
            op("dve", lambda e: e.tensor_tensor(out=ffT[:, j, 0:NTOK], in0=upb[1][:, 0:NTOK], in1=acc[0][:, 0:NTOK], op=ALU.mult),
               reads=[B_up[1], B_acc[0]], writes=[B_ff])
        if s == 0:
            op("pool", lambda e: e.tensor_scalar(out=halo[:].rearrange("p a b -> p (a b)"), in0=halo[:].rearrange("p a b -> p (a b)"),
                                                  scalar1=L.hv[:, 0:1], scalar2=None, op0=ALU.mult), reads=[L.B_tv], writes=[B_halo])
        for tt, i in enumerate(rs):
            pr_, B_p = x1t[tt], B_x1[tt]
            for half in range(2):
                bk = ((4, 5), (6, 0))[tt % 2][half]
                for j in range(22):
                    op("pe", lambda e: e.matmul(PS[bk][:, :], lhsT=ffT[:, j, tt * 128:(tt + 1) * 128], rhs=WDN[:, j, half * 512:(half + 1) * 512],
                                                start=(j == 0), stop=(j == 21)), reads=[B_W, B_ff], writes=[PSB[bk]])
                hs = slice(half * 512, (half + 1) * 512)
                op("dve", lambda e: e.tensor_tensor(out=gtmp[half][:], in0=PS[bk][:, :], in1=L.gates_bc[:, 1, hs], op=ALU.mult),
                   reads=[PSB[bk], L.B_gbc], writes=[B_gt[half]])
                op("pool", lambda e: e.scalar_tensor_tensor(out=pr_[:, hs], in0=pr_[:, hs], scalar=ALPHA, in1=gtmp[half][:], op0=ALU.mult, op1=ALU.add),
                   reads=[B_gt[half]], writes=[B_p]) if False else \
                op("dve", lambda e: e.scalar_tensor_tensor(out=pr_[:, hs], in0=pr_[:, hs], scalar=ALPHA, in1=gtmp[half][:], op0=ALU.mult, op1=ALU.add),
                   reads=[B_gt[half]], writes=[B_p])
            if i >= 1:
                ln_affine_store(kb, L, pr_, B_p, stats, mv, rstd, B_st, gb, B_gb, L.out_d[(i - 1) * 128:i * 128, :])


_NC = [None]


def kernel(**inputs):
    f = lambda a: np.ascontiguousarray(np.asarray(a, dtype=np.float32))
    if _NC[0] is None:
        _NC[0] = build()
    nc = _NC[0]
    tabs = [make_tables(0), make_tables(1)]
    shared = {
        "w_ada": f(inputs["w_ada"][0]), "b_ada": f(inputs["b_ada"][0]).reshape(1, -1), "w_in": f(inputs["w_in"][0]),
        "pe_ck": f(inputs["pe_ck"][0]), "w_ck1": f(inputs["w_ck1"][0]), "w_ck2": f(inputs["w_ck2"][0]),
        "pe_cv": f(inputs["pe_cv"][0]), "w_cv1": f(inputs["w_cv1"][0]), "w_cv2": f(inputs["w_cv2"][0]),
        "w_nsa_out": f(inputs["w_nsa_out"][0]),
        "s5_a_re": f(inputs["s5_a_re"][0]), "s5_a_im": f(inputs["s5_a_im"][0]),
        "s5_b_re": f(inputs["s5_b_re"][0]), "s5_b_im": f(inputs["s5_b_im"][0]),
        "s5_c_re": f(inputs["s5_c_re"][0]), "s5_c_im": f(inputs["s5_c_im"][0]),
        "s5_d": f(inputs["s5_d"][0]).reshape(512, 1), "s5_log_dt": f(inputs["s5_log_dt"][0]).reshape(1, 32),
        "w_s5_glu": f(inputs["w_s5_glu"][0]), "w_o": f(inputs["w_o"][0]),
        "ln1_g": f(inputs["ln1_g"][0]).reshape(1, -1), "ln1_b": f(inputs["ln1_b"][0]).reshape(1, -1),
        "w_up": f(inputs["w_up"][0]), "conv_w": f(inputs["conv_w"][0]), "conv_b": f(inputs["conv_b"][0]).reshape(1, -1),
        "w_down": f(inputs["w_down"][0]),
        "ln2_g": f(inputs["ln2_g"][0]).reshape(1, -1), "ln2_b": f(inputs["ln2_b"][0]).reshape(1, -1),
    }
    tabf = [{"tb_" + k: f(v) for k, v in tabs[hf].items()} for hf in range(2)]
    x = np.asarray(inputs["x"], dtype=np.float32)
    c = np.asarray(inputs["c"], dtype=np.float32)
    in_maps = []
    for core in range(8):
        b, hf = core // 2, core % 2
        m = dict(shared)
        m.update(tabf[hf])
        if hf == 1:
            m["x"] = f(x[b])
        else:
            xp = np.zeros((T, D), np.float32)
            xp[T // 2:] = x[b][:T // 2]
            m["x"] = xp
        m["cvec"] = f(c[b].reshape(8, 128).T)
        in_maps.append(m)
    res = run_bass_kernel_spmd(nc, in_maps, core_ids=list(range(8)))
    kernel.last = res
    out = np.empty((4, T, D), np.float32)
    for core in range(8):
        b, hf = core // 2, core % 2
        out[b, hf * (T // 2):(hf + 1) * (T // 2)] = np.asarray(res.results[core]["out"], dtype=np.float32)
    return out
```

```python
from contextlib import ExitStack
import math
import numpy as np
import concourse.bass as bass
import concourse.mybir as mybir
from concourse.bass_utils import run_bass_kernel_spmd

F32 = mybir.dt.float32
BF16 = mybir.dt.bfloat16
AF = mybir.ActivationFunctionType
ALU = mybir.AluOpType

T = 4096
NT = 32
D = 1024
KC = 8
DFF = 2816
NEG = -30000.0
ALPHA = 2.0 ** 0.25
STOP = [99]
Q0 = 15
NQ = NT - Q0
DBG = {}


class Buf:
    __slots__ = ("name", "w", "r", "dsem", "dcnt")

    def __init__(self, name):
        self.name = name
        self.w = None
        self.r = {}
        self.dsem = None
        self.dcnt = 0


class KB:
    def __init__(self, nc, es):
        self.nc = nc
        self.ges = es
        self.es = es
        self.eng = {"pe": nc.tensor, "act": nc.scalar, "dve": nc.vector,
                    "pool": nc.gpsimd, "sp": nc.sync}
        self.sem = {}
        self.cnt = {}
        self.seen = {}
        for e in self.eng:
            self.sem[e] = es.enter_context(nc.semaphore("s_" + e))
            self.cnt[e] = 0
            self.seen[e] = {}
        self.pool_sems = []
        self.live = []
        self.n = 0
        self.uid = 0

    def sb(self, name, shape, dt):
        self.uid += 1
        return self.es.enter_context(self.nc.sbuf_tensor(f"{name}_{self.uid}", list(shape), dt))

    def ps(self, name, shape, dt):
        return self.ges.enter_context(self.nc.psum_tensor(name, list(shape), dt))

    def buf(self, name="b"):
        return Buf(name)

    def _wait(self, e, ev):
        if ev is None:
            return
        sem, val = ev
        key = id(sem)
        if self.seen[e].get(key, 0) >= val:
            return
        self.eng[e].wait_ge(sem, val)
        self.seen[e][key] = val

    def _deps(self, e, reads, writes):
        for b in reads:
            self._wait(e, b.w)
        for b in writes:
            self._wait(e, b.w)
            for ev in list(b.r.values()):
                self._wait(e, ev)

    def _record(self, ev, reads, writes):
        for b in reads:
            if b not in writes:
                b.r[id(ev[0])] = ev
        for b in writes:
            b.w = ev
            b.r = {}

    def op(self, e, fn, reads=(), writes=()):
        self._deps(e, reads, writes)
        ins = fn(self.eng[e])
        self.cnt[e] += 1
        ins.then_inc(self.sem[e], 1)
        ev = (self.sem[e], self.cnt[e])
        if e == "pe":
            self.seen[e][id(self.sem[e])] = self.cnt[e]
        self._record(ev, reads, writes)
        self.n += 1
        return ins

    def dma(self, q, out, in_, dbuf, reads=(), writes=(), **kw):
        self._deps(q, reads, writes)
        if dbuf.dsem is None:
            if self.pool_sems:
                dbuf.dsem, dbuf.dcnt = self.pool_sems.pop()
            else:
                dbuf.dsem = self.ges.enter_context(self.nc.semaphore(f"d{len(self.live)}_{self.n}"))
                dbuf.dcnt = 0
            self.live.append(dbuf)
        ins = self.eng[q].dma_start(out=out, in_=in_, **kw)
        dbuf.dcnt += 16
        ins.then_inc(dbuf.dsem, 16)
        ev = (dbuf.dsem, dbuf.dcnt)
        self._record(ev, reads, writes)
        self.n += 1
        return ins

    def barrier(self, release=True):
        for e in self.eng:
            for f in self.eng:
                if f != e and self.cnt[f] > 0:
                    self._wait(e, (self.sem[f], self.cnt[f]))
            for b in self.live:
                self._wait(e, (b.dsem, b.dcnt))
        if release:
            for b in self.live:
                self.pool_sems.append((b.dsem, b.dcnt))
                b.dsem = None
            self.live = []


def make_tables(hf=1):
    t = {}
    t["ident"] = np.eye(128, dtype=np.float32)
    slopes = (2.0 ** (-8.0 * (np.arange(8) + 1) / 8)).astype(np.float32)
    tl = np.arange(128, dtype=np.float32)
    qa = np.zeros((3, 8, 128), np.float32)
    qb = np.zeros((3, 8, 128), np.float32)
    for h in range(8):
        qa[0, h, :] = slopes[h]
        qa[1, h, :] = slopes[h] * 128.0
        qa[2, h, :] = -slopes[h] * tl
        qb[2, h, :] = -slopes[h] * 128.0
    t["qaugA"] = qa
    t["qaugB"] = qb
    key = np.arange(T)
    kt = np.zeros((64, T), np.float32)
    kt[0] = key % 128
    kt[1] = key // 128
    kt[2] = 1.0
    if hf == 0:
        kt[1, :T // 2] = -8192.0
    for j in range(1, 62):
        kt[2 + j] = (key // 64 == j)
    t["kaug_tok"] = kt
    m = np.arange(256)
    pos = 16 * m + 15
    kc = np.zeros((3, 256), np.float32)
    kc[0] = pos % 128
    kc[1] = pos // 128
    kc[2] = 1.0
    kc[1, 0] = -8192.0
    if hf == 0:
        kc[1, :129] = -8192.0
    t["kaug_cmp"] = kc
    tv = np.ones((128, NT), np.float32)
    f0 = np.full((128, 64), -3e9, np.float32)
    hv = np.ones((128, 1), np.float32)
    if hf == 0:
        tv[:, :NT // 2] = 0.0
        f0[:, 32] = 1e9
        hv[:] = 0.0
    else:
        f0[:, 0] = 1e9
    t["tilevalid"] = tv
    t["f0"] = f0
    t["hv"] = hv
    kl = np.arange(128)[:, None]
    tq = np.arange(128)[None, :]
    t["tric"] = np.where(kl > tq, NEG, 0.0).astype(np.float32)
    t["triw"] = np.where(kl <= tq, NEG, 0.0).astype(np.float32)
    mc = np.zeros((128, 16, 128), np.float32)
    for dl in range(16):
        mc[:, dl, :] = np.where(16 * kl + 15 - tq > 128 * dl, NEG, 0.0)
    t["maskc"] = mc
    ov = np.zeros((256, 64), np.float32)
    for mm in range(1, 256):
        for jj in range(64):
            if 4 * jj <= mm <= 4 * jj + 4:
                ov[mm, jj] = 1.0
    t["ov"] = ov.reshape(2, 128, 64).transpose(1, 0, 2).copy()
    mw = np.zeros((128, 128), np.float32)
    aw = np.zeros((128, 128), np.float32)
    up = (np.arange(128) >= 64).astype(np.float32)
    for jw in range(128):
        jr = jw - 64
        if jr < -1:
            mw[:, jw] = 1.0
        elif jr == -1:
            mw[:, jw] = up
            aw[:, jw] = (1.0 - up) * 1e9
        elif jr == 0:
            aw[:, jw] = 1e9
        elif jr == 1:
            aw[:, jw] = np.where(up > 0, 1e9, -1e9)
        else:
            aw[:, jw] = -1e9
    t["mskw"] = mw
    t["addw"] = aw
    par = np.zeros((128, 2), np.float32)
    kk = np.arange(128)
    par[:, 0] = ((kk // 16) % 2 == 0)
    par[:, 1] = ((kk // 16) % 2 == 1)
    t["par"] = par
    return t


TABLE_SHAPES = {k: v.shape for k, v in make_tables().items()}


def build():
    nc = bass.Bass("TRN2", target_bir_lowering=False)

    def din(name, shape):
        return nc.dram_tensor(name, list(shape), F32, kind="ExternalInput").ap()

    x_d = din("x", [T, D])
    c_d = din("cvec", [128, 8])
    w_ada = din("w_ada", [D, 6 * D])
    b_ada = din("b_ada", [1, 6 * D])
    w_in = din("w_in", [D, 3864])
    pe_ck = din("pe_ck", [32, 64])
    w_ck1 = din("w_ck1", [32, 64, 128])
    w_ck2 = din("w_ck2", [128, 64])
    pe_cv = din("pe_cv", [32, 64])
    w_cv1 = din("w_cv1", [32, 64, 128])
    w_cv2 = din("w_cv2", [128, 64])
    w_nsa_out = din("w_nsa_out", [512, D])
    a_re = din("s5_a_re", [32, 64])
    a_im = din("s5_a_im", [32, 64])
    b_re = din("s5_b_re", [32, 64, 16])
    b_im = din("s5_b_im", [32, 64, 16])
    c_re = din("s5_c_re", [32, 16, 64])
    c_im = din("s5_c_im", [32, 16, 64])
    s5_d = din("s5_d", [512, 1])
    log_dt = din("s5_log_dt", [1, 32])
    w_glu = din("w_s5_glu", [512, 2 * D])
    w_o = din("w_o", [D, D])
    ln1_g = din("ln1_g", [1, D])
    ln1_b = din("ln1_b", [1, D])
    w_up = din("w_up", [D, 2 * DFF])
    conv_w = din("conv_w", [3, 2 * DFF])
    conv_b = din("conv_b", [1, 2 * DFF])
    w_down = din("w_down", [DFF, D])
    ln2_g = din("ln2_g", [1, D])
    ln2_b = din("ln2_b", [1, D])
    tb = {k: din("tb_" + k, shp) for k, shp in TABLE_SHAPES.items()}
    out_d = nc.dram_tensor("out", [T // 2, D], F32, kind="ExternalOutput").ap()
    x1_d = nc.dram_tensor("x1_scratch", [NQ * 128, D], F32, kind="Internal").ap()
    dbg_d = {}
    for k, shp in DBG.items():
        dbg_d[k] = nc.dram_tensor("dbg_" + k, list(shp), F32, kind="ExternalOutput").ap()

    with ExitStack() as ges:
        kb = KB(nc, ges)
        op = kb.op
        PS = [kb.ps(f"ps{i}", [128, 512], F32) for i in range(7)]
        PSB = [kb.buf(f"ps{i}") for i in range(7)]
        PT = kb.ps("pt", [128, 1024], BF16)
        PTB = kb.buf("pt")
        PTB2 = PTB
        PTB3 = PTB
        ident_f = kb.sb("ident_f", [128, 128], F32)
        ident_b = kb.sb("ident_b", [128, 128], BF16)
        gates_bc = kb.sb("gates_bc", [128, 2, D], F32)
        modT = kb.sb("modT", [128, 32], F32)
        one11 = kb.sb("one11", [1, 1], F32)
        B_id = kb.buf("ident")
        B_gbc = kb.buf("gbc")
        B_modT = kb.buf("modT")
        B_one = kb.buf("one")
        tilevalid = kb.sb("tilevalid", [128, NT], F32)
        hv = kb.sb("hv", [128, 1], F32)
        B_tv = kb.buf("tv")
        kb.dma("sp", tilevalid[:], tb["tilevalid"], B_tv, writes=[B_tv])
        kb.dma("sp", hv[:], tb["hv"], B_tv, writes=[B_tv])
        kb.dma("sp", ident_f[:], tb["ident"], B_id, writes=[B_id])
        cw_g = kb.sb("cwt_g", [128, 44, 3], F32)
        cb_g = kb.sb("cbt_g", [128, 44], F32)
        B_cwg = kb.buf("cwg")
        op("dve", lambda e: e.tensor_copy(out=ident_b[:], in_=ident_f[:]), reads=[B_id], writes=[B_id])
        op("pool", lambda e: e.memset(one11[:], 1.0), writes=[B_one])

        def dbg(name, src_ap, rbufs):
            if name in dbg_d:
                b = kb.buf("dbg")
                kb.dma("sp", dbg_d[name], src_ap, b, reads=rbufs)
                kb.live

        with ExitStack() as es0:
            kb.es = es0
            c_sb = kb.sb("c_sb", [128, 8], F32)
            sc = kb.sb("sc", [128, 8], F32)
            sc_bc = kb.sb("sc_bc", [128, 8, 128], F32)
            bada = kb.sb("bada", [128, 6 * D], F32)
            mod_bc = kb.sb("mod_bc", [128, 6 * D], F32)
            wst = [kb.sb(f"wst{i}", [128, 8, 512], F32) for i in range(2)]
            B_c, B_sc, B_bada, B_mod = kb.buf("c"), kb.buf("sc"), kb.buf("bada"), kb.buf("mod")
            B_wst = [kb.buf("wst0"), kb.buf("wst1")]
            kb.dma("sp", c_sb[:], c_d, B_c, writes=[B_c])
            kb.dma("sp", bada[:], b_ada.partition_broadcast(128).rearrange("p o n -> p (o n)"), B_bada, writes=[B_bada])
            op("act", lambda e: e.activation(out=sc[:], in_=c_sb[:], func=AF.Silu), reads=[B_c], writes=[B_sc])
            op("dve", lambda e: e.tensor_copy(out=sc_bc[:], in_=sc[:].unsqueeze(2).to_broadcast([128, 8, 128])),
               reads=[B_sc], writes=[B_sc])
            for j in range(12):
                st = wst[j % 2]
                kb.dma("sp", st[:], w_ada[:, j * 512:(j + 1) * 512].rearrange("(kc p) n -> p kc n", p=128),
                       B_wst[j % 2], writes=[B_wst[j % 2]])
                for kc in range(KC):
                    op("pe", lambda e: e.matmul(PS[j % 2][:, :], lhsT=sc_bc[:, kc, :], rhs=st[:, kc, :],
                                                start=(kc == 0), stop=(kc == KC - 1)),
                       reads=[B_sc, B_wst[j % 2]], writes=[PSB[j % 2]])
                op("dve", lambda e: e.tensor_tensor(out=mod_bc[:, j * 512:(j + 1) * 512], in0=PS[j % 2][:, :],
                                                     in1=bada[:, j * 512:(j + 1) * 512], op=ALU.add),
                   reads=[PSB[j % 2], B_bada], writes=[B_mod])
            op("dve", lambda e: e.tensor_copy(out=gates_bc[:, 0, :], in_=mod_bc[:, 2 * D:3 * D]), reads=[B_mod], writes=[B_gbc])
            op("dve", lambda e: e.tensor_copy(out=gates_bc[:, 1, :], in_=mod_bc[:, 5 * D:6 * D]), reads=[B_mod], writes=[B_gbc])
            for wi, w in enumerate((0, 1, 3, 4)):
                for fc in range(8):
                    col = w * D + fc * 128
                    idx = wi * 8 + fc
                    op("pe", lambda e: e.matmul(PS[2][:, idx:idx + 1], lhsT=mod_bc[0:1, col:col + 128],
                                                rhs=one11[0:1, 0:1], start=True, stop=True),
                       reads=[B_mod, B_one], writes=[PSB[2]])
            op("dve", lambda e: e.tensor_copy(out=modT[:], in_=PS[2][:, 0:32]), reads=[PSB[2]], writes=[B_modT])
            op("dve", lambda e: e.tensor_scalar(out=modT[:, 8:16], in0=modT[:, 8:16], scalar1=1.0, scalar2=None, op0=ALU.add),
               reads=[B_modT], writes=[B_modT])
            op("dve", lambda e: e.tensor_scalar(out=modT[:, 24:32], in0=modT[:, 24:32], scalar1=1.0, scalar2=None, op0=ALU.add),
               reads=[B_modT], writes=[B_modT])
            dbg("modT", modT[:], [B_modT])
            kb.barrier()
        kb.es = ges

        def ln_to_hT(xt, B_x, xn, B_xn, stats, mv, rstd, B_st, hT_out, B_hT, mcol):
            for hf in range(2):
                op("dve", lambda e: e.bn_stats(out=stats[:, hf, :], in_=xt[:, hf * 512:(hf + 1) * 512]),
                   reads=[B_x], writes=[B_st])
            op("dve", lambda e: e.bn_aggr(out=mv[:], in_=stats[:]), reads=[B_st], writes=[B_st])
            op("dve", lambda e: e.tensor_scalar(out=rstd[:], in0=mv[:, 1:2], scalar1=1e-5, scalar2=None, op0=ALU.add),
               reads=[B_st], writes=[B_st])
            op("act", lambda e: e.activation(out=rstd[:], in_=rstd[:], func=AF.Sqrt), reads=[B_st], writes=[B_st])
            op("dve", lambda e: e.reciprocal(out=rstd[:], in_=rstd[:]), reads=[B_st], writes=[B_st])
            op("dve", lambda e: e.tensor_scalar(out=xn[:], in0=xt[:], scalar1=mv[:, 0:1], scalar2=rstd[:, 0:1],
                                                 op0=ALU.subtract, op1=ALU.mult), reads=[B_x, B_st], writes=[B_xn])
            for kc in range(KC):
                op("pe", lambda e: e.transpose(out=PT[:, kc * 128:(kc + 1) * 128], in_=xn[:, kc * 128:(kc + 1) * 128],
                                               identity=ident_b[:]), reads=[B_xn, B_id], writes=[PTB, PTB2] if kc == 7 else [PTB])
            for kc in range(KC):
                op("act", lambda e: e.activation(out=hT_out[:, kc, :], in_=PT[:, kc * 128:(kc + 1) * 128], func=AF.Identity,
                                                 scale=modT[:, mcol + 8 + kc:mcol + 9 + kc], bias=modT[:, mcol + kc:mcol + kc + 1]),
                   reads=[PTB, B_modT], writes=[B_hT])

        def load_cast(dst_fn, src_ap, rows_kc, ncols, stg, B_stg, B_dst, eng="dve"):
            kb.dma("sp", stg[:, 0:rows_kc, 0:ncols], src_ap.rearrange("(kc p) n -> p kc n", p=128), B_stg, writes=[B_stg])
            for kc in range(rows_kc):
                if kc % 2 == 0:
                    op("dve", lambda e: e.tensor_copy(out=dst_fn(kc), in_=stg[:, kc, 0:ncols]), reads=[B_stg], writes=[B_dst])
                else:
                    op("act", lambda e: e.activation(out=dst_fn(kc), in_=stg[:, kc, 0:ncols], func=AF.Copy), reads=[B_stg], writes=[B_dst])

        with ExitStack() as esA:
            kb.es = esA
            oaT = kb.sb("oaT", [128, 4, NQ * 128], BF16)
            uT = kb.sb("uT", [128, 4, T], BF16)
            B_oaT = [kb.buf(f"oaT{i}") for i in range(NT)]
            B_uT = [kb.buf(f"uT{i}") for i in range(NT)]
            s5_ar = kb.sb("s5ar", [128, 16], F32); s5_ai = kb.sb("s5ai", [128, 16], F32); s5_ldt = kb.sb("s5ldt", [128, 16], F32)
            s5_par = kb.sb("s5par", [128, 2], F32); s5_dsk = kb.sb("s5dsk", [128, 4], F32)
            B_s5in = kb.buf("s5in")
            kb.dma("sp", s5_ar[:], a_re.rearrange("(pr g2) p -> (g2 p) pr", g2=2), B_s5in, writes=[B_s5in], allow_slow_non_contiguous=True)
            kb.dma("sp", s5_ai[:], a_im.rearrange("(pr g2) p -> (g2 p) pr", g2=2), B_s5in, writes=[B_s5in], allow_slow_non_contiguous=True)
            for g2 in range(2):
                kb.dma("sp", s5_ldt[g2 * 64:(g2 + 1) * 64, :],
                       log_dt.rearrange("o (pr g2) -> o g2 pr", g2=2)[:, g2, :].partition_broadcast(64).rearrange("p o n -> p (o n)"),
                       B_s5in, writes=[B_s5in], allow_slow_non_contiguous=True)
            kb.dma("sp", s5_par[:], tb["par"], B_s5in, writes=[B_s5in])
            kb.dma("sp", s5_dsk[:], s5_d.rearrange("(ct q) o -> q (ct o)", ct=4), B_s5in, writes=[B_s5in], allow_slow_non_contiguous=True)
            if STOP[0] >= 1:
                pass_a1(nc, kb, locals())
            kb.barrier()
            WMp = kb.sb("WMp", [128, 8, 2048], BF16)
            WGLp = kb.sb("WGLp", [128, 4, 2048], BF16)
            WOp = kb.sb("WOp", [128, 8, D], BF16)
            B_Wpre = kb.buf("Wpre")
            if STOP[0] >= 2:
                pass_s5(nc, kb, locals())
            kb.barrier()
            if STOP[0] >= 3:
                pass_a2(nc, kb, locals())
            kb.barrier()
        kb.es = ges
        if STOP[0] >= 4:
            with ExitStack() as esB:
                kb.es = esB
                pass_b(nc, kb, locals())
                kb.barrier()
            kb.es = ges
        kb.barrier(release=False)
    return nc


class NS:
    def __init__(self, d):
        self.__dict__.update(d)


def pass_a1(nc, kb, Ld):
    outer = Ld['esA']
    with ExitStack() as es1:
        d2 = dict(Ld)
        d2['esA'] = es1
        kb.es = es1
        _pass_a1_body(nc, kb, d2)
        kb.barrier()
    kb.es = outer


def _pass_a1_body(nc, kb, Ld):
    L = NS(Ld)
    op = kb.op
    PS, PSB, PT, PTB = L.PS, L.PSB, L.PT, L.PTB
    tb = L.tb
    sb, buf = kb.sb, kb.buf
    WQK = sb("WQK", [128, 8, 1088], BF16)
    WV = sb("WV", [128, 8, 280], BF16)
    WU = sb("WU", [128, 8, 512], BF16)
    KTs = sb("KTs", [128, 2, T], BF16)
    KTw = sb("KTw", [128, 2, 1024], BF16)
    V1 = sb("V1", [128, NT, 4, 65], BF16)
    kcTa = sb("kcTa", [128, 2, 256], BF16)
    vcx = sb("vcx", [128, 2, 2, 129], BF16)
    triC = sb("triC", [128, 4, 128], BF16)
    triW = sb("triW", [128, 4, 128], BF16)
    maskC = sb("maskC", [128, 16, 128], BF16)
    cw1 = sb("cw1", [128, 2, 32, 128], BF16)
    cw2 = sb("cw2", [128, 2, 64], BF16)
    peT = sb("peT", [128, 2, 32], BF16)
    cbias = sb("cbias", [128, 2], F32)
    qA = sb("qA", [67, 8, 128], BF16)
    qB = sb("qB", [67, 8, 128], BF16)
    f0t = sb("f0t", [128, 64], F32)
    mskw = sb("mskw", [128, 128], F32)
    addw = sb("addw", [128, 128], F32)
    B_W, B_KT, B_V1 = buf("W"), [buf(f"KT{i}") for i in range(NT)], [buf(f"V1{i}") for i in range(NT)]
    B_kc, B_vcx, B_cst = buf("kc"), buf("vcx"), buf("cst")
    with ExitStack() as ess:
        kb.es = ess
        stgs = [sb(f"stg{i}", [128, 8, 512], F32) for i in range(3)]
        B_stgs = [buf(f"stg{i}") for i in range(3)]
        rotc = [0]

        def rot():
            rotc[0] += 1
            return stgs[rotc[0] % 3], B_stgs[rotc[0] % 3]

        stg, B_stg = rot()
        op("pool", lambda e: e.memset(KTw[:], 0.0), writes=B_KT)
        op("pool", lambda e: e.memset(cw1[:], 0.0), writes=[B_cst])
        op("pool", lambda e: e.memset(peT[:], 0.0), writes=[B_cst])
        op("pool", lambda e: e.memset(WQK[:, :, 1024:1088], 0.0), writes=[B_W])
        L.load_cast(lambda kc: WQK[:, kc, 0:512], L.w_in[:, 0:512], 8, 512, stg, B_stg, B_W)
        stg, B_stg = rot()
        kb.dma("sp", stg[:, :, :], L.w_in[:, 512:1024].rearrange("(kc p) n -> p kc n", p=128), B_stg, writes=[B_stg])
        for kc in range(8):
            op("dve", lambda e: e.tensor_copy(out=WQK[:, kc, 768:1024], in_=stg[:, kc, 0:256]), reads=[B_stg], writes=[B_W])
            op("dve", lambda e: e.tensor_copy(out=WQK[:, kc, 512:640], in_=stg[:, kc, 256:384]), reads=[B_stg], writes=[B_W])
            op("dve", lambda e: e.tensor_copy(out=WV[:, kc, 0:128], in_=stg[:, kc, 384:512]), reads=[B_stg], writes=[B_W])
        stg, B_stg = rot()
        kb.dma("sp", stg[:, :, 0:280], L.w_in[:, 1024:1304].rearrange("(kc p) n -> p kc n", p=128), B_stg, writes=[B_stg])
        for kc in range(8):
            op("dve", lambda e: e.tensor_copy(out=WQK[:, kc, 640:768], in_=stg[:, kc, 0:128]), reads=[B_stg], writes=[B_W])
            op("dve", lambda e: e.tensor_copy(out=WV[:, kc, 128:280], in_=stg[:, kc, 128:280]), reads=[B_stg], writes=[B_W])
        stg, B_stg = rot()
        L.load_cast(lambda kc: WU[:, kc, :], L.w_in[:, 1304:1816], 8, 512, stg, B_stg, B_W)
        for kv, (w1d, w2d, ped) in enumerate(((L.w_ck1, L.w_ck2, L.pe_ck), (L.w_cv1, L.w_cv2, L.pe_cv))):
            for lh in range(4):
                stg, B_stg = rot()
                kb.dma("sp", stg[0:64, :, :].rearrange("p a (b e) -> p (a b) e", e=128)[:, 0:8, :],
                       w1d[lh * 8:(lh + 1) * 8].rearrange("l d e -> d l e"), B_stg, writes=[B_stg])
                op("dve", lambda e: e.tensor_copy(out=cw1[0:64, kv, lh * 8:(lh + 1) * 8, :],
                                                   in_=stg[0:64, :, :].rearrange("p a (b e) -> p (a b) e", e=128)[:, 0:8, :]),
                   reads=[B_stg], writes=[B_cst])
            kb.dma("sp", stg[:, 0, 0:64], w2d, B_stg, writes=[B_stg])
            op("dve", lambda e: e.tensor_copy(out=cw2[:, kv, :], in_=stg[:, 0, 0:64]), reads=[B_stg], writes=[B_cst])
            kb.dma("sp", stg[0:64, 0, 0:32], ped.rearrange("l d -> d l"), B_stg, writes=[B_stg], allow_slow_non_contiguous=True)
            op("dve", lambda e: e.tensor_copy(out=peT[0:64, kv, :], in_=stg[0:64, 0, 0:32]), reads=[B_stg], writes=[B_cst])
        stg, B_stg = rot()
        for tname, dst in (("tric", triC), ("triw", triW)):
            kb.dma("sp", stg[:, 0, 0:128], tb[tname], B_stg, writes=[B_stg])
            op("dve", lambda e: e.tensor_copy(out=dst[:], in_=stg[:, 0, 0:128].unsqueeze(1).to_broadcast([128, 4, 128])),
               reads=[B_stg], writes=[B_cst])
        kb.dma("sp", stg[:, 0:4, :].rearrange("p a b -> p (a b)"), tb["maskc"].rearrange("p a b -> p (a b)"), B_stg, writes=[B_stg])
        op("dve", lambda e: e.tensor_copy(out=maskC[:].rearrange("p a b -> p (a b)"),
                                           in_=stg[:, 0:4, :].rearrange("p a b -> p (a b)")), reads=[B_stg], writes=[B_cst])
        stg, B_stg = rot()
        op("pool", lambda e: e.memset(vcx[:], 0.0), writes=[B_vcx])
        op("pool", lambda e: e.memset(vcx[:, :, :, 64:65], 1.0), writes=[B_vcx])
        kb.dma("sp", stg[:, 0, 0:128], tb["ov"].rearrange("p a b -> p (a b)"), B_stg, writes=[B_stg])
        for g in range(2):
            op("dve", lambda e: e.tensor_copy(out=vcx[:, :, g, 65:129],
                                               in_=stg[:, 0, 0:128].rearrange("p (a b) -> p a b", b=64)),
               reads=[B_stg], writes=[B_vcx])
        op("pool", lambda e: e.memset(V1[:, :, :, 64:65], 1.0), writes=B_V1)
        op("pool", lambda e: e.memset(kcTa[:], 0.0), writes=[B_kc])
        stg, B_stg = rot()
        for q4 in range(2):
            stg, B_stg = rot()
            kb.dma("sp", stg[64:128, :, :].rearrange("p a b -> p (a b)")[:, 0:2048], tb["kaug_tok"][:, q4 * 2048:(q4 + 1) * 2048],
                   B_stg, writes=[B_stg])
            for g in range(2):
                op("dve", lambda e: e.tensor_copy(out=KTs[64:128, g, q4 * 2048:(q4 + 1) * 2048],
                                                   in_=stg[64:128, :, :].rearrange("p a b -> p (a b)")[:, 0:2048]), reads=[B_stg], writes=B_KT)
        stg, B_stg = rot()
        kb.dma("sp", stg[64:67, 0, 0:256], tb["kaug_cmp"], B_stg, writes=[B_stg])
        for g in range(2):
            op("dve", lambda e: e.tensor_copy(out=kcTa[64:67, g, :], in_=stg[64:67, 0, 0:256]), reads=[B_stg], writes=[B_kc])
        for qt, qn in ((qA, "qaugA"), (qB, "qaugB")):
            stg, B_stg = rot()
            kb.dma("sp", stg[64:67, 0:2, :].rearrange("p a b -> p (a b)"), tb[qn].rearrange("p a b -> p (a b)"), B_stg, writes=[B_stg])
            op("dve", lambda e: e.tensor_copy(out=qt[64:67, :, :].rearrange("p a b -> p (a b)"),
                                               in_=stg[64:67, 0:2, :].rearrange("p a b -> p (a b)")), reads=[B_stg], writes=[B_cst])
        kb.dma("sp", f0t[:], tb["f0"], B_cst, writes=[B_cst])
        kb.dma("sp", mskw[:], tb["mskw"], B_cst, writes=[B_cst])
        kb.dma("sp", addw[:], tb["addw"], B_cst, writes=[B_cst])
        for kv in range(2):
            for l in range(32):
                op("pe", lambda e: e.matmul(PS[2][:, kv:kv + 1], lhsT=cw1[:, kv, l, :], rhs=peT[:, kv, l:l + 1],
                                            start=(l == 0), stop=(l == 31)), reads=[B_cst], writes=[PSB[2]])
        op("dve", lambda e: e.tensor_copy(out=cbias[:], in_=PS[2][:, 0:2]), reads=[PSB[2]], writes=[B_cst])
        kb.barrier()
    kb.es = L.esA
    xt = [sb(f"xt{i}", [128, D], F32) for i in range(3)]
    B_xt = [buf("xt0"), buf("xt1"), buf("xt2")]
    xn = sb("xn", [128, D], BF16); B_xn = buf("xn")
    stats = sb("stats", [128, 2, 6], F32); mv = sb("mv", [128, 2], F32); rstd = sb("rstd", [128, 1], F32)
    B_st = buf("st")
    hTs = [sb(f"hT{i}", [128, 8, 128], BF16) for i in range(2)]; B_hTs = [buf("hT0"), buf("hT1")]
    QTa = sb("QTa", [128, 8, 128], BF16); B_Q = buf("Q")
    cmpT = sb("cmpT", [128, 2, 2, 144], BF16); B_cmp = buf("cmp")
    hact = sb("hact", [128, 2, 2, 8], BF16); B_hact = buf("hact")
    hpad = sb("hpad", [128, 2, 128], BF16); B_hpad = buf("hpad")
    gsig = sb("gsig", [128, 24], F32); B_gs = buf("gsig")
    Pt = [sb(f"Pt{i}", [128, 512], BF16) for i in range(4)]
    B_Pt = [buf(f"Pt{i}") for i in range(4)]
    osb = [sb(f"osb{g}", [128, 3, 4, 65], F32) for g in range(2)]; B_osb = [buf("osb0"), buf("osb1")]
    uimp = [sb(f"uimp{g}", [128, 4, 64], F32) for g in range(2)]
    PTB2 = L.PTB2
    imp = sb("imp", [128, 64], F32); imp2 = sb("imp2", [128, 64], F32); impw = sb("impw", [128, 64], F32)
    m8 = sb("m8", [128, 16], F32)
    B_imp = buf("imp")
    QS = [sb(f"QS{g}", [128, 4, 128], BF16) for g in range(2)]; B_QS = [buf("QS0"), buf("QS1")]
    den = [sb(f"den{g}", [128, 3, 4], F32) for g in range(2)]; fac = [sb(f"fac{g}", [128, 3, 4], F32) for g in range(2)]
    B_fac = [buf("fac0"), buf("fac1")]
    otok = sb("otok", [128, 8, 64], BF16); B_ot = buf("otok")
    otmp = sb("otmp", [128, 4, 64], F32); B_otmp = buf("otmp")
    op("pool", lambda e: e.memset(cmpT[:], 0.0), writes=[B_cmp])
    op("pool", lambda e: e.memset(QTa[:], 0.0), writes=[B_Q])
    for g in range(2):
        op("pool", lambda e: e.memset(QS[g][:], 0.0), writes=[B_QS[g]])
    sels = [sb(f"sel128_{g}", [128, 128], BF16) for g in range(2)]
    B_sel = [buf("sel0"), buf("sel1")]
    for g in range(2):
        op("pool", lambda e: e.memset(sels[g][:], 0.0), writes=[B_sel[g]])
    pcount = [0]

    def next_pt():
        pcount[0] += 1
        return pcount[0] % 4

    def scount_next(c=[0]):
        c[0] += 1
        return (0, 1, 6)[c[0] % 3]

    modT, ident_b = L.modT, L.ident_b

    def ln_a(t):
        x_, B_x = xt[t % 3], B_xt[t % 3]
        for hf in range(2):
            op("dve", lambda e: e.bn_stats(out=stats[:, hf, :], in_=x_[:, hf * 512:(hf + 1) * 512]), reads=[B_x], writes=[B_st])
        op("dve", lambda e: e.bn_aggr(out=mv[:], in_=stats[:]), reads=[B_st], writes=[B_st])
        op("dve", lambda e: e.tensor_scalar(out=rstd[:], in0=mv[:, 1:2], scalar1=1e-5, scalar2=None, op0=ALU.add), reads=[B_st], writes=[B_st])
        op("act", lambda e: e.activation(out=rstd[:], in_=rstd[:], func=AF.Sqrt), reads=[B_st], writes=[B_st])
        op("dve", lambda e: e.reciprocal(out=rstd[:], in_=rstd[:]), reads=[B_st], writes=[B_st])
        op("dve", lambda e: e.tensor_scalar(out=xn[:], in0=x_[:], scalar1=mv[:, 0:1], scalar2=rstd[:, 0:1], op0=ALU.subtract, op1=ALU.mult),
           reads=[B_x, B_st], writes=[B_xn])

    def ln_b(t):
        h_, B_h = hTs[t % 2], B_hTs[t % 2]
        for kc in range(KC):
            op("pe", lambda e: e.transpose(out=PT[:, kc * 128:(kc + 1) * 128], in_=xn[:, kc * 128:(kc + 1) * 128], identity=ident_b[:]),
               reads=[B_xn, L.B_id], writes=[PTB, PTB2] if kc == 7 else ([PTB, L.PTB3] if kc == 6 else [PTB]))
        for kc in range(KC):
            op("act", lambda e: e.activation(out=h_[:, kc, :], in_=PT[:, kc * 128:(kc + 1) * 128], func=AF.Identity,
                                             scale=modT[:, 8 + kc:9 + kc], bias=modT[:, kc:kc + 1]),
               reads=[PTB, PTB2, L.B_modT] if kc == 7 else ([PTB, L.PTB3, L.B_modT] if kc == 6 else [PTB, L.B_modT]), writes=[B_h])

    pend_ot = [None]

    def flush_ot():
        if pend_ot[0] is None:
            return
        ti = pend_ot[0]
        pend_ot[0] = None
        osl = slice((ti - Q0) * 128, (ti - Q0 + 1) * 128)
        for c in range(4):
            op("pe", lambda e: e.transpose(out=PT[:, c * 128:(c + 1) * 128], in_=otok[:, 2 * c:2 * c + 2, :].rearrange("p a b -> p (a b)"),
                                           identity=L.ident_b[:]), reads=[B_ot, L.B_id], writes=[PTB])
        op("act", lambda e: e.activation(out=L.oaT[:, :, osl], in_=PT[:, 0:512].rearrange("p (a b) -> p a b", b=128), func=AF.Copy),
           reads=[PTB], writes=[L.B_oaT[ti]])

    for t0 in range(2):
        kb.dma("sp", xt[t0][:], L.x_d[t0 * 128:(t0 + 1) * 128, :], B_xt[t0], writes=[B_xt[t0]])
    ln_a(0)
    ln_b(0)
    for i in range(NT):
        if i + 2 < NT:
            kb.dma("sp", xt[(i + 2) % 3][:], L.x_d[(i + 2) * 128:(i + 3) * 128, :], B_xt[(i + 2) % 3], writes=[B_xt[(i + 2) % 3]])
        if i + 1 < NT:
            ln_a(i + 1)
        hT, B_hT = hTs[i % 2], B_hTs[i % 2]
        lnb_done = [i + 1 >= NT]
        tsl = slice(i * 128, (i + 1) * 128)
        isq = i >= Q0
        for g in (range(2) if isq else ()):
            for hh in range(4):
                h = g * 4 + hh
                for kc in range(KC):
                    op("pe", lambda e: e.matmul(PS[g][:, hh * 128:(hh + 1) * 128], lhsT=WQK[:, kc, h * 64:h * 64 + 128],
                                                rhs=hT[:, kc, :], start=(kc == 0), stop=(kc == KC - 1)),
                       reads=[B_W, B_hT], writes=[PSB[g]])
            op("act", lambda e: e.activation(out=QTa[0:64, g * 4:(g + 1) * 4, :].rearrange("p a b -> p (a b)"),
                                             in_=PS[g][0:64, :], func=AF.Copy, scale=0.125), reads=[PSB[g]], writes=[B_Q])
        if isq:
            op("dve", lambda e: e.scalar_tensor_tensor(out=QTa[64:67, :, :], in0=qB[64:67, :, :], scalar=float(i), in1=qA[64:67, :, :],
                                                        op0=ALU.mult, op1=ALU.add), reads=[B_cst], writes=[B_Q])
        for grp in range(2):
            for s4 in range(4):
                c0 = 512 + grp * 256 + s4 * 64
                for kc in range(KC):
                    op("pe", lambda e: e.matmul(PS[2 + grp][:, s4 * 128:(s4 + 1) * 128], lhsT=WQK[:, kc, c0:c0 + 128],
                                                rhs=hT[:, kc, :], start=(kc == 0), stop=(kc == KC - 1)),
                       reads=[B_W, B_hT], writes=[PSB[2 + grp]])
        wsl = slice((i % 8) * 128, (i % 8 + 1) * 128)
        op("act", lambda e: e.activation(out=KTs[0:64, :, tsl], in_=PS[2][0:64, 0:256].rearrange("p (b c) -> p b c", b=2),
                                         func=AF.Copy), reads=[PSB[2]], writes=[B_KT[i]])
        op("act", lambda e: e.activation(out=KTw[0:64, :, wsl], in_=PS[2][0:64, 256:512].rearrange("p (b c) -> p b c", b=2),
                                         func=AF.Copy), reads=[PSB[2]], writes=[B_KT[i]])
        op("dve", lambda e: e.tensor_copy(out=KTw[64:67, :, wsl], in_=KTs[64:67, :, tsl]), reads=[B_cst], writes=[B_KT[i]])
        op("dve", lambda e: e.tensor_copy(out=cmpT[0:64, :, :, 16:144], in_=PS[3][0:64, :].rearrange("p (a b c) -> p a b c", a=2, b=2)),
           reads=[PSB[3]], writes=[B_cmp])
        flush_ot()
        for kc in range(KC):
            op("pe", lambda e: e.matmul(PS[4][:, 0:280], lhsT=hT[:, kc, :], rhs=WV[:, kc, :], start=(kc == 0), stop=(kc == KC - 1)),
               reads=[B_W, B_hT], writes=[PSB[4]])
        op("dve", lambda e: e.tensor_copy(out=V1[:, i, :, 0:64], in_=PS[4][:, 0:256].rearrange("p (a b) -> p a b", b=64)),
           reads=[PSB[4]], writes=[B_V1[i]])
        op("act", lambda e: e.activation(out=gsig[:], in_=PS[4][:, 256:280], func=AF.Sigmoid), reads=[PSB[4]], writes=[B_gs])
        if not isq and not lnb_done[0]:
            ln_b(i + 1)
            lnb_done[0] = True
        for ct in range(4):
            for kc in range(KC):
                op("pe", lambda e: e.matmul(PS[5][:, ct * 128:(ct + 1) * 128], lhsT=WU[:, kc, ct * 128:(ct + 1) * 128],
                                            rhs=hT[:, kc, :], start=(kc == 0), stop=(kc == KC - 1)),
                   reads=[B_W, B_hT], writes=[PSB[5]])
        op("act", lambda e: e.activation(out=L.uT[:, :, tsl], in_=PS[5][:, :].rearrange("p (a b) -> p a b", b=128), func=AF.Identity,
                                         scale=L.tilevalid[:, i:i + 1]), reads=[PSB[5], L.B_tv], writes=[L.B_uT[i]])
        for kv in range(2):
            for g in range(2):
                o0 = (kv * 2 + g) * 8
                for l in range(32):
                    op("pe", lambda e: e.matmul(PS[6][:, o0:o0 + 8], lhsT=cw1[:, kv, l, :], rhs=cmpT[:, kv, g, l:l + 113:16],
                                                start=(l == 0), stop=(l == 31)), reads=[B_cst, B_cmp], writes=[PSB[6]])
            op("act", lambda e: e.activation(out=hact[:, kv, :, :].rearrange("p a b -> p (a b)"), in_=PS[6][:, kv * 16:(kv + 1) * 16],
                                             func=AF.Silu, bias=cbias[:, kv:kv + 1]), reads=[PSB[6], B_cst], writes=[B_hact])
        op("dve", lambda e: e.tensor_copy(out=cmpT[0:64, :, :, 0:16], in_=cmpT[0:64, :, :, 128:144]), reads=[B_cmp], writes=[B_cmp])
        for g in range(2):
            op("pe", lambda e: e.matmul(PS[6][:, 64 + g * 8:64 + (g + 1) * 8], lhsT=cw2[:].rearrange("p a b -> p (a b)"), rhs=hact[:, 0, g, :],
                                        start=True, stop=True), reads=[B_cst, B_hact], writes=[PSB[6]])
        op("dve", lambda e: e.tensor_copy(out=kcTa[0:64, :, 8 * i:8 * i + 8], in_=PS[6][0:64, 64:80].rearrange("p (a b) -> p a b", b=8)),
           reads=[PSB[6]], writes=[B_kc])
        mt_i, mo = (8 * i) // 128, (8 * i) % 128
        op("pool", lambda e: e.memset(hpad[:], 0.0), writes=[B_hpad])
        op("pool", lambda e: e.tensor_copy(out=hpad[:, :, mo:mo + 8], in_=hact[:, 1, :, :]), reads=[B_hact], writes=[B_hpad])
        for g in range(2):
            op("pe", lambda e: e.matmul(PS[6][:, 128 + g * 64:128 + (g + 1) * 64], lhsT=hpad[:, g, :], rhs=cw2[:, 1, :],
                                        start=True, stop=True), reads=[B_cst, B_hpad], writes=[PSB[6]])
        op("dve", lambda e: e.tensor_tensor(out=vcx[:, mt_i, :, 0:64], in0=PS[6][:, 128:256].rearrange("p (a b) -> p a b", b=64),
                                             in1=vcx[:, mt_i, :, 0:64], op=ALU.add), reads=[PSB[6]], writes=[B_vcx])
        if isq:
            OB = [PS[2], PS[3], PS[4], PS[5]]
            OBB = [PSB[2], PSB[3], PSB[4], PSB[5]]
            Qgs = [QTa[:, g * 4:(g + 1) * 4, :].rearrange("p a b -> p (a b)") for g in range(2)]
            QSf = [QS[g][:].rearrange("p a b -> p (a b)") for g in range(2)]

            def emit_score(job):
                kind, g, kidx, first, last = job
                sbk = scount_next()
                extra = []
                if kind == "c":
                    mt = kidx
                    dl = i - 16 * mt
                    lhs, rl = kcTa[:, g, mt * 128:(mt + 1) * 128], [B_kc, B_Q]
                    if dl < 16:
                        for hh in range(4):
                            extra.append((PS[sbk][:, hh * 128:(hh + 1) * 128], L.ident_b[:], maskC[:, dl, :], [L.B_id, B_cst]))
                else:
                    kt = kidx
                    if kind == "s":
                        lhs = KTs[:, g, kt * 128:(kt + 1) * 128]
                    else:
                        lhs = KTw[:, g, (kt % 8) * 128:(kt % 8 + 1) * 128]
                        if kt == i - 4:
                            extra.append((PS[sbk][:, :], L.ident_b[:], triW[:].rearrange("p a b -> p (a b)"), [L.B_id, B_cst]))
                    if kt == i:
                        extra.append((PS[sbk][:, :], L.ident_b[:], triC[:].rearrange("p a b -> p (a b)"), [L.B_id, B_cst]))
                    rl = [B_KT[kt], B_QS[g] if kind == "s" else B_Q]
                op("pe", lambda e: e.matmul(PS[sbk][:, :], lhsT=lhs, rhs=(QSf[g] if kind == "s" else Qgs[g]), start=True, stop=(len(extra) == 0)),
                   reads=rl, writes=[PSB[sbk]])
                for xi, (oap, lt, rh, rb) in enumerate(extra):
                    op("pe", lambda e: e.matmul(oap, lhsT=lt, rhs=rh, start=False, stop=(xi == len(extra) - 1), skip_group_check=True),
                       reads=rb, writes=[PSB[sbk]])
                p = next_pt()
                op("act", lambda e: e.activation(out=Pt[p][:], in_=PS[sbk][:, :], func=AF.Exp), reads=[PSB[sbk]], writes=[B_Pt[p]])
                return p

            def emit_pv(job, p):
                kind, g, kidx, first, last = job
                if kind == "c":
                    rhs, ncol, rb = vcx[:, kidx, g, :], 129, B_vcx
                else:
                    vi = g if kind == "s" else 2 + g
                    rhs, ncol, rb = V1[:, kidx, vi, :], 65, B_V1[kidx]
                for hh in range(4):
                    op("pe", lambda e: e.matmul(OB[hh][:, 0:ncol], lhsT=Pt[p][:, hh * 128:(hh + 1) * 128], rhs=rhs,
                                                start=first, stop=last), reads=[B_Pt[p], rb], writes=[OBB[hh]])

            def epilogue(kind, g):
                bi = {"c": 0, "s": 1, "w": 2}[kind]
                o_, B_o = osb[g], B_osb[g]
                for hh in range(4):
                    if hh % 2:
                        op("dve", lambda e: e.tensor_copy(out=o_[:, bi, hh, :], in_=OB[hh][:, 0:65]), reads=[OBB[hh]], writes=[B_o])
                    else:
                        op("act", lambda e: e.activation(out=o_[:, bi, hh, :], in_=OB[hh][:, 0:65], func=AF.Copy), reads=[OBB[hh]], writes=[B_o])
                    if kind == "c":
                        op("dve", lambda e: e.tensor_copy(out=uimp[g][:, hh, :], in_=OB[hh][:, 65:129]), reads=[OBB[hh]], writes=[B_o])
                if kind == "c":
                    dn, B_d = den[g], B_fac[g]
                    op("dve", lambda e: e.tensor_scalar(out=dn[:, 0, :], in0=o_[:, 0, :, 64], scalar1=1e-30, scalar2=None, op0=ALU.max),
                       reads=[B_o], writes=[B_d])
                    op("dve", lambda e: e.reciprocal(out=dn[:, 0, :], in_=dn[:, 0, :]), reads=[B_d], writes=[B_d])
                    op("dve", lambda e: e.tensor_scalar(out=imp[:], in0=uimp[g][:, 0, :], scalar1=dn[:, 0, 0:1], scalar2=None, op0=ALU.mult),
                       reads=[B_o, B_d], writes=[B_imp])
                    for hh in range(1, 4):
                        op("dve", lambda e: e.scalar_tensor_tensor(out=imp[:], in0=uimp[g][:, hh, :], scalar=dn[:, 0, hh:hh + 1], in1=imp[:],
                                                                    op0=ALU.mult, op1=ALU.add), reads=[B_o, B_d], writes=[B_imp])
                    w0 = 64 - 2 * i
                    op("dve", lambda e: e.tensor_tensor(out=imp2[:], in0=imp[:], in1=mskw[:, w0:w0 + 64], op=ALU.mult),
                       reads=[B_imp, B_cst], writes=[B_imp])
                    op("dve", lambda e: e.tensor_tensor(out=imp2[:], in0=imp2[:], in1=addw[:, w0:w0 + 64], op=ALU.add),
                       reads=[B_imp, B_cst], writes=[B_imp])
                    op("dve", lambda e: e.tensor_tensor(out=imp2[:], in0=imp2[:], in1=f0t[:], op=ALU.max), reads=[B_cst], writes=[B_imp])
                    op("dve", lambda e: e.max(out=m8[:, 0:8], in_=imp2[:]), reads=[B_imp], writes=[B_imp])
                    op("dve", lambda e: e.match_replace(out=impw[:], in_to_replace=m8[:, 0:8], in_values=imp2[:], imm_value=-3e9),
                       reads=[B_imp], writes=[B_imp])
                    op("dve", lambda e: e.max(out=m8[:, 8:16], in_=impw[:]), reads=[B_imp], writes=[B_imp])
                    op("dve", lambda e: e.tensor_scalar(out=sels[g][:, 67:128], in0=imp2[:, 1:62], scalar1=m8[:, 15:16], scalar2=None, op0=ALU.is_ge),
                       reads=[B_imp], writes=[B_sel[g]])
                if kind == "s":
                    dn, fc_, B_d = den[g], fac[g], B_fac[g]
                    op("dve", lambda e: e.tensor_scalar(out=dn[:, 1:3, :], in0=o_[:, 1:3, :, 64], scalar1=1e-30, scalar2=None, op0=ALU.max),
                       reads=[B_o], writes=[B_d])
                    op("dve", lambda e: e.reciprocal(out=dn[:, 1:3, :], in_=dn[:, 1:3, :]), reads=[B_d], writes=[B_d])
                    op("dve", lambda e: e.tensor_tensor(out=fc_[:], in0=dn[:], in1=gsig[:, g * 12:(g + 1) * 12].rearrange("p (h b) -> p b h", b=3),
                                                         op=ALU.mult), reads=[B_d, B_gs], writes=[B_d])
                    for hh in range(4):
                        h = g * 4 + hh
                        op("dve", lambda e: e.tensor_scalar(out=otmp[:, hh, :], in0=o_[:, 0, hh, 0:64], scalar1=fc_[:, 0, hh:hh + 1], scalar2=None,
                                                             op0=ALU.mult), reads=[B_o, B_d], writes=[B_otmp])
                        op("dve", lambda e: e.scalar_tensor_tensor(out=otmp[:, hh, :], in0=o_[:, 1, hh, 0:64], scalar=fc_[:, 1, hh:hh + 1],
                                                                    in1=otmp[:, hh, :], op0=ALU.mult, op1=ALU.add),
                           reads=[B_o, B_d], writes=[B_otmp])
                        op("dve", lambda e: e.scalar_tensor_tensor(out=otok[:, h, :], in0=o_[:, 2, hh, 0:64], scalar=fc_[:, 2, hh:hh + 1],
                                                                    in1=otmp[:, hh, :], op0=ALU.mult, op1=ALU.add),
                           reads=[B_o, B_d, B_otmp], writes=[B_ot])

            def epi_c_tail(g):
                c0, pb_ = (896, PTB2) if g == 0 else (768, L.PTB3)
                op("pe", lambda e: e.transpose(out=PT[:, c0:c0 + 128], in_=sels[g][:], identity=L.ident_b[:]), reads=[B_sel[g], L.B_id], writes=[pb_])
                op("dve", lambda e: e.tensor_scalar(out=QS[g][64:128], in0=PT[64:128, c0:c0 + 128].unsqueeze(1).to_broadcast([64, 4, 128]),
                                                     scalar1=-1.0, scalar2=-NEG, op0=ALU.add, op1=ALU.mult), reads=[pb_], writes=[B_QS[g]])
                op("pool", lambda e: e.tensor_copy(out=QS[g][0:67], in_=QTa[0:67, g * 4:(g + 1) * 4, :]), reads=[B_Q], writes=[B_QS[g]])

            jobs = []
            for kind in ("c", "w", "s"):
                for g in range(2):
                    if kind == "c":
                        ks = [0] if 8 * i + 7 < 128 else [0, 1]
                    elif kind == "w":
                        ks = list(range(max(0, i - 4), i + 1))
                    else:
                        ks = list(range(0, i + 1))
                    for n_, k_ in enumerate(ks):
                        jobs.append((kind, g, k_, n_ == 0, n_ == len(ks) - 1))
            pend = []
            n_s = [0]
            for job in jobs:
                if job[0] == "s" and job[3] and job[1] == 0:
                    epi_c_tail(0)
                    epi_c_tail(1)
                    n_s[0] = 0
                if job[0] == "s":
                    n_s[0] += 1
                    if n_s[0] == 4 and not lnb_done[0]:
                        ln_b(i + 1)
                        lnb_done[0] = True
                p = emit_score(job)
                pend.append((job, p))
                if len(pend) > 2:
                    pj = pend.pop(0)
                    emit_pv(*pj)
                    if pj[0][4]:
                        epilogue(pj[0][0], pj[0][1])
            for pj in pend:
                emit_pv(*pj)
                if pj[0][4]:
                    epilogue(pj[0][0], pj[0][1])
        if not lnb_done[0]:
            ln_b(i + 1)
            lnb_done[0] = True
        if isq:
            pend_ot[0] = i
    flush_ot()
    if "oaT" in L.dbg_d:
        b = kb.buf("dbg")
        st2 = sb("dbgst", [128, 4, 512], F32)
        op("dve", lambda e: e.tensor_copy(out=st2[:], in_=L.oaT[:, :, 0:512]), reads=L.B_oaT, writes=[b])
        kb.dma("sp", L.dbg_d["oaT"], st2[:], b, reads=[b])


def pass_s5(nc, kb, Ld):
    L = NS(Ld)
    op = kb.op
    PS, PSB, PT, PTB = L.PS, L.PSB, L.PT, L.PTB
    sb, buf = kb.sb, kb.buf
    PI = math.pi
    with ExitStack() as es5:
        kb.es = es5
        BBT = sb("BBT", [128, 16, 2, 128], BF16)
        BBTn = sb("BBTn", [128, 16, 128], BF16)
        Cq = sb("Cq", [128, 16, 4, 128], BF16)
        EI = sb("EI", [128, 2, 16, 128], F32)
        EF = sb("EF", [128, 2, 16, 128], F32)
        L128 = sb("L128", [128, 2, 16], F32)
        dsk = sb("dsk", [128, 4], F32)
        ones = sb("ones", [128, 128], F32)
        carry = sb("carry", [128, 2, 16], F32)
        zl = sb("zl", [128, 2, 16], F32)
        B_tab, B_car, B_zl = buf("tab"), buf("carry"), buf("zl")
        with ExitStack() as est:
            kb.es = est
            ar, ai, ldt = L.s5_ar, L.s5_ai, L.s5_ldt
            dt = sb("dt", [128, 16], F32); lrd = sb("lrd", [128, 16], F32); ang = sb("ang", [128, 16], F32)
            mag = sb("mag", [128, 16], F32); mgi = sb("mgi", [128, 16], F32)
            sn = sb("sn", [128, 16], F32); cs = sb("cs", [128, 16], F32); tmp = sb("tmp", [128, 16], F32); tmp2 = sb("tmp2", [128, 16], F32)
            lb = sb("lb", [128, 2, 16], F32); lbi = sb("lbi", [128, 2, 16], F32); pw = sb("pw", [128, 2, 16], F32)
            coef = sb("coef", [128, 2, 16], F32); den = sb("dens", [128, 16], F32)
            bsb = sb("bsb", [128, 2, 16, 16], F32); bb = sb("bb", [128, 2, 16, 16], F32); bt = sb("bt", [128, 16, 16], F32)
            Apr = sb("Apr", [128, 128], BF16)
            Cn = sb("Cn", [128, 2, 4, 64], F32)
            par = L.s5_par
            et1 = sb("et1", [128, 16, 64], F32); et2 = sb("et2", [128, 16, 64], F32)
            B_s = L.B_s5in; B_apr = buf("apr"); B_et = buf("et")
            for g2 in range(2):
                for ri, bd in enumerate((L.b_re, L.b_im)):
                    kb.dma("sp", bsb[g2 * 64:(g2 + 1) * 64, ri, :, :], bd.rearrange("(pr g2) p h -> g2 p pr h", g2=2)[g2],
                           B_s, writes=[B_s])
            for ri, cd in enumerate((L.c_re, L.c_im)):
                kb.dma("sp", Cn[:, ri, :, :], cd.rearrange("(ct gl) h p -> (gl h) ct p", ct=4), B_s, writes=[B_s])
            op("dve", lambda e: e.tensor_copy(out=dsk[:], in_=L.s5_dsk[:]), reads=[B_s], writes=[B_tab])
            op("pool", lambda e: e.memset(ones[:], 1.0), writes=[B_tab])
            op("pool", lambda e: e.memset(carry[:], 0.0), writes=[B_car])
            op("pool", lambda e: e.memset(Cq[:], 0.0), writes=[B_tab])
            R, W = [B_s], [B_s]
            dv = lambda f: op("dve", f, reads=R, writes=W)
            ac = lambda f: op("act", f, reads=R, writes=W)
            ac(lambda e: e.activation(out=dt[:], in_=ldt[:], func=AF.Exp))
            dv(lambda e: e.tensor_scalar(out=ar[:], in0=ar[:], scalar1=-1e-4, scalar2=None, op0=ALU.min))
            dv(lambda e: e.tensor_tensor(out=lrd[:], in0=ar[:], in1=dt[:], op=ALU.mult))
            dv(lambda e: e.tensor_tensor(out=ang[:], in0=ai[:], in1=dt[:], op=ALU.mult))
            ac(lambda e: e.activation(out=mag[:], in_=lrd[:], func=AF.Exp))
            ac(lambda e: e.activation(out=mgi[:], in_=lrd[:], func=AF.Exp, scale=-1.0))
            ti = sb("ti", [128, 16], mybir.dt.int32)

            def rred(dst, shift):
                dv(lambda e: e.tensor_scalar(out=tmp2[:], in0=ang[:], scalar1=shift, scalar2=None, op0=ALU.add))
                dv(lambda e: e.tensor_scalar(out=tmp[:], in0=tmp2[:], scalar1=1.0 / (2 * PI), scalar2=None, op0=ALU.mult))
                dv(lambda e: e.tensor_copy(out=ti[:], in_=tmp[:]))
                dv(lambda e: e.tensor_copy(out=tmp[:], in_=ti[:]))
                dv(lambda e: e.scalar_tensor_tensor(out=tmp2[:], in0=tmp[:], scalar=-2 * PI, in1=tmp2[:], op0=ALU.mult, op1=ALU.add))
                dv(lambda e: e.tensor_scalar(out=tmp[:], in0=tmp2[:], scalar1=PI, scalar2=2 * PI, op0=ALU.is_gt, op1=ALU.mult))
                dv(lambda e: e.tensor_tensor(out=tmp2[:], in0=tmp2[:], in1=tmp[:], op=ALU.subtract))
                dv(lambda e: e.tensor_scalar(out=tmp[:], in0=tmp2[:], scalar1=-PI, scalar2=2 * PI, op0=ALU.is_lt, op1=ALU.mult))
                dv(lambda e: e.tensor_tensor(out=tmp2[:], in0=tmp2[:], in1=tmp[:], op=ALU.add))
                ac(lambda e: e.activation(out=dst[:], in_=tmp2[:], func=AF.Sin))

            rred(sn, 0.0)
            rred(cs, 0.5 * PI)
            dv(lambda e: e.tensor_tensor(out=lb[:, 0, :], in0=mag[:], in1=cs[:], op=ALU.mult))
            dv(lambda e: e.tensor_tensor(out=lb[:, 1, :], in0=mag[:], in1=sn[:], op=ALU.mult))
            dv(lambda e: e.tensor_tensor(out=lbi[:, 0, :], in0=mgi[:], in1=cs[:], op=ALU.mult))
            dv(lambda e: e.scalar_tensor_tensor(out=lbi[:, 1, :], in0=mgi[:], scalar=-1.0, in1=sn[:], op0=ALU.mult, op1=ALU.mult))
            dv(lambda e: e.tensor_tensor(out=den[:], in0=ar[:], in1=ar[:], op=ALU.mult))
            dv(lambda e: e.tensor_tensor(out=tmp[:], in0=ai[:], in1=ai[:], op=ALU.mult))
            dv(lambda e: e.tensor_tensor(out=den[:], in0=den[:], in1=tmp[:], op=ALU.add))
            dv(lambda e: e.reciprocal(out=den[:], in_=den[:]))
            dv(lambda e: e.tensor_scalar(out=tmp2[:], in0=lb[:, 0, :], scalar1=-1.0, scalar2=None, op0=ALU.add))
            dv(lambda e: e.tensor_tensor(out=tmp[:], in0=tmp2[:], in1=ar[:], op=ALU.mult))
            dv(lambda e: e.tensor_tensor(out=coef[:, 0, :], in0=lb[:, 1, :], in1=ai[:], op=ALU.mult))
            dv(lambda e: e.tensor_tensor(out=coef[:, 0, :], in0=coef[:, 0, :], in1=tmp[:], op=ALU.add))
            dv(lambda e: e.tensor_tensor(out=coef[:, 0, :], in0=coef[:, 0, :], in1=den[:], op=ALU.mult))
            dv(lambda e: e.tensor_tensor(out=tmp[:], in0=tmp2[:], in1=ai[:], op=ALU.mult))
            dv(lambda e: e.tensor_tensor(out=coef[:, 1, :], in0=lb[:, 1, :], in1=ar[:], op=ALU.mult))
            dv(lambda e: e.tensor_tensor(out=coef[:, 1, :], in0=coef[:, 1, :], in1=tmp[:], op=ALU.subtract))
            dv(lambda e: e.tensor_tensor(out=coef[:, 1, :], in0=coef[:, 1, :], in1=den[:], op=ALU.mult))
            cbr = lambda k: coef[:, k, :].unsqueeze(2).to_broadcast([128, 16, 16])
            dv(lambda e: e.tensor_tensor(out=bb[:, 0], in0=bsb[:, 0], in1=cbr(0), op=ALU.mult))
            dv(lambda e: e.tensor_tensor(out=bt[:], in0=bsb[:, 1], in1=cbr(1), op=ALU.mult))
            dv(lambda e: e.tensor_tensor(out=bb[:, 0], in0=bb[:, 0], in1=bt[:], op=ALU.subtract))
            dv(lambda e: e.tensor_tensor(out=bb[:, 1], in0=bsb[:, 1], in1=cbr(0), op=ALU.mult))
            dv(lambda e: e.tensor_tensor(out=bt[:], in0=bsb[:, 0], in1=cbr(1), op=ALU.mult))
            dv(lambda e: e.tensor_tensor(out=bb[:, 1], in0=bb[:, 1], in1=bt[:], op=ALU.add))
            Apr2 = sb("Apr2", [128, 128], BF16)
            Aprs, B_aprs = [Apr, Apr2], [B_apr, buf("apr2")]
            ptv = [PS[0][:, 0:64].bitcast(BF16), PS[1][:, 0:64].bitcast(BF16)]
            it = 0
            for pr in range(16):
                prl = pr % 4
                for ri in range(2):
                    A_, B_A, pv, B_pv = Aprs[it % 2], B_aprs[it % 2], ptv[it % 2], PSB[it % 2]
                    it += 1
                    op("pool", lambda e: e.memset(A_[:], 0.0), writes=[B_A])
                    op("dve", lambda e: e.tensor_copy(out=A_[0:64, 32 * prl:32 * prl + 16], in_=bb[0:64, ri, pr, :]), reads=[B_s], writes=[B_A])
                    op("dve", lambda e: e.tensor_copy(out=A_[64:128, 32 * prl + 16:32 * prl + 32], in_=bb[64:128, ri, pr, :]),
                       reads=[B_s], writes=[B_A])
                    op("pe", lambda e: e.transpose(out=pv, in_=A_[:], identity=L.ident_b[:]), reads=[B_A, L.B_id], writes=[B_pv])
                    op("act", lambda e: e.activation(out=BBT[:, pr, ri, :], in_=pv, func=AF.Copy), reads=[B_pv], writes=[B_tab])
                    if ri == 1:
                        op("act", lambda e: e.activation(out=BBTn[:, pr, :], in_=pv, func=AF.Copy, scale=-1.0),
                           reads=[B_pv], writes=[B_tab])
            for ct in range(4):
                for ri in range(2):
                    for g2 in range(2):
                        op("dve", lambda e: e.tensor_scalar(out=Apr[:, g2 * 64:(g2 + 1) * 64], in0=Cn[:, ri, ct, :], scalar1=par[:, g2:g2 + 1],
                                                             scalar2=None, op0=ALU.mult), reads=[B_s], writes=[B_apr])
                    op("pe", lambda e: e.transpose(out=PT[:, 0:128], in_=Apr[:], identity=L.ident_b[:]), reads=[B_apr, L.B_id], writes=[PTB])
                    for prl in range(4):
                        pr = ct * 4 + prl
                        sl = slice(32 * prl, 32 * prl + 32)
                        if ri == 0:
                            op("dve", lambda e: e.tensor_copy(out=Cq[:, pr, 0, sl], in_=PT[:, sl]), reads=[PTB], writes=[B_tab])
                            op("dve", lambda e: e.tensor_scalar(out=Cq[:, pr, 3, sl], in0=PT[:, sl], scalar1=-1.0, scalar2=None, op0=ALU.mult),
                               reads=[PTB], writes=[B_tab])
                        else:
                            for k in (1, 2):
                                op("dve", lambda e: e.tensor_scalar(out=Cq[:, pr, k, sl], in0=PT[:, sl], scalar1=-1.0, scalar2=None,
                                                                     op0=ALU.mult), reads=[PTB], writes=[B_tab])
            for tabl, base in ((EF, lb), (EI, lbi)):
                op("pool", lambda e: e.memset(tabl[:, 0, :, 0:1], 1.0), writes=[B_tab])
                op("pool", lambda e: e.memset(tabl[:, 1, :, 0:1], 0.0), writes=[B_tab])
                op("dve", lambda e: e.tensor_copy(out=pw[:], in_=base[:]), reads=[B_s], writes=[B_s])
                for k in range(7):
                    n = 1 << k
                    pbr = lambda c: pw[:, c, :].unsqueeze(2).to_broadcast([128, 16, n])
                    RW = dict(reads=[B_s, B_tab, B_et], writes=[B_tab, B_et])
                    op("dve", lambda e: e.tensor_tensor(out=et1[:, :, 0:n], in0=tabl[:, 0, :, 0:n], in1=pbr(0), op=ALU.mult), **RW)
                    op("dve", lambda e: e.tensor_tensor(out=et2[:, :, 0:n], in0=tabl[:, 1, :, 0:n], in1=pbr(1), op=ALU.mult), **RW)
                    op("dve", lambda e: e.tensor_tensor(out=tabl[:, 0, :, n:2 * n], in0=et1[:, :, 0:n], in1=et2[:, :, 0:n], op=ALU.subtract), **RW)
                    op("dve", lambda e: e.tensor_tensor(out=et1[:, :, 0:n], in0=tabl[:, 0, :, 0:n], in1=pbr(1), op=ALU.mult), **RW)
                    op("dve", lambda e: e.tensor_tensor(out=et2[:, :, 0:n], in0=tabl[:, 1, :, 0:n], in1=pbr(0), op=ALU.mult), **RW)
                    op("dve", lambda e: e.tensor_tensor(out=tabl[:, 1, :, n:2 * n], in0=et1[:, :, 0:n], in1=et2[:, :, 0:n], op=ALU.add), **RW)
                    op("dve", lambda e: e.tensor_tensor(out=tmp[:], in0=pw[:, 0, :], in1=pw[:, 0, :], op=ALU.mult), **RW)
                    op("dve", lambda e: e.tensor_tensor(out=tmp2[:], in0=pw[:, 1, :], in1=pw[:, 1, :], op=ALU.mult), **RW)
                    op("dve", lambda e: e.tensor_tensor(out=pw[:, 1, :], in0=pw[:, 0, :], in1=pw[:, 1, :], op=ALU.mult), **RW)
                    op("dve", lambda e: e.tensor_scalar(out=pw[:, 1, :], in0=pw[:, 1, :], scalar1=2.0, scalar2=None, op0=ALU.mult), **RW)
                    op("dve", lambda e: e.tensor_tensor(out=pw[:, 0, :], in0=tmp[:], in1=tmp2[:], op=ALU.subtract), **RW)
                if tabl is EF:
                    op("dve", lambda e: e.tensor_copy(out=L128[:], in_=pw[:]), reads=[B_s], writes=[B_tab])
            kb.barrier()
        kb.es = es5
        tA = [sb(f"tA{i}", [128, 4, 128], F32) for i in range(2)]
        Wt = [sb(f"Wt{i}", [128, 2, 128], F32) for i in range(2)]
        Z = [sb(f"Z{i}", [128, 2, 128], F32) for i in range(2)]
        Qp = [sb(f"Qp{i}", [128, 4, 128], BF16) for i in range(2)]
        ys = [sb(f"ys{i}", [128, 128], F32) for i in range(2)]
        yt = [sb(f"yt{i}", [128, 128], F32) for i in range(2)]
        sg = [sb(f"sg{i}", [128, 128], F32) for i in range(2)]
        ctmp = sb("ctmp", [128, 2, 16], F32)
        wacc = [sb(f"wacc{i}", [128, 2], F32) for i in range(2)]
        B_wacc = [buf("wacc0"), buf("wacc1")]
        B_tA, B_W, B_Z, B_Qp = [[buf(f"{n}{i}") for i in range(2)] for n in ("tA", "Wt", "Z", "Qp")]
        B_ys = [buf("ys0"), buf("ys1")]
        def stage_a_pe(c, pr):
            ct, pb = pr // 4, pr % 2
            csl = slice(c * 128, (c + 1) * 128)
            lts = (BBT[:, pr, 0, :], BBT[:, pr, 1, :], BBTn[:, pr, :], BBT[:, pr, 0, :])
            for q4, lt in enumerate(lts):
                op("pe", lambda e: e.matmul(PS[pb][:, q4 * 128:(q4 + 1) * 128], lhsT=lt, rhs=L.uT[:, ct, csl],
                                            start=True, stop=True), reads=[B_tab, L.B_uT[c]], writes=[PSB[pb]])

        def stage_a(c, pr):
            ct, pb = pr // 4, pr % 2
            bu = PS[pb][:, :].rearrange("p (k r b) -> p k r b", k=2, r=2)
            if c < Q0:
                op("dve", lambda e: e.tensor_tensor(out=tA[pb][:].rearrange("p (r k) b -> p k r b", r=2), in0=bu,
                                                     in1=EI[:, :, pr, :].unsqueeze(2).to_broadcast([128, 2, 2, 128]), op=ALU.mult),
                   reads=[PSB[pb], B_tab], writes=[B_tA[pb]])
                return
            op("dve", lambda e: e.tensor_tensor(out=tA[pb][:].rearrange("p (k r) b -> p k r b", k=2), in0=bu,
                                                 in1=EI[:, :, pr, :].unsqueeze(2).to_broadcast([128, 2, 2, 128]), op=ALU.mult),
               reads=[PSB[pb], B_tab], writes=[B_tA[pb]])
            op("pool", lambda e: e.tensor_tensor(out=Wt[pb][:], in0=tA[pb][:, 0:2, :], in1=tA[pb][:, 2:4, :], op=ALU.add),
               reads=[B_tA[pb]], writes=[B_W[pb]])

        def stage_b(c, pr):
            ct, prl, pb = pr // 4, pr % 4, pr % 2
            csl = slice(c * 128, (c + 1) * 128)
            for ri in (range(2) if c < Q0 else ()):
                op("act", lambda e: e.activation(out=tA[pb][:, 2 * ri:2 * ri + 2, :], in_=tA[pb][:, 2 * ri:2 * ri + 2, :], func=AF.Copy,
                                                 accum_out=wacc[pb][:, ri:ri + 1]), reads=[B_tA[pb]], writes=[B_tA[pb], B_wacc[pb]])
            if c < Q0:
                op("pool", lambda e: e.tensor_tensor(out=zl[:, :, pr:pr + 1], in0=wacc[pb][:, :].unsqueeze(2), in1=carry[:, :, pr:pr + 1], op=ALU.add),
                   reads=[B_wacc[pb], B_car], writes=[B_zl])
            for ri in (range(2) if c >= Q0 else ()):
                op("dve", lambda e: e.tensor_tensor_scan(out=Z[pb][:, ri, :], data0=ones[:], data1=Wt[pb][:, ri, :],
                                                         initial=carry[:, ri, pr:pr + 1], op0=ALU.mult, op1=ALU.add),
                   reads=[B_W[pb], B_tab, B_car], writes=[B_Z[pb]])
            if c >= Q0:
                op("pool", lambda e: e.tensor_copy(out=zl[:, :, pr:pr + 1], in_=Z[pb][:, :, 127:128]), reads=[B_Z[pb]], writes=[B_zl])
                q = Qp[pb]
                op("dve", lambda e: e.tensor_tensor(out=q[:].rearrange("p (k r) b -> p k r b", k=2),
                                                     in0=Z[pb][:].unsqueeze(1).to_broadcast([128, 2, 2, 128]),
                                                     in1=EF[:, :, pr, :].unsqueeze(2).to_broadcast([128, 2, 2, 128]), op=ALU.mult),
                   reads=[B_Z[pb], B_tab], writes=[B_Qp[pb]])
                yb = 2 + ct % 2
                for k in range(4):
                    op("pe", lambda e: e.matmul(PS[yb][:, 0:128], lhsT=Cq[:, pr, k, :], rhs=q[:, k, :],
                                                start=(prl == 0 and k == 0), stop=(prl == 3 and k == 3)),
                       reads=[B_tab, B_Qp[pb]], writes=[PSB[yb]])
                if prl == 3:
                    cb2 = ct % 2
                    op("dve", lambda e: e.scalar_tensor_tensor(out=ys[cb2][:], in0=L.uT[:, ct, csl], scalar=dsk[:, ct:ct + 1], in1=PS[yb][:, 0:128],
                                                                op0=ALU.mult, op1=ALU.add), reads=[PSB[yb], L.B_uT[c], B_tab], writes=[B_ys[cb2]])
                    op("pool", lambda e: e.tensor_tensor(out=yt[cb2][:], in0=ys[cb2][:], in1=ys[cb2][:], op=ALU.mult),
                       reads=[B_ys[cb2]], writes=[B_ys[cb2]])
                    op("pool", lambda e: e.tensor_scalar(out=yt[cb2][:], in0=yt[cb2][:], scalar1=0.044715, scalar2=1.0, op0=ALU.mult, op1=ALU.add),
                       reads=[B_ys[cb2]], writes=[B_ys[cb2]])
                    op("pool", lambda e: e.tensor_tensor(out=yt[cb2][:], in0=yt[cb2][:], in1=ys[cb2][:], op=ALU.mult),
                       reads=[B_ys[cb2]], writes=[B_ys[cb2]])
                    op("act", lambda e: e.activation(out=sg[cb2][:], in_=yt[cb2][:], func=AF.Sigmoid, scale=1.5957691216057308),
                       reads=[B_ys[cb2]], writes=[B_ys[cb2]])
                    op("pool", lambda e: e.tensor_tensor(out=L.uT[:, ct, csl], in0=ys[cb2][:], in1=sg[cb2][:], op=ALU.mult),
                       reads=[B_ys[cb2]], writes=[L.B_uT[c]])
            if pr == 15:
                RWc = dict(reads=[B_zl, B_tab, B_car], writes=[B_car])
                op("dve", lambda e: e.tensor_tensor(out=ctmp[:, 0, :], in0=L128[:, 0, :], in1=zl[:, 0, :], op=ALU.mult), **RWc)
                op("dve", lambda e: e.tensor_tensor(out=ctmp[:, 1, :], in0=L128[:, 1, :], in1=zl[:, 1, :], op=ALU.mult), **RWc)
                op("dve", lambda e: e.tensor_tensor(out=carry[:, 0, :], in0=ctmp[:, 0, :], in1=ctmp[:, 1, :], op=ALU.subtract), **RWc)
                op("dve", lambda e: e.tensor_tensor(out=ctmp[:, 0, :], in0=L128[:, 0, :], in1=zl[:, 1, :], op=ALU.mult), **RWc)
                op("dve", lambda e: e.tensor_tensor(out=ctmp[:, 1, :], in0=L128[:, 1, :], in1=zl[:, 0, :], op=ALU.mult), **RWc)
                op("dve", lambda e: e.tensor_tensor(out=carry[:, 1, :], in0=ctmp[:, 0, :], in1=ctmp[:, 1, :], op=ALU.add), **RWc)

        for k3 in range(3):
            kb.dma("sp", L.cw_g[:, :, k3], L.conv_w[k3:k3 + 1, :].rearrange("o (j p) -> p (o j)", p=128), L.B_cwg, writes=[L.B_cwg],
                   allow_slow_non_contiguous=True)
        kb.dma("sp", L.cb_g[:], L.conv_b.rearrange("o (j p) -> p (o j)", p=128), L.B_cwg, writes=[L.B_cwg], allow_slow_non_contiguous=True)
        stgp = sb("stgp", [128, 8, 128], F32)
        B_stgp = buf("stgp")

        def pre_step(dst_fn, src_ap, rows_kc):
            kb.dma("sp", stgp[:, 0:rows_kc, :], src_ap.rearrange("(kc p) n -> p kc n", p=128), B_stgp, writes=[B_stgp])
            for kc in range(rows_kc):
                op("act", lambda e: e.activation(out=dst_fn(kc), in_=stgp[:, kc, :], func=AF.Copy), reads=[B_stgp], writes=[L.B_Wpre])

        pre_steps = []
        for q in range(16):
            pre_steps.append((lambda kc, q=q: L.WMp[:, kc, q * 128:(q + 1) * 128], L.w_in[:, 1816 + q * 128:1816 + (q + 1) * 128], 8))
        for q in range(16):
            pre_steps.append((lambda kc, q=q: L.WGLp[:, kc, q * 128:(q + 1) * 128], L.w_glu[:, q * 128:(q + 1) * 128], 4))
        for q in range(8):
            pre_steps.append((lambda kc, q=q: L.WOp[:, kc, q * 128:(q + 1) * 128], L.w_o[:, q * 128:(q + 1) * 128], 8))
        seq = [(c, pr) for c in range(NT) for pr in range(16)]
        stage_a_pe(*seq[0])
        stage_a_pe(*seq[1])
        stage_a(*seq[0])
        for k in range(len(seq)):
            if seq[k][1] in (0, 4, 8, 12) and seq[k][0] >= Q0 + 1 and pre_steps:
                pre_step(*pre_steps.pop(0))
            if k + 2 < len(seq):
                stage_a_pe(*seq[k + 2])
            if k + 1 < len(seq):
                stage_a(*seq[k + 1])
            stage_b(*seq[k])
        while pre_steps:
            pre_step(*pre_steps.pop(0))
        if "gyT" in L.dbg_d:
            b = kb.buf("dbg")
            st2 = sb("dbgst5", [128, 4, 512], F32)
            op("dve", lambda e: e.tensor_copy(out=st2[:], in_=L.uT[:, :, 0:512]), reads=L.B_uT, writes=[b])
            kb.dma("sp", L.dbg_d["gyT"], st2[:], b, reads=[b])
        kb.barrier()
    kb.es = L.esA


def ln_affine_store(kb, L, pre, B_pre, stats, mv, rstd, B_st, gb, B_gb, dst_ap, q="sp"):
    op = kb.op
    for hf in range(2):
        op("dve", lambda e: e.bn_stats(out=stats[:, hf, :], in_=pre[:, hf * 512:(hf + 1) * 512]), reads=[B_pre], writes=[B_st])
    op("dve", lambda e: e.bn_aggr(out=mv[:], in_=stats[:]), reads=[B_st], writes=[B_st])
    op("dve", lambda e: e.tensor_scalar(out=rstd[:], in0=mv[:, 1:2], scalar1=1e-5, scalar2=None, op0=ALU.add),
       reads=[B_st], writes=[B_st])
    op("act", lambda e: e.activation(out=rstd[:], in_=rstd[:], func=AF.Sqrt), reads=[B_st], writes=[B_st])
    op("dve", lambda e: e.reciprocal(out=rstd[:], in_=rstd[:]), reads=[B_st], writes=[B_st])
    op("dve", lambda e: e.tensor_scalar(out=pre[:], in0=pre[:], scalar1=mv[:, 0:1], scalar2=rstd[:, 0:1], op0=ALU.subtract, op1=ALU.mult),
       reads=[B_st], writes=[B_pre])
    op("pool", lambda e: e.tensor_tensor(out=pre[:], in0=pre[:], in1=gb[:, 0, :], op=ALU.mult), reads=[B_gb], writes=[B_pre])
    op("pool", lambda e: e.tensor_tensor(out=pre[:], in0=pre[:], in1=gb[:, 1, :], op=ALU.add), reads=[B_gb], writes=[B_pre])
    kb.dma(q, dst_ap, pre[:], B_pre, reads=[B_pre])


def pass_a2(nc, kb, Ld):
    L = NS(Ld)
    op = kb.op
    PS, PSB, PT, PTB = L.PS, L.PSB, L.PT, L.PTB
    sb, buf = kb.sb, kb.buf
    with ExitStack() as es2:
        kb.es = es2
        WM, WGL, WO = L.WMp, L.WGLp, L.WOp
        WNO = sb("WNO", [128, 4, D], BF16)
        gb = sb("gb1", [128, 2, D], F32)
        B_W, B_gb = L.B_Wpre, buf("gb1")
        with ExitStack() as ess:
            kb.es = ess
            stgs = [sb(f"stg2{i}", [128, 8, 512], F32) for i in range(2)]
            B_stgs = [buf(f"stg2{i}") for i in range(2)]
            for q2 in range(2):
                L.load_cast(lambda kc: WNO[:, kc, q2 * 512:(q2 + 1) * 512], L.w_nsa_out[:, q2 * 512:(q2 + 1) * 512], 4, 512,
                            stgs[q2], B_stgs[q2], B_W)
            kb.dma("sp", gb[:, 0, :], L.ln1_g.partition_broadcast(128).rearrange("p o n -> p (o n)"), B_gb, writes=[B_gb])
            kb.dma("sp", gb[:, 1, :], L.ln1_b.partition_broadcast(128).rearrange("p o n -> p (o n)"), B_gb, writes=[B_gb])
            kb.barrier()
        kb.es = es2
        xts = [[sb(f"xt2{k}{i}", [128, D], F32) for i in range(4)] for k in range(2)]
        B_xts = [[buf(f"xt2{k}{i}") for i in range(4)] for k in range(2)]
        xn = sb("xn2", [128, D], BF16); B_xn = buf("xn2")
        stats = sb("stats2", [128, 2, 6], F32); mv = sb("mv2", [128, 2], F32); rstd = sb("rstd2", [128, 1], F32)
        B_st = buf("st2")
        hTs2 = [sb(f"hT2{k}", [128, 8, 512], BF16) for k in range(2)]; B_hTs2 = [buf("hT20"), buf("hT21")]
        sga = sb("sga", [128, 3, 512], F32); B_sg = [buf("sga0"), buf("sga1")]
        t1 = sb("t1", [128, 512], F32); t2 = sb("t2", [128, 512], F32); B_t = buf("t12")
        mixT = sb("mixT", [128, 8, 512], BF16); B_mix = buf("mixT")
        gtmp, B_gt = [t1, t2], [B_t, B_t]
        steps = [[0]] + [list(range(r, r + 4)) for r in range(1, NQ, 4)]
        modT, ident_b = L.modT, L.ident_b

        def ln_a2(x_, B_x):
            for hf in range(2):
                op("dve", lambda e: e.bn_stats(out=stats[:, hf, :], in_=x_[:, hf * 512:(hf + 1) * 512]), reads=[B_x], writes=[B_st])
            op("dve", lambda e: e.bn_aggr(out=mv[:], in_=stats[:]), reads=[B_st], writes=[B_st])
            op("dve", lambda e: e.tensor_scalar(out=rstd[:], in0=mv[:, 1:2], scalar1=1e-5, scalar2=None, op0=ALU.add), reads=[B_st], writes=[B_st])
            op("act", lambda e: e.activation(out=rstd[:], in_=rstd[:], func=AF.Sqrt), reads=[B_st], writes=[B_st])
            op("dve", lambda e: e.reciprocal(out=rstd[:], in_=rstd[:]), reads=[B_st], writes=[B_st])
            op("dve", lambda e: e.tensor_scalar(out=xn[:], in0=x_[:], scalar1=mv[:, 0:1], scalar2=rstd[:, 0:1], op0=ALU.subtract, op1=ALU.mult),
               reads=[B_x, B_st], writes=[B_xn])

        def ln_b2(dst, B_dst):
            for kc in range(KC):
                op("pe", lambda e: e.transpose(out=PT[:, kc * 128:(kc + 1) * 128], in_=xn[:, kc * 128:(kc + 1) * 128], identity=ident_b[:]),
                   reads=[B_xn, L.B_id], writes=[PTB])
            for kc in range(KC):
                op("act", lambda e: e.activation(out=dst[:, kc, :], in_=PT[:, kc * 128:(kc + 1) * 128], func=AF.Identity,
                                                 scale=modT[:, 8 + kc:9 + kc], bias=modT[:, kc:kc + 1]), reads=[PTB, L.B_modT], writes=[B_dst])

        def load_step(si):
            for tt, r in enumerate(steps[si]):
                i = Q0 + r
                kb.dma("sp", xts[si % 2][tt][:], L.x_d[i * 128:(i + 1) * 128, :], B_xts[si % 2][tt], writes=[B_xts[si % 2][tt]])

        load_step(0)
        for tt in range(len(steps[0])):
            ln_a2(xts[0][tt], B_xts[0][tt])
            ln_b2(hTs2[0][:, :, tt * 128:(tt + 1) * 128], B_hTs2[0])
        for si, rs in enumerate(steps):
            NTOK = 128 * len(rs)
            r0 = rs[0]
            xt, B_xt = xts[si % 2], B_xts[si % 2]
            hT, B_hT = hTs2[si % 2], B_hTs2[si % 2]
            nxt = steps[si + 1] if si + 1 < len(steps) else []
            if nxt:
                load_step(si + 1)
            osl = slice(r0 * 128, r0 * 128 + NTOK)
            tsl = slice((Q0 + r0) * 128, (Q0 + r0) * 128 + NTOK)
            B_us = [L.B_uT[Q0 + r] for r in rs]
            B_os = [L.B_oaT[Q0 + r] for r in rs]
            for fc in range(8):
                fsl = slice(fc * 128, (fc + 1) * 128)
                for half in range(2):
                    for kc in range(KC):
                        op("pe", lambda e: e.matmul(PS[half][:, 0:NTOK], lhsT=WM[:, kc, half * D + fc * 128:half * D + (fc + 1) * 128],
                                                    rhs=hT[:, kc, 0:NTOK], start=(kc == 0), stop=(kc == KC - 1)), reads=[B_W, B_hT], writes=[PSB[half]])
                    op("act", lambda e: e.activation(out=sga[:, half, 0:NTOK], in_=PS[half][:, 0:NTOK], func=AF.Sigmoid),
                       reads=[PSB[half]], writes=[B_sg[0]])
                for c in range(4):
                    op("pe", lambda e: e.matmul(PS[2][:, 0:NTOK], lhsT=WNO[:, c, fsl], rhs=L.oaT[:, c, osl], start=(c == 0), stop=(c == 3)),
                       reads=[B_W] + B_os, writes=[PSB[2]])
                op("dve", lambda e: e.tensor_tensor(out=t1[:, 0:NTOK], in0=sga[:, 0, 0:NTOK], in1=PS[2][:, 0:NTOK], op=ALU.mult),
                   reads=[B_sg[0], PSB[2]], writes=[B_t])
                for half in range(2):
                    for c in range(4):
                        op("pe", lambda e: e.matmul(PS[3 + half][:, 0:NTOK], lhsT=WGL[:, c, half * D + fc * 128:half * D + (fc + 1) * 128],
                                                    rhs=L.uT[:, c, tsl], start=(c == 0), stop=(c == 3)), reads=[B_W] + B_us, writes=[PSB[3 + half]])
                op("act", lambda e: e.activation(out=sga[:, 2, 0:NTOK], in_=PS[4][:, 0:NTOK], func=AF.Sigmoid), reads=[PSB[4]], writes=[B_sg[1]])
                op("dve", lambda e: e.tensor_tensor(out=t2[:, 0:NTOK], in0=sga[:, 2, 0:NTOK], in1=PS[3][:, 0:NTOK], op=ALU.mult),
                   reads=[B_sg[1], PSB[3]], writes=[B_t])
                op("dve", lambda e: e.tensor_tensor(out=t2[:, 0:NTOK], in0=t2[:, 0:NTOK], in1=sga[:, 1, 0:NTOK], op=ALU.mult),
                   reads=[B_sg[0]], writes=[B_t])
                op("dve", lambda e: e.tensor_tensor(out=mixT[:, fc, 0:NTOK], in0=t1[:, 0:NTOK], in1=t2[:, 0:NTOK], op=ALU.add),
                   reads=[B_t], writes=[B_mix])
                tn = fc // 2
                if tn < len(nxt):
                    if fc % 2 == 0:
                        ln_a2(xts[(si + 1) % 2][tn], B_xts[(si + 1) % 2][tn])
                    else:
                        ln_b2(hTs2[(si + 1) % 2][:, :, tn * 128:(tn + 1) * 128], B_hTs2[(si + 1) % 2])
            for tt, r in enumerate(rs):
                pr_, B_p = xt[tt], B_xt[tt]
                for half in range(2):
                    for fc in range(8):
                        op("pe", lambda e: e.matmul(PS[5 + half][:, :], lhsT=mixT[:, fc, tt * 128:(tt + 1) * 128], rhs=WO[:, fc, half * 512:(half + 1) * 512],
                                                    start=(fc == 0), stop=(fc == 7)), reads=[B_W, B_mix], writes=[PSB[5 + half]])
                    hs = slice(half * 512, (half + 1) * 512)
                    op("dve", lambda e: e.tensor_tensor(out=gtmp[half][:], in0=PS[5 + half][:, :], in1=L.gates_bc[:, 0, hs], op=ALU.mult),
                       reads=[PSB[5 + half], L.B_gbc], writes=[B_gt[half]])
                    op("dve", lambda e: e.scalar_tensor_tensor(out=pr_[:, hs], in0=pr_[:, hs], scalar=ALPHA, in1=gtmp[half][:], op0=ALU.mult, op1=ALU.add),
                       reads=[B_gt[half]], writes=[B_p])
                ln_affine_store(kb, L, pr_, B_p, stats, mv, rstd, B_st, gb, B_gb, L.x1_d[r * 128:(r + 1) * 128, :])
        kb.barrier()
    kb.es = L.esA


def pass_b(nc, kb, Ld):
    L = NS(Ld)
    op = kb.op
    PS, PSB, PT, PTB = L.PS, L.PSB, L.PT, L.PTB
    sb, buf = kb.sb, kb.buf
    NJ = 44
    WUP = sb("WUP", [128, 8, 2 * DFF], BF16)
    WDN = sb("WDN", [128, 22, D], BF16)
    cw, cb = L.cw_g, L.cb_g
    gb = sb("gb2", [128, 2, D], F32)
    halo = sb("halo", [128, NJ, 2], F32)
    B_W, B_gb, B_cw, B_halo = buf("WB"), buf("gb2"), L.B_cwg, buf("halo")
    with ExitStack() as ess:
        kb.es = ess
        stgs = [sb(f"stgb{i}", [128, 8, 512], F32) for i in range(3)]
        B_stgs = [buf(f"stgb{i}") for i in range(3)]
        for q in range(11):
            L.load_cast(lambda kc: WUP[:, kc, q * 512:(q + 1) * 512], L.w_up[:, q * 512:(q + 1) * 512], 8, 512, stgs[q % 3], B_stgs[q % 3], B_W)
        for half in range(2):
            for q in range(3):
                stg, B_stg = stgs[(half * 3 + q + 2) % 3], B_stgs[(half * 3 + q + 2) % 3]
                r0, nr = q * 8, (8 if q < 2 else 6)
                kb.dma("sp", stg[:, 0:nr, :], L.w_down[r0 * 128:(r0 + nr) * 128, half * 512:(half + 1) * 512].rearrange("(kc p) n -> p kc n", p=128),
                       B_stg, writes=[B_stg])
                for kc in range(nr):
                    op("dve", lambda e: e.tensor_copy(out=WDN[:, r0 + kc, half * 512:(half + 1) * 512], in_=stg[:, kc, :]),
                       reads=[B_stg], writes=[B_W])
        kb.dma("sp", gb[:, 0, :], L.ln2_g.partition_broadcast(128).rearrange("p o n -> p (o n)"), B_gb, writes=[B_gb])
        kb.dma("sp", gb[:, 1, :], L.ln2_b.partition_broadcast(128).rearrange("p o n -> p (o n)"), B_gb, writes=[B_gb])
        op("pool", lambda e: e.memset(halo[:], 0.0), writes=[B_halo])
        kb.barrier()
    kb.es = L.esB
    NTOK = 512
    x1t = [sb(f"x1t{i}", [128, D], F32) for i in range(4)]
    B_x1 = [buf(f"x1t{i}") for i in range(4)]
    xn = sb("xnb", [128, D], BF16); B_xn = buf("xnb")
    stats = sb("statsb", [128, 2, 6], F32); mv = sb("mvb", [128, 2], F32); rstd = sb("rstdb", [128, 1], F32)
    B_st = buf("stb")
    h2T = sb("h2T", [128, 8, NTOK], BF16); B_h2 = buf("h2T")
    upb = [sb(f"upb{i}", [128, NTOK + 2], F32) for i in range(2)]
    acc = [sb(f"acc{i}", [128, NTOK], F32) for i in range(2)]
    B_up = [buf("up0"), buf("up1")]
    B_acc = [buf("acc0"), buf("acc1")]
    ffT = sb("ffT", [128, 22, NTOK], BF16); B_ff = buf("ffT")
    gtmp, B_gt = acc, B_acc
    steps = [[0]] + [list(range(r, r + 4)) for r in range(1, NQ, 4)]
    for s, rs in enumerate(steps):
        NTOK = 128 * len(rs)
        for tt, i in enumerate(rs):
            kb.dma("sp", x1t[tt][:], L.x1_d[i * 128:(i + 1) * 128, :], B_x1[tt], writes=[B_x1[tt]])
            L.ln_to_hT(x1t[tt], B_x1[tt], xn, B_xn, stats, mv, rstd, B_st, h2T[:, :, tt * 128:(tt + 1) * 128], B_h2, 16)
        for j in range(22):
            for w, ch in enumerate((j, j + 22)):
                pbk = (2 * j + w) % 4
                for kc in range(KC):
                    op("pe", lambda e: e.matmul(PS[pbk][:, 0:NTOK], lhsT=WUP[:, kc, ch * 128:(ch + 1) * 128], rhs=h2T[:, kc, 0:NTOK],
                                                start=(kc == 0), stop=(kc == KC - 1)), reads=[B_W, B_h2], writes=[PSB[pbk]])
                u_, B_u, a_, B_a = upb[w], B_up[w], acc[w], B_acc[w]
                op("pool", lambda e: e.tensor_copy(out=u_[:, 0:2], in_=halo[:, ch, :]), reads=[B_halo], writes=[B_u])
                op("act", lambda e: e.activation(out=u_[:, 2:NTOK + 2], in_=PS[pbk][:, 0:NTOK], func=AF.Copy), reads=[PSB[pbk]], writes=[B_u])
                op("pool", lambda e: e.tensor_copy(out=halo[:, ch, :], in_=u_[:, NTOK:NTOK + 2]), reads=[B_u], writes=[B_halo])
                op("act", lambda e: e.activation(out=a_[:, 0:NTOK], in_=u_[:, 2:NTOK + 2], func=AF.Identity, scale=cw[:, ch, 2:3], bias=cb[:, ch:ch + 1]),
                   reads=[B_u, B_cw], writes=[B_a])
                op("dve", lambda e: e.scalar_tensor_tensor(out=a_[:, 0:NTOK], in0=u_[:, 1:NTOK + 1], scalar=cw[:, ch, 1:2], in1=a_[:, 0:NTOK],
                                                            op0=ALU.mult, op1=ALU.add), reads=[B_u, B_cw], writes=[B_a])
                op("dve", lambda e: e.scalar_tensor_tensor(out=a_[:, 0:NTOK], in0=u_[:, 0:NTOK], scalar=cw[:, ch, 0:1], in1=a_[:, 0:NTOK],
                                                            op0=ALU.mult, op1=ALU.add), reads=[B_u, B_cw], writes=[B_a])
            op("act", lambda e: e.activation(out=upb[1][:, 0:NTOK], in_=acc[1][:, 0:NTOK], func=AF.Silu), reads=[B_acc[1]], writes=[B_up[1]])
            op("dve", lambda e: e.tensor_tensor(out=ffT[:, j, 0:NTOK], in0=upb[1][:, 0:NTOK], in1=acc[0][:, 0:NTOK], op=ALU.mult),
               reads=[B_up[1], B_acc[0]], writes=[B_ff])
        if s == 0:
            op("pool", lambda e: e.tensor_scalar(out=halo[:].rearrange("p a b -> p (a b)"), in0=halo[:].rearrange("p a b -> p (a b)"),
                                                  scalar1=L.hv[:, 0:1], scalar2=None, op0=ALU.mult), reads=[L.B_tv], writes=[B_halo])
        for tt, i in enumerate(rs):
            pr_, B_p = x1t[tt], B_x1[tt]
            for half in range(2):
                bk = ((4, 5), (6, 0))[tt % 2][half]
                for j in range(22):
                    op("pe", lambda e: e.matmul(PS[bk][:, :], lhsT=ffT[:, j, tt * 128:(tt + 1) * 128], rhs=WDN[:, j, half * 512:(half + 1) * 512],
                                                start=(j == 0), stop=(j == 21)), reads=[B_W, B_ff], writes=[PSB[bk]])
                hs = slice(half * 512, (half + 1) * 512)
                op("dve", lambda e: e.tensor_tensor(out=gtmp[half][:], in0=PS[bk][:, :], in1=L.gates_bc[:, 1, hs], op=ALU.mult),
                   reads=[PSB[bk], L.B_gbc], writes=[B_gt[half]])
                op("pool", lambda e: e.scalar_tensor_tensor(out=pr_[:, hs], in0=pr_[:, hs], scalar=ALPHA, in1=gtmp[half][:], op0=ALU.mult, op1=ALU.add),
                   reads=[B_gt[half]], writes=[B_p]) if False else \
                op("dve", lambda e: e.scalar_tensor_tensor(out=pr_[:, hs], in0=pr_[:, hs], scalar=ALPHA, in1=gtmp[half][:], op0=ALU.mult, op1=ALU.add),
                   reads=[B_gt[half]], writes=[B_p])
            if i >= 1:
                ln_affine_store(kb, L, pr_, B_p, stats, mv, rstd, B_st, gb, B_gb, L.out_d[(i - 1) * 128:i * 128, :])


_NC = [None]


def kernel(**inputs):
    f = lambda a: np.ascontiguousarray(np.asarray(a, dtype=np.float32))
    if _NC[0] is None:
        _NC[0] = build()
    nc = _NC[0]
    tabs = [make_tables(0), make_tables(1)]
    shared = {
        "w_ada": f(inputs["w_ada"][0]), "b_ada": f(inputs["b_ada"][0]).reshape(1, -1), "w_in": f(inputs["w_in"][0]),
        "pe_ck": f(inputs["pe_ck"][0]), "w_ck1": f(inputs["w_ck1"][0]), "w_ck2": f(inputs["w_ck2"][0]),
        "pe_cv": f(inputs["pe_cv"][0]), "w_cv1": f(inputs["w_cv1"][0]), "w_cv2": f(inputs["w_cv2"][0]),
        "w_nsa_out": f(inputs["w_nsa_out"][0]),
        "s5_a_re": f(inputs["s5_a_re"][0]), "s5_a_im": f(inputs["s5_a_im"][0]),
        "s5_b_re": f(inputs["s5_b_re"][0]), "s5_b_im": f(inputs["s5_b_im"][0]),
        "s5_c_re": f(inputs["s5_c_re"][0]), "s5_c_im": f(inputs["s5_c_im"][0]),
        "s5_d": f(inputs["s5_d"][0]).reshape(512, 1), "s5_log_dt": f(inputs["s5_log_dt"][0]).reshape(1, 32),
        "w_s5_glu": f(inputs["w_s5_glu"][0]), "w_o": f(inputs["w_o"][0]),
        "ln1_g": f(inputs["ln1_g"][0]).reshape(1, -1), "ln1_b": f(inputs["ln1_b"][0]).reshape(1, -1),
        "w_up": f(inputs["w_up"][0]), "conv_w": f(inputs["conv_w"][0]), "conv_b": f(inputs["conv_b"][0]).reshape(1, -1),
        "w_down": f(inputs["w_down"][0]),
        "ln2_g": f(inputs["ln2_g"][0]).reshape(1, -1), "ln2_b": f(inputs["ln2_b"][0]).reshape(1, -1),
    }
    tabf = [{"tb_" + k: f(v) for k, v in tabs[hf].items()} for hf in range(2)]
    x = np.asarray(inputs["x"], dtype=np.float32)
    c = np.asarray(inputs["c"], dtype=np.float32)
    in_maps = []
    for core in range(8):
        b, hf = core // 2, core % 2
        m = dict(shared)
        m.update(tabf[hf])
        if hf == 1:
            m["x"] = f(x[b])
        else:
            xp = np.zeros((T, D), np.float32)
            xp[T // 2:] = x[b][:T // 2]
            m["x"] = xp
        m["cvec"] = f(c[b].reshape(8, 128).T)
        in_maps.append(m)
    res = run_bass_kernel_spmd(nc, in_maps, core_ids=list(range(8)))
    kernel.last = res
    out = np.empty((4, T, D), np.float32)
    for core in range(8):
        b, hf = core // 2, core % 2
        out[b, hf * (T // 2):(hf + 1) * (T // 2)] = np.asarray(res.results[core]["out"], dtype=np.float32)
    return out
```
